# Optimizing a Trainium2 kernel written in Bass

```python
import math
import jax, jax.numpy as jnp
from jax import lax
import numpy as np

D_MODEL = 1024
BATCH = 2
SEQ = 8192
DEPTH = 4

GRID_W = 64
CTX_LEN = 256
HEAD_DIM = 64
D_FF = 4 * D_MODEL
NORM_EPS = 1e-6
ROPE_BASE = 10000.0
NEG_INF = -1e30

GROUP_W = D_MODEL // 4
D_MIX = 4 * GROUP_W

NA_HEADS = GROUP_W // HEAD_DIM
WIN_H = 8
WIN_W = 16
NA_QBLK_W = 16
NA_KSPAN_W = 32

SWA_HEADS = GROUP_W // HEAD_DIM
SWA_KV_HEADS = 2
SWA_WINDOW = 128
SWA_BLOCK = 128

MLA_HEADS = 4
MLA_Q_LORA = 256
MLA_KV_LORA = 128
MLA_NOPE = 64
MLA_ROPE = 32
MLA_V = GROUP_W // MLA_HEADS
MLA_BLOCK = 128

SSD_INNER = GROUP_W
SSD_HEAD_DIM = 64
SSD_HEADS = SSD_INNER // SSD_HEAD_DIM
SSD_GROUPS = 2
SSD_STATE = 128
SSD_CONV = 5
SSD_CHUNK = 128
SSD_CONV_CH = SSD_INNER + 2 * SSD_GROUPS * SSD_STATE

NA_IN = 3 * NA_HEADS * HEAD_DIM
SWA_IN = (SWA_HEADS + 2 * SWA_KV_HEADS) * HEAD_DIM
MLA_IN = MLA_Q_LORA + MLA_KV_LORA + MLA_ROPE
SSD_IN = SSD_INNER + SSD_CONV_CH + 2 * SSD_HEADS
N_IN = NA_IN + SWA_IN + MLA_IN + SSD_IN
IN_SPLITS = (NA_IN, NA_IN + SWA_IN, NA_IN + SWA_IN + MLA_IN)

kernel_name = "hybrid_parallel_groups_dit_block"


def rmsnorm(x, g):
    xf = x.astype(jnp.float32)
    y = xf * lax.rsqrt(jnp.mean(xf * xf, axis=-1, keepdims=True) + NORM_EPS)
    return (y * g.astype(jnp.float32)).astype(x.dtype)


def modulate(x, shift, scale):
    return x * (1 + scale) + shift


def axial_angles(n_tok, dim):
    nf = dim // 4
    inv = 1.0 / (ROPE_BASE ** (jnp.arange(nf, dtype=jnp.float32) / nf))
    t = jnp.arange(n_tok)
    row = (t // GRID_W).astype(jnp.float32)
    col = (t % GRID_W).astype(jnp.float32)
    return row[:, None] * inv, col[:, None] * inv


def rope_half(x, ang):
    nf = ang.shape[-1]
    cos = jnp.cos(ang)[:, None, :]
    sin = jnp.sin(ang)[:, None, :]
    x1, x2 = x[..., :nf], x[..., nf:]
    return jnp.concatenate([x1 * cos - x2 * sin, x2 * cos + x1 * sin], axis=-1).astype(x.dtype)


def axial_rope(x, ang):
    ang_r, ang_c = ang
    half = x.shape[-1] // 2
    return jnp.concatenate([rope_half(x[..., :half], ang_r), rope_half(x[..., half:], ang_c)], axis=-1)


def context_attention(q, k, v, scale, sink=None):
    rep = q.shape[2] // k.shape[2]
    k = jnp.repeat(k, rep, axis=2)
    v = jnp.repeat(v, rep, axis=2)
    s = jnp.einsum('bqhd,bkhd->bhqk', q, k).astype(jnp.float32) * scale
    n_keys = s.shape[-1]
    if sink is not None:
        s_sink = jnp.broadcast_to(sink.astype(jnp.float32)[None, :, None, None], s.shape[:-1] + (1,))
        s = jnp.concatenate([s, s_sink], axis=-1)
    p = jax.nn.softmax(s, axis=-1)[..., :n_keys].astype(v.dtype)
    o = jnp.einsum('bhqk,bkhd->bqhd', p, v)
    return o.reshape(o.shape[0], o.shape[1], -1)


def neighbourhood_attention(p_lat, p_ctx, rpb, need_ctx):
    Bsz, S, _ = p_lat.shape
    Lc = p_ctx.shape[1]
    rows = S // GRID_W
    kh = min(WIN_H, rows)
    nb = GRID_W // NA_QBLK_W
    nk = kh * NA_KSPAN_W
    scale = HEAD_DIM ** -0.5
    q, k, v = [t.reshape(Bsz, S, NA_HEADS, HEAD_DIM) for t in jnp.split(p_lat, 3, axis=-1)]
    qc, kc, vc = [t.reshape(Bsz, Lc, NA_HEADS, HEAD_DIM) for t in jnp.split(p_ctx, 3, axis=-1)]
    r = jnp.arange(rows)
    row_idx = jnp.clip(r - kh // 2, 0, rows - kh)[:, None] + jnp.arange(kh)
    j = jnp.arange(nb)
    col_idx = jnp.clip(j * NA_QBLK_W - WIN_W // 2, 0, GRID_W - NA_KSPAN_W)[:, None] + jnp.arange(NA_KSPAN_W)

    def gather(t):
        t = t.reshape(Bsz, rows, GRID_W, NA_HEADS, HEAD_DIM)[:, row_idx][:, :, :, col_idx]
        t = t.transpose(0, 1, 3, 2, 4, 5, 6)
        return t.reshape(Bsz, rows, nb, nk, NA_HEADS, HEAD_DIM)

    kb, vb = gather(k), gather(v)
    qb = q.reshape(Bsz, rows, nb, NA_QBLK_W, NA_HEADS, HEAD_DIM)
    qcol = j[:, None] * NA_QBLK_W + jnp.arange(NA_QBLK_W)
    cstart = jnp.clip(qcol - WIN_W // 2, 0, GRID_W - WIN_W)
    kcol = col_idx[:, None, :]
    valid = (kcol >= cstart[..., None]) & (kcol < cstart[..., None] + WIN_W)
    valid = jnp.broadcast_to(valid[:, :, None, :], (nb, NA_QBLK_W, kh, NA_KSPAN_W)).reshape(nb, NA_QBLK_W, nk)
    dr = row_idx - r[:, None] + (WIN_H - 1)
    dc = jnp.clip(kcol - qcol[..., None] + (WIN_W - 1), 0, 2 * WIN_W - 2)
    bias = rpb[:, dr[:, None, None, :, None], dc[None, :, :, None, :]]
    bias = bias.reshape(NA_HEADS, rows, nb, NA_QBLK_W, nk).transpose(1, 2, 0, 3, 4)
    s = jnp.einsum('brnqhd,brnkhd->brnhqk', qb, kb).astype(jnp.float32) * scale + bias.astype(jnp.float32)
    s = jnp.where(valid[None, None, :, None], s, NEG_INF)
    s_ctx = jnp.einsum('brnqhd,bkhd->brnhqk', qb, kc).astype(jnp.float32) * scale
    p = jax.nn.softmax(jnp.concatenate([s, s_ctx], axis=-1), axis=-1).astype(v.dtype)
    o = (jnp.einsum('brnhqk,brnkhd->brnqhd', p[..., :nk], vb)
         + jnp.einsum('brnhqk,bkhd->brnqhd', p[..., nk:], vc))
    o = o.reshape(Bsz, S, NA_HEADS * HEAD_DIM)
    o_ctx = context_attention(qc, kc, vc, scale) if need_ctx else None
    return o, o_ctx


def window_attention(p_lat, p_ctx, sink, ang, need_ctx):
    Bsz, S, _ = p_lat.shape
    R = SWA_HEADS // SWA_KV_HEADS
    nblk = S // SWA_BLOCK
    scale = HEAD_DIM ** -0.5

    def heads(p):
        L = p.shape[1]
        q, k, v = jnp.split(p, [SWA_HEADS * HEAD_DIM, (SWA_HEADS + SWA_KV_HEADS) * HEAD_DIM], axis=-1)
        return (q.reshape(Bsz, L, SWA_HEADS, HEAD_DIM), k.reshape(Bsz, L, SWA_KV_HEADS, HEAD_DIM),
                v.reshape(Bsz, L, SWA_KV_HEADS, HEAD_DIM))

    q, k, v = heads(p_lat)
    q, k = axial_rope(q, ang), axial_rope(k, ang)
    qc, kc, vc = heads(p_ctx)
    nc = kc.shape[1]
    qb = q.reshape(Bsz, nblk, SWA_BLOCK, SWA_KV_HEADS, R, HEAD_DIM)

    def band(t):
        tp = jnp.pad(t, ((0, 0), (SWA_BLOCK, SWA_BLOCK), (0, 0), (0, 0)))
        tp = tp.reshape(Bsz, nblk + 2, SWA_BLOCK, SWA_KV_HEADS, HEAD_DIM)
        return jnp.concatenate([tp[:, :-2], tp[:, 1:-1], tp[:, 2:]], axis=2)

    kb, vb = band(k), band(v)
    nk = 3 * SWA_BLOCK
    qpos = jnp.arange(nblk)[:, None] * SWA_BLOCK + jnp.arange(SWA_BLOCK)
    kpos = (jnp.arange(nblk)[:, None] - 1) * SWA_BLOCK + jnp.arange(nk)
    valid = ((jnp.abs(qpos[:, :, None] - kpos[:, None, :]) <= SWA_WINDOW)
             & (kpos[:, None, :] >= 0) & (kpos[:, None, :] < S))
    s = jnp.einsum('bnqgrd,bnkgd->bngrqk', qb, kb).astype(jnp.float32) * scale
    s = jnp.where(valid[None, :, None, None], s, NEG_INF)
    s_ctx = jnp.einsum('bnqgrd,bkgd->bngrqk', qb, kc).astype(jnp.float32) * scale
    s_sink = jnp.broadcast_to(sink.astype(jnp.float32).reshape(1, 1, SWA_KV_HEADS, R, 1, 1), s.shape[:-1] + (1,))
    p = jax.nn.softmax(jnp.concatenate([s, s_ctx, s_sink], axis=-1), axis=-1).astype(v.dtype)
    o = (jnp.einsum('bngrqk,bnkgd->bnqgrd', p[..., :nk], vb)
         + jnp.einsum('bngrqk,bkgd->bnqgrd', p[..., nk:nk + nc], vc))
    o = o.reshape(Bsz, S, SWA_HEADS * HEAD_DIM)
    o_ctx = context_attention(qc, kc, vc, scale, sink) if need_ctx else None
    return o, o_ctx


def latent_attention(p_lat, p_ctx, g_q, g_kv, w_uq, w_ukv, ang, need_ctx):
    Bsz, S, _ = p_lat.shape
    scale = (MLA_NOPE + MLA_ROPE) ** -0.5

    def project(p, rotate):
        L = p.shape[1]
        cq, ckv, kr = jnp.split(p, [MLA_Q_LORA, MLA_Q_LORA + MLA_KV_LORA], axis=-1)
        q = (rmsnorm(cq, g_q) @ w_uq).reshape(Bsz, L, MLA_HEADS, MLA_NOPE + MLA_ROPE)
        kv = (rmsnorm(ckv, g_kv) @ w_ukv).reshape(Bsz, L, MLA_HEADS, MLA_NOPE + MLA_V)
        qn, qr = q[..., :MLA_NOPE], q[..., MLA_NOPE:]
        kn, v = kv[..., :MLA_NOPE], kv[..., MLA_NOPE:]
        kr = kr[:, :, None, :]
        if rotate:
            qr, kr = axial_rope(qr, ang), axial_rope(kr, ang)
        return qn, qr, kn, kr, v

    qn, qr, kn, kr, v = project(p_lat, True)
    qnc, qrc, knc, krc, vc = project(p_ctx, False)
    kn_all = jnp.concatenate([knc, kn], axis=1)
    kr_all = jnp.concatenate([krc, kr], axis=1)[:, :, 0]
    v_all = jnp.concatenate([vc, v], axis=1)
    nblk = S // MLA_BLOCK

    def blocks(t):
        return t.reshape(Bsz, nblk, MLA_BLOCK, *t.shape[2:]).swapaxes(0, 1)

    def attend(qs):
        qn_b, qr_b = qs
        s = (jnp.einsum('bqhd,bkhd->bhqk', qn_b, kn_all)
             + jnp.einsum('bqhd,bkd->bhqk', qr_b, kr_all)).astype(jnp.float32) * scale
        p = jax.nn.softmax(s, axis=-1).astype(v_all.dtype)
        return jnp.einsum('bhqk,bkhd->bqhd', p, v_all)

    o = lax.map(attend, (blocks(qn), blocks(qr)))
    o = o.swapaxes(0, 1).reshape(Bsz, S, MLA_HEADS * MLA_V)
    o_ctx = None
    if need_ctx:
        q_full = jnp.concatenate([qnc, qrc], axis=-1)
        k_full = jnp.concatenate([knc, jnp.broadcast_to(krc, knc.shape[:-1] + (MLA_ROPE,))], axis=-1)
        o_ctx = context_attention(q_full, k_full, vc, scale)
    return o, o_ctx


def centred_depthwise_conv(x, w, b):
    L = x.shape[1]
    pad = SSD_CONV // 2
    xp = jnp.pad(x, ((0, 0), (pad, pad), (0, 0)))
    acc = xp[:, 0:L] * w[0]
    for i in range(1, SSD_CONV):
        acc = acc + xp[:, i:i + L] * w[i]
    return jax.nn.silu(acc + b)


def ssd_scan(x, dt, a, bm, cm, h0):
    f32 = jnp.float32
    Bsz, L, H, P = x.shape
    G, N = bm.shape[2], bm.shape[3]
    nc, Q = L // SSD_CHUNK, SSD_CHUNK
    xdt = (x.astype(f32) * dt[..., None]).reshape(Bsz, nc, Q, H, P)
    bh = jnp.repeat(bm.astype(f32), H // G, axis=2).reshape(Bsz, nc, Q, H, N)
    ch = jnp.repeat(cm.astype(f32), H // G, axis=2).reshape(Bsz, nc, Q, H, N)
    cum = jnp.cumsum((dt * a).reshape(Bsz, nc, Q, H), axis=2)
    lower = jnp.tril(jnp.ones((Q, Q), dtype=bool))
    seg = jnp.exp(jnp.where(lower[None, None, :, :, None],
                            cum[:, :, :, None, :] - cum[:, :, None, :, :], -jnp.inf))
    cb = jnp.einsum('bcihn,bcjhn->bcijh', ch, bh) * seg
    y_diag = jnp.einsum('bcijh,bcjhp->bcihp', cb, xdt)
    decay_end = jnp.exp(cum[:, :, -1:, :] - cum)
    states = jnp.einsum('bcjhn,bcjh,bcjhp->bchpn', bh, decay_end, xdt)
    chunk_decay = jnp.exp(cum[:, :, -1, :])

    def step(h, inp):
        s_c, d_c = inp
        return h * d_c[:, :, None, None] + s_c, h

    h_last, h_start = lax.scan(step, h0, (jnp.moveaxis(states, 1, 0), jnp.moveaxis(chunk_decay, 1, 0)))
    h_start = jnp.moveaxis(h_start, 0, 1)
    y_off = jnp.einsum('bcihn,bchpn->bcihp', ch, h_start) * jnp.exp(cum)[..., None]
    y = (y_diag + y_off).reshape(Bsz, L, H, P)
    return y, h_last


def flip_seq(t, rev):
    return jnp.flip(t, axis=1) if rev else t


def ssd_mixer(p_lat, p_ctx, conv_w, conv_b, dt_bias, a_log, d_skip, g_norm, need_ctx):
    f32 = jnp.float32
    a = -jnp.exp(a_log.astype(f32))

    def prepare(p):
        Bsz, L, _ = p.shape
        z, xbc, dt = jnp.split(p, [SSD_INNER, SSD_INNER + SSD_CONV_CH], axis=-1)
        xbc = centred_depthwise_conv(xbc, conv_w, conv_b)
        x, bm, cm = jnp.split(xbc, [SSD_INNER, SSD_INNER + SSD_GROUPS * SSD_STATE], axis=-1)
        dt = jax.nn.softplus(dt.astype(f32).reshape(Bsz, L, 2, SSD_HEADS) + dt_bias.astype(f32))
        return (z, x.reshape(Bsz, L, SSD_HEADS, SSD_HEAD_DIM), bm.reshape(Bsz, L, SSD_GROUPS, SSD_STATE),
                cm.reshape(Bsz, L, SSD_GROUPS, SSD_STATE), dt)

    zl, xl, bl, cl, dtl = prepare(p_lat)
    zc, xc, bc, cc, dtc = prepare(p_ctx)
    Bsz = xl.shape[0]
    skip = d_skip.astype(f32)[:, None]
    y_lat = xl.astype(f32) * skip
    y_ctx = xc.astype(f32) * skip
    h0 = jnp.zeros((Bsz, SSD_HEADS, SSD_HEAD_DIM, SSD_STATE), f32)
    for direction in range(2):
        rev = direction == 1
        yc_d, h_ctx = ssd_scan(flip_seq(xc, rev), flip_seq(dtc[:, :, direction], rev), a[direction],
                               flip_seq(bc, rev), flip_seq(cc, rev), h0)
        yl_d, _ = ssd_scan(flip_seq(xl, rev), flip_seq(dtl[:, :, direction], rev), a[direction],
                           flip_seq(bl, rev), flip_seq(cl, rev), h_ctx)
        y_lat = y_lat + flip_seq(yl_d, rev)
        y_ctx = y_ctx + flip_seq(yc_d, rev)

    def gate_out(y, z):
        Bz, L = y.shape[:2]
        return rmsnorm(y.reshape(Bz, L, SSD_INNER) * jax.nn.silu(z.astype(f32)), g_norm).astype(z.dtype)

    o = gate_out(y_lat, zl)
    o_ctx = gate_out(y_ctx, zc) if need_ctx else None
    return o, o_ctx


def sq_relu_mlp(x, w1, w2):
    return jnp.square(jax.nn.relu(x @ w1)) @ w2


def setup_inputs(seed: int = 0) -> dict:
    key = jax.random.key(seed)
    ks = jax.random.split(key, 26)
    f32 = jnp.float32

    def nrm(k, shape, scale):
        return jax.random.normal(k, shape, f32) * scale

    def gain(k, shape):
        return 1.0 + 0.02 * jax.random.normal(k, shape, f32)

    dt0 = jnp.exp(jax.random.uniform(ks[16], (DEPTH, 2, SSD_HEADS), f32, math.log(1e-3), math.log(1e-1)))
    return {
        "x": nrm(ks[0], (BATCH, SEQ, D_MODEL), 1.0),
        "c": nrm(ks[1], (BATCH, D_MODEL), 1.0),
        "ctx": nrm(ks[2], (BATCH, CTX_LEN, D_MODEL), 1.0),
        "c_ctx": nrm(ks[3], (D_MODEL,), 1.0),
        "w_mod": nrm(ks[4], (DEPTH, D_MODEL, 6 * D_MODEL), 0.5 * D_MODEL ** -0.5),
        "b_mod": nrm(ks[5], (DEPTH, 6 * D_MODEL), 0.01),
        "g_norm1": gain(ks[6], (DEPTH, D_MODEL)),
        "w_in": nrm(ks[7], (DEPTH, D_MODEL, N_IN), D_MODEL ** -0.5),
        "na_rpb": nrm(ks[8], (DEPTH, NA_HEADS, 2 * WIN_H - 1, 2 * WIN_W - 1), 0.1),
        "swa_sink": nrm(ks[9], (DEPTH, SWA_HEADS), 0.5),
        "mla_g_q": gain(ks[10], (DEPTH, MLA_Q_LORA)),
        "mla_g_kv": gain(ks[11], (DEPTH, MLA_KV_LORA)),
        "mla_w_uq": nrm(ks[12], (DEPTH, MLA_Q_LORA, MLA_HEADS * (MLA_NOPE + MLA_ROPE)), MLA_Q_LORA ** -0.5),
        "mla_w_ukv": nrm(ks[13], (DEPTH, MLA_KV_LORA, MLA_HEADS * (MLA_NOPE + MLA_V)), MLA_KV_LORA ** -0.5),
        "ssd_conv_w": nrm(ks[14], (DEPTH, SSD_CONV, SSD_CONV_CH), SSD_CONV ** -0.5),
        "ssd_conv_b": nrm(ks[15], (DEPTH, SSD_CONV_CH), 0.01),
        "ssd_dt_bias": dt0 + jnp.log(-jnp.expm1(-dt0)),
        "ssd_a_log": jnp.log(jax.random.uniform(ks[17], (DEPTH, 2, SSD_HEADS), f32, 1.0, 16.0)),
        "ssd_d": 1.0 + 0.1 * jax.random.normal(ks[18], (DEPTH, SSD_HEADS), f32),
        "ssd_g_norm": gain(ks[19], (DEPTH, SSD_INNER)),
        "w_out": nrm(ks[20], (DEPTH, D_MIX, D_MODEL), D_MIX ** -0.5),
        "g_norm2": gain(ks[21], (DEPTH, D_MODEL)),
        "w_mlp1": nrm(ks[22], (DEPTH, D_MODEL, D_FF), D_MODEL ** -0.5),
        "w_mlp2": nrm(ks[23], (DEPTH, D_FF, D_MODEL), D_FF ** -0.5),
        "g_final": gain(ks[24], (D_MODEL,)),
    }


def reference(x, c, ctx, c_ctx, w_mod, b_mod, g_norm1, w_in, na_rpb, swa_sink, mla_g_q, mla_g_kv,
              mla_w_uq, mla_w_ukv, ssd_conv_w, ssd_conv_b, ssd_dt_bias, ssd_a_log, ssd_d, ssd_g_norm,
              w_out, g_norm2, w_mlp1, w_mlp2, g_final):
    S = x.shape[1]
    ang_swa = axial_angles(S, HEAD_DIM)
    ang_mla = axial_angles(S, MLA_ROPE)
    c_act = jax.nn.silu(c)[:, None, :]
    cc_act = jax.nn.silu(c_ctx)
    h, hc = x, ctx
    for l in range(DEPTH):
        need_ctx = l < DEPTH - 1
        m = jnp.split(c_act @ w_mod[l] + b_mod[l], 6, axis=-1)
        mc = jnp.split(cc_act @ w_mod[l] + b_mod[l], 6, axis=-1)

        p = modulate(rmsnorm(h, g_norm1[l]), m[0], m[1]) @ w_in[l]
        pc = modulate(rmsnorm(hc, g_norm1[l]), mc[0], mc[1]) @ w_in[l]
        pa, pb, pm, pd = jnp.split(p, IN_SPLITS, axis=-1)
        pa_c, pb_c, pm_c, pd_c = jnp.split(pc, IN_SPLITS, axis=-1)
        oa, oa_c = neighbourhood_attention(pa, pa_c, na_rpb[l], need_ctx)
        ob, ob_c = window_attention(pb, pb_c, swa_sink[l], ang_swa, need_ctx)
        om, om_c = latent_attention(pm, pm_c, mla_g_q[l], mla_g_kv[l], mla_w_uq[l], mla_w_ukv[l], ang_mla, need_ctx)
        od, od_c = ssd_mixer(pd, pd_c, ssd_conv_w[l], ssd_conv_b[l], ssd_dt_bias[l], ssd_a_log[l], ssd_d[l],
                             ssd_g_norm[l], need_ctx)
        h = h + m[2] * (jnp.concatenate([oa, ob, om, od], axis=-1) @ w_out[l])

        h = h + m[5] * sq_relu_mlp(modulate(rmsnorm(h, g_norm2[l]), m[3], m[4]), w_mlp1[l], w_mlp2[l])

        if need_ctx:
            hc = hc + mc[2] * (jnp.concatenate([oa_c, ob_c, om_c, od_c], axis=-1) @ w_out[l])
            hc = hc + mc[5] * sq_relu_mlp(modulate(rmsnorm(hc, g_norm2[l]), mc[3], mc[4]), w_mlp1[l], w_mlp2[l])
    return rmsnorm(h, g_final)
```

```python
import numpy as np
from contextlib import ExitStack
import concourse.bass as bass
import concourse.mybir as mybir

F32 = mybir.dt.float32
BF16 = mybir.dt.bfloat16
AF = mybir.ActivationFunctionType
ALU = mybir.AluOpType
AX = mybir.AxisListType


class Res:
    __slots__ = ("name", "w", "r")

    def __init__(self, name):
        self.name = name
        self.w = None
        self.r = []


class Prog:
    ENG = ("pe", "act", "dve", "pool", "sp")

    def __init__(self):
        self.nc = bass.Bass("TRN2", target_bir_lowering=False)
        self.es = ExitStack()
        self.root = self.es
        nc = self.nc
        self.e = {"pe": nc.tensor, "act": nc.scalar, "dve": nc.vector, "pool": nc.gpsimd, "sp": nc.sync}
        self.sem = {}
        self.cnt = {}
        for k in self.ENG:
            self.sem[k] = self.es.enter_context(nc.semaphore("s_" + k))
            self.cnt[k] = 0
        self.seen = {k: {} for k in self.ENG}
        self.ndma = 0
        self.out_toks = []
        self.nwait = 0
        self.nres = 0

    def dram(self, name, shape, dt, kind):
        return self.nc.dram_tensor(name, list(shape), dt, kind=kind).ap()

    def _u(self, name):
        self.nuniq = getattr(self, "nuniq", 0) + 1
        return f"{name}_u{self.nuniq}"

    def sb(self, name, shape, dt):
        return self.es.enter_context(self.nc.sbuf_tensor(self._u(name), list(shape), dt))

    def ps(self, name, shape, dt=F32):
        return self.es.enter_context(self.nc.psum_tensor(self._u(name), list(shape), dt))

    def res(self, name=None):
        self.nres += 1
        return Res(name or f"r{self.nres}")

    def dsem(self, name):
        free = getattr(self, "free_dsems", None)
        if free:
            key = free.pop()
        else:
            name = self._u(name)
            s = self.root.enter_context(self.nc.semaphore("d_" + name))
            key = ("d", name)
            self.sem[key] = s
            self.cnt[key] = 0
        if getattr(self, "_scope_dsems", None):
            self._scope_dsems[-1].append(key)
        return key

    def _deps(self, eng, reads, writes):
        need = {}

        def add(t, same_ok):
            if t is None:
                return
            sk, val, src = t
            if src == eng and not same_ok:
                return
            if need.get(sk, 0) < val:
                need[sk] = val

        for r in reads:
            add(r.w, True)
        for w in writes:
            add(w.w, False)
            for t in w.r:
                add(t, False)
        out = []
        seen = self.seen[eng]
        for sk, val in need.items():
            if seen.get(sk, 0) >= val:
                continue
            seen[sk] = val
            out.append((sk, val))
        return out

    def _emit_waits(self, eng, waits):
        e = self.e[eng]
        for sk, val in waits:
            e.wait_ge(self.sem[sk], val)
            self.nwait += 1

    def _record(self, tok, reads, writes):
        for r in reads:
            r.r.append(tok)
        for w in writes:
            w.w = tok
            w.r = []

    def op(self, eng, fn, reads=(), writes=(), pe_chain=False):
        reads = [r for r in reads if r is not None]
        writes = [w for w in writes if w is not None]
        if eng == "pe":
            waits = self._deps_pe(reads, writes)
        else:
            waits = self._deps(eng, reads, writes)
        self._emit_waits(eng, waits)
        ins = fn()
        self.cnt[eng] += 1
        ins.then_inc(self.sem[eng], 1)
        tok = (eng, self.cnt[eng], eng)
        self._record(tok, reads, writes)
        return ins

    def _deps_pe(self, reads, writes):
        need = {}

        def add(t):
            if t is None:
                return
            sk, val, src = t
            if src == "pe":
                return
            if need.get(sk, 0) < val:
                need[sk] = val

        for r in reads:
            add(r.w)
        for w in writes:
            add(w.w)
            for t in w.r:
                add(t)
        out = []
        seen = self.seen["pe"]
        for sk, val in need.items():
            if seen.get(sk, 0) >= val:
                continue
            seen[sk] = val
            out.append((sk, val))
        return out

    def dma(self, q, dst, src, dsem, reads=(), writes=(), final=False, **kw):
        reads = [r for r in reads if r is not None]
        writes = [w for w in writes if w is not None]
        waits = self._deps(q, reads, writes)
        self._emit_waits(q, waits)
        ins = self.e[q].dma_start(out=dst, in_=src, **kw)
        self.cnt[dsem] += 16
        ins.then_inc(self.sem[dsem], 16)
        tok = (dsem, self.cnt[dsem], "dma")
        self._record(tok, reads, writes)
        self.ndma += 1
        if final:
            self.out_toks.append(tok)
        return ins

    def barrier(self):
        for eng in self.ENG:
            for sk, val in self.cnt.items():
                if val > 0 and self.seen[eng].get(sk, 0) < val:
                    self.e[eng].wait_ge(self.sem[sk], val)
                    self.seen[eng][sk] = val

    def push_scope(self):
        self._scopes = getattr(self, "_scopes", [])
        self._scopes.append(self.es)
        self.es = ExitStack()
        self._scope_dsems = getattr(self, "_scope_dsems", [])
        self._scope_dsems.append([])

    def pop_scope(self):
        self.barrier()
        self.es.close()
        self.es = self._scopes.pop()
        self.free_dsems = getattr(self, "free_dsems", [])
        self.free_dsems.extend(self._scope_dsems.pop())

    def finish(self):
        toks = list(self.out_toks)
        need = {}
        for sk, val, _ in toks:
            need[sk] = max(need.get(sk, 0), val)
        for sk, val in need.items():
            self.e["sp"].wait_ge(self.sem[sk], val)

    def close(self):
        self.es.close()


class Rot:
    def __init__(self, P, name, n, shape, dt, psum=False, dma=False):
        self.t, self.r, self.d = [], [], []
        for i in range(n):
            self.t.append(P.ps(f"{name}{i}", shape, dt) if psum else P.sb(f"{name}{i}", shape, dt))
            self.r.append(P.res(f"{name}{i}"))
            self.d.append(P.dsem(f"{name}{i}") if dma else None)
        self.i = -1
        self.n = n

    def next(self):
        self.i = (self.i + 1) % self.n
        return self.t[self.i], self.r[self.i], self.d[self.i]


EPS = 1e-6


class AttnCtx:
    def __init__(self, P):
        self.P = P
        nc = P.nc
        self.panels = Rot(P, "pan", 3, [128, 512], F32, psum=True)
        self.accs = Rot(P, "acc", 2, [128, 512], F32, psum=True)
        self.misc = Rot(P, "mps", 2, [128, 512], F32, psum=True)
        self.pT = Rot(P, "pT", 3, [128, 512], BF16)
        self.sT = Rot(P, "sT", 2, [128, 512], F32)
        self.rec = Rot(P, "rec", 2, [128, 4], F32)
        self.zeros = P.sb("zeros", [128, 512], BF16)
        self.rz = P.res()
        P.op("dve", lambda: nc.vector.memset(self.zeros[:], 0.0), writes=[self.rz])
        self.ev = 0


def attend(A, ncol, merged, q_aps, q_res, key_items, scale, out_aps, out_res, sinkexp=None, sink_res=None):
    P = A.P
    nc = P.nc
    W = ncol * 128
    acc, racc, _ = A.accs.next()
    P.op("pe", lambda: nc.tensor.matmul(acc[:, 0:ncol * 65], lhsT=A.zeros[:, 0:128], rhs=A.zeros[:, 0:ncol * 65], start=True, stop=False),
         reads=[A.rz], writes=[racc])
    nk = len(key_items)

    def score(it):
        pan, rpan, _ = A.panels.next()
        if merged:
            parts_k = it["k"][0]
            parts_q = q_aps[0]
            for pi, (kp, qp) in enumerate(zip(parts_k, parts_q)):
                P.op("pe", lambda kp=kp, qp=qp, pi=pi: nc.tensor.matmul(pan[:, 0:W], lhsT=kp, rhs=qp, start=(pi == 0), stop=(pi == len(parts_k) - 1)),
                     reads=list(it["res"]) + list(q_res), writes=[rpan])
        else:
            for c in range(ncol):
                parts_k = it["k"][c]
                parts_q = q_aps[c]
                for pi, (kp, qp) in enumerate(zip(parts_k, parts_q)):
                    P.op("pe", lambda kp=kp, qp=qp, pi=pi, c=c, n=len(parts_k): nc.tensor.matmul(
                        pan[:, c * 128:(c + 1) * 128], lhsT=kp, rhs=qp, start=(pi == 0), stop=(pi == n - 1)),
                        reads=list(it["res"]) + list(q_res), writes=[rpan])
        pt, rpt, _ = A.pT.next()
        if it.get("bias") is not None:
            st, rst, _ = A.sT.next()
            P.op("dve", lambda: nc.vector.scalar_tensor_tensor(out=st[:, 0:W], in0=pan[:, 0:W], scalar=scale, in1=it["bias"], op0=ALU.mult, op1=ALU.add),
                 reads=[rpan] + list(it.get("bres", [])), writes=[rst])
            P.op("act", lambda: nc.scalar.activation(out=pt[:, 0:W], in_=st[:, 0:W], func=AF.Exp), reads=[rst], writes=[rpt])
        else:
            P.op("act", lambda: nc.scalar.activation(out=pt[:, 0:W], in_=pan[:, 0:W], func=AF.Exp, scale=scale), reads=[rpan], writes=[rpt])
        return pt, rpt

    nxt = score(key_items[0])
    for ki, it in enumerate(key_items):
        pt, rpt = nxt
        if ki + 1 < nk:
            nxt = score(key_items[ki + 1])
        for c in range(ncol):
            P.op("pe", lambda c=c: nc.tensor.matmul(acc[:, c * 65:(c + 1) * 65], lhsT=pt[:, c * 128:(c + 1) * 128], rhs=it["v"][c], start=False, stop=(ki == nk - 1)),
                 reads=[rpt] + list(it["res"]), writes=[racc])
    rec, rrec, _ = A.rec.next()
    den = acc[:, 64:64 + 65 * (ncol - 1) + 1:65]
    if sinkexp is not None:
        P.op("dve", lambda: nc.vector.tensor_tensor(out=rec[:, 0:ncol], in0=den, in1=sinkexp, op=ALU.add), reads=[racc, sink_res], writes=[rrec])
        P.op("dve", lambda: nc.vector.reciprocal(out=rec[:, 0:ncol], in_=rec[:, 0:ncol]), reads=[rrec], writes=[rrec])
    else:
        P.op("dve", lambda: nc.vector.reciprocal(out=rec[:, 0:ncol], in_=den), reads=[racc], writes=[rrec])
    for c in range(ncol):
        A.ev += 1
        if A.ev % 2 == 0:
            P.op("act", lambda c=c: nc.scalar.activation(out=out_aps[c], in_=acc[:, c * 65:c * 65 + 64], func=AF.Copy, scale=rec[:, c:c + 1]),
                 reads=[racc, rrec], writes=[out_res])
        else:
            P.op("dve", lambda c=c: nc.vector.tensor_scalar(out=out_aps[c], in0=acc[:, c * 65:c * 65 + 64], scalar1=rec[:, c:c + 1], scalar2=None,
                                                            op0=ALU.mult), reads=[racc, rrec], writes=[out_res])


def load_cast(P, dst, src, res, dsem):
    P.dma("pool", dst, src, dsem, writes=[res])


def load_v(P, vt, v_d, ntile, nh, res, dsem):
    nc = P.nc
    P.op("dve", lambda: nc.vector.memset(vt[:], 1.0), writes=[res])
    src = v_d.rearrange("(t p) (h d) -> p t h d", p=128, h=nh)
    for t in range(ntile):
        P.dma("pool", vt[:, t, :, 0:64], src[:, t, :, :], dsem, writes=[res])


def rope_to(P, dst, x_d, xs_d, cos_t, sin_t, rtab, n, npart, stg, res):
    nc = P.nc
    a, ra, da = stg.next()
    b, rb, db = stg.next()
    P.dma("sp", a[:npart, :n], x_d, da, writes=[ra])
    P.dma("sp", b[:npart, :n], xs_d, db, writes=[rb])
    P.op("dve", lambda: nc.vector.tensor_tensor(out=a[:npart, :n], in0=a[:npart, :n], in1=cos_t, op=ALU.mult), reads=[ra, rtab], writes=[ra])
    P.op("pool", lambda: nc.gpsimd.tensor_tensor(out=b[:npart, :n], in0=b[:npart, :n], in1=sin_t, op=ALU.mult), reads=[rb, rtab], writes=[rb])
    P.op("dve", lambda: nc.vector.tensor_tensor(out=dst, in0=a[:npart, :n], in1=b[:npart, :n], op=ALU.add), reads=[ra, rb], writes=[res])


def build_T(do_na=True, do_swa=True, do_mla=True):
    P = Prog()
    nc = P.nc
    A = AttnCtx(P)
    NQ = 2048
    NC = 256
    rout = P.res("out")
    ost = Rot(P, "ost", 2, [128, 4, 256], F32, dma=True)

    if do_na:
        P.push_scope()
        EXT = 2816
        q_d = P.dram("na_q", [256, NQ], F32, "ExternalInput")
        k_d = P.dram("na_k", [256, EXT], F32, "ExternalInput")
        v_d = P.dram("na_v", [EXT, 256], F32, "ExternalInput")
        qc_d = P.dram("na_qc", [256, NC], F32, "ExternalInput")
        kc_d = P.dram("na_kc", [256, NC], F32, "ExternalInput")
        vc_d = P.dram("na_vc", [NC, 256], F32, "ExternalInput")
        b_d = P.dram("na_bias", [5, 128, 7, 512], F32, "ExternalInput")
        o_d = P.dram("o_na", [NQ + NC, 256], F32, "ExternalOutput")
        q = P.sb("naq", [64, 4, NQ], BF16); rq = P.res(); d1 = P.dsem("naq")
        k = P.sb("nak", [64, 4, EXT], BF16); rk = P.res(); d2 = P.dsem("nak")
        v = P.sb("nav", [128, EXT // 128, 4, 65], BF16); rv = P.res(); d3 = P.dsem("nav")
        qc = P.sb("naqc", [64, 4, NC], BF16); kc = P.sb("nakc", [64, 4, NC], BF16); vc = P.sb("navc", [128, 2, 4, 65], BF16)
        rc = P.res(); d4 = P.dsem("nac")
        for h in range(4):
            load_cast(P, q[:, h, :], q_d[h * 64:(h + 1) * 64, :], rq, d1)
            load_cast(P, k[:, h, :], k_d[h * 64:(h + 1) * 64, :], rk, d2)
            load_cast(P, qc[:, h, :], qc_d[h * 64:(h + 1) * 64, :], rc, d4)
            load_cast(P, kc[:, h, :], kc_d[h * 64:(h + 1) * 64, :], rc, d4)
        load_v(P, v, v_d, EXT // 128, 4, rv, d3)
        load_v(P, vc, vc_d, 2, 4, rc, d4)
        bias = Rot(P, "nab", 2, [128, 7, 512], F32, dma=True)
        scale = 64 ** -0.5
        for t in range(16 + 2):
            items = []
            if t < 16:
                pat = 0 if t == 0 else 1 if t == 1 else 3 if t == 14 else 4 if t == 15 else 2
                bt, rb, db = bias.next()
                P.dma("sp", bt[:], b_d[pat], db, writes=[rb])
                qa = [[q[:, h, t * 128:(t + 1) * 128]] for h in range(4)]
                for kt in range(7):
                    et = t + kt
                    items.append(dict(k=[[k[:, h, et * 128:(et + 1) * 128]] for h in range(4)], v=[v[:, et, h, :] for h in range(4)],
                                      bias=bt[:, kt, :], bres=[rb], res=[rk, rv]))
                qres = [rq]
            else:
                tc = t - 16
                qa = [[qc[:, h, tc * 128:(tc + 1) * 128]] for h in range(4)]
                qres = [rc]
            for kt in range(2):
                items.append(dict(k=[[kc[:, h, kt * 128:(kt + 1) * 128]] for h in range(4)], v=[vc[:, kt, h, :] for h in range(4)], bias=None, res=[rc]))
            o, ro, do = ost.next()
            attend(A, 4, False, qa, qres, items, scale, [o[:, 0, h * 64:(h + 1) * 64] for h in range(4)], ro)
            P.dma("sp", o_d[t * 128:(t + 1) * 128, :], o[:, 0, :], do, reads=[ro], writes=[rout], final=True)
        P.pop_scope()

    if do_swa:
        P.push_scope()
        EXT = 2304
        q_d = P.dram("sw_q", [256, NQ], F32, "ExternalInput")
        qs_d = P.dram("sw_qs", [256, NQ], F32, "ExternalInput")
        k_d = P.dram("sw_k", [128, EXT], F32, "ExternalInput")
        ks_d = P.dram("sw_ks", [128, EXT], F32, "ExternalInput")
        cs_d = P.dram("sw_cs", [2, 64, EXT], F32, "ExternalInput")
        v_d = P.dram("sw_v", [EXT, 128], F32, "ExternalInput")
        qc_d = P.dram("sw_qc", [256, NC], F32, "ExternalInput")
        kc_d = P.dram("sw_kc", [128, NC], F32, "ExternalInput")
        vc_d = P.dram("sw_vc", [NC, 128], F32, "ExternalInput")
        b_d = P.dram("sw_bias", [128, 4, 128], F32, "ExternalInput")
        sk_d = P.dram("sw_sink", [128, 4], F32, "ExternalInput")
        o_d = P.dram("o_sw", [NQ + NC, 256], F32, "ExternalOutput")
        q = P.sb("swq", [64, 4, NQ], BF16); rq = P.res()
        k = P.sb("swk", [64, 2, EXT], BF16); rk = P.res()
        v = P.sb("swv", [128, EXT // 128, 2, 65], BF16); rv = P.res(); d3 = P.dsem("swv")
        qc = P.sb("swqc", [64, 4, NC], BF16); kc = P.sb("swkc", [64, 2, NC], BF16); vc = P.sb("swvc", [128, 2, 2, 65], BF16)
        rc = P.res(); d4 = P.dsem("swc")
        cs = P.sb("swcs", [64, 2, EXT], F32); rcs = P.res(); d5 = P.dsem("swcs")
        bs = P.sb("swb", [128, 4, 128], F32); rbs = P.res()
        sk = P.sb("swsk", [128, 4], F32); rsk = P.res()
        P.dma("sp", cs[:, 0, :], cs_d[0], d5, writes=[rcs])
        P.dma("sp", cs[:, 1, :], cs_d[1], d5, writes=[rcs])
        P.dma("sp", bs[:], b_d, d5, writes=[rbs])
        P.dma("sp", sk[:], sk_d, d5, writes=[rsk])
        P.op("act", lambda: nc.scalar.activation(out=sk[:], in_=sk[:], func=AF.Exp), reads=[rsk], writes=[rsk])
        stg = Rot(P, "swstg", 4, [64, EXT], F32, dma=True)
        for h in range(4):
            rope_to(P, q[:, h, :], q_d[h * 64:(h + 1) * 64, :], qs_d[h * 64:(h + 1) * 64, :], cs[:, 0, 128:128 + NQ], cs[:, 1, 128:128 + NQ], rcs, NQ, 64, stg, rq)
            load_cast(P, qc[:, h, :], qc_d[h * 64:(h + 1) * 64, :], rc, d4)
        for g in range(2):
            rope_to(P, k[:, g, :], k_d[g * 64:(g + 1) * 64, :], ks_d[g * 64:(g + 1) * 64, :], cs[:, 0, :], cs[:, 1, :], rcs, EXT, 64, stg, rk)
            load_cast(P, kc[:, g, :], kc_d[g * 64:(g + 1) * 64, :], rc, d4)
        load_v(P, v, v_d, EXT // 128, 2, rv, d3)
        load_v(P, vc, vc_d, 2, 2, rc, d4)
        bp = P.sb("swbp", [128, 4, 4, 128], F32); rbp = P.res()
        for kind in range(4):
            for h in range(4):
                P.op("dve", lambda kind=kind, h=h: nc.vector.tensor_copy(out=bp[:, kind, h, :], in_=bs[:, kind, :]), reads=[rbs], writes=[rbp])
        scale = 64 ** -0.5
        for t in range(16 + 2):
            items = []
            if t < 16:
                qa = [[q[:, h, t * 128:(t + 1) * 128]] for h in range(4)]
                qres = [rq]
                for kt in range(3):
                    et = t + kt
                    if kt == 0:
                        b = bp[:, 0 if t == 0 else 1, :, :].rearrange('p h q -> p (h q)')
                    elif kt == 2:
                        b = bp[:, 3 if t == 15 else 2, :, :].rearrange('p h q -> p (h q)')
                    else:
                        b = None
                    items.append(dict(k=[[k[:, h // 2, et * 128:(et + 1) * 128]] for h in range(4)], v=[v[:, et, h // 2, :] for h in range(4)],
                                      bias=b, bres=[rbp], res=[rk, rv]))
            else:
                tc = t - 16
                qa = [[qc[:, h, tc * 128:(tc + 1) * 128]] for h in range(4)]
                qres = [rc]
            for kt in range(2):
                items.append(dict(k=[[kc[:, h // 2, kt * 128:(kt + 1) * 128]] for h in range(4)], v=[vc[:, kt, h // 2, :] for h in range(4)], bias=None, res=[rc]))
            o, ro, do = ost.next()
            attend(A, 4, False, qa, qres, items, scale, [o[:, 0, h * 64:(h + 1) * 64] for h in range(4)], ro, sinkexp=sk[:, 0:4], sink_res=rsk)
            P.dma("sp", o_d[t * 128:(t + 1) * 128, :], o[:, 0, :], do, reads=[ro], writes=[rout], final=True)
        P.pop_scope()

    if do_mla:
        P.push_scope()
        NK = 8448
        NQA = NQ + NC
        ckv_d = P.dram("m_ckv", [128, NK], F32, "ExternalInput")
        kr_d = P.dram("m_kr", [32, NK], F32, "ExternalInput")
        krs_d = P.dram("m_krs", [32, NK], F32, "ExternalInput")
        kcs_d = P.dram("m_kcs", [2, 32, NK], F32, "ExternalInput")
        cq_d = P.dram("m_cq", [256, NQA], F32, "ExternalInput")
        qcs_d = P.dram("m_qcs", [2, 32, NQA], F32, "ExternalInput")
        gkv_d = P.dram("m_gkv", [128, 1], F32, "ExternalInput")
        gq_d = P.dram("m_gq", [128, 2], F32, "ExternalInput")
        wkn_d = P.dram("m_wkn", [128, 256], F32, "ExternalInput")
        wkv_d = P.dram("m_wkv", [128, 256], F32, "ExternalInput")
        wqn_d = P.dram("m_wqn", [256, 256], F32, "ExternalInput")
        wqr_d = P.dram("m_wqr", [256, 128], F32, "ExternalInput")
        wqrs_d = P.dram("m_wqrs", [256, 128], F32, "ExternalInput")
        o_d = P.dram("o_ml", [NQA, 256], F32, "ExternalOutput")

        ones = P.sb("mones", [128, 128], BF16); rones = P.res()
        P.op("dve", lambda: nc.vector.memset(ones[:], 1.0), writes=[rones])
        dsm = P.dsem("msmall")
        gkv = P.sb("gkv", [128, 1], F32); gq = P.sb("gq", [128, 2], F32); rg = P.res()
        P.dma("sp", gkv[:], gkv_d, dsm, writes=[rg]); P.dma("sp", gq[:], gq_d, dsm, writes=[rg])
        wkn = P.sb("wkn", [128, 256], BF16); wkv = P.sb("wkv", [128, 256], BF16)
        wqn = P.sb("wqn", [128, 2, 256], BF16); wqr = P.sb("wqr", [128, 2, 128], BF16); wqrs = P.sb("wqrs", [128, 2, 128], BF16)
        rw = P.res(); dw = P.dsem("mw")
        load_cast(P, wkn[:], wkn_d, rw, dw); load_cast(P, wkv[:], wkv_d, rw, dw)
        for c in range(2):
            load_cast(P, wqn[:, c, :], wqn_d[c * 128:(c + 1) * 128, :], rw, dw)
            load_cast(P, wqr[:, c, :], wqr_d[c * 128:(c + 1) * 128, :], rw, dw)
            load_cast(P, wqrs[:, c, :], wqrs_d[c * 128:(c + 1) * 128, :], rw, dw)

        kn = P.sb("kn", [128, 2, NK], BF16); rkn = P.res()
        kr = P.sb("kr", [32, NK], BF16); rkr = P.res()
        vm = P.sb("vm", [128, NK // 128, 4, 65], BF16); rvm = P.res()
        qn = P.sb("qn", [128, 2, NQA], BF16); rqn = P.res()
        qr = P.sb("qr", [32, 4, NQA], BF16); rqr = P.res()
        P.op("dve", lambda: nc.vector.memset(vm[:], 1.0), writes=[rvm])

        xin = Rot(P, "mx", 2, [128, 2, 512], F32, dma=True)
        sq = Rot(P, "msq", 2, [128, 2, 512], BF16)
        rms = Rot(P, "mrms", 2, [128, 512], F32)
        tt = Rot(P, "mtt", 2, [128, 512], F32)
        xn = Rot(P, "mxn", 2, [128, 2, 512], BF16)
        tab = Rot(P, "mtab", 2, [32, 2, 512], F32, dma=True)
        rr = Rot(P, "mrr", 4, [32, 512], F32, dma=True)
        evi = [0]

        def evac(dst, src, reads, writes):
            evi[0] += 1
            if evi[0] % 2 == 0:
                P.op("act", lambda: nc.scalar.copy(out=dst, in_=src), reads=reads, writes=writes)
            else:
                P.op("dve", lambda: nc.vector.tensor_copy(out=dst, in_=src), reads=reads, writes=writes)

        def norm_block(src_d, kc, t0, n, g):
            x, rx, dx = xin.next()
            for c in range(kc):
                P.dma("sp", x[:, c, :n], src_d[c * 128:(c + 1) * 128, t0:t0 + n], dx, writes=[rx])
            s, rs, _ = sq.next()
            for c in range(kc):
                P.op("act", lambda c=c: nc.scalar.activation(out=s[:, c, :n], in_=x[:, c, :n], func=AF.Square), reads=[rx], writes=[rs])
            pt, rpt, _ = A.misc.next()
            for c in range(kc):
                P.op("pe", lambda c=c: nc.tensor.matmul(pt[:, :n], lhsT=ones[:], rhs=s[:, c, :n], start=(c == 0), stop=(c == kc - 1)), reads=[rones, rs], writes=[rpt])
            r, rrr, _ = rms.next()
            P.op("act", lambda: nc.scalar.activation(out=r[:, :n], in_=pt[:, :n], func=AF.Sqrt, scale=1.0 / (kc * 128), bias=EPS), reads=[rpt], writes=[rrr])
            P.op("dve", lambda: nc.vector.reciprocal(out=r[:, :n], in_=r[:, :n]), reads=[rrr], writes=[rrr])
            y, ry, _ = xn.next()
            for c in range(kc):
                P.op("dve", lambda c=c: nc.vector.scalar_tensor_tensor(out=y[:, c, :n], in0=x[:, c, :n], scalar=g[:, c:c + 1], in1=r[:, :n],
                                                                     op0=ALU.mult, op1=ALU.mult), reads=[rx, rrr, rg], writes=[ry])
            return y, ry

        def rope_block(dst, x_d, xs_d, cs_d, t0, n, res):
            tb, rtb, dtb = tab.next()
            P.dma("sp", tb[:, 0, :n], cs_d[0][:, t0:t0 + n], dtb, writes=[rtb])
            P.dma("sp", tb[:, 1, :n], cs_d[1][:, t0:t0 + n], dtb, writes=[rtb])
            a, ra, da = rr.next(); b, rb, db = rr.next()
            P.dma("sp", a[:, :n], x_d[:, t0:t0 + n], da, writes=[ra])
            P.dma("sp", b[:, :n], xs_d[:, t0:t0 + n], db, writes=[rb])
            P.op("dve", lambda: nc.vector.tensor_tensor(out=a[:, :n], in0=a[:, :n], in1=tb[:, 0, :n], op=ALU.mult), reads=[ra, rtb], writes=[ra])
            P.op("pool", lambda: nc.gpsimd.tensor_tensor(out=b[:, :n], in0=b[:, :n], in1=tb[:, 1, :n], op=ALU.mult), reads=[rb, rtb], writes=[rb])
            P.op("dve", lambda: nc.vector.tensor_tensor(out=dst, in0=a[:, :n], in1=b[:, :n], op=ALU.add), reads=[ra, rb], writes=[res])

        for t0 in range(0, NK, 512):
            n = min(512, NK - t0)
            y, ry = norm_block(ckv_d, 1, t0, n, gkv)
            for pr in range(2):
                pt, rpt, _ = A.misc.next()
                P.op("pe", lambda pr=pr, pt=pt: nc.tensor.matmul(pt[:, :n], lhsT=wkn[:, pr * 128:(pr + 1) * 128], rhs=y[:, 0, :n], start=True, stop=True),
                     reads=[rw, ry], writes=[rpt])
                evac(kn[:, pr, t0:t0 + n], pt[:, :n], [rpt], [rkn])
            for tt_ in range(n // 128):
                kt = t0 // 128 + tt_
                pt, rpt, _ = A.misc.next()
                P.op("pe", lambda tt_=tt_, pt=pt: nc.tensor.matmul(pt[:, 0:256], lhsT=y[:, 0, tt_ * 128:(tt_ + 1) * 128], rhs=wkv[:], start=True, stop=True),
                     reads=[rw, ry], writes=[rpt])
                evac(vm[:, kt, :, 0:64], pt[:, 0:256].rearrange("p (h d) -> p h d", h=4), [rpt], [rvm])
            rope_block(kr[:, t0:t0 + n], kr_d, krs_d, kcs_d, t0, n, rkr)
        for t0 in range(0, NQA, 512):
            n = min(512, NQA - t0)
            y, ry = norm_block(cq_d, 2, t0, n, gq)
            for pr in range(2):
                pt, rpt, _ = A.misc.next()
                for c in range(2):
                    P.op("pe", lambda pr=pr, pt=pt, c=c: nc.tensor.matmul(pt[:, :n], lhsT=wqn[:, c, pr * 128:(pr + 1) * 128], rhs=y[:, c, :n],
                                                                         start=(c == 0), stop=(c == 1)), reads=[rw, ry], writes=[rpt])
                evac(qn[:, pr, t0:t0 + n], pt[:, :n], [rpt], [rqn])
            tb, rtb, dtb = tab.next()
            P.dma("sp", tb[:, 0, :n], qcs_d[0][:, t0:t0 + n], dtb, writes=[rtb])
            P.dma("sp", tb[:, 1, :n], qcs_d[1][:, t0:t0 + n], dtb, writes=[rtb])
            for h in range(4):
                pa, rpa, _ = A.misc.next()
                for c in range(2):
                    P.op("pe", lambda pa=pa, c=c, h=h: nc.tensor.matmul(pa[0:32, :n], lhsT=wqr[:, c, h * 32:(h + 1) * 32], rhs=y[:, c, :n],
                                                                       start=(c == 0), stop=(c == 1)), reads=[rw, ry], writes=[rpa])
                a, ra, _ = rr.next()
                P.op("dve", lambda a=a, pa=pa: nc.vector.tensor_tensor(out=a[:, :n], in0=pa[0:32, :n], in1=tb[:, 0, :n], op=ALU.mult), reads=[rpa, rtb], writes=[ra])
                pb, rpb, _ = A.misc.next()
                for c in range(2):
                    P.op("pe", lambda pb=pb, c=c, h=h: nc.tensor.matmul(pb[0:32, :n], lhsT=wqrs[:, c, h * 32:(h + 1) * 32], rhs=y[:, c, :n],
                                                                       start=(c == 0), stop=(c == 1)), reads=[rw, ry], writes=[rpb])
                b, rb, _ = rr.next()
                P.op("dve", lambda b=b, pb=pb: nc.vector.tensor_tensor(out=b[:, :n], in0=pb[0:32, :n], in1=tb[:, 1, :n], op=ALU.mult), reads=[rpb, rtb], writes=[rb])
                P.op("pool", lambda a=a, b=b, h=h: nc.gpsimd.tensor_tensor(out=qr[:, h, t0:t0 + n], in0=a[:, :n], in1=b[:, :n], op=ALU.add), reads=[ra, rb], writes=[rqr])
        scale = 96 ** -0.5
        groups = [(g * 512, 4, NK // 128) for g in range(4)] + [(2048, 2, 2)]
        for (q0, ncol, nkt) in groups:
            o, ro, do = ost.next()
            for h in range(4):
                hp, pr = (h % 2) * 64, h // 2
                qa = [[qn[hp:hp + 64, pr, q0:q0 + ncol * 128], qr[:, h, q0:q0 + ncol * 128]]]
                items = []
                for kt in range(nkt):
                    items.append(dict(k=[[kn[hp:hp + 64, pr, kt * 128:(kt + 1) * 128], kr[:, kt * 128:(kt + 1) * 128]]],
                                      v=[vm[:, kt, h, :]] * ncol, bias=None, res=[rkn, rkr, rvm]))
                attend(A, ncol, True, qa, [rqn, rqr], items, scale, [o[:, c, h * 64:(h + 1) * 64] for c in range(ncol)], ro)
            P.dma("sp", o_d[q0:q0 + ncol * 128, :].rearrange("(c p) f -> p c f", p=128), o[:, 0:ncol, :], do, reads=[ro], writes=[rout], final=True)
        P.pop_scope()
    P.finish()
    P.close()
    return P


D = 1024
KC = 8
EPS = 1e-6
NTOK = 2304
NQ = 2048
NCX = 256
NCOL = 2728
NA0, SW0, ML0, SS0 = 0, 768, 1280, 1696
SBROWS = 1312
R_NAK, R_SWK, R_CKV, R_XBC, R_KR = 0, 256, 384, 512, 1280
YCH = 2816
G4 = [[0, 1, 2, 3], [4, 5, 6, 7]]
LSEQ = 8448
NCH = LSEQ // 128


class RotView:
    def __init__(self, rots):
        self.t, self.r, self.d = [], [], []
        for ro in rots:
            self.t += ro.t; self.r += ro.r; self.d += ro.d
        self.i = -1
        self.n = len(self.t)

    def next(self):
        self.i = (self.i + 1) % self.n
        return self.t[self.i], self.r[self.i], self.d[self.i]


def allgather(P, src, dst, reads, writes):
    nc = P.nc
    if "cc" not in P.sem:
        P.sem["cc"] = P.root.enter_context(nc.semaphore("s_cc"))
        P.cnt["cc"] = 0
    P._emit_waits("pool", P._deps("pool", reads, writes))
    ins = nc.gpsimd.collective_compute("AllGather", mybir.AluOpType.bypass, replica_groups=G4, ins=[src.opt()], outs=[dst.opt()])
    P.cnt["cc"] += 1
    ins.then_inc(P.sem["cc"])
    P._record(("cc", P.cnt["cc"], "dma"), reads, writes)
    nc.gpsimd.wait_ge(P.sem["cc"], P.cnt["cc"])
    P.seen["pool"]["cc"] = P.cnt["cc"]


def seg_lat(t0, n):
    out = []
    t = t0
    while t < t0 + n:
        r = t // 2048
        ln = min(t0 + n, (r + 1) * 2048) - t
        out.append((r, t - r * 2048, ln, t - t0))
        t += ln
    return out


def build_M(nlayers=4, final_layer=3, NLW=4):
    P = Prog()
    nc = P.nc
    X = lambda name, shape: P.dram(name, shape, F32, "ExternalInput")
    hT0_d = X("hT0", [D, NTOK]); cv_d = X("cv", [128, KC, 2]); flg_d = X("flg", [128, 16])
    selx_d = X("selx", [128, 2, 64]); selb_d = X("selb", [128, 2, 128])
    w_in_d = X("w_in", [NLW, D, NCOL]); w_mod_d = X("w_mod", [NLW, D, 6 * D]); bmod_d = X("bmod", [NLW, 128, 48])
    g1_d = X("g1", [NLW, 128, KC]); gv_d = X("gv", [NLW, 128, 20])
    wo_d = X("w_out", [NLW, D, D]); w1_d = X("w1", [NLW, D, 4 * D]); w2_d = X("w2", [NLW, 4 * D, D])
    nab_d = X("na_bias", [NLW, 5, 128, 7, 512]); swb_d = X("sw_bias", [128, 4, 128]); swk_d = X("sw_sink", [NLW, 128, 4])
    swcs_d = X("sw_cs", [2, 64, 2304]); mkcs_d = X("m_kcs", [2, 32, LSEQ]); mqcs_d = X("m_qcs", [2, 32, NTOK])
    mgkv_d = X("m_gkv", [NLW, 128, 1]); mgq_d = X("m_gq", [NLW, 128, 2])
    mwkn_d = X("m_wkn", [NLW, 128, 256]); mwkv_d = X("m_wkv", [NLW, 128, 256])
    mwqn_d = X("m_wqn", [NLW, 256, 256]); mwqr_d = X("m_wqr", [NLW, 256, 128]); mwqrs_d = X("m_wqrs", [NLW, 256, 128])
    scw_d = X("s_cw", [NLW, 128, 3, 6]); spar_d = X("s_par", [NLW, 128, 8]); su_d = X("s_u", [2, 128, 128]); id_d = X("ident", [128, 128])
    out_d = P.dram("out", [D, NQ], F32, "ExternalOutput")
    S_ = lambda name, shape: nc.dram_tensor(name, list(shape), F32).ap()
    hT_s = [S_("hTa", [D, NTOK]), S_("hTb", [D, NTOK])]
    pT_d = S_("pT", [NCOL, NTOK]); vtok_d = S_("vtok", [NTOK, 392])
    sb_nr = [64] * 20 + [32]
    SBc = [S_(f"SBc{c}", [sb_nr[c], NTOK]) for c in range(21)]
    RBc = [S_(f"RBc{c}", [4 * sb_nr[c], NTOK]) for c in range(21)]
    vg_nr = [512] * 4 + [256]
    VSc = [S_(f"VSc{i}", [vg_nr[i], 392]) for i in range(5)]
    VGc = [S_(f"VGc{i}", [4 * vg_nr[i], 392]) for i in range(5)]
    YSc = [S_(f"YSc{i}", [64, YCH]) for i in range(3)]
    YGc = [S_(f"YGc{i}", [256, YCH]) for i in range(3)]
    h2_d = S_("h2", [D, NTOK])

    def rbuf(r, row0, nrows, c0, c1):
        c = row0 // 64
        assert row0 + nrows <= 64 * c + sb_nr[c], (row0, nrows)
        o = r * sb_nr[c] + row0 - 64 * c
        return RBc[c][o:o + nrows, c0:c1]

    def vgbuf(r, row0, nrows, c0, c1):
        i = row0 // 512
        assert row0 + nrows <= 512 * i + vg_nr[i], (row0, nrows)
        o = r * vg_nr[i] + row0 - 512 * i
        return VGc[i][o:o + nrows, c0:c1]
    r_pT, r_vtok, r_SB, r_RB1, r_VG, r_YS, r_YG, r_h2, r_RB2 = (P.res() for _ in range(9))
    r_hT = [P.res(), P.res()]

    A = AttnCtx(P)
    extra = Rot(P, "xps", 1, [128, 512], F32, psum=True)
    gen = RotView([A.misc, extra, A.panels, A.accs])
    ones = P.sb("ones", [128, 128], BF16); rones = P.res()
    P.op("dve", lambda: nc.vector.memset(ones[:], 1.0), writes=[rones])
    onesf = P.sb("onesf", [128, 128], F32); ronesf = P.res()
    P.op("pool", lambda: nc.gpsimd.memset(onesf[:], 1.0), writes=[ronesf])
    dc = P.dsem("const")
    ident = P.sb("ident", [128, 128], F32); rid = P.res(); P.dma("sp", ident[:], id_d, dc, writes=[rid])
    flg = P.sb("flg", [128, 16], F32); rflg = P.res(); P.dma("sp", flg[:], flg_d, dc, writes=[rflg])
    U = P.sb("U", [128, 2, 128], F32); rU = P.res()
    P.dma("sp", U[:, 0, :], su_d[0], dc, writes=[rU]); P.dma("sp", U[:, 1, :], su_d[1], dc, writes=[rU])
    selx = P.sb("selx", [128, 2, 64], F32); selb = P.sb("selb", [128, 2, 128], F32); rsel = P.res()
    P.dma("sp", selx[:], selx_d, dc, writes=[rsel]); P.dma("sp", selb[:], selb_d, dc, writes=[rsel])
    cvs = P.sb("cvs", [128, KC, 2], F32); rcv = P.res(); P.dma("sp", cvs[:], cv_d, dc, writes=[rcv])
    ca = P.sb("ca", [128, KC, 2], F32); rca = P.res()
    P.op("act", lambda: nc.scalar.activation(out=ca[:], in_=cvs[:], func=AF.Silu), reads=[rcv], writes=[rca])
    moL = [P.sb("mo", [128, 48, 2], F32) for _ in range(2)]; rmoL = [P.res(), P.res()]
    gs1L = [P.sb("gs1", [128, KC, 2], F32) for _ in range(2)]; gs2L = [P.sb("gs2", [128, KC, 2], F32) for _ in range(2)]; rgsL = [P.res(), P.res()]
    gvsL = [P.sb("gvs", [128, 28], F32) for _ in range(2)]; rgvL = [P.res(), P.res()]
    oTs = P.sb("oTs", [128, 6, NTOK], BF16); roT = P.res()
    evi = [0]

    def evac(dst, src, reads, writes):
        evi[0] += 1
        if evi[0] % 2 == 0:
            P.op("act", lambda: nc.scalar.copy(out=dst, in_=src), reads=reads, writes=writes)
        else:
            P.op("dve", lambda: nc.vector.tensor_copy(out=dst, in_=src), reads=reads, writes=writes)

    def rstd_of(x, rx, kc, n, r, rr, s, rs, feat):
        for k in range(kc):
            P.op("act", lambda k=k: nc.scalar.activation(out=s[:, k, :n], in_=x[:, k, :n], func=AF.Square), reads=[rx], writes=[rs])
        pt, rpt, _ = gen.next()
        for k in range(kc):
            P.op("pe", lambda k=k: nc.tensor.matmul(pt[:, :n], lhsT=ones[:], rhs=s[:, k, :n], start=(k == 0), stop=(k == kc - 1)), reads=[rones, rs], writes=[rpt])
        P.op("act", lambda: nc.scalar.activation(out=r[:, :n], in_=pt[:, :n], func=AF.Sqrt, scale=1.0 / feat, bias=EPS), reads=[rpt], writes=[rr])
        P.op("dve", lambda: nc.vector.reciprocal(out=r[:, :n], in_=r[:, :n]), reads=[rr], writes=[rr])

    def emit_mod(l):
        mo, rmo, gs1, gs2, rgs, gvs, rgv = moL[l % 2], rmoL[l % 2], gs1L[l % 2], gs2L[l % 2], rgsL[l % 2], gvsL[l % 2], rgvL[l % 2]
        P.push_scope()
        dm = P.dsem(f"mod{l}")
        bm = P.sb("bm", [128, 48], F32); rbm = P.res()
        P.dma("sp", bm[:], bmod_d[l], dm, writes=[rbm])
        P.dma("sp", gvs[:, 0:20], gv_d[l], dm, writes=[rgv]); P.dma("sp", gvs[:, 20:28], g1_d[l], dm, writes=[rgv])
        wm = Rot(P, "wm", 2, [128, KC, 1024], F32, dma=True)
        for grp in range(6):
            w, rw_, dw_ = wm.next()
            for k in range(KC):
                P.dma("sp", w[:, k, :], w_mod_d[l][k * 128:(k + 1) * 128, grp * 1024:(grp + 1) * 1024], dw_, writes=[rw_])
            pt, rpt, _ = gen.next()
            for cb in range(8):
                for k in range(KC):
                    P.op("pe", lambda cb=cb, k=k: nc.tensor.matmul(pt[:, cb * 2:cb * 2 + 2], lhsT=w[:, k, cb * 128:(cb + 1) * 128], rhs=ca[:, k, :],
                                                                 start=(k == 0), stop=(k == KC - 1)), reads=[rw_, rca], writes=[rpt])
            for j in range(2):
                P.op("dve", lambda j=j: nc.vector.tensor_tensor(out=mo[:, grp * 8:(grp + 1) * 8, j], in0=pt[:, j:16:2], in1=bm[:, grp * 8:(grp + 1) * 8], op=ALU.add),
                     reads=[rpt, rbm], writes=[rmo])
        for j in range(2):
            P.op("dve", lambda j=j: nc.vector.scalar_tensor_tensor(out=gs1[:, :, j], in0=mo[:, 8:16, j], scalar=1.0, in1=gvs[:, 20:28], op0=ALU.add, op1=ALU.mult),
                 reads=[rmo, rgv], writes=[rgs])
            P.op("dve", lambda j=j: nc.vector.scalar_tensor_tensor(out=gs2[:, :, j], in0=mo[:, 32:40, j], scalar=1.0, in1=gvs[:, 0:8], op0=ALU.add, op1=ALU.mult),
                 reads=[rmo, rgv], writes=[rgs])
        P.pop_scope()


    for l in range(nlayers):
        final = (l == final_layer)
        hT_d, r_hin = (hT0_d, None) if l == 0 else (hT_s[(l - 1) % 2], r_hT[(l - 1) % 2])
        hTn_d, r_hn = hT_s[l % 2], r_hT[l % 2]

        mo, rmo, gs1, gs2, rgs, gvs, rgv = moL[l % 2], rmoL[l % 2], gs1L[l % 2], gs2L[l % 2], rgsL[l % 2], gvsL[l % 2], rgvL[l % 2]
        if l == 0:
            emit_mod(0)
        P.push_scope()
        wsb = P.sb("wsb", [128, KC, NCOL], BF16); rw = P.res(); dw = P.dsem(f"w{l}")
        wv = P.sb("wv", [128, KC, 392], BF16)
        for k in range(KC):
            for c0 in range(0, NCOL, 2048):
                c1 = min(NCOL, c0 + 2048)
                P.dma("pool", wsb[:, k, c0:c1], w_in_d[l][k * 128:(k + 1) * 128, c0:c1], dw, writes=[rw])
            P.dma("pool", wv[:, k, 0:256], w_in_d[l][k * 128:(k + 1) * 128, 512:768], dw, writes=[rw])
            P.dma("pool", wv[:, k, 256:384], w_in_d[l][k * 128:(k + 1) * 128, SW0 + 384:SW0 + 512], dw, writes=[rw])
            P.dma("pool", wv[:, k, 384:392], w_in_d[l][k * 128:(k + 1) * 128, SS0 + 1024:SS0 + 1032], dw, writes=[rw])
        hin = Rot(P, "hin", 2, [128, KC, 512], F32, dma=True)
        sq = Rot(P, "sq", 2, [128, KC, 512], BF16)
        xm = Rot(P, "xm", 2, [128, KC, 512], BF16)
        tt = Rot(P, "tt", 2, [128, 512], F32)
        rms = Rot(P, "rms", 2, [128, 512], F32)
        ost = Rot(P, "ost", 4, [128, 512], F32, dma=True)
        blocks = [(t0, 512, 0) for t0 in range(0, NQ, 512)] + [(NQ, 256, 1)]
        ncb = (NCOL + 127) // 128
        for (t0, n, j) in blocks:
            h, rh, dh = hin.next()
            P.dma("sp", h[:, :, :n], hT_d.rearrange("(k p) t -> p k t", p=128)[:, :, t0:t0 + n], dh, reads=[r_hin], writes=[rh])
            s, rs, _ = sq.next(); r, rr, _ = rms.next()
            rstd_of(h, rh, KC, n, r, rr, s, rs, D)
            x, rx, _ = xm.next()
            for k in range(KC):
                t, rt, _ = tt.next()
                P.op("dve", lambda k=k, t=t: nc.vector.tensor_tensor(out=t[:, :n], in0=h[:, k, :n], in1=r[:, :n], op=ALU.mult), reads=[rh, rr], writes=[rt])
                P.op("act", lambda k=k, t=t: nc.scalar.activation(out=x[:, k, :n], in_=t[:, :n], func=AF.Identity, scale=gs1[:, k, j:j + 1], bias=mo[:, k, j:j + 1]),
                     reads=[rt, rgs, rmo], writes=[rx])
            for cb in range(ncb):
                c0 = cb * 128
                m = min(128, NCOL - c0)
                pt, rpt, _ = gen.next()
                for k in range(KC):
                    P.op("pe", lambda k=k, pt=pt: nc.tensor.matmul(pt[:m, :n], lhsT=wsb[:, k, c0:c0 + m], rhs=x[:, k, :n], start=(k == 0), stop=(k == KC - 1)),
                         reads=[rw, rx], writes=[rpt])
                o, ro, do = ost.next()
                evac(o[:m, :n], pt[:m, :n], [rpt], [ro])
                P.dma("sp", pT_d[c0:c0 + m, t0:t0 + n], o[:m, :n], do, reads=[ro], writes=[r_pT])
            for ti in range(n // 128):
                pt, rpt, _ = gen.next()
                for k in range(KC):
                    P.op("pe", lambda k=k, pt=pt: nc.tensor.matmul(pt[:, 0:392], lhsT=x[:, k, ti * 128:(ti + 1) * 128], rhs=wv[:, k, :], start=(k == 0), stop=(k == KC - 1)),
                         reads=[rw, rx], writes=[rpt])
                o, ro, do = ost.next()
                evac(o[:, 0:392], pt[:, 0:392], [rpt], [ro])
                P.dma("sp", vtok_d[t0 + ti * 128:t0 + (ti + 1) * 128, :], o[:, 0:392], do, reads=[ro], writes=[r_vtok])
        P.pop_scope()

        dx = P.dsem(f"x1_{l}")
        for (d0, s0, nr) in ((R_NAK, 256, 256), (R_SWK, SW0 + 256, 128), (R_CKV, ML0 + 256, 128), (R_XBC, SS0 + 256, 768), (R_KR, ML0 + 384, 32)):
            for rr0 in range(0, nr, 64):
                n_ = min(64, nr - rr0)
                P.dma("sp", SBc[(d0 + rr0) // 64][0:n_, :], pT_d[s0 + rr0:s0 + rr0 + n_, :], dx, reads=[r_pT], writes=[r_SB])
        for i in range(5):
            P.dma("sp", VSc[i][:, :], vtok_d[512 * i:512 * i + vg_nr[i], :], dx, reads=[r_vtok], writes=[r_SB])
        P.barrier()
        for c in range(8, 20):
            allgather(P, SBc[c], RBc[c], [r_SB], [r_RB1])
        for i in range(5):
            allgather(P, VSc[i], VGc[i], [r_SB], [r_VG])
        if l + 1 < nlayers:
            emit_mod(l + 1)
        for c in list(range(8)) + [20]:
            allgather(P, SBc[c], RBc[c], [r_SB], [r_RB2])

        P.push_scope()
        ds_ = P.dsem(f"ssm{l}")
        cw = P.sb("cw", [128, 3, 6], F32); rcw = P.res(); P.dma("sp", cw[:], scw_d[l], ds_, writes=[rcw])
        par = P.sb("par", [128, 8], F32); rpar = P.res(); P.dma("sp", par[:], spar_d[l], ds_, writes=[rpar])
        dtall = P.sb("dtall", [128, NCH, 8], F32); rdta_ = P.res()
        P.dma("sp", dtall[:, 0:2, :], vgbuf(0, NQ, 256, 384, 392).rearrange("(c p) e -> p c e", p=128), ds_, reads=[r_VG], writes=[rdta_])
        for r in range(4):
            for i in range(4):
                P.dma("sp", dtall[:, 2 + 16 * r + 4 * i:2 + 16 * r + 4 * (i + 1), :], vgbuf(r, 512 * i, 512, 384, 392).rearrange("(c p) e -> p c e", p=128), ds_,
                      reads=[r_VG], writes=[rdta_])
        dt = P.sb("dt", [128, NCH, 2], F32); rdt = P.res()
        dta = P.sb("dta", [128, NCH, 2], F32); rdta = P.res()
        for d in range(2):
            P.op("dve", lambda d=d: nc.vector.tensor_scalar(out=dt[:, :, d], in0=dtall[:, :, d * 4], scalar1=flg[:, 0:1], scalar2=None, op0=ALU.mult),
                 reads=[rdta_, rflg], writes=[rdt])
            for hh in range(1, 4):
                P.op("dve", lambda d=d, hh=hh: nc.vector.scalar_tensor_tensor(out=dt[:, :, d], in0=dtall[:, :, d * 4 + hh], scalar=flg[:, hh:hh + 1], in1=dt[:, :, d],
                                                                            op0=ALU.mult, op1=ALU.add), reads=[rdta_, rflg, rdt], writes=[rdt])
        av = P.sb("av", [128, 2], F32); rav = P.res()
        P.op("act", lambda: nc.scalar.activation(out=av[:], in_=par[:, 2:4], func=AF.Exp), reads=[rpar], writes=[rav])
        P.op("dve", lambda: nc.vector.tensor_scalar(out=av[:], in0=av[:], scalar1=-1.0, scalar2=None, op0=ALU.mult), reads=[rav], writes=[rav])
        for d in range(2):
            P.op("act", lambda d=d: nc.scalar.activation(out=dt[:, :, d], in_=dt[:, :, d], func=AF.Exp, bias=par[:, d:d + 1]), reads=[rdt, rpar], writes=[rdt])
        for d in range(2):
            P.op("act", lambda d=d: nc.scalar.activation(out=dt[:, :, d], in_=dt[:, :, d], func=AF.Ln, bias=1.0), reads=[rdt], writes=[rdt])
        for d in range(2):
            P.op("dve", lambda d=d: nc.vector.tensor_scalar(out=dta[:, :, d], in0=dt[:, :, d], scalar1=av[:, d:d + 1], scalar2=None, op0=ALU.mult),
                 reads=[rdt, rav], writes=[rdta])
        xT = P.sb("xT", [64, LSEQ], F32); rx_ = P.res()
        bT = P.sb("bT", [128, LSEQ], BF16); rb_ = P.res()
        cT = P.sb("cT", [128, LSEQ], BF16); rc_ = P.res()
        yb = P.sb("yb", [64, LSEQ], F32); ry_ = P.res()
        P.push_scope()
        raw = Rot(P, "raw", 6, [128, 2, 512], F32, dma=True)
        cacc = Rot(P, "cacc", 3, [128, 508], F32)
        CB = 508
        for gi, (row0, npart, sel, dst, rdst) in enumerate(((R_XBC, 64, selx, xT, rx_), (R_XBC + 256, 128, selb, bT, rb_), (R_XBC + 512, 128, selb, cT, rc_))):
            for (s0, sl, is_ctx) in ((0, 256, True), (256, 8192, False)):
                for t0 in range(0, sl, CB):
                    n = min(CB, sl - t0)
                    rt_, rr_, dr_ = raw.next()
                    lo = max(t0 - 2, 0); hi = min(t0 + n + 2, sl)
                    if lo > t0 - 2 or hi < t0 + n + 2:
                        P.op("dve", lambda rt_=rt_: nc.vector.memset(rt_[:], 0.0), writes=[rr_])
                    if is_ctx:
                        segs = [(0, NQ + lo, hi - lo, lo - (t0 - 2))]
                    else:
                        segs = [(r, ls, ln, do_ + lo - (t0 - 2)) for (r, ls, ln, do_) in seg_lat(lo, hi - lo)]
                    for (r, ls, ln, do_) in segs:
                        for kc in range(2):
                            for hf in range(2):
                                P.dma("sp", rt_[hf * 64:(hf + 1) * 64, kc, do_:do_ + ln], rbuf(r, row0 + kc * 128 + hf * 64, 64, ls, ls + ln), dr_,
                                      reads=[r_RB1], writes=[rr_])
                    pt, rpt, _ = gen.next()
                    for kc in range(2):
                        P.op("pe", lambda kc=kc, pt=pt: nc.tensor.matmul(pt[:npart, 0:n + 4], lhsT=sel[:, kc, :], rhs=rt_[:, kc, 0:n + 4], start=(kc == 0), stop=(kc == 1)),
                             reads=[rsel, rr_], writes=[rpt])
                    a, ra, _ = cacc.next()
                    P.op("dve", lambda: nc.vector.tensor_scalar(out=a[:npart, :n], in0=pt[:npart, 0:n], scalar1=cw[:npart, gi, 0:1], scalar2=None, op0=ALU.mult),
                         reads=[rpt, rcw], writes=[ra])
                    for k in range(1, 5):
                        P.op("dve", lambda k=k: nc.vector.scalar_tensor_tensor(out=a[:npart, :n], in0=pt[:npart, k:k + n], scalar=cw[:npart, gi, k:k + 1], in1=a[:npart, :n],
                                                                             op0=ALU.mult, op1=ALU.add), reads=[rpt, rcw, ra], writes=[ra])
                    P.op("act", lambda: nc.scalar.activation(out=dst[:npart, s0 + t0:s0 + t0 + n], in_=a[:npart, :n], func=AF.Silu, bias=cw[:npart, gi, 5:6]),
                         reads=[ra, rcw], writes=[rdst])
        P.pop_scope()
        xtokA = P.sb("xtokA", [128, NCH, 64], F32); rxt = P.res()
        btokA = P.sb("btokA", [128, NCH, 128], BF16); rbtk = P.res()
        gt = Rot(P, "gt", 2, [128, 128], F32)
        for c in range(NCH):
            sl = slice(c * 128, (c + 1) * 128)
            px, rpx, _ = gen.next()
            P.op("pe", lambda: nc.tensor.transpose(px[:, 0:64], xT[:, sl], ident[0:64, 0:64]), reads=[rx_, rid], writes=[rpx])
            evac(xtokA[:, c, :], px[:, 0:64], [rpx], [rxt])
            btf, rbtf, _ = gt.next()
            P.op("act", lambda: nc.scalar.copy(out=btf[:], in_=bT[:, sl]), reads=[rb_], writes=[rbtf])
            pb, rpb, _ = gen.next()
            P.op("pe", lambda: nc.tensor.transpose(pb[:, 0:128], btf[:], ident[:]), reads=[rbtf, rid], writes=[rpb])
            evac(btokA[:, c, :], pb[:, 0:128], [rpb], [rbtk])
        ryc = [P.res() for _ in range(NCH)]
        for c in range(NCH):
            sl = slice(c * 128, (c + 1) * 128)
            P.op("pool", lambda: nc.gpsimd.tensor_scalar(out=yb[:, sl], in0=xT[:, sl], scalar1=par[0:64, 4:5], scalar2=None, op0=ALU.mult), reads=[rx_, rpar], writes=[ryc[c]])
        hs = [P.sb("hs", [128, 64], F32) for _ in range(2)]; rhs_ = [P.res(), P.res()]
        hsb = [P.sb("hsb", [128, 64], BF16) for _ in range(2)]; rhsb = [P.res(), P.res()]
        NS = 4
        dtab = Rot(P, "dtab", NS, [128, 128], F32); ccol = Rot(P, "ccol", 2 * NS, [128, 4], F32)
        dd = Rot(P, "dd", NS, [128, 128], F32); ee = Rot(P, "ee", NS, [128, 128], F32)
        gm = Rot(P, "gm", NS, [128, 128], F32); mt = Rot(P, "mt", NS, [128, 128], BF16)
        ecr = Rot(P, "ecr", NS, [128, 128], F32); cs = Rot(P, "cs", NS, [128, 128], BF16)
        xdt = Rot(P, "xdt", NS, [128, 64], BF16); xdd = Rot(P, "xdd", NS, [128, 64], BF16)
        sst = Rot(P, "sst", NS, [128, 64], F32); hsr = Rot(P, "hsr", NS, [128, 64], BF16)
        hcur = [None, None]
        for d in range(2):
            P.op("dve", lambda d=d: nc.vector.memset(hs[d][:], 0.0), writes=[rhs_[d]])
            P.op("dve", lambda d=d: nc.vector.memset(hsb[d][:], 0.0), writes=[rhsb[d]])
        orders = [list(range(NCH)), [1, 0] + list(range(NCH - 1, 1, -1))]

        def p1(d, c):
            sl = slice(c * 128, (c + 1) * 128)
            tot_col = 127 if d == 0 else 0
            da, rda, _ = dtab.next()
            P.op("pool", lambda: nc.gpsimd.tensor_scalar(out=da[:], in0=onesf[:], scalar1=dta[:, c, d:d + 1], scalar2=None, op0=ALU.mult), reads=[ronesf, rdta], writes=[rda])
            pcr, rpcr, _ = gen.next()
            P.op("pe", lambda: nc.tensor.matmul(pcr[:, 0:128], lhsT=da[:], rhs=U[:, d, :], start=True, stop=True), reads=[rda, rU], writes=[rpcr])
            P.op("pe", lambda: nc.tensor.matmul(pcr[:, 128:130], lhsT=U[:, d, :], rhs=dta[:, c, :], start=True, stop=True), reads=[rU, rdta], writes=[rpcr])
            cc, rcc, _ = ccol.next()
            P.op("act", lambda: nc.scalar.copy(out=cc[:, 0:1], in_=pcr[:, 128 + d:129 + d]), reads=[rpcr], writes=[rcc])
            P.op("act", lambda: nc.scalar.copy(out=cc[:, 1:2], in_=pcr[:, tot_col:tot_col + 1]), reads=[rpcr], writes=[rcc])
            P.op("act", lambda: nc.scalar.activation(out=cc[:, 3:4], in_=cc[:, 1:2], func=AF.Exp), reads=[rcc], writes=[rcc])
            dt_, rdd, _ = dd.next()
            P.op("dve", lambda: nc.vector.tensor_scalar(out=dt_[:], in0=pcr[:, 0:128], scalar1=cc[:, 0:1], scalar2=0.0, op0=ALU.subtract, op1=ALU.min), reads=[rpcr, rcc], writes=[rdd])
            e, re_, _ = ee.next()
            P.op("act", lambda: nc.scalar.activation(out=e[:], in_=dt_[:], func=AF.Exp), reads=[rdd], writes=[re_])
            er, rer, _ = ecr.next()
            P.op("act", lambda: nc.scalar.activation(out=er[:], in_=pcr[:, 0:128], func=AF.Exp), reads=[rpcr], writes=[rer])
            pg, rpg, _ = gen.next()
            P.op("pe", lambda: nc.tensor.matmul(pg[:, 0:128], lhsT=bT[:, sl], rhs=cT[:, sl], start=True, stop=True), reads=[rb_, rc_], writes=[rpg])
            g, rg_, _ = gm.next()
            P.op("dve", lambda: nc.vector.tensor_tensor(out=g[:], in0=pg[:, 0:128], in1=U[:, d, :], op=ALU.mult), reads=[rpg, rU], writes=[rg_])
            m, rm, _ = mt.next()
            P.op("pool", lambda: nc.gpsimd.tensor_tensor(out=m[:], in0=g[:], in1=e[:], op=ALU.mult), reads=[rg_, re_], writes=[rm])
            csb, rcs, _ = cs.next()
            P.op("pool", lambda: nc.gpsimd.tensor_tensor(out=csb[:], in0=cT[:, sl], in1=er[:], op=ALU.mult), reads=[rc_, rer], writes=[rcs])
            xd, rxd, _ = xdt.next()
            P.op("dve", lambda: nc.vector.tensor_scalar(out=xd[:], in0=xtokA[:, c, :], scalar1=dt[:, c, d:d + 1], scalar2=None, op0=ALU.mult), reads=[rxt, rdt], writes=[rxd])
            de, rde, _ = ccol.next()
            P.op("act", lambda: nc.scalar.activation(out=de[:, 0:1], in_=cc[:, 0:1], func=AF.Exp, scale=-1.0, bias=cc[:, 1:2]), reads=[rcc], writes=[rde])
            P.op("dve", lambda: nc.vector.tensor_tensor(out=de[:, 1:2], in0=de[:, 0:1], in1=dt[:, c, d:d + 1], op=ALU.mult), reads=[rde, rdt], writes=[rde])
            xe, rxe, _ = xdd.next()
            P.op("pool", lambda: nc.gpsimd.tensor_scalar(out=xe[:], in0=xtokA[:, c, :], scalar1=de[:, 1:2], scalar2=None, op0=ALU.mult), reads=[rxt, rde], writes=[rxe])
            pst, rpst, _ = gen.next()
            P.op("pe", lambda: nc.tensor.matmul(pst[:, 0:64], lhsT=btokA[:, c, :], rhs=xe[:], start=True, stop=True), reads=[rbtk, rxe], writes=[rpst])
            st_, rst_, _ = sst.next()
            P.op("act", lambda: nc.scalar.copy(out=st_[:], in_=pst[:, 0:64]), reads=[rpst], writes=[rst_])
            return (cc, rcc, m, rm, csb, rcs, xd, rxd, st_, rst_)

        def p2(d, c, hnd):
            cc, rcc, m, rm, csb, rcs, xd, rxd, st_, rst_ = hnd
            sl = slice(c * 128, (c + 1) * 128)
            hb, rhb = hcur[d] if hcur[d] is not None else (hsb[d], rhsb[d])
            py, rpy, _ = gen.next()
            P.op("pe", lambda: nc.tensor.matmul(py[0:64, 0:128], lhsT=xd[:], rhs=m[:], start=True, stop=False), reads=[rxd, rm], writes=[rpy])
            P.op("pe", lambda: nc.tensor.matmul(py[0:64, 0:128], lhsT=hb[:], rhs=csb[:], start=False, stop=True), reads=[rhb, rcs], writes=[rpy])
            P.op("dve", lambda: nc.vector.scalar_tensor_tensor(out=hs[d][:], in0=hs[d][:], scalar=cc[:, 3:4], in1=st_[:], op0=ALU.mult, op1=ALU.add),
                 reads=[rhs_[d], rcc, rst_], writes=[rhs_[d]])
            hn, rhn, _ = hsr.next()
            P.op("dve", lambda: nc.vector.tensor_copy(out=hn[:], in_=hs[d][:]), reads=[rhs_[d]], writes=[rhn])
            hcur[d] = (hn, rhn)
            P.op("dve", lambda: nc.vector.tensor_tensor(out=yb[:, sl], in0=yb[:, sl], in1=py[0:64, 0:128], op=ALU.add), reads=[ryc[c], rpy], writes=[ryc[c]])

        pend = None
        for s_ in range(NCH + 1):
            cur = None
            if s_ < NCH:
                cur = [(d, orders[d][s_], p1(d, orders[d][s_])) for d in range(2)]
            if pend is not None:
                for (d, c, hnd) in pend:
                    p2(d, c, hnd)
            pend = cur
        ry_all = ryc
        for i in range(3):
            P.dma("sp", YSc[i][:, :], yb[:, i * YCH:(i + 1) * YCH], ds_, reads=ry_all, writes=[r_YS])
        P.pop_scope()
        for i in range(3):
            allgather(P, YSc[i], YGc[i], [r_YS], [r_YG])

        P.push_scope()
        ost = Rot(P, "tost", 2, [128, 4, 256], F32)

        def emit_oT(o, ro, ncol, tile0, chunk0, mla):
            for c in range(ncol):
                for half in range(2):
                    pt, rpt, _ = A.misc.next()
                    P.op("pe", lambda c=c, half=half, pt=pt: nc.tensor.transpose(pt[:, 0:128], o[:, c, half * 128:(half + 1) * 128], ident[:]), reads=[ro, rid], writes=[rpt])
                    evac(oTs[:, chunk0 + half, (tile0 + c) * 128:(tile0 + c + 1) * 128], pt[:, 0:128], [rpt], [roT])

        P.push_scope()
        EXT = 2816
        dn = P.dsem(f"na{l}")
        q = P.sb("naq", [64, 4, NQ], BF16); rq = P.res()
        k = P.sb("nak", [64, 4, EXT], BF16); rk = P.res()
        v = P.sb("nav", [128, EXT // 128, 4, 65], BF16); rv = P.res()
        qc = P.sb("naqc", [64, 4, NCX], BF16); kc_ = P.sb("nakc", [64, 4, NCX], BF16); vc = P.sb("navc", [128, 2, 4, 65], BF16); rc = P.res()
        P.op("dve", lambda: nc.vector.memset(v[:], 1.0), writes=[rv])
        P.op("dve", lambda: nc.vector.memset(vc[:], 1.0), writes=[rc])
        hk = Rot(P, "hk", 2, [64, 4, 384], F32, dma=True)
        hacc = Rot(P, "hacc", 2, [64, 384], F32)
        hv = Rot(P, "hv", 2, [128, 4, 768], F32, dma=True)
        hvacc = Rot(P, "hvacc", 2, [128, 768], F32)

        def halo_k(dst, row0, npart, width, side, hkr, haccr, rowperm=None):
            t_, rt_, dt__ = hkr.next()
            c0 = NQ - width if side == 0 else 0
            for r in range(4):
                for (d0_, s0_, nr_) in (rowperm or ((0, 0, npart),)):
                    P.dma("sp", t_[d0_:d0_ + nr_, r, :width], rbuf(r, row0 + s0_, nr_, c0, c0 + width), dt__, reads=[r_RB2], writes=[rt_])
            a_, ra_, _ = haccr.next()
            f0 = 4 if side == 0 else 8
            P.op("dve", lambda: nc.vector.tensor_scalar(out=a_[:npart, :width], in0=t_[:npart, 0, :width], scalar1=flg[:npart, f0:f0 + 1], scalar2=None, op0=ALU.mult),
                 reads=[rt_, rflg], writes=[ra_])
            for r in range(1, 4):
                P.op("dve", lambda r=r: nc.vector.scalar_tensor_tensor(out=a_[:npart, :width], in0=t_[:npart, r, :width], scalar=flg[:npart, f0 + r:f0 + r + 1], in1=a_[:npart, :width],
                                                                     op0=ALU.mult, op1=ALU.add), reads=[rt_, rflg, ra_], writes=[ra_])
            return a_, ra_

        def halo_v(col0, ncolv, ntile, side, hvr, hvaccr):
            t_, rt_, dt__ = hvr.next()
            r0 = NQ - ntile * 128 if side == 0 else 0
            for r in range(4):
                P.dma("sp", t_[:, r, 0:ntile * ncolv].rearrange("p (t c) -> p t c", c=ncolv),
                      vgbuf(r, r0, ntile * 128, col0, col0 + ncolv).rearrange("(t p) c -> p t c", p=128), dt__, reads=[r_VG], writes=[rt_])
            a_, ra_, _ = hvaccr.next()
            f0 = 4 if side == 0 else 8
            w_ = ntile * ncolv
            P.op("dve", lambda: nc.vector.tensor_scalar(out=a_[:, :w_], in0=t_[:, 0, :w_], scalar1=flg[:, f0:f0 + 1], scalar2=None, op0=ALU.mult), reads=[rt_, rflg], writes=[ra_])
            for r in range(1, 4):
                P.op("dve", lambda r=r: nc.vector.scalar_tensor_tensor(out=a_[:, :w_], in0=t_[:, r, :w_], scalar=flg[:, f0 + r:f0 + r + 1], in1=a_[:, :w_], op0=ALU.mult, op1=ALU.add),
                     reads=[rt_, rflg, ra_], writes=[ra_])
            return a_, ra_

        for h in range(4):
            P.dma("pool", q[:, h, :], pT_d[h * 64:(h + 1) * 64, 0:NQ], dn, reads=[r_pT], writes=[rq])
            P.dma("pool", k[:, h, 384:384 + NQ], pT_d[256 + h * 64:256 + (h + 1) * 64, 0:NQ], dn, reads=[r_pT], writes=[rk])
            P.dma("pool", qc[:, h, :], pT_d[h * 64:(h + 1) * 64, NQ:NTOK], dn, reads=[r_pT], writes=[rc])
            P.dma("pool", kc_[:, h, :], pT_d[256 + h * 64:256 + (h + 1) * 64, NQ:NTOK], dn, reads=[r_pT], writes=[rc])
            for side in range(2):
                a_, ra_ = halo_k(None, R_NAK + h * 64, 64, 384, side, hk, hacc)
                off = 0 if side == 0 else 384 + NQ
                P.op("act", lambda a_=a_, off=off, h=h: nc.scalar.copy(out=k[:, h, off:off + 384], in_=a_[:64, :384]), reads=[ra_], writes=[rk])
        vsrc = vtok_d[:, 0:256].rearrange("(t p) (h d) -> p t h d", p=128, h=4)
        for t in range(16):
            P.dma("pool", v[:, 3 + t, :, 0:64], vsrc[:, t, :, :], dn, reads=[r_vtok], writes=[rv])
        for t in range(2):
            P.dma("pool", vc[:, t, :, 0:64], vsrc[:, 16 + t, :, :], dn, reads=[r_vtok], writes=[rc])
        for side in range(2):
            a_, ra_ = halo_v(0, 256, 3, side, hv, hvacc)
            t0_ = 0 if side == 0 else 19
            for t in range(3):
                P.op("act", lambda a_=a_, t=t, t0_=t0_: nc.scalar.copy(out=v[:, t0_ + t, :, 0:64], in_=a_[:, t * 256:(t + 1) * 256].rearrange("p (h d) -> p h d", h=4)),
                     reads=[ra_], writes=[rv])
        bias = Rot(P, "nab", 2, [128, 7, 512], F32, dma=True)
        scale = 64 ** -0.5
        for t in range(18):
            items = []
            if t < 16:
                pat = 0 if t == 0 else 1 if t == 1 else 3 if t == 14 else 4 if t == 15 else 2
                bt_, rb, db = bias.next()
                P.dma("sp", bt_[:], nab_d[l][pat], db, writes=[rb])
                qa = [[q[:, h, t * 128:(t + 1) * 128]] for h in range(4)]
                for kt in (range(1, 6) if pat == 2 else range(7)):
                    et = t + kt
                    items.append(dict(k=[[k[:, h, et * 128:(et + 1) * 128]] for h in range(4)], v=[v[:, et, h, :] for h in range(4)], bias=bt_[:, kt, :], bres=[rb], res=[rk, rv]))
                qres = [rq]
            else:
                tc = t - 16
                qa = [[qc[:, h, tc * 128:(tc + 1) * 128]] for h in range(4)]
                qres = [rc]
            for kt in range(2):
                items.append(dict(k=[[kc_[:, h, kt * 128:(kt + 1) * 128]] for h in range(4)], v=[vc[:, kt, h, :] for h in range(4)], bias=None, res=[rc]))
            o, ro, _ = ost.next()
            attend(A, 4, False, qa, qres, items, scale, [o[:, 0, h * 64:(h + 1) * 64] for h in range(4)], ro)
            emit_oT(o, ro, 1, t, 0, False)
        P.pop_scope()

        P.push_scope()
        EXT = 2304
        dn = P.dsem(f"sw{l}")
        q = P.sb("swq", [64, 4, NQ], BF16); rq = P.res()
        k = P.sb("swk", [64, 2, EXT], BF16); rk = P.res()
        v = P.sb("swv", [128, EXT // 128, 2, 65], BF16); rv = P.res()
        qc = P.sb("swqc", [64, 4, NCX], BF16); kc_ = P.sb("swkc", [64, 2, NCX], BF16); vc = P.sb("swvc", [128, 2, 2, 65], BF16); rc = P.res()
        P.op("dve", lambda: nc.vector.memset(v[:], 1.0), writes=[rv])
        P.op("dve", lambda: nc.vector.memset(vc[:], 1.0), writes=[rc])
        cs_ = P.sb("swcs", [64, 2, EXT], F32); rcs_ = P.res()
        bs = P.sb("swb", [128, 4, 128], F32); rbs = P.res()
        sk = P.sb("swsk", [128, 4], F32); rsk = P.res()
        P.dma("sp", cs_[:, 0, :], swcs_d[0], dn, writes=[rcs_]); P.dma("sp", cs_[:, 1, :], swcs_d[1], dn, writes=[rcs_])
        P.dma("sp", bs[:], swb_d, dn, writes=[rbs]); P.dma("sp", sk[:], swk_d[l], dn, writes=[rsk])
        P.op("act", lambda: nc.scalar.activation(out=sk[:], in_=sk[:], func=AF.Exp), reads=[rsk], writes=[rsk])
        stg = Rot(P, "swstg", 4, [64, EXT], F32, dma=True)
        hk2 = Rot(P, "hk2", 2, [64, 4, 128], F32, dma=True)
        hacc2 = Rot(P, "hacc2", 4, [64, 128], F32)
        hv2 = Rot(P, "hv2", 2, [128, 4, 128], F32, dma=True)
        hvacc2 = Rot(P, "hvacc2", 2, [128, 128], F32)
        perm = ((0, 16), (16, 0), (32, 48), (48, 32))

        def rope_rows(dst, res, row0, col0, n, tab0, extra_writer=None):
            a, ra, da = stg.next(); b, rb_, db = stg.next()
            P.dma("sp", a[:, :n], pT_d[row0:row0 + 64, col0:col0 + n], da, reads=[r_pT], writes=[ra])
            for (d0, s0) in perm:
                P.dma("sp", b[d0:d0 + 16, :n], pT_d[row0 + s0:row0 + s0 + 16, col0:col0 + n], db, reads=[r_pT], writes=[rb_])
            P.op("dve", lambda: nc.vector.tensor_tensor(out=a[:, :n], in0=a[:, :n], in1=cs_[:, 0, tab0:tab0 + n], op=ALU.mult), reads=[ra, rcs_], writes=[ra])
            P.op("pool", lambda: nc.gpsimd.tensor_tensor(out=b[:, :n], in0=b[:, :n], in1=cs_[:, 1, tab0:tab0 + n], op=ALU.mult), reads=[rb_, rcs_], writes=[rb_])
            P.op("dve", lambda: nc.vector.tensor_tensor(out=dst, in0=a[:, :n], in1=b[:, :n], op=ALU.add), reads=[ra, rb_], writes=[res])

        for h in range(4):
            rope_rows(q[:, h, :], rq, SW0 + h * 64, 0, NQ, 128)
            P.dma("pool", qc[:, h, :], pT_d[SW0 + h * 64:SW0 + (h + 1) * 64, NQ:NTOK], dn, reads=[r_pT], writes=[rc])
        for g in range(2):
            rope_rows(k[:, g, 128:128 + NQ], rk, SW0 + 256 + g * 64, 0, NQ, 128)
            P.dma("pool", kc_[:, g, :], pT_d[SW0 + 256 + g * 64:SW0 + 256 + (g + 1) * 64, NQ:NTOK], dn, reads=[r_pT], writes=[rc])
            for side in range(2):
                a_, ra_ = halo_k(None, R_SWK + g * 64, 64, 128, side, hk2, hacc2)
                b_, rb2 = halo_k(None, R_SWK + g * 64, 64, 128, side, hk2, hacc2, rowperm=[(d0, s0, 16) for (d0, s0) in perm])
                tab0 = 0 if side == 0 else 128 + NQ
                P.op("dve", lambda a_=a_, tab0=tab0: nc.vector.tensor_tensor(out=a_[:64, :128], in0=a_[:64, :128], in1=cs_[:, 0, tab0:tab0 + 128], op=ALU.mult), reads=[ra_, rcs_], writes=[ra_])
                P.op("dve", lambda b_=b_, tab0=tab0: nc.vector.tensor_tensor(out=b_[:64, :128], in0=b_[:64, :128], in1=cs_[:, 1, tab0:tab0 + 128], op=ALU.mult), reads=[rb2, rcs_], writes=[rb2])
                P.op("dve", lambda a_=a_, b_=b_, g=g, tab0=tab0: nc.vector.tensor_tensor(out=k[:, g, tab0:tab0 + 128], in0=a_[:64, :128], in1=b_[:64, :128], op=ALU.add),
                     reads=[ra_, rb2], writes=[rk])
        vsrc = vtok_d[:, 256:384].rearrange("(t p) (h d) -> p t h d", p=128, h=2)
        for t in range(16):
            P.dma("pool", v[:, 1 + t, :, 0:64], vsrc[:, t, :, :], dn, reads=[r_vtok], writes=[rv])
        for t in range(2):
            P.dma("pool", vc[:, t, :, 0:64], vsrc[:, 16 + t, :, :], dn, reads=[r_vtok], writes=[rc])
        for side in range(2):
            a_, ra_ = halo_v(256, 128, 1, side, hv2, hvacc2)
            t0_ = 0 if side == 0 else 17
            P.op("act", lambda a_=a_, t0_=t0_: nc.scalar.copy(out=v[:, t0_, :, 0:64], in_=a_[:, 0:128].rearrange("p (h d) -> p h d", h=2)), reads=[ra_], writes=[rv])
        bp = P.sb("swbp", [128, 4, 4, 128], F32); rbp = P.res()
        for kind in range(4):
            for h in range(4):
                P.op("dve", lambda kind=kind, h=h: nc.vector.tensor_copy(out=bp[:, kind, h, :], in_=bs[:, kind, :]), reads=[rbs], writes=[rbp])
        for t in range(18):
            items = []
            if t < 16:
                qa = [[q[:, h, t * 128:(t + 1) * 128]] for h in range(4)]
                qres = [rq]
                for kt in range(3):
                    et = t + kt
                    if kt == 0:
                        b = bp[:, 0 if t == 0 else 1, :, :].rearrange('p h q -> p (h q)')
                    elif kt == 2:
                        b = bp[:, 3 if t == 15 else 2, :, :].rearrange('p h q -> p (h q)')
                    else:
                        b = None
                    items.append(dict(k=[[k[:, h // 2, et * 128:(et + 1) * 128]] for h in range(4)], v=[v[:, et, h // 2, :] for h in range(4)], bias=b, bres=[rbp], res=[rk, rv]))
            else:
                tc = t - 16
                qa = [[qc[:, h, tc * 128:(tc + 1) * 128]] for h in range(4)]
                qres = [rc]
            for kt in range(2):
                items.append(dict(k=[[kc_[:, h // 2, kt * 128:(kt + 1) * 128]] for h in range(4)], v=[vc[:, kt, h // 2, :] for h in range(4)], bias=None, res=[rc]))
            o, ro, _ = ost.next()
            attend(A, 4, False, qa, qres, items, scale, [o[:, 0, h * 64:(h + 1) * 64] for h in range(4)], ro, sinkexp=sk[:, 0:4], sink_res=rsk)
            emit_oT(o, ro, 1, t, 2, False)
        P.pop_scope()

        P.push_scope()
        NK = LSEQ
        dsm = P.dsem(f"ml{l}")
        gkv = P.sb("gkv", [128, 1], F32); gq = P.sb("gq", [128, 2], F32); rg = P.res()
        P.dma("sp", gkv[:], mgkv_d[l], dsm, writes=[rg]); P.dma("sp", gq[:], mgq_d[l], dsm, writes=[rg])
        wkn = P.sb("wkn", [128, 256], BF16); wkv = P.sb("wkv", [128, 256], BF16)
        wqn = P.sb("wqn", [128, 2, 256], BF16); wqr = P.sb("wqr", [128, 2, 128], BF16); wqrs = P.sb("wqrs", [128, 2, 128], BF16)
        rw = P.res(); dw = P.dsem(f"mw{l}")
        P.dma("pool", wkn[:], mwkn_d[l], dw, writes=[rw]); P.dma("pool", wkv[:], mwkv_d[l], dw, writes=[rw])
        for c in range(2):
            P.dma("pool", wqn[:, c, :], mwqn_d[l][c * 128:(c + 1) * 128, :], dw, writes=[rw])
            P.dma("pool", wqr[:, c, :], mwqr_d[l][c * 128:(c + 1) * 128, :], dw, writes=[rw])
            P.dma("pool", wqrs[:, c, :], mwqrs_d[l][c * 128:(c + 1) * 128, :], dw, writes=[rw])
        K96 = P.sb("K96", [96, 4, NK], BF16); rkn = P.res(); rkr = P.res()
        vm = P.sb("vm", [128, NK // 128, 4, 65], BF16); rvm = P.res()
        Q96 = P.sb("Q96", [96, 4, NTOK], BF16); rqn = P.res(); rqr = P.res()
        rrow = P.sb("rrow", [65, 512], F32); rrr_ = P.res()
        bcs = P.sb("bcs", [64, 512], F32); rbcs = P.res()
        P.op("dve", lambda: nc.vector.memset(vm[:], 1.0), writes=[rvm])
        xin = Rot(P, "mx", 2, [128, 2, 512], F32, dma=True)
        sq = Rot(P, "msq", 1, [128, 2, 512], BF16)
        rms = Rot(P, "mrms", 1, [128, 512], F32)
        xn = Rot(P, "mxn", 2, [128, 2, 512], BF16)
        tab = Rot(P, "mtab", 2, [32, 2, 512], F32, dma=True)
        rr4 = Rot(P, "mrr", 4, [32, 512], F32, dma=True)
        perm32 = ((0, 8), (8, 0), (16, 24), (24, 16))

        def key_src(row0, nrows, t0, n):
            out = []
            if t0 < 256:
                ln = min(n, 256 - t0)
                out.append((rbuf(0, row0, nrows, NQ + t0, NQ + t0 + ln), 0))
                if ln < n:
                    for (r, ls, l2, do_) in seg_lat(0, n - ln):
                        out.append((rbuf(r, row0, nrows, ls, ls + l2), ln + do_))
            else:
                for (r, ls, l2, do_) in seg_lat(t0 - 256, n):
                    out.append((rbuf(r, row0, nrows, ls, ls + l2), do_))
            return out

        def norm_blk(loads, kc, n, g):
            x, rx, dx = xin.next()
            for (c, p0, ap, off) in loads:
                P.dma("sp", x[p0:p0 + ap.shape[0], c, off:off + ap.shape[1]], ap, dx, reads=[r_RB2, r_pT], writes=[rx])
            s, rs, _ = sq.next(); r, rr, _ = rms.next()
            rstd_of(x, rx, kc, n, r, rr, s, rs, kc * 128)
            y, ry, _ = xn.next()
            for c in range(kc):
                P.op("dve", lambda c=c: nc.vector.scalar_tensor_tensor(out=y[:, c, :n], in0=x[:, c, :n], scalar=g[:, c:c + 1], in1=r[:, :n], op0=ALU.mult, op1=ALU.mult),
                     reads=[rx, rr, rg], writes=[ry])
            return y, ry

        for t0 in range(0, NK, 512):
            n = min(512, NK - t0)
            y, ry = norm_blk([(0, hf * 64, ap, off) for hf in range(2) for (ap, off) in key_src(R_CKV + hf * 64, 64, t0, n)], 1, n, gkv)
            for pr in range(2):
                pt, rpt, _ = A.misc.next()
                P.op("pe", lambda pr=pr, pt=pt: nc.tensor.matmul(pt[:, :n], lhsT=wkn[:, pr * 128:(pr + 1) * 128], rhs=y[:, 0, :n], start=True, stop=True), reads=[rw, ry], writes=[rpt])
                evac(K96[0:64, 2 * pr, t0:t0 + n], pt[0:64, :n], [rpt], [rkn])
                evac(K96[0:64, 2 * pr + 1, t0:t0 + n], pt[64:128, :n], [rpt], [rkn])
            for tt_ in range(n // 128):
                kt = t0 // 128 + tt_
                pt, rpt, _ = A.misc.next()
                P.op("pe", lambda tt_=tt_, pt=pt: nc.tensor.matmul(pt[:, 0:256], lhsT=y[:, 0, tt_ * 128:(tt_ + 1) * 128], rhs=wkv[:], start=True, stop=True), reads=[rw, ry], writes=[rpt])
                evac(vm[:, kt, :, 0:64], pt[:, 0:256].rearrange("p (h d) -> p h d", h=4), [rpt], [rvm])
            tb, rtb, dtb = tab.next()
            P.dma("sp", tb[:, 0, :n], mkcs_d[0][:, t0:t0 + n], dtb, writes=[rtb]); P.dma("sp", tb[:, 1, :n], mkcs_d[1][:, t0:t0 + n], dtb, writes=[rtb])
            a, ra, da = rr4.next(); b, rb_, db = rr4.next()
            for (ap, off) in key_src(R_KR, 32, t0, n):
                P.dma("sp", a[:, off:off + ap.shape[1]], ap, da, reads=[r_RB2], writes=[ra])
            for (d0, s0) in perm32:
                for (ap, off) in key_src(R_KR + s0, 8, t0, n):
                    P.dma("sp", b[d0:d0 + 8, off:off + ap.shape[1]], ap, db, reads=[r_RB2], writes=[rb_])
            P.op("dve", lambda: nc.vector.tensor_tensor(out=a[:, :n], in0=a[:, :n], in1=tb[:, 0, :n], op=ALU.mult), reads=[ra, rtb], writes=[ra])
            P.op("pool", lambda: nc.gpsimd.tensor_tensor(out=b[:, :n], in0=b[:, :n], in1=tb[:, 1, :n], op=ALU.mult), reads=[rb_, rtb], writes=[rb_])
            for h4 in range(4):
                P.op("dve" if h4 % 2 == 0 else "pool", lambda h4=h4: (nc.vector if h4 % 2 == 0 else nc.gpsimd).tensor_tensor(out=K96[64:96, h4, t0:t0 + n], in0=a[:, :n], in1=b[:, :n], op=ALU.add),
                     reads=[ra, rb_], writes=[rkr])
        for t0 in range(0, NTOK, 512):
            n = min(512, NTOK - t0)
            y, ry = norm_blk([(c, 0, pT_d[ML0 + c * 128:ML0 + (c + 1) * 128, t0:t0 + n], 0) for c in range(2)], 2, n, gq)
            for pr in range(2):
                pt, rpt, _ = A.misc.next()
                for c in range(2):
                    P.op("pe", lambda pr=pr, pt=pt, c=c: nc.tensor.matmul(pt[:, :n], lhsT=wqn[:, c, pr * 128:(pr + 1) * 128], rhs=y[:, c, :n], start=(c == 0), stop=(c == 1)),
                         reads=[rw, ry], writes=[rpt])
                evac(Q96[0:64, 2 * pr, t0:t0 + n], pt[0:64, :n], [rpt], [rqn])
                evac(Q96[0:64, 2 * pr + 1, t0:t0 + n], pt[64:128, :n], [rpt], [rqn])
            tb, rtb, dtb = tab.next()
            P.dma("sp", tb[:, 0, :n], mqcs_d[0][:, t0:t0 + n], dtb, writes=[rtb]); P.dma("sp", tb[:, 1, :n], mqcs_d[1][:, t0:t0 + n], dtb, writes=[rtb])
            for h in range(4):
                pa, rpa, _ = A.misc.next()
                for c in range(2):
                    P.op("pe", lambda pa=pa, c=c, h=h: nc.tensor.matmul(pa[0:32, :n], lhsT=wqr[:, c, h * 32:(h + 1) * 32], rhs=y[:, c, :n], start=(c == 0), stop=(c == 1)),
                         reads=[rw, ry], writes=[rpa])
                a, ra, _ = rr4.next()
                P.op("dve", lambda a=a, pa=pa: nc.vector.tensor_tensor(out=a[:, :n], in0=pa[0:32, :n], in1=tb[:, 0, :n], op=ALU.mult), reads=[rpa, rtb], writes=[ra])
                pb, rpb, _ = A.misc.next()
                for c in range(2):
                    P.op("pe", lambda pb=pb, c=c, h=h: nc.tensor.matmul(pb[0:32, :n], lhsT=wqrs[:, c, h * 32:(h + 1) * 32], rhs=y[:, c, :n], start=(c == 0), stop=(c == 1)),
                         reads=[rw, ry], writes=[rpb])
                b, rb_, _ = rr4.next()
                P.op("dve", lambda b=b, pb=pb: nc.vector.tensor_tensor(out=b[:, :n], in0=pb[0:32, :n], in1=tb[:, 1, :n], op=ALU.mult), reads=[rpb, rtb], writes=[rb_])
                P.op("pool", lambda a=a, b=b, h=h: nc.gpsimd.tensor_tensor(out=Q96[64:96, h, t0:t0 + n], in0=a[:, :n], in1=b[:, :n], op=ALU.add), reads=[ra, rb_], writes=[rqr])
        scale = 96 ** -0.5
        groups = [(g * 512, 4, NK // 128) for g in range(4)] + [(2048, 2, 2)]
        for (q0, ncol, nkt) in groups:
            W_ = ncol * 128
            for h in range(4):
                acc, racc, _ = A.accs.next()
                def score(kt):
                    pan, rpan, _ = A.panels.next()
                    P.op("pe", lambda: nc.tensor.matmul(pan[:, 0:W_], lhsT=K96[0:96, h, kt * 128:(kt + 1) * 128], rhs=Q96[0:96, h, q0:q0 + W_], start=True, stop=True),
                         reads=[rkn, rkr, rqn, rqr], writes=[rpan])
                    pt_, rpt_, _ = A.pT.next()
                    P.op("act", lambda: nc.scalar.activation(out=pt_[:, 0:W_], in_=pan[:, 0:W_], func=AF.Exp, scale=scale), reads=[rpan], writes=[rpt_])
                    return pt_, rpt_
                nxt = score(0)
                for kt in range(nkt):
                    pt_, rpt_ = nxt
                    if kt + 1 < nkt:
                        nxt = score(kt + 1)
                    P.op("pe", lambda: nc.tensor.matmul(acc[0:65, 0:W_], lhsT=vm[:, kt, h, :], rhs=pt_[:, 0:W_], start=(kt == 0), stop=(kt == nkt - 1)),
                         reads=[rvm, rpt_], writes=[racc])
                P.op("dve", lambda: nc.vector.reciprocal(out=rrow[64:65, 0:W_], in_=acc[64:65, 0:W_]), reads=[racc], writes=[rrr_])
                pb_, rpb_, _ = A.misc.next()
                P.op("pe", lambda: nc.tensor.matmul(pb_[0:64, 0:W_], lhsT=onesf[64:65, 0:64], rhs=rrow[64:65, 0:W_], start=True, stop=True), reads=[ronesf, rrr_], writes=[rpb_])
                P.op("act", lambda: nc.scalar.copy(out=bcs[:, 0:W_], in_=pb_[0:64, 0:W_]), reads=[rpb_], writes=[rbcs])
                hp_ = (h % 2) * 64
                P.op("dve", lambda: nc.vector.tensor_tensor(out=oTs[hp_:hp_ + 64, 4 + h // 2, q0:q0 + W_], in0=acc[0:64, 0:W_], in1=bcs[:, 0:W_], op=ALU.mult),
                     reads=[racc, rbcs], writes=[roT])
        P.pop_scope()
        P.pop_scope()

        NB = 256
        fblocks = [(t0, NB, 0 if t0 < NQ else 1) for t0 in range(0, NTOK, NB)]
        P.push_scope()
        wo = P.sb("wo", [128, KC, D], BF16); rwo = P.res(); dwo = P.dsem(f"wo{l}")
        for kk in range(KC):
            P.dma("pool", wo[:, kk, :], wo_d[l][kk * 128:(kk + 1) * 128, :], dwo, writes=[rwo])
        hin = Rot(P, "fhin", 3, [128, KC, NB], F32, dma=True)
        ycand = Rot(P, "ycand", 3, [128, 2, 4, NB], F32, dma=True)
        yin = Rot(P, "yin", 3, [128, 2, NB], F32)
        zin = Rot(P, "zin", 3, [128, 2, NB], F32, dma=True)
        sq = Rot(P, "fsq", 2, [128, 2, NB], BF16)
        rms = Rot(P, "frms", 2, [128, NB], F32)
        od = Rot(P, "fod", 2, [128, 2, NB], BF16)
        for (t0, n, j) in fblocks:
            h, rh, dh = hin.next()
            P.dma("sp", h[:, :, :n], hT_d.rearrange("(k p) t -> p k t", p=128)[:, :, t0:t0 + n], dh, reads=[r_hin], writes=[rh])
            y, ry, _ = yin.next()
            if j == 0:
                yc, ryc, dyc = ycand.next()
                for kk in range(2):
                    for r in range(4):
                        col = 256 + r * NQ + t0
                        P.dma("sp", yc[:, kk, r, :n], YGc[col // YCH][kk * 128:(kk + 1) * 128, col % YCH:col % YCH + n], dyc, reads=[r_YG], writes=[ryc])
                for kk in range(2):
                    P.op("dve", lambda kk=kk: nc.vector.tensor_scalar(out=y[:, kk, :n], in0=yc[:, kk, 0, :n], scalar1=flg[:, 0:1], scalar2=None, op0=ALU.mult),
                         reads=[ryc, rflg], writes=[ry])
                    for r in range(1, 4):
                        P.op("dve", lambda kk=kk, r=r: nc.vector.scalar_tensor_tensor(out=y[:, kk, :n], in0=yc[:, kk, r, :n], scalar=flg[:, r:r + 1], in1=y[:, kk, :n],
                                                                                    op0=ALU.mult, op1=ALU.add), reads=[ryc, rflg, ry], writes=[ry])
            else:
                yc, ryc, dyc = ycand.next()
                for kk in range(2):
                    P.dma("sp", yc[:, kk, 0, :n], YGc[0][kk * 128:(kk + 1) * 128, 0:256], dyc, reads=[r_YG], writes=[ryc])
                P.op("dve", lambda: nc.vector.tensor_copy(out=y[:, :, :n], in_=yc[:, :, 0, :n]), reads=[ryc], writes=[ry])
            z, rz, dz = zin.next()
            P.dma("sp", z[:, :, :n], pT_d[SS0:SS0 + 256, :].rearrange("(k p) t -> p k t", p=128)[:, :, t0:t0 + n], dz, reads=[r_pT], writes=[rz])
            P.op("act", lambda: nc.scalar.activation(out=z[:, :, :n], in_=z[:, :, :n], func=AF.Silu), reads=[rz], writes=[rz])
            P.op("dve", lambda: nc.vector.tensor_tensor(out=y[:, :, :n], in0=y[:, :, :n], in1=z[:, :, :n], op=ALU.mult), reads=[ry, rz], writes=[ry])
            s, rs, _ = sq.next(); r, rr, _ = rms.next()
            rstd_of(y, ry, 2, n, r, rr, s, rs, 256)
            odt, rod, _ = od.next()
            for kk in range(2):
                P.op("dve", lambda kk=kk: nc.vector.scalar_tensor_tensor(out=odt[:, kk, :n], in0=y[:, kk, :n], scalar=gvs[:, 16 + kk:17 + kk], in1=r[:, :n], op0=ALU.mult, op1=ALU.mult),
                     reads=[ry, rgv, rr], writes=[rod])
            for cb in range(KC):
                pt, rpt, _ = gen.next()
                for kk in range(KC):
                    rhs = oTs[:, kk, t0:t0 + n] if kk < 6 else odt[:, kk - 6, :n]
                    P.op("pe", lambda kk=kk, rhs=rhs, pt=pt: nc.tensor.matmul(pt[:, :n], lhsT=wo[:, kk, cb * 128:(cb + 1) * 128], rhs=rhs, start=(kk == 0), stop=(kk == KC - 1)),
                         reads=[rwo, roT, rod], writes=[rpt])
                P.op("dve", lambda cb=cb, pt=pt: nc.vector.scalar_tensor_tensor(out=h[:, cb, :n], in0=pt[:, :n], scalar=mo[:, 16 + cb, j:j + 1], in1=h[:, cb, :n], op0=ALU.mult, op1=ALU.add),
                     reads=[rpt, rmo, rh], writes=[rh])
            P.dma("sp", h2_d.rearrange("(k p) t -> p k t", p=128)[:, :, t0:t0 + n], h[:, :, :n], dh, reads=[rh], writes=[r_h2])
        P.pop_scope()
        P.push_scope()
        w1 = P.sb("w1", [128, KC, 4 * D], BF16); rw1 = P.res(); dw1 = P.dsem(f"w1{l}")
        w2 = P.sb("w2", [128, 32, D], BF16); rw2 = P.res(); dw2 = P.dsem(f"w2{l}")
        for kk in range(KC):
            for c0 in range(0, 4 * D, 2048):
                P.dma("pool", w1[:, kk, c0:c0 + 2048], w1_d[l][kk * 128:(kk + 1) * 128, c0:c0 + 2048], dw1, writes=[rw1])
        for kk in range(32):
            P.dma("pool", w2[:, kk, :], w2_d[l][kk * 128:(kk + 1) * 128, :], dw2, writes=[rw2])
        hin = Rot(P, "h2in", 1, [128, KC, NB], F32, dma=True)
        sq = Rot(P, "sq2", 1, [128, KC, NB], BF16)
        rms = Rot(P, "rms2", 2, [128, NB], F32)
        tt = Rot(P, "tt2", 2, [128, NB], F32)
        xm = Rot(P, "xm2", 1, [128, KC, NB], BF16)
        at = Rot(P, "at", 1, [128, 32, NB], BF16)
        rl = Rot(P, "rl", 2, [128, NB], F32)
        for (t0, n, j) in fblocks:
            if final and j == 1:
                continue
            h, rh, dh = hin.next()
            P.dma("sp", h[:, :, :n], h2_d.rearrange("(k p) t -> p k t", p=128)[:, :, t0:t0 + n], dh, reads=[r_h2], writes=[rh])
            s, rs, _ = sq.next(); r, rr, _ = rms.next()
            rstd_of(h, rh, KC, n, r, rr, s, rs, D)
            x, rx, _ = xm.next()
            for kk in range(KC):
                t, rt, _ = tt.next()
                P.op("dve", lambda kk=kk, t=t: nc.vector.tensor_tensor(out=t[:, :n], in0=h[:, kk, :n], in1=r[:, :n], op=ALU.mult), reads=[rh, rr], writes=[rt])
                P.op("act", lambda kk=kk, t=t: nc.scalar.activation(out=x[:, kk, :n], in_=t[:, :n], func=AF.Identity, scale=gs2[:, kk, j:j + 1], bias=mo[:, 24 + kk, j:j + 1]),
                     reads=[rt, rgs, rmo], writes=[rx])
            a, ra, _ = at.next()
            for cb in range(32):
                pt, rpt, _ = gen.next()
                for kk in range(KC):
                    P.op("pe", lambda kk=kk, pt=pt: nc.tensor.matmul(pt[:, :n], lhsT=w1[:, kk, cb * 128:(cb + 1) * 128], rhs=x[:, kk, :n], start=(kk == 0), stop=(kk == KC - 1)),
                         reads=[rw1, rx], writes=[rpt])
                qq, rq_, _ = rl.next()
                P.op("act", lambda pt=pt, qq=qq: nc.scalar.activation(out=qq[:, :n], in_=pt[:, :n], func=AF.Relu), reads=[rpt], writes=[rq_])
                P.op("pool", lambda cb=cb, qq=qq: nc.gpsimd.tensor_tensor(out=a[:, cb, :n], in0=qq[:, :n], in1=qq[:, :n], op=ALU.mult), reads=[rq_], writes=[ra])
            for cb in range(KC):
                pt, rpt, _ = gen.next()
                for kk in range(32):
                    P.op("pe", lambda kk=kk, pt=pt: nc.tensor.matmul(pt[:, :n], lhsT=w2[:, kk, cb * 128:(cb + 1) * 128], rhs=a[:, kk, :n], start=(kk == 0), stop=(kk == 31)),
                         reads=[rw2, ra], writes=[rpt])
                P.op("dve", lambda cb=cb, pt=pt: nc.vector.scalar_tensor_tensor(out=h[:, cb, :n], in0=pt[:, :n], scalar=mo[:, 40 + cb, j:j + 1], in1=h[:, cb, :n], op0=ALU.mult, op1=ALU.add),
                     reads=[rpt, rmo, rh], writes=[rh])
            if final:
                s, rs, _ = sq.next(); r, rr, _ = rms.next()
                rstd_of(h, rh, KC, n, r, rr, s, rs, D)
                for kk in range(KC):
                    P.op("dve", lambda kk=kk: nc.vector.scalar_tensor_tensor(out=h[:, kk, :n], in0=h[:, kk, :n], scalar=gvs[:, 8 + kk:9 + kk], in1=r[:, :n], op0=ALU.mult, op1=ALU.mult),
                         reads=[rh, rgv, rr], writes=[rh])
                P.dma("sp", out_d.rearrange("(k p) t -> p k t", p=128)[:, :, t0:t0 + n], h[:, :, :n], dh, reads=[rh], writes=[P.res()], final=True)
            else:
                last = (l == nlayers - 1)
                if last and j == 0:
                    P.dma("sp", out_d.rearrange("(k p) t -> p k t", p=128)[:, :, t0:t0 + n], h[:, :, :n], dh, reads=[rh], writes=[P.res()], final=True)
                P.dma("sp", hTn_d.rearrange("(k p) t -> p k t", p=128)[:, :, t0:t0 + n], h[:, :, :n], dh, reads=[rh], writes=[r_hn])
        P.pop_scope()
    P.finish()
    P.close()
    return P


NEG = -1e30
NA0, SW0, ML0, SS0 = 0, 768, 1280, 1696


def cvec(v, n):
    return np.ascontiguousarray(v.reshape(n, 128).T)


def swap_idx(dim):
    q = dim // 4
    idx = np.arange(dim)
    blk = (idx // q) % 2
    return np.where(blk == 0, idx + q, idx - q)


def rope_tables(pos, dim):
    nf = dim // 4
    inv = (1.0 / (10000.0 ** (np.arange(nf, dtype=np.float32) / nf))).astype(np.float32)
    row = (pos // 64).astype(np.float32)
    col = (pos % 64).astype(np.float32)
    d = np.arange(dim)
    f = d % nf
    p = np.where((d < dim // 2)[:, None], row[None, :], col[None, :]).astype(np.float32)
    ang = (p * inv[f][:, None]).astype(np.float32)
    sign = np.where(((d // nf) % 2) == 0, -1.0, 1.0).astype(np.float32)
    return np.cos(ang).astype(np.float32), (np.sin(ang) * sign[:, None]).astype(np.float32)


def ext_rows(a, lo, hi):
    S = a.shape[0]
    out = np.zeros((hi - lo,) + a.shape[1:], a.dtype)
    l2, h2 = max(lo, 0), min(hi, S)
    out[l2 - lo:h2 - lo] = a[l2:h2]
    return out


def na_bias_tile(rpb, T):
    q = np.arange(128)
    r = 2 * T + q // 64
    qc = q % 64
    rs = np.clip(r - 4, 0, 120)
    cst = np.clip(qc - 8, 0, 48)
    out = np.full((128, 7, 4, 128), NEG, np.float32)
    i = np.arange(128)
    for kt in range(7):
        krow = 2 * T - 6 + 2 * kt + i // 64
        kcol = i % 64
        valid = ((krow[:, None] >= rs[None, :]) & (krow[:, None] < rs[None, :] + 8) & (krow[:, None] >= 0) & (krow[:, None] < 128)
                 & (kcol[:, None] >= cst[None, :]) & (kcol[:, None] < cst[None, :] + 16))
        dr = np.clip(krow[:, None] - r[None, :] + 7, 0, 14)
        dc = np.clip(kcol[:, None] - qc[None, :] + 15, 0, 30)
        for h in range(4):
            out[:, kt, h, :] = np.where(valid, rpb[h][dr, dc], NEG)
    return out


def prep_T(p_b, pc_b, j, W, l):
    o0 = 2048 * j
    T = lambda a: np.ascontiguousarray(a.T)
    m = {}
    own = p_b[o0:o0 + 2048]
    m["na_q"] = T(own[:, 0:256])
    e = ext_rows(p_b[:, 256:768], o0 - 384, o0 + 2048 + 384)
    m["na_k"] = T(e[:, 0:256]); m["na_v"] = np.ascontiguousarray(e[:, 256:512])
    m["na_qc"] = T(pc_b[:, 0:256]); m["na_kc"] = T(pc_b[:, 256:512]); m["na_vc"] = np.ascontiguousarray(pc_b[:, 512:768])
    rpb = W["na_rpb"][l]
    m["na_bias"] = np.stack([na_bias_tile(rpb, 16 * j + t).reshape(128, 7, 512) for t in (0, 1, 8 if j in (0, 3) else 2, 14, 15)])
    sw = swap_idx(64)
    q = own[:, SW0:SW0 + 256].reshape(2048, 4, 64)
    m["sw_q"] = T(q.reshape(2048, 256)); m["sw_qs"] = T(q[:, :, sw].reshape(2048, 256))
    e = ext_rows(p_b[:, SW0 + 256:SW0 + 512], o0 - 128, o0 + 2048 + 128)
    k = e[:, 0:128].reshape(2304, 2, 64)
    m["sw_k"] = T(k.reshape(2304, 128)); m["sw_ks"] = T(k[:, :, sw].reshape(2304, 128))
    m["sw_v"] = np.ascontiguousarray(e[:, 128:256])
    pos = np.arange(o0 - 128, o0 + 2048 + 128)
    c, s = rope_tables(np.clip(pos, 0, 8191), 64)
    m["sw_cs"] = np.stack([c, s])
    m["sw_qc"] = T(pc_b[:, SW0:SW0 + 256]); m["sw_kc"] = T(pc_b[:, SW0 + 256:SW0 + 384]); m["sw_vc"] = np.ascontiguousarray(pc_b[:, SW0 + 384:SW0 + 512])
    i = np.arange(128)
    prev = np.where(i[:, None] >= i[None, :], 0.0, NEG).astype(np.float32)
    nxt = np.where(i[:, None] <= i[None, :], 0.0, NEG).astype(np.float32)
    allneg = np.full((128, 128), NEG, np.float32)
    m["sw_bias"] = np.ascontiguousarray(np.stack([allneg if j == 0 else prev, prev, nxt, allneg if j == 3 else nxt], 1))
    m["sw_sink"] = np.ascontiguousarray(np.broadcast_to(W["swa_sink"][l][None, :], (128, 4)))
    sw32 = swap_idx(32)
    allk = np.concatenate([pc_b[:, ML0 + 256:ML0 + 416], p_b[:, ML0 + 256:ML0 + 416]], 0)
    m["m_ckv"] = T(allk[:, 0:128]); m["m_kr"] = T(allk[:, 128:160]); m["m_krs"] = T(allk[:, 128:160][:, sw32])
    c, s = rope_tables(np.arange(8192), 32)
    c = np.concatenate([np.ones((32, 256), np.float32), c], 1); s = np.concatenate([np.zeros((32, 256), np.float32), s], 1)
    m["m_kcs"] = np.stack([c, s])
    m["m_cq"] = T(np.concatenate([own[:, ML0:ML0 + 256], pc_b[:, ML0:ML0 + 256]], 0))
    c, s = rope_tables(np.arange(o0, o0 + 2048), 32)
    c = np.concatenate([c, np.ones((32, 256), np.float32)], 1); s = np.concatenate([s, np.zeros((32, 256), np.float32)], 1)
    m["m_qcs"] = np.stack([c, s])
    m["m_gkv"] = np.ascontiguousarray(W["mla_g_kv"][l].reshape(128, 1)); m["m_gq"] = cvec(W["mla_g_q"][l], 2)
    wkv = W["mla_w_ukv"][l].reshape(128, 4, 128)
    m["m_wkn"] = np.ascontiguousarray(wkv[:, :, 0:64].reshape(128, 256)); m["m_wkv"] = np.ascontiguousarray(wkv[:, :, 64:128].reshape(128, 256))
    wq = W["mla_w_uq"][l].reshape(256, 4, 96)
    m["m_wqn"] = np.ascontiguousarray(wq[:, :, 0:64].reshape(256, 256))
    m["m_wqr"] = np.ascontiguousarray(wq[:, :, 64:96].reshape(256, 128))
    m["m_wqrs"] = np.ascontiguousarray(wq[:, :, 64:96][:, :, sw32].reshape(256, 128))
    return m


def prep_S(p_b, pc_b, hd, W, l):
    g = hd // 2
    T = lambda a: np.ascontiguousarray(a.T)
    allp = np.concatenate([pc_b[:, SS0:], p_b[:, SS0:]], 0)
    xbc = allp[:, 256:1024]
    m = {}
    m["s_x"] = T(xbc[:, hd * 64:(hd + 1) * 64])
    m["s_b"] = T(xbc[:, 256 + g * 128:256 + (g + 1) * 128])
    m["s_c"] = T(xbc[:, 512 + g * 128:512 + (g + 1) * 128])
    cwv = np.concatenate([W["ssd_conv_w"][l], W["ssd_conv_b"][l][None, :]], 0)
    cw = np.zeros((128, 3, 6), np.float32)
    cw[:64, 0, :] = cwv[:, hd * 64:(hd + 1) * 64].T
    cw[:, 1, :] = cwv[:, 256 + g * 128:256 + (g + 1) * 128].T
    cw[:, 2, :] = cwv[:, 512 + g * 128:512 + (g + 1) * 128].T
    m["s_cw"] = cw
    dt = allp[:, 1024:1032].reshape(-1, 2, 4)[:, :, hd]
    m["s_dt"] = np.ascontiguousarray(dt.reshape(66, 128, 2).transpose(1, 0, 2))
    par = np.zeros((128, 8), np.float32)
    par[:, 0:2] = W["ssd_dt_bias"][l][:, hd]; par[:, 2:4] = W["ssd_a_log"][l][:, hd]; par[:, 4] = W["ssd_d"][l][hd]
    m["s_par"] = par
    i = np.arange(128)
    m["s_u"] = np.stack([(i[:, None] <= i[None, :]), (i[:, None] >= i[None, :])]).astype(np.float32)
    m["s_id"] = np.eye(128, dtype=np.float32)
    return m


def prep_M(I, core):
    b, j = core // 4, core % 4
    T = lambda a: np.ascontiguousarray(a.T)
    L = 4
    m = {}
    m["hT0"] = T(np.concatenate([I['x'][b, 2048 * j:2048 * (j + 1)], I['ctx'][b]], 0))
    m["cv"] = np.ascontiguousarray(np.stack([cvec(I['c'][b], 8), cvec(I['c_ctx'], 8)], -1))
    flg = np.zeros((128, 16), np.float32)
    flg[:, j] = 1.0
    if j > 0: flg[:, 4 + j - 1] = 1.0
    if j < 3: flg[:, 8 + j + 1] = 1.0
    m["flg"] = flg
    selx = np.zeros((128, 2, 64), np.float32); selb = np.zeros((128, 2, 128), np.float32)
    for d in range(64): selx[(j % 2) * 64 + d, j // 2, d] = 1.0
    for n in range(128): selb[n, j // 2, n] = 1.0
    m["selx"] = selx; m["selb"] = selb
    m["w_in"] = I['w_in']; m["w_mod"] = I['w_mod']; m["w_out"] = I['w_out']; m["w1"] = I['w_mlp1']; m["w2"] = I['w_mlp2']
    m["bmod"] = np.stack([cvec(I['b_mod'][l], 48) for l in range(L)])
    m["g1"] = np.stack([cvec(I['g_norm1'][l], 8) for l in range(L)])
    gv = np.zeros((L, 128, 20), np.float32)
    for l in range(L):
        gv[l, :, 0:8] = cvec(I['g_norm2'][l], 8); gv[l, :, 8:16] = cvec(I['g_final'], 8); gv[l, :, 16:18] = cvec(I['ssd_g_norm'][l], 2)
    m["gv"] = gv
    m["na_bias"] = np.stack([np.stack([na_bias_tile(I['na_rpb'][l], 16 * j + t).reshape(128, 7, 512) for t in (0, 1, 8 if j in (0, 3) else 2, 14, 15)]) for l in range(L)])
    i = np.arange(128)
    prev = np.where(i[:, None] >= i[None, :], 0.0, NEG).astype(np.float32)
    nxt = np.where(i[:, None] <= i[None, :], 0.0, NEG).astype(np.float32)
    allneg = np.full((128, 128), NEG, np.float32)
    m["sw_bias"] = np.ascontiguousarray(np.stack([allneg if j == 0 else prev, prev, nxt, allneg if j == 3 else nxt], 1))
    m["sw_sink"] = np.stack([np.ascontiguousarray(np.broadcast_to(I['swa_sink'][l][None, :], (128, 4))) for l in range(L)])
    o0 = 2048 * j
    c, s = rope_tables(np.clip(np.arange(o0 - 128, o0 + 2048 + 128), 0, 8191), 64)
    m["sw_cs"] = np.stack([c, s])
    c, s = rope_tables(np.arange(8192), 32)
    m["m_kcs"] = np.stack([np.concatenate([np.ones((32, 256), np.float32), c], 1), np.concatenate([np.zeros((32, 256), np.float32), s], 1)])
    c, s = rope_tables(np.arange(o0, o0 + 2048), 32)
    m["m_qcs"] = np.stack([np.concatenate([c, np.ones((32, 256), np.float32)], 1), np.concatenate([s, np.zeros((32, 256), np.float32)], 1)])
    sw32 = swap_idx(32)
    m["m_gkv"] = np.stack([I['mla_g_kv'][l].reshape(128, 1) for l in range(L)])
    m["m_gq"] = np.stack([cvec(I['mla_g_q'][l], 2) for l in range(L)])
    wkv = I['mla_w_ukv'].reshape(L, 128, 4, 128)
    m["m_wkn"] = np.ascontiguousarray(wkv[:, :, :, 0:64].reshape(L, 128, 256)); m["m_wkv"] = np.ascontiguousarray(wkv[:, :, :, 64:128].reshape(L, 128, 256))
    wq = I['mla_w_uq'].reshape(L, 256, 4, 96)
    m["m_wqn"] = np.ascontiguousarray(wq[:, :, :, 0:64].reshape(L, 256, 256))
    m["m_wqr"] = np.ascontiguousarray(wq[:, :, :, 64:96].reshape(L, 256, 128))
    m["m_wqrs"] = np.ascontiguousarray(wq[:, :, :, 64:96][:, :, :, sw32].reshape(L, 256, 128))
    hd, g = j, j // 2
    cw = np.zeros((L, 128, 3, 6), np.float32); par = np.zeros((L, 128, 8), np.float32)
    for l in range(L):
        cwv = np.concatenate([I['ssd_conv_w'][l], I['ssd_conv_b'][l][None, :]], 0)
        cw[l, :64, 0, :] = cwv[:, hd * 64:(hd + 1) * 64].T
        cw[l, :, 1, :] = cwv[:, 256 + g * 128:256 + (g + 1) * 128].T
        cw[l, :, 2, :] = cwv[:, 512 + g * 128:512 + (g + 1) * 128].T
        par[l, :, 0:2] = I['ssd_dt_bias'][l][:, hd]; par[l, :, 2:4] = I['ssd_a_log'][l][:, hd]; par[l, :, 4] = I['ssd_d'][l][hd]
    m["s_cw"] = cw; m["s_par"] = par
    m["s_u"] = np.stack([(i[:, None] <= i[None, :]), (i[:, None] >= i[None, :])]).astype(np.float32)
    m["ident"] = np.eye(128, dtype=np.float32)
    return m


from concourse.bass_utils import run_bass_kernel_spmd

_PROG = {}


def kernel(**inputs):
    I = {k: np.asarray(v, dtype=np.float32) for k, v in inputs.items()}
    if "M" not in _PROG:
        _PROG["M"] = build_M(4, 3, 4)
    P = _PROG["M"]
    in_maps = [prep_M(I, c) for c in range(8)]
    res = run_bass_kernel_spmd(P.nc, in_maps, core_ids=list(range(8)))
    out = np.stack([np.concatenate([res.results[b * 4 + j]["out"].T for j in range(4)], 0) for b in range(2)])
    return np.ascontiguousarray(out.astype(np.float32))
```

```python
import numpy as np
from contextlib import ExitStack
import concourse.bass as bass
import concourse.mybir as mybir

F32 = mybir.dt.float32
BF16 = mybir.dt.bfloat16
AF = mybir.ActivationFunctionType
ALU = mybir.AluOpType
AX = mybir.AxisListType


class Res:
    __slots__ = ("name", "w", "r")

    def __init__(self, name):
        self.name = name
        self.w = None
        self.r = []


class Prog:
    ENG = ("pe", "act", "dve", "pool", "sp")

    def __init__(self):
        self.nc = bass.Bass("TRN2", target_bir_lowering=False)
        self.es = ExitStack()
        self.root = self.es
        nc = self.nc
        self.e = {"pe": nc.tensor, "act": nc.scalar, "dve": nc.vector, "pool": nc.gpsimd, "sp": nc.sync}
        self.sem = {}
        self.cnt = {}
        for k in self.ENG:
            self.sem[k] = self.es.enter_context(nc.semaphore("s_" + k))
            self.cnt[k] = 0
        self.seen = {k: {} for k in self.ENG}
        self.ndma = 0
        self.out_toks = []
        self.nwait = 0
        self.nres = 0

    def dram(self, name, shape, dt, kind):
        return self.nc.dram_tensor(name, list(shape), dt, kind=kind).ap()

    def _u(self, name):
        self.nuniq = getattr(self, "nuniq", 0) + 1
        return f"{name}_u{self.nuniq}"

    def sb(self, name, shape, dt):
        return self.es.enter_context(self.nc.sbuf_tensor(self._u(name), list(shape), dt))

    def ps(self, name, shape, dt=F32):
        return self.es.enter_context(self.nc.psum_tensor(self._u(name), list(shape), dt))

    def res(self, name=None):
        self.nres += 1
        return Res(name or f"r{self.nres}")

    def dsem(self, name):
        free = getattr(self, "free_dsems", None)
        if free:
            key = free.pop()
        else:
            name = self._u(name)
            s = self.root.enter_context(self.nc.semaphore("d_" + name))
            key = ("d", name)
            self.sem[key] = s
            self.cnt[key] = 0
        if getattr(self, "_scope_dsems", None):
            self._scope_dsems[-1].append(key)
        return key

    def _deps(self, eng, reads, writes):
        need = {}

        def add(t, same_ok):
            if t is None:
                return
            sk, val, src = t
            if src == eng and not same_ok:
                return
            if need.get(sk, 0) < val:
                need[sk] = val

        for r in reads:
            add(r.w, True)
        for w in writes:
            add(w.w, False)
            for t in w.r:
                add(t, False)
        out = []
        seen = self.seen[eng]
        for sk, val in need.items():
            if seen.get(sk, 0) >= val:
                continue
            seen[sk] = val
            out.append((sk, val))
        return out

    def _emit_waits(self, eng, waits):
        e = self.e[eng]
        for sk, val in waits:
            e.wait_ge(self.sem[sk], val)
            self.nwait += 1

    def _record(self, tok, reads, writes):
        for r in reads:
            r.r.append(tok)
        for w in writes:
            w.w = tok
            w.r = []

    def op(self, eng, fn, reads=(), writes=(), pe_chain=False):
        reads = [r for r in reads if r is not None]
        writes = [w for w in writes if w is not None]
        if eng == "pe":
            waits = self._deps_pe(reads, writes)
        else:
            waits = self._deps(eng, reads, writes)
        self._emit_waits(eng, waits)
        ins = fn()
        self.cnt[eng] += 1
        ins.then_inc(self.sem[eng], 1)
        tok = (eng, self.cnt[eng], eng)
        self._record(tok, reads, writes)
        return ins

    def _deps_pe(self, reads, writes):
        need = {}

        def add(t):
            if t is None:
                return
            sk, val, src = t
            if src == "pe":
                return
            if need.get(sk, 0) < val:
                need[sk] = val

        for r in reads:
            add(r.w)
        for w in writes:
            add(w.w)
            for t in w.r:
                add(t)
        out = []
        seen = self.seen["pe"]
        for sk, val in need.items():
            if seen.get(sk, 0) >= val:
                continue
            seen[sk] = val
            out.append((sk, val))
        return out

    def dma(self, q, dst, src, dsem, reads=(), writes=(), final=False, **kw):
        reads = [r for r in reads if r is not None]
        writes = [w for w in writes if w is not None]
        waits = self._deps(q, reads, writes)
        self._emit_waits(q, waits)
        ins = self.e[q].dma_start(out=dst, in_=src, **kw)
        self.cnt[dsem] += 16
        ins.then_inc(self.sem[dsem], 16)
        tok = (dsem, self.cnt[dsem], "dma")
        self._record(tok, reads, writes)
        self.ndma += 1
        if final:
            self.out_toks.append(tok)
        return ins

    def barrier(self):
        for eng in self.ENG:
            for sk, val in self.cnt.items():
                if val > 0 and self.seen[eng].get(sk, 0) < val:
                    self.e[eng].wait_ge(self.sem[sk], val)
                    self.seen[eng][sk] = val

    def push_scope(self):
        self._scopes = getattr(self, "_scopes", [])
        self._scopes.append(self.es)
        self.es = ExitStack()
        self._scope_dsems = getattr(self, "_scope_dsems", [])
        self._scope_dsems.append([])

    def pop_scope(self):
        self.barrier()
        self.es.close()
        self.es = self._scopes.pop()
        self.free_dsems = getattr(self, "free_dsems", [])
        self.free_dsems.extend(self._scope_dsems.pop())

    def finish(self):
        toks = list(self.out_toks)
        need = {}
        for sk, val, _ in toks:
            need[sk] = max(need.get(sk, 0), val)
        for sk, val in need.items():
            self.e["sp"].wait_ge(self.sem[sk], val)

    def close(self):
        self.es.close()


class Rot:
    def __init__(self, P, name, n, shape, dt, psum=False, dma=False):
        self.t, self.r, self.d = [], [], []
        for i in range(n):
            self.t.append(P.ps(f"{name}{i}", shape, dt) if psum else P.sb(f"{name}{i}", shape, dt))
            self.r.append(P.res(f"{name}{i}"))
            self.d.append(P.dsem(f"{name}{i}") if dma else None)
        self.i = -1
        self.n = n

    def next(self):
        self.i = (self.i + 1) % self.n
        return self.t[self.i], self.r[self.i], self.d[self.i]


EPS = 1e-6


class AttnCtx:
    def __init__(self, P):
        self.P = P
        nc = P.nc
        self.panels = Rot(P, "pan", 3, [128, 512], F32, psum=True)
        self.accs = Rot(P, "acc", 2, [128, 512], F32, psum=True)
        self.misc = Rot(P, "mps", 2, [128, 512], F32, psum=True)
        self.pT = Rot(P, "pT", 3, [128, 512], BF16)
        self.sT = Rot(P, "sT", 2, [128, 512], F32)
        self.rec = Rot(P, "rec", 2, [128, 4], F32)
        self.zeros = P.sb("zeros", [128, 512], BF16)
        self.rz = P.res()
        P.op("dve", lambda: nc.vector.memset(self.zeros[:], 0.0), writes=[self.rz])
        self.ev = 0


def attend(A, ncol, merged, q_aps, q_res, key_items, scale, out_aps, out_res, sinkexp=None, sink_res=None):
    P = A.P
    nc = P.nc
    W = ncol * 128
    acc, racc, _ = A.accs.next()
    P.op("pe", lambda: nc.tensor.matmul(acc[:, 0:ncol * 65], lhsT=A.zeros[:, 0:128], rhs=A.zeros[:, 0:ncol * 65], start=True, stop=False),
         reads=[A.rz], writes=[racc])
    nk = len(key_items)

    def score(it):
        pan, rpan, _ = A.panels.next()
        if merged:
            parts_k = it["k"][0]
            parts_q = q_aps[0]
            for pi, (kp, qp) in enumerate(zip(parts_k, parts_q)):
                P.op("pe", lambda kp=kp, qp=qp, pi=pi: nc.tensor.matmul(pan[:, 0:W], lhsT=kp, rhs=qp, start=(pi == 0), stop=(pi == len(parts_k) - 1)),
                     reads=list(it["res"]) + list(q_res), writes=[rpan])
        else:
            for c in range(ncol):
                parts_k = it["k"][c]
                parts_q = q_aps[c]
                for pi, (kp, qp) in enumerate(zip(parts_k, parts_q)):
                    P.op("pe", lambda kp=kp, qp=qp, pi=pi, c=c, n=len(parts_k): nc.tensor.matmul(
                        pan[:, c * 128:(c + 1) * 128], lhsT=kp, rhs=qp, start=(pi == 0), stop=(pi == n - 1)),
                        reads=list(it["res"]) + list(q_res), writes=[rpan])
        pt, rpt, _ = A.pT.next()
        if it.get("bias") is not None:
            st, rst, _ = A.sT.next()
            P.op("dve", lambda: nc.vector.scalar_tensor_tensor(out=st[:, 0:W], in0=pan[:, 0:W], scalar=scale, in1=it["bias"], op0=ALU.mult, op1=ALU.add),
                 reads=[rpan] + list(it.get("bres", [])), writes=[rst])
            P.op("act", lambda: nc.scalar.activation(out=pt[:, 0:W], in_=st[:, 0:W], func=AF.Exp), reads=[rst], writes=[rpt])
        else:
            P.op("act", lambda: nc.scalar.activation(out=pt[:, 0:W], in_=pan[:, 0:W], func=AF.Exp, scale=scale), reads=[rpan], writes=[rpt])
        return pt, rpt

    nxt = score(key_items[0])
    for ki, it in enumerate(key_items):
        pt, rpt = nxt
        if ki + 1 < nk:
            nxt = score(key_items[ki + 1])
        for c in range(ncol):
            P.op("pe", lambda c=c: nc.tensor.matmul(acc[:, c * 65:(c + 1) * 65], lhsT=pt[:, c * 128:(c + 1) * 128], rhs=it["v"][c], start=False, stop=(ki == nk - 1)),
                 reads=[rpt] + list(it["res"]), writes=[racc])
    rec, rrec, _ = A.rec.next()
    den = acc[:, 64:64 + 65 * (ncol - 1) + 1:65]
    if sinkexp is not None:
        P.op("dve", lambda: nc.vector.tensor_tensor(out=rec[:, 0:ncol], in0=den, in1=sinkexp, op=ALU.add), reads=[racc, sink_res], writes=[rrec])
        P.op("dve", lambda: nc.vector.reciprocal(out=rec[:, 0:ncol], in_=rec[:, 0:ncol]), reads=[rrec], writes=[rrec])
    else:
        P.op("dve", lambda: nc.vector.reciprocal(out=rec[:, 0:ncol], in_=den), reads=[racc], writes=[rrec])
    for c in range(ncol):
        A.ev += 1
        if A.ev % 2 == 0:
            P.op("act", lambda c=c: nc.scalar.activation(out=out_aps[c], in_=acc[:, c * 65:c * 65 + 64], func=AF.Copy, scale=rec[:, c:c + 1]),
                 reads=[racc, rrec], writes=[out_res])
        else:
            P.op("dve", lambda c=c: nc.vector.tensor_scalar(out=out_aps[c], in0=acc[:, c * 65:c * 65 + 64], scalar1=rec[:, c:c + 1], scalar2=None,
                                                            op0=ALU.mult), reads=[racc, rrec], writes=[out_res])


def load_cast(P, dst, src, res, dsem):
    P.dma("pool", dst, src, dsem, writes=[res])


def load_v(P, vt, v_d, ntile, nh, res, dsem):
    nc = P.nc
    P.op("dve", lambda: nc.vector.memset(vt[:], 1.0), writes=[res])
    src = v_d.rearrange("(t p) (h d) -> p t h d", p=128, h=nh)
    for t in range(ntile):
        P.dma("pool", vt[:, t, :, 0:64], src[:, t, :, :], dsem, writes=[res])


def rope_to(P, dst, x_d, xs_d, cos_t, sin_t, rtab, n, npart, stg, res):
    nc = P.nc
    a, ra, da = stg.next()
    b, rb, db = stg.next()
    P.dma("sp", a[:npart, :n], x_d, da, writes=[ra])
    P.dma("sp", b[:npart, :n], xs_d, db, writes=[rb])
    P.op("dve", lambda: nc.vector.tensor_tensor(out=a[:npart, :n], in0=a[:npart, :n], in1=cos_t, op=ALU.mult), reads=[ra, rtab], writes=[ra])
    P.op("pool", lambda: nc.gpsimd.tensor_tensor(out=b[:npart, :n], in0=b[:npart, :n], in1=sin_t, op=ALU.mult), reads=[rb, rtab], writes=[rb])
    P.op("dve", lambda: nc.vector.tensor_tensor(out=dst, in0=a[:npart, :n], in1=b[:npart, :n], op=ALU.add), reads=[ra, rb], writes=[res])


def build_T(do_na=True, do_swa=True, do_mla=True):
    P = Prog()
    nc = P.nc
    A = AttnCtx(P)
    NQ = 2048
    NC = 256
    rout = P.res("out")
    ost = Rot(P, "ost", 2, [128, 4, 256], F32, dma=True)

    if do_na:
        P.push_scope()
        EXT = 2816
        q_d = P.dram("na_q", [256, NQ], F32, "ExternalInput")
        k_d = P.dram("na_k", [256, EXT], F32, "ExternalInput")
        v_d = P.dram("na_v", [EXT, 256], F32, "ExternalInput")
        qc_d = P.dram("na_qc", [256, NC], F32, "ExternalInput")
        kc_d = P.dram("na_kc", [256, NC], F32, "ExternalInput")
        vc_d = P.dram("na_vc", [NC, 256], F32, "ExternalInput")
        b_d = P.dram("na_bias", [5, 128, 7, 512], F32, "ExternalInput")
        o_d = P.dram("o_na", [NQ + NC, 256], F32, "ExternalOutput")
        q = P.sb("naq", [64, 4, NQ], BF16); rq = P.res(); d1 = P.dsem("naq")
        k = P.sb("nak", [64, 4, EXT], BF16); rk = P.res(); d2 = P.dsem("nak")
        v = P.sb("nav", [128, EXT // 128, 4, 65], BF16); rv = P.res(); d3 = P.dsem("nav")
        qc = P.sb("naqc", [64, 4, NC], BF16); kc = P.sb("nakc", [64, 4, NC], BF16); vc = P.sb("navc", [128, 2, 4, 65], BF16)
        rc = P.res(); d4 = P.dsem("nac")
        for h in range(4):
            load_cast(P, q[:, h, :], q_d[h * 64:(h + 1) * 64, :], rq, d1)
            load_cast(P, k[:, h, :], k_d[h * 64:(h + 1) * 64, :], rk, d2)
            load_cast(P, qc[:, h, :], qc_d[h * 64:(h + 1) * 64, :], rc, d4)
            load_cast(P, kc[:, h, :], kc_d[h * 64:(h + 1) * 64, :], rc, d4)
        load_v(P, v, v_d, EXT // 128, 4, rv, d3)
        load_v(P, vc, vc_d, 2, 4, rc, d4)
        bias = Rot(P, "nab", 2, [128, 7, 512], F32, dma=True)
        scale = 64 ** -0.5
        for t in range(16 + 2):
            items = []
            if t < 16:
                pat = 0 if t == 0 else 1 if t == 1 else 3 if t == 14 else 4 if t == 15 else 2
                bt, rb, db = bias.next()
                P.dma("sp", bt[:], b_d[pat], db, writes=[rb])
                qa = [[q[:, h, t * 128:(t + 1) * 128]] for h in range(4)]
                for kt in range(7):
                    et = t + kt
                    items.append(dict(k=[[k[:, h, et * 128:(et + 1) * 128]] for h in range(4)], v=[v[:, et, h, :] for h in range(4)],
                                      bias=bt[:, kt, :], bres=[rb], res=[rk, rv]))
                qres = [rq]
            else:
                tc = t - 16
                qa = [[qc[:, h, tc * 128:(tc + 1) * 128]] for h in range(4)]
                qres = [rc]
            for kt in range(2):
                items.append(dict(k=[[kc[:, h, kt * 128:(kt + 1) * 128]] for h in range(4)], v=[vc[:, kt, h, :] for h in range(4)], bias=None, res=[rc]))
            o, ro, do = ost.next()
            attend(A, 4, False, qa, qres, items, scale, [o[:, 0, h * 64:(h + 1) * 64] for h in range(4)], ro)
            P.dma("sp", o_d[t * 128:(t + 1) * 128, :], o[:, 0, :], do, reads=[ro], writes=[rout], final=True)
        P.pop_scope()

    if do_swa:
        P.push_scope()
        EXT = 2304
        q_d = P.dram("sw_q", [256, NQ], F32, "ExternalInput")
        qs_d = P.dram("sw_qs", [256, NQ], F32, "ExternalInput")
        k_d = P.dram("sw_k", [128, EXT], F32, "ExternalInput")
        ks_d = P.dram("sw_ks", [128, EXT], F32, "ExternalInput")
        cs_d = P.dram("sw_cs", [2, 64, EXT], F32, "ExternalInput")
        v_d = P.dram("sw_v", [EXT, 128], F32, "ExternalInput")
        qc_d = P.dram("sw_qc", [256, NC], F32, "ExternalInput")
        kc_d = P.dram("sw_kc", [128, NC], F32, "ExternalInput")
        vc_d = P.dram("sw_vc", [NC, 128], F32, "ExternalInput")
        b_d = P.dram("sw_bias", [128, 4, 128], F32, "ExternalInput")
        sk_d = P.dram("sw_sink", [128, 4], F32, "ExternalInput")
        o_d = P.dram("o_sw", [NQ + NC, 256], F32, "ExternalOutput")
        q = P.sb("swq", [64, 4, NQ], BF16); rq = P.res()
        k = P.sb("swk", [64, 2, EXT], BF16); rk = P.res()
        v = P.sb("swv", [128, EXT // 128, 2, 65], BF16); rv = P.res(); d3 = P.dsem("swv")
        qc = P.sb("swqc", [64, 4, NC], BF16); kc = P.sb("swkc", [64, 2, NC], BF16); vc = P.sb("swvc", [128, 2, 2, 65], BF16)
        rc = P.res(); d4 = P.dsem("swc")
        cs = P.sb("swcs", [64, 2, EXT], F32); rcs = P.res(); d5 = P.dsem("swcs")
        bs = P.sb("swb", [128, 4, 128], F32); rbs = P.res()
        sk = P.sb("swsk", [128, 4], F32); rsk = P.res()
        P.dma("sp", cs[:, 0, :], cs_d[0], d5, writes=[rcs])
        P.dma("sp", cs[:, 1, :], cs_d[1], d5, writes=[rcs])
        P.dma("sp", bs[:], b_d, d5, writes=[rbs])
        P.dma("sp", sk[:], sk_d, d5, writes=[rsk])
        P.op("act", lambda: nc.scalar.activation(out=sk[:], in_=sk[:], func=AF.Exp), reads=[rsk], writes=[rsk])
        stg = Rot(P, "swstg", 4, [64, EXT], F32, dma=True)
        for h in range(4):
            rope_to(P, q[:, h, :], q_d[h * 64:(h + 1) * 64, :], qs_d[h * 64:(h + 1) * 64, :], cs[:, 0, 128:128 + NQ], cs[:, 1, 128:128 + NQ], rcs, NQ, 64, stg, rq)
            load_cast(P, qc[:, h, :], qc_d[h * 64:(h + 1) * 64, :], rc, d4)
        for g in range(2):
            rope_to(P, k[:, g, :], k_d[g * 64:(g + 1) * 64, :], ks_d[g * 64:(g + 1) * 64, :], cs[:, 0, :], cs[:, 1, :], rcs, EXT, 64, stg, rk)
            load_cast(P, kc[:, g, :], kc_d[g * 64:(g + 1) * 64, :], rc, d4)
        load_v(P, v, v_d, EXT // 128, 2, rv, d3)
        load_v(P, vc, vc_d, 2, 2, rc, d4)
        bp = P.sb("swbp", [128, 4, 4, 128], F32); rbp = P.res()
        for kind in range(4):
            for h in range(4):
                P.op("dve", lambda kind=kind, h=h: nc.vector.tensor_copy(out=bp[:, kind, h, :], in_=bs[:, kind, :]), reads=[rbs], writes=[rbp])
        scale = 64 ** -0.5
        for t in range(16 + 2):
            items = []
            if t < 16:
                qa = [[q[:, h, t * 128:(t + 1) * 128]] for h in range(4)]
                qres = [rq]
                for kt in range(3):
                    et = t + kt
                    if kt == 0:
                        b = bp[:, 0 if t == 0 else 1, :, :].rearrange('p h q -> p (h q)')
                    elif kt == 2:
                        b = bp[:, 3 if t == 15 else 2, :, :].rearrange('p h q -> p (h q)')
                    else:
                        b = None
                    items.append(dict(k=[[k[:, h // 2, et * 128:(et + 1) * 128]] for h in range(4)], v=[v[:, et, h // 2, :] for h in range(4)],
                                      bias=b, bres=[rbp], res=[rk, rv]))
            else:
                tc = t - 16
                qa = [[qc[:, h, tc * 128:(tc + 1) * 128]] for h in range(4)]
                qres = [rc]
            for kt in range(2):
                items.append(dict(k=[[kc[:, h // 2, kt * 128:(kt + 1) * 128]] for h in range(4)], v=[vc[:, kt, h // 2, :] for h in range(4)], bias=None, res=[rc]))
            o, ro, do = ost.next()
            attend(A, 4, False, qa, qres, items, scale, [o[:, 0, h * 64:(h + 1) * 64] for h in range(4)], ro, sinkexp=sk[:, 0:4], sink_res=rsk)
            P.dma("sp", o_d[t * 128:(t + 1) * 128, :], o[:, 0, :], do, reads=[ro], writes=[rout], final=True)
        P.pop_scope()

    if do_mla:
        P.push_scope()
        NK = 8448
        NQA = NQ + NC
        ckv_d = P.dram("m_ckv", [128, NK], F32, "ExternalInput")
        kr_d = P.dram("m_kr", [32, NK], F32, "ExternalInput")
        krs_d = P.dram("m_krs", [32, NK], F32, "ExternalInput")
        kcs_d = P.dram("m_kcs", [2, 32, NK], F32, "ExternalInput")
        cq_d = P.dram("m_cq", [256, NQA], F32, "ExternalInput")
        qcs_d = P.dram("m_qcs", [2, 32, NQA], F32, "ExternalInput")
        gkv_d = P.dram("m_gkv", [128, 1], F32, "ExternalInput")
        gq_d = P.dram("m_gq", [128, 2], F32, "ExternalInput")
        wkn_d = P.dram("m_wkn", [128, 256], F32, "ExternalInput")
        wkv_d = P.dram("m_wkv", [128, 256], F32, "ExternalInput")
        wqn_d = P.dram("m_wqn", [256, 256], F32, "ExternalInput")
        wqr_d = P.dram("m_wqr", [256, 128], F32, "ExternalInput")
        wqrs_d = P.dram("m_wqrs", [256, 128], F32, "ExternalInput")
        o_d = P.dram("o_ml", [NQA, 256], F32, "ExternalOutput")

        ones = P.sb("mones", [128, 128], BF16); rones = P.res()
        P.op("dve", lambda: nc.vector.memset(ones[:], 1.0), writes=[rones])
        dsm = P.dsem("msmall")
        gkv = P.sb("gkv", [128, 1], F32); gq = P.sb("gq", [128, 2], F32); rg = P.res()
        P.dma("sp", gkv[:], gkv_d, dsm, writes=[rg]); P.dma("sp", gq[:], gq_d, dsm, writes=[rg])
        wkn = P.sb("wkn", [128, 256], BF16); wkv = P.sb("wkv", [128, 256], BF16)
        wqn = P.sb("wqn", [128, 2, 256], BF16); wqr = P.sb("wqr", [128, 2, 128], BF16); wqrs = P.sb("wqrs", [128, 2, 128], BF16)
        rw = P.res(); dw = P.dsem("mw")
        load_cast(P, wkn[:], wkn_d, rw, dw); load_cast(P, wkv[:], wkv_d, rw, dw)
        for c in range(2):
            load_cast(P, wqn[:, c, :], wqn_d[c * 128:(c + 1) * 128, :], rw, dw)
            load_cast(P, wqr[:, c, :], wqr_d[c * 128:(c + 1) * 128, :], rw, dw)
            load_cast(P, wqrs[:, c, :], wqrs_d[c * 128:(c + 1) * 128, :], rw, dw)

        kn = P.sb("kn", [128, 2, NK], BF16); rkn = P.res()
        kr = P.sb("kr", [32, NK], BF16); rkr = P.res()
        vm = P.sb("vm", [128, NK // 128, 4, 65], BF16); rvm = P.res()
        qn = P.sb("qn", [128, 2, NQA], BF16); rqn = P.res()
        qr = P.sb("qr", [32, 4, NQA], BF16); rqr = P.res()
        P.op("dve", lambda: nc.vector.memset(vm[:], 1.0), writes=[rvm])

        xin = Rot(P, "mx", 2, [128, 2, 512], F32, dma=True)
        sq = Rot(P, "msq", 2, [128, 2, 512], BF16)
        rms = Rot(P, "mrms", 2, [128, 512], F32)
        tt = Rot(P, "mtt", 2, [128, 512], F32)
        xn = Rot(P, "mxn", 2, [128, 2, 512], BF16)
        tab = Rot(P, "mtab", 2, [32, 2, 512], F32, dma=True)
        rr = Rot(P, "mrr", 4, [32, 512], F32, dma=True)
        evi = [0]

        def evac(dst, src, reads, writes):
            evi[0] += 1
            if evi[0] % 2 == 0:
                P.op("act", lambda: nc.scalar.copy(out=dst, in_=src), reads=reads, writes=writes)
            else:
                P.op("dve", lambda: nc.vector.tensor_copy(out=dst, in_=src), reads=reads, writes=writes)

        def norm_block(src_d, kc, t0, n, g):
            x, rx, dx = xin.next()
            for c in range(kc):
                P.dma("sp", x[:, c, :n], src_d[c * 128:(c + 1) * 128, t0:t0 + n], dx, writes=[rx])
            s, rs, _ = sq.next()
            for c in range(kc):
                P.op("act", lambda c=c: nc.scalar.activation(out=s[:, c, :n], in_=x[:, c, :n], func=AF.Square), reads=[rx], writes=[rs])
            pt, rpt, _ = A.misc.next()
            for c in range(kc):
                P.op("pe", lambda c=c: nc.tensor.matmul(pt[:, :n], lhsT=ones[:], rhs=s[:, c, :n], start=(c == 0), stop=(c == kc - 1)), reads=[rones, rs], writes=[rpt])
            r, rrr, _ = rms.next()
            P.op("act", lambda: nc.scalar.activation(out=r[:, :n], in_=pt[:, :n], func=AF.Sqrt, scale=1.0 / (kc * 128), bias=EPS), reads=[rpt], writes=[rrr])
            P.op("dve", lambda: nc.vector.reciprocal(out=r[:, :n], in_=r[:, :n]), reads=[rrr], writes=[rrr])
            y, ry, _ = xn.next()
            for c in range(kc):
                P.op("dve", lambda c=c: nc.vector.scalar_tensor_tensor(out=y[:, c, :n], in0=x[:, c, :n], scalar=g[:, c:c + 1], in1=r[:, :n],
                                                                     op0=ALU.mult, op1=ALU.mult), reads=[rx, rrr, rg], writes=[ry])
            return y, ry

        def rope_block(dst, x_d, xs_d, cs_d, t0, n, res):
            tb, rtb, dtb = tab.next()
            P.dma("sp", tb[:, 0, :n], cs_d[0][:, t0:t0 + n], dtb, writes=[rtb])
            P.dma("sp", tb[:, 1, :n], cs_d[1][:, t0:t0 + n], dtb, writes=[rtb])
            a, ra, da = rr.next(); b, rb, db = rr.next()
            P.dma("sp", a[:, :n], x_d[:, t0:t0 + n], da, writes=[ra])
            P.dma("sp", b[:, :n], xs_d[:, t0:t0 + n], db, writes=[rb])
            P.op("dve", lambda: nc.vector.tensor_tensor(out=a[:, :n], in0=a[:, :n], in1=tb[:, 0, :n], op=ALU.mult), reads=[ra, rtb], writes=[ra])
            P.op("pool", lambda: nc.gpsimd.tensor_tensor(out=b[:, :n], in0=b[:, :n], in1=tb[:, 1, :n], op=ALU.mult), reads=[rb, rtb], writes=[rb])
            P.op("dve", lambda: nc.vector.tensor_tensor(out=dst, in0=a[:, :n], in1=b[:, :n], op=ALU.add), reads=[ra, rb], writes=[res])

        for t0 in range(0, NK, 512):
            n = min(512, NK - t0)
            y, ry = norm_block(ckv_d, 1, t0, n, gkv)
            for pr in range(2):
                pt, rpt, _ = A.misc.next()
                P.op("pe", lambda pr=pr, pt=pt: nc.tensor.matmul(pt[:, :n], lhsT=wkn[:, pr * 128:(pr + 1) * 128], rhs=y[:, 0, :n], start=True, stop=True),
                     reads=[rw, ry], writes=[rpt])
                evac(kn[:, pr, t0:t0 + n], pt[:, :n], [rpt], [rkn])
            for tt_ in range(n // 128):
                kt = t0 // 128 + tt_
                pt, rpt, _ = A.misc.next()
                P.op("pe", lambda tt_=tt_, pt=pt: nc.tensor.matmul(pt[:, 0:256], lhsT=y[:, 0, tt_ * 128:(tt_ + 1) * 128], rhs=wkv[:], start=True, stop=True),
                     reads=[rw, ry], writes=[rpt])
                evac(vm[:, kt, :, 0:64], pt[:, 0:256].rearrange("p (h d) -> p h d", h=4), [rpt], [rvm])
            rope_block(kr[:, t0:t0 + n], kr_d, krs_d, kcs_d, t0, n, rkr)
        for t0 in range(0, NQA, 512):
            n = min(512, NQA - t0)
            y, ry = norm_block(cq_d, 2, t0, n, gq)
            for pr in range(2):
                pt, rpt, _ = A.misc.next()
                for c in range(2):
                    P.op("pe", lambda pr=pr, pt=pt, c=c: nc.tensor.matmul(pt[:, :n], lhsT=wqn[:, c, pr * 128:(pr + 1) * 128], rhs=y[:, c, :n],
                                                                         start=(c == 0), stop=(c == 1)), reads=[rw, ry], writes=[rpt])
                evac(qn[:, pr, t0:t0 + n], pt[:, :n], [rpt], [rqn])
            tb, rtb, dtb = tab.next()
            P.dma("sp", tb[:, 0, :n], qcs_d[0][:, t0:t0 + n], dtb, writes=[rtb])
            P.dma("sp", tb[:, 1, :n], qcs_d[1][:, t0:t0 + n], dtb, writes=[rtb])
            for h in range(4):
                pa, rpa, _ = A.misc.next()
                for c in range(2):
                    P.op("pe", lambda pa=pa, c=c, h=h: nc.tensor.matmul(pa[0:32, :n], lhsT=wqr[:, c, h * 32:(h + 1) * 32], rhs=y[:, c, :n],
                                                                       start=(c == 0), stop=(c == 1)), reads=[rw, ry], writes=[rpa])
                a, ra, _ = rr.next()
                P.op("dve", lambda a=a, pa=pa: nc.vector.tensor_tensor(out=a[:, :n], in0=pa[0:32, :n], in1=tb[:, 0, :n], op=ALU.mult), reads=[rpa, rtb], writes=[ra])
                pb, rpb, _ = A.misc.next()
                for c in range(2):
                    P.op("pe", lambda pb=pb, c=c, h=h: nc.tensor.matmul(pb[0:32, :n], lhsT=wqrs[:, c, h * 32:(h + 1) * 32], rhs=y[:, c, :n],
                                                                       start=(c == 0), stop=(c == 1)), reads=[rw, ry], writes=[rpb])
                b, rb, _ = rr.next()
                P.op("dve", lambda b=b, pb=pb: nc.vector.tensor_tensor(out=b[:, :n], in0=pb[0:32, :n], in1=tb[:, 1, :n], op=ALU.mult), reads=[rpb, rtb], writes=[rb])
                P.op("pool", lambda a=a, b=b, h=h: nc.gpsimd.tensor_tensor(out=qr[:, h, t0:t0 + n], in0=a[:, :n], in1=b[:, :n], op=ALU.add), reads=[ra, rb], writes=[rqr])
        scale = 96 ** -0.5
        groups = [(g * 512, 4, NK // 128) for g in range(4)] + [(2048, 2, 2)]
        for (q0, ncol, nkt) in groups:
            o, ro, do = ost.next()
            for h in range(4):
                hp, pr = (h % 2) * 64, h // 2
                qa = [[qn[hp:hp + 64, pr, q0:q0 + ncol * 128], qr[:, h, q0:q0 + ncol * 128]]]
                items = []
                for kt in range(nkt):
                    items.append(dict(k=[[kn[hp:hp + 64, pr, kt * 128:(kt + 1) * 128], kr[:, kt * 128:(kt + 1) * 128]]],
                                      v=[vm[:, kt, h, :]] * ncol, bias=None, res=[rkn, rkr, rvm]))
                attend(A, ncol, True, qa, [rqn, rqr], items, scale, [o[:, c, h * 64:(h + 1) * 64] for c in range(ncol)], ro)
            P.dma("sp", o_d[q0:q0 + ncol * 128, :].rearrange("(c p) f -> p c f", p=128), o[:, 0:ncol, :], do, reads=[ro], writes=[rout], final=True)
        P.pop_scope()
    P.finish()
    P.close()
    return P


D = 1024
KC = 8
EPS = 1e-6
NTOK = 2304
NQ = 2048
NCX = 256
NCOL = 2728
NA0, SW0, ML0, SS0 = 0, 768, 1280, 1696
SBROWS = 1312
R_NAK, R_SWK, R_CKV, R_XBC, R_KR = 0, 256, 384, 512, 1280
YCH = 2816
G4 = [[0, 1, 2, 3], [4, 5, 6, 7]]
LSEQ = 8448
NCH = LSEQ // 128


class RotView:
    def __init__(self, rots):
        self.t, self.r, self.d = [], [], []
        for ro in rots:
            self.t += ro.t; self.r += ro.r; self.d += ro.d
        self.i = -1
        self.n = len(self.t)

    def next(self):
        self.i = (self.i + 1) % self.n
        return self.t[self.i], self.r[self.i], self.d[self.i]


def allgather(P, src, dst, reads, writes):
    nc = P.nc
    if "cc" not in P.sem:
        P.sem["cc"] = P.root.enter_context(nc.semaphore("s_cc"))
        P.cnt["cc"] = 0
    P._emit_waits("pool", P._deps("pool", reads, writes))
    ins = nc.gpsimd.collective_compute("AllGather", mybir.AluOpType.bypass, replica_groups=G4, ins=[src.opt()], outs=[dst.opt()])
    P.cnt["cc"] += 1
    ins.then_inc(P.sem["cc"])
    P._record(("cc", P.cnt["cc"], "dma"), reads, writes)
    nc.gpsimd.wait_ge(P.sem["cc"], P.cnt["cc"])
    P.seen["pool"]["cc"] = P.cnt["cc"]


def seg_lat(t0, n):
    out = []
    t = t0
    while t < t0 + n:
        r = t // 2048
        ln = min(t0 + n, (r + 1) * 2048) - t
        out.append((r, t - r * 2048, ln, t - t0))
        t += ln
    return out


def build_M(nlayers=4, final_layer=3, NLW=4):
    P = Prog()
    nc = P.nc
    X = lambda name, shape: P.dram(name, shape, F32, "ExternalInput")
    hT0_d = X("hT0", [D, NTOK]); cv_d = X("cv", [128, KC, 2]); flg_d = X("flg", [128, 16])
    selx_d = X("selx", [128, 2, 64]); selb_d = X("selb", [128, 2, 128])
    w_in_d = X("w_in", [NLW, D, NCOL]); w_mod_d = X("w_mod", [NLW, D, 6 * D]); bmod_d = X("bmod", [NLW, 128, 48])
    g1_d = X("g1", [NLW, 128, KC]); gv_d = X("gv", [NLW, 128, 20])
    wo_d = X("w_out", [NLW, D, D]); w1_d = X("w1", [NLW, D, 4 * D]); w2_d = X("w2", [NLW, 4 * D, D])
    nab_d = X("na_bias", [NLW, 5, 128, 7, 512]); swb_d = X("sw_bias", [128, 4, 128]); swk_d = X("sw_sink", [NLW, 128, 4])
    swcs_d = X("sw_cs", [2, 64, 2304]); mkcs_d = X("m_kcs", [2, 32, LSEQ]); mqcs_d = X("m_qcs", [2, 32, NTOK])
    mgkv_d = X("m_gkv", [NLW, 128, 1]); mgq_d = X("m_gq", [NLW, 128, 2])
    mwkn_d = X("m_wkn", [NLW, 128, 256]); mwkv_d = X("m_wkv", [NLW, 128, 256])
    mwqn_d = X("m_wqn", [NLW, 256, 256]); mwqr_d = X("m_wqr", [NLW, 256, 128]); mwqrs_d = X("m_wqrs", [NLW, 256, 128])
    scw_d = X("s_cw", [NLW, 128, 3, 6]); spar_d = X("s_par", [NLW, 128, 8]); su_d = X("s_u", [2, 128, 128]); id_d = X("ident", [128, 128])
    out_d = P.dram("out", [D, NQ], F32, "ExternalOutput")
    S_ = lambda name, shape: nc.dram_tensor(name, list(shape), F32).ap()
    hT_s = [S_("hTa", [D, NTOK]), S_("hTb", [D, NTOK])]
    pT_d = S_("pT", [NCOL, NTOK]); vtok_d = S_("vtok", [NTOK, 392])
    sb_nr = [64] * 20 + [32]
    SBc = [S_(f"SBc{c}", [sb_nr[c], NTOK]) for c in range(21)]
    RBc = [S_(f"RBc{c}", [4 * sb_nr[c], NTOK]) for c in range(21)]
    vg_nr = [512] * 4 + [256]
    VSc = [S_(f"VSc{i}", [vg_nr[i], 392]) for i in range(5)]
    VGc = [S_(f"VGc{i}", [4 * vg_nr[i], 392]) for i in range(5)]
    YSc = [S_(f"YSc{i}", [64, YCH]) for i in range(3)]
    YGc = [S_(f"YGc{i}", [256, YCH]) for i in range(3)]
    h2_d = S_("h2", [D, NTOK])

    def rbuf(r, row0, nrows, c0, c1):
        c = row0 // 64
        assert row0 + nrows <= 64 * c + sb_nr[c], (row0, nrows)
        o = r * sb_nr[c] + row0 - 64 * c
        return RBc[c][o:o + nrows, c0:c1]

    def vgbuf(r, row0, nrows, c0, c1):
        i = row0 // 512
        assert row0 + nrows <= 512 * i + vg_nr[i], (row0, nrows)
        o = r * vg_nr[i] + row0 - 512 * i
        return VGc[i][o:o + nrows, c0:c1]
    r_pT, r_vtok, r_SB, r_RB1, r_VG, r_YS, r_YG, r_h2, r_RB2 = (P.res() for _ in range(9))
    r_hT = [P.res(), P.res()]

    A = AttnCtx(P)
    extra = Rot(P, "xps", 1, [128, 512], F32, psum=True)
    gen = RotView([A.misc, extra, A.panels, A.accs])
    ones = P.sb("ones", [128, 128], BF16); rones = P.res()
    P.op("dve", lambda: nc.vector.memset(ones[:], 1.0), writes=[rones])
    onesf = P.sb("onesf", [128, 128], F32); ronesf = P.res()
    P.op("pool", lambda: nc.gpsimd.memset(onesf[:], 1.0), writes=[ronesf])
    dc = P.dsem("const")
    ident = P.sb("ident", [128, 128], F32); rid = P.res(); P.dma("sp", ident[:], id_d, dc, writes=[rid])
    flg = P.sb("flg", [128, 16], F32); rflg = P.res(); P.dma("sp", flg[:], flg_d, dc, writes=[rflg])
    U = P.sb("U", [128, 2, 128], F32); rU = P.res()
    P.dma("sp", U[:, 0, :], su_d[0], dc, writes=[rU]); P.dma("sp", U[:, 1, :], su_d[1], dc, writes=[rU])
    selx = P.sb("selx", [128, 2, 64], F32); selb = P.sb("selb", [128, 2, 128], F32); rsel = P.res()
    P.dma("sp", selx[:], selx_d, dc, writes=[rsel]); P.dma("sp", selb[:], selb_d, dc, writes=[rsel])
    cvs = P.sb("cvs", [128, KC, 2], F32); rcv = P.res(); P.dma("sp", cvs[:], cv_d, dc, writes=[rcv])
    ca = P.sb("ca", [128, KC, 2], F32); rca = P.res()
    P.op("act", lambda: nc.scalar.activation(out=ca[:], in_=cvs[:], func=AF.Silu), reads=[rcv], writes=[rca])
    moL = [P.sb("mo", [128, 48, 2], F32) for _ in range(2)]; rmoL = [P.res(), P.res()]
    gs1L = [P.sb("gs1", [128, KC, 2], F32) for _ in range(2)]; gs2L = [P.sb("gs2", [128, KC, 2], F32) for _ in range(2)]; rgsL = [P.res(), P.res()]
    gvsL = [P.sb("gvs", [128, 28], F32) for _ in range(2)]; rgvL = [P.res(), P.res()]
    roT = P.res()
    evi = [0]

    def evac(dst, src, reads, writes):
        evi[0] += 1
        if evi[0] % 2 == 0:
            P.op("act", lambda: nc.scalar.copy(out=dst, in_=src), reads=reads, writes=writes)
        else:
            P.op("dve", lambda: nc.vector.tensor_copy(out=dst, in_=src), reads=reads, writes=writes)

    def rstd_of(x, rx, kc, n, r, rr, s, rs, feat):
        for k in range(kc):
            P.op("act", lambda k=k: nc.scalar.activation(out=s[:, k, :n], in_=x[:, k, :n], func=AF.Square), reads=[rx], writes=[rs])
        pt, rpt, _ = gen.next()
        for k in range(kc):
            P.op("pe", lambda k=k: nc.tensor.matmul(pt[:, :n], lhsT=ones[:], rhs=s[:, k, :n], start=(k == 0), stop=(k == kc - 1)), reads=[rones, rs], writes=[rpt])
        P.op("act", lambda: nc.scalar.activation(out=r[:, :n], in_=pt[:, :n], func=AF.Sqrt, scale=1.0 / feat, bias=EPS), reads=[rpt], writes=[rr])
        P.op("dve", lambda: nc.vector.reciprocal(out=r[:, :n], in_=r[:, :n]), reads=[rr], writes=[rr])

    def emit_mod(l):
        mo, rmo, gs1, gs2, rgs, gvs, rgv = moL[l % 2], rmoL[l % 2], gs1L[l % 2], gs2L[l % 2], rgsL[l % 2], gvsL[l % 2], rgvL[l % 2]
        P.push_scope()
        dm = P.dsem(f"mod{l}")
        bm = P.sb("bm", [128, 48], F32); rbm = P.res()
        P.dma("sp", bm[:], bmod_d[l], dm, writes=[rbm])
        P.dma("sp", gvs[:, 0:20], gv_d[l], dm, writes=[rgv]); P.dma("sp", gvs[:, 20:28], g1_d[l], dm, writes=[rgv])
        wm = Rot(P, "wm", 2, [128, KC, 1024], F32, dma=True)
        for grp in range(6):
            w, rw_, dw_ = wm.next()
            for k in range(KC):
                P.dma("sp", w[:, k, :], w_mod_d[l][k * 128:(k + 1) * 128, grp * 1024:(grp + 1) * 1024], dw_, writes=[rw_])
            pt, rpt, _ = gen.next()
            for cb in range(8):
                for k in range(KC):
                    P.op("pe", lambda cb=cb, k=k: nc.tensor.matmul(pt[:, cb * 2:cb * 2 + 2], lhsT=w[:, k, cb * 128:(cb + 1) * 128], rhs=ca[:, k, :],
                                                                 start=(k == 0), stop=(k == KC - 1)), reads=[rw_, rca], writes=[rpt])
            for j in range(2):
                P.op("dve", lambda j=j: nc.vector.tensor_tensor(out=mo[:, grp * 8:(grp + 1) * 8, j], in0=pt[:, j:16:2], in1=bm[:, grp * 8:(grp + 1) * 8], op=ALU.add),
                     reads=[rpt, rbm], writes=[rmo])
        for j in range(2):
            P.op("dve", lambda j=j: nc.vector.scalar_tensor_tensor(out=gs1[:, :, j], in0=mo[:, 8:16, j], scalar=1.0, in1=gvs[:, 20:28], op0=ALU.add, op1=ALU.mult),
                 reads=[rmo, rgv], writes=[rgs])
            P.op("dve", lambda j=j: nc.vector.scalar_tensor_tensor(out=gs2[:, :, j], in0=mo[:, 32:40, j], scalar=1.0, in1=gvs[:, 0:8], op0=ALU.add, op1=ALU.mult),
                 reads=[rmo, rgv], writes=[rgs])
        P.pop_scope()


    for l in range(nlayers):
        final = (l == final_layer)
        hT_d, r_hin = (hT0_d, None) if l == 0 else (hT_s[(l - 1) % 2], r_hT[(l - 1) % 2])
        hTn_d, r_hn = hT_s[l % 2], r_hT[l % 2]

        mo, rmo, gs1, gs2, rgs, gvs, rgv = moL[l % 2], rmoL[l % 2], gs1L[l % 2], gs2L[l % 2], rgsL[l % 2], gvsL[l % 2], rgvL[l % 2]
        if l == 0:
            emit_mod(0)
        P.push_scope()
        wsb = P.sb("wsb", [128, KC, NCOL], BF16); rw = P.res(); dw = P.dsem(f"w{l}")
        wv = P.sb("wv", [128, KC, 392], BF16)
        for k in range(KC):
            for c0 in range(0, NCOL, 2048):
                c1 = min(NCOL, c0 + 2048)
                P.dma("pool", wsb[:, k, c0:c1], w_in_d[l][k * 128:(k + 1) * 128, c0:c1], dw, writes=[rw])
            P.dma("pool", wv[:, k, 0:256], w_in_d[l][k * 128:(k + 1) * 128, 512:768], dw, writes=[rw])
            P.dma("pool", wv[:, k, 256:384], w_in_d[l][k * 128:(k + 1) * 128, SW0 + 384:SW0 + 512], dw, writes=[rw])
            P.dma("pool", wv[:, k, 384:392], w_in_d[l][k * 128:(k + 1) * 128, SS0 + 1024:SS0 + 1032], dw, writes=[rw])
        hin = Rot(P, "hin", 2, [128, KC, 512], F32, dma=True)
        sq = Rot(P, "sq", 2, [128, KC, 512], BF16)
        xm = Rot(P, "xm", 2, [128, KC, 512], BF16)
        tt = Rot(P, "tt", 2, [128, 512], F32)
        rms = Rot(P, "rms", 2, [128, 512], F32)
        ost = Rot(P, "ost", 4, [128, 512], F32, dma=True)
        blocks = [(t0, 512, 0) for t0 in range(0, NQ, 512)] + [(NQ, 256, 1)]
        ncb = (NCOL + 127) // 128
        for (t0, n, j) in blocks:
            h, rh, dh = hin.next()
            P.dma("sp", h[:, :, :n], hT_d.rearrange("(k p) t -> p k t", p=128)[:, :, t0:t0 + n], dh, reads=[r_hin], writes=[rh])
            s, rs, _ = sq.next(); r, rr, _ = rms.next()
            rstd_of(h, rh, KC, n, r, rr, s, rs, D)
            x, rx, _ = xm.next()
            for k in range(KC):
                t, rt, _ = tt.next()
                P.op("dve", lambda k=k, t=t: nc.vector.tensor_tensor(out=t[:, :n], in0=h[:, k, :n], in1=r[:, :n], op=ALU.mult), reads=[rh, rr], writes=[rt])
                P.op("act", lambda k=k, t=t: nc.scalar.activation(out=x[:, k, :n], in_=t[:, :n], func=AF.Identity, scale=gs1[:, k, j:j + 1], bias=mo[:, k, j:j + 1]),
                     reads=[rt, rgs, rmo], writes=[rx])
            for cb in range(ncb):
                c0 = cb * 128
                m = min(128, NCOL - c0)
                pt, rpt, _ = gen.next()
                for k in range(KC):
                    P.op("pe", lambda k=k, pt=pt: nc.tensor.matmul(pt[:m, :n], lhsT=wsb[:, k, c0:c0 + m], rhs=x[:, k, :n], start=(k == 0), stop=(k == KC - 1)),
                         reads=[rw, rx], writes=[rpt])
                o, ro, do = ost.next()
                evac(o[:m, :n], pt[:m, :n], [rpt], [ro])
                P.dma("sp", pT_d[c0:c0 + m, t0:t0 + n], o[:m, :n], do, reads=[ro], writes=[r_pT])
            for ti in range(n // 128):
                pt, rpt, _ = gen.next()
                for k in range(KC):
                    P.op("pe", lambda k=k, pt=pt: nc.tensor.matmul(pt[:, 0:392], lhsT=x[:, k, ti * 128:(ti + 1) * 128], rhs=wv[:, k, :], start=(k == 0), stop=(k == KC - 1)),
                         reads=[rw, rx], writes=[rpt])
                o, ro, do = ost.next()
                evac(o[:, 0:392], pt[:, 0:392], [rpt], [ro])
                P.dma("sp", vtok_d[t0 + ti * 128:t0 + (ti + 1) * 128, :], o[:, 0:392], do, reads=[ro], writes=[r_vtok])
        P.pop_scope()

        dx = P.dsem(f"x1_{l}")
        for (d0, s0, nr) in ((R_NAK, 256, 256), (R_SWK, SW0 + 256, 128), (R_CKV, ML0 + 256, 128), (R_XBC, SS0 + 256, 768), (R_KR, ML0 + 384, 32)):
            for rr0 in range(0, nr, 64):
                n_ = min(64, nr - rr0)
                P.dma("sp", SBc[(d0 + rr0) // 64][0:n_, :], pT_d[s0 + rr0:s0 + rr0 + n_, :], dx, reads=[r_pT], writes=[r_SB])
        for i in range(5):
            P.dma("sp", VSc[i][:, :], vtok_d[512 * i:512 * i + vg_nr[i], :], dx, reads=[r_vtok], writes=[r_SB])
        P.barrier()
        for c in range(8, 20):
            allgather(P, SBc[c], RBc[c], [r_SB], [r_RB1])
        for i in range(5):
            allgather(P, VSc[i], VGc[i], [r_SB], [r_VG])
        if l + 1 < nlayers:
            emit_mod(l + 1)
        for c in list(range(8)) + [20]:
            allgather(P, SBc[c], RBc[c], [r_SB], [r_RB2])

        P.push_scope()
        ds_ = P.dsem(f"ssm{l}")
        cw = P.sb("cw", [128, 3, 6], F32); rcw = P.res(); P.dma("sp", cw[:], scw_d[l], ds_, writes=[rcw])
        par = P.sb("par", [128, 8], F32); rpar = P.res(); P.dma("sp", par[:], spar_d[l], ds_, writes=[rpar])
        dtall = P.sb("dtall", [128, NCH, 8], F32); rdta_ = P.res()
        P.dma("sp", dtall[:, 0:2, :], vgbuf(0, NQ, 256, 384, 392).rearrange("(c p) e -> p c e", p=128), ds_, reads=[r_VG], writes=[rdta_])
        for r in range(4):
            for i in range(4):
                P.dma("sp", dtall[:, 2 + 16 * r + 4 * i:2 + 16 * r + 4 * (i + 1), :], vgbuf(r, 512 * i, 512, 384, 392).rearrange("(c p) e -> p c e", p=128), ds_,
                      reads=[r_VG], writes=[rdta_])
        dt = P.sb("dt", [128, NCH, 2], F32); rdt = P.res()
        dta = P.sb("dta", [128, NCH, 2], F32); rdta = P.res()
        for d in range(2):
            P.op("dve", lambda d=d: nc.vector.tensor_scalar(out=dt[:, :, d], in0=dtall[:, :, d * 4], scalar1=flg[:, 0:1], scalar2=None, op0=ALU.mult),
                 reads=[rdta_, rflg], writes=[rdt])
            for hh in range(1, 4):
                P.op("dve", lambda d=d, hh=hh: nc.vector.scalar_tensor_tensor(out=dt[:, :, d], in0=dtall[:, :, d * 4 + hh], scalar=flg[:, hh:hh + 1], in1=dt[:, :, d],
                                                                            op0=ALU.mult, op1=ALU.add), reads=[rdta_, rflg, rdt], writes=[rdt])
        av = P.sb("av", [128, 2], F32); rav = P.res()
        P.op("act", lambda: nc.scalar.activation(out=av[:], in_=par[:, 2:4], func=AF.Exp), reads=[rpar], writes=[rav])
        P.op("dve", lambda: nc.vector.tensor_scalar(out=av[:], in0=av[:], scalar1=-1.0, scalar2=None, op0=ALU.mult), reads=[rav], writes=[rav])
        for d in range(2):
            P.op("act", lambda d=d: nc.scalar.activation(out=dt[:, :, d], in_=dt[:, :, d], func=AF.Exp, bias=par[:, d:d + 1]), reads=[rdt, rpar], writes=[rdt])
        for d in range(2):
            P.op("act", lambda d=d: nc.scalar.activation(out=dt[:, :, d], in_=dt[:, :, d], func=AF.Ln, bias=1.0), reads=[rdt], writes=[rdt])
        for d in range(2):
            P.op("dve", lambda d=d: nc.vector.tensor_scalar(out=dta[:, :, d], in0=dt[:, :, d], scalar1=av[:, d:d + 1], scalar2=None, op0=ALU.mult),
                 reads=[rdt, rav], writes=[rdta])
        xT = P.sb("xT", [64, LSEQ], F32); rx_ = P.res()
        bT = P.sb("bT", [128, LSEQ], BF16); rb_ = P.res()
        cT = P.sb("cT", [128, LSEQ], BF16); rc_ = P.res()
        yb = P.sb("yb", [64, LSEQ], F32); ry_ = P.res()
        P.push_scope()
        raw = Rot(P, "raw", 6, [128, 2, 512], F32, dma=True)
        cacc = Rot(P, "cacc", 3, [128, 508], F32)
        CB = 508
        for gi, (row0, npart, sel, dst, rdst) in enumerate(((R_XBC, 64, selx, xT, rx_), (R_XBC + 256, 128, selb, bT, rb_), (R_XBC + 512, 128, selb, cT, rc_))):
            for (s0, sl, is_ctx) in ((0, 256, True), (256, 8192, False)):
                for t0 in range(0, sl, CB):
                    n = min(CB, sl - t0)
                    rt_, rr_, dr_ = raw.next()
                    lo = max(t0 - 2, 0); hi = min(t0 + n + 2, sl)
                    if lo > t0 - 2 or hi < t0 + n + 2:
                        P.op("dve", lambda rt_=rt_: nc.vector.memset(rt_[:], 0.0), writes=[rr_])
                    if is_ctx:
                        segs = [(0, NQ + lo, hi - lo, lo - (t0 - 2))]
                    else:
                        segs = [(r, ls, ln, do_ + lo - (t0 - 2)) for (r, ls, ln, do_) in seg_lat(lo, hi - lo)]
                    for (r, ls, ln, do_) in segs:
                        for kc in range(2):
                            for hf in range(2):
                                P.dma("sp", rt_[hf * 64:(hf + 1) * 64, kc, do_:do_ + ln], rbuf(r, row0 + kc * 128 + hf * 64, 64, ls, ls + ln), dr_,
                                      reads=[r_RB1], writes=[rr_])
                    pt, rpt, _ = gen.next()
                    for kc in range(2):
                        P.op("pe", lambda kc=kc, pt=pt: nc.tensor.matmul(pt[:npart, 0:n + 4], lhsT=sel[:, kc, :], rhs=rt_[:, kc, 0:n + 4], start=(kc == 0), stop=(kc == 1)),
                             reads=[rsel, rr_], writes=[rpt])
                    a, ra, _ = cacc.next()
                    P.op("dve", lambda: nc.vector.tensor_scalar(out=a[:npart, :n], in0=pt[:npart, 0:n], scalar1=cw[:npart, gi, 0:1], scalar2=None, op0=ALU.mult),
                         reads=[rpt, rcw], writes=[ra])
                    for k in range(1, 5):
                        P.op("dve", lambda k=k: nc.vector.scalar_tensor_tensor(out=a[:npart, :n], in0=pt[:npart, k:k + n], scalar=cw[:npart, gi, k:k + 1], in1=a[:npart, :n],
                                                                             op0=ALU.mult, op1=ALU.add), reads=[rpt, rcw, ra], writes=[ra])
                    P.op("act", lambda: nc.scalar.activation(out=dst[:npart, s0 + t0:s0 + t0 + n], in_=a[:npart, :n], func=AF.Silu, bias=cw[:npart, gi, 5:6]),
                         reads=[ra, rcw], writes=[rdst])
        P.pop_scope()
        xtokA = P.sb("xtokA", [128, NCH, 64], F32); rxt = P.res()
        btokA = P.sb("btokA", [128, NCH, 128], BF16); rbtk = P.res()
        gt = Rot(P, "gt", 2, [128, 128], F32)
        for c in range(NCH):
            sl = slice(c * 128, (c + 1) * 128)
            px, rpx, _ = gen.next()
            P.op("pe", lambda: nc.tensor.transpose(px[:, 0:64], xT[:, sl], ident[0:64, 0:64]), reads=[rx_, rid], writes=[rpx])
            evac(xtokA[:, c, :], px[:, 0:64], [rpx], [rxt])
            btf, rbtf, _ = gt.next()
            P.op("act", lambda: nc.scalar.copy(out=btf[:], in_=bT[:, sl]), reads=[rb_], writes=[rbtf])
            pb, rpb, _ = gen.next()
            P.op("pe", lambda: nc.tensor.transpose(pb[:, 0:128], btf[:], ident[:]), reads=[rbtf, rid], writes=[rpb])
            evac(btokA[:, c, :], pb[:, 0:128], [rpb], [rbtk])
        ryc = [P.res() for _ in range(NCH)]
        for c in range(NCH):
            sl = slice(c * 128, (c + 1) * 128)
            P.op("pool", lambda: nc.gpsimd.tensor_scalar(out=yb[:, sl], in0=xT[:, sl], scalar1=par[0:64, 4:5], scalar2=None, op0=ALU.mult), reads=[rx_, rpar], writes=[ryc[c]])
        hs = [P.sb("hs", [128, 64], F32) for _ in range(2)]; rhs_ = [P.res(), P.res()]
        hsb = [P.sb("hsb", [128, 64], BF16) for _ in range(2)]; rhsb = [P.res(), P.res()]
        NS = 4
        dtab = Rot(P, "dtab", NS, [128, 128], F32); ccol = Rot(P, "ccol", 2 * NS, [128, 4], F32)
        dd = Rot(P, "dd", NS, [128, 128], F32); ee = Rot(P, "ee", NS, [128, 128], F32)
        gm = Rot(P, "gm", NS, [128, 128], F32); mt = Rot(P, "mt", NS, [128, 128], BF16)
        ecr = Rot(P, "ecr", NS, [128, 128], F32); cs = Rot(P, "cs", NS, [128, 128], BF16)
        xdt = Rot(P, "xdt", NS, [128, 64], BF16); xdd = Rot(P, "xdd", NS, [128, 64], BF16)
        sst = Rot(P, "sst", NS, [128, 64], F32); hsr = Rot(P, "hsr", NS, [128, 64], BF16)
        hcur = [None, None]
        for d in range(2):
            P.op("dve", lambda d=d: nc.vector.memset(hs[d][:], 0.0), writes=[rhs_[d]])
            P.op("dve", lambda d=d: nc.vector.memset(hsb[d][:], 0.0), writes=[rhsb[d]])
        orders = [list(range(NCH)), [1, 0] + list(range(NCH - 1, 1, -1))]

        def p1(d, c):
            sl = slice(c * 128, (c + 1) * 128)
            tot_col = 127 if d == 0 else 0
            da, rda, _ = dtab.next()
            P.op("pool", lambda: nc.gpsimd.tensor_scalar(out=da[:], in0=onesf[:], scalar1=dta[:, c, d:d + 1], scalar2=None, op0=ALU.mult), reads=[ronesf, rdta], writes=[rda])
            pcr, rpcr, _ = gen.next()
            P.op("pe", lambda: nc.tensor.matmul(pcr[:, 0:128], lhsT=da[:], rhs=U[:, d, :], start=True, stop=True), reads=[rda, rU], writes=[rpcr])
            P.op("pe", lambda: nc.tensor.matmul(pcr[:, 128:130], lhsT=U[:, d, :], rhs=dta[:, c, :], start=True, stop=True), reads=[rU, rdta], writes=[rpcr])
            cc, rcc, _ = ccol.next()
            P.op("dve", lambda: nc.vector.tensor_copy(out=cc[:, 0:1], in_=pcr[:, 128 + d:129 + d]), reads=[rpcr], writes=[rcc])
            P.op("dve", lambda: nc.vector.tensor_copy(out=cc[:, 1:2], in_=pcr[:, tot_col:tot_col + 1]), reads=[rpcr], writes=[rcc])
            P.op("dve", lambda: nc.vector.tensor_tensor(out=cc[:, 2:3], in0=cc[:, 0:1], in1=cc[:, 1:2], op=ALU.subtract), reads=[rcc], writes=[rcc])
            P.op("act", lambda: nc.scalar.activation(out=cc[:, 3:4], in_=cc[:, 1:2], func=AF.Exp), reads=[rcc], writes=[rcc])
            dt_, rdd, _ = dd.next()
            P.op("dve", lambda: nc.vector.tensor_scalar(out=dt_[:], in0=pcr[:, 0:128], scalar1=cc[:, 0:1], scalar2=0.0, op0=ALU.subtract, op1=ALU.min), reads=[rpcr, rcc], writes=[rdd])
            e, re_, _ = ee.next()
            P.op("act", lambda: nc.scalar.activation(out=e[:], in_=dt_[:], func=AF.Exp), reads=[rdd], writes=[re_])
            er, rer, _ = ecr.next()
            P.op("act", lambda: nc.scalar.activation(out=er[:], in_=pcr[:, 0:128], func=AF.Exp), reads=[rpcr], writes=[rer])
            pg, rpg, _ = gen.next()
            P.op("pe", lambda: nc.tensor.matmul(pg[:, 0:128], lhsT=bT[:, sl], rhs=cT[:, sl], start=True, stop=True), reads=[rb_, rc_], writes=[rpg])
            g, rg_, _ = gm.next()
            P.op("dve", lambda: nc.vector.tensor_tensor(out=g[:], in0=pg[:, 0:128], in1=U[:, d, :], op=ALU.mult), reads=[rpg, rU], writes=[rg_])
            m, rm, _ = mt.next()
            P.op("pool", lambda: nc.gpsimd.tensor_tensor(out=m[:], in0=g[:], in1=e[:], op=ALU.mult), reads=[rg_, re_], writes=[rm])
            csb, rcs, _ = cs.next()
            P.op("pool", lambda: nc.gpsimd.tensor_tensor(out=csb[:], in0=cT[:, sl], in1=er[:], op=ALU.mult), reads=[rc_, rer], writes=[rcs])
            xd, rxd, _ = xdt.next()
            P.op("dve", lambda: nc.vector.tensor_scalar(out=xd[:], in0=xtokA[:, c, :], scalar1=dt[:, c, d:d + 1], scalar2=None, op0=ALU.mult), reads=[rxt, rdt], writes=[rxd])
            de, rde, _ = ccol.next()
            P.op("act", lambda: nc.scalar.activation(out=de[:, 0:1], in_=cc[:, 2:3], func=AF.Exp, scale=-1.0), reads=[rcc], writes=[rde])
            P.op("dve", lambda: nc.vector.tensor_tensor(out=de[:, 1:2], in0=de[:, 0:1], in1=dt[:, c, d:d + 1], op=ALU.mult), reads=[rde, rdt], writes=[rde])
            xe, rxe, _ = xdd.next()
            P.op("dve", lambda: nc.vector.tensor_scalar(out=xe[:], in0=xtokA[:, c, :], scalar1=de[:, 1:2], scalar2=None, op0=ALU.mult), reads=[rxt, rde], writes=[rxe])
            pst, rpst, _ = gen.next()
            P.op("pe", lambda: nc.tensor.matmul(pst[:, 0:64], lhsT=btokA[:, c, :], rhs=xe[:], start=True, stop=True), reads=[rbtk, rxe], writes=[rpst])
            st_, rst_, _ = sst.next()
            P.op("act", lambda: nc.scalar.copy(out=st_[:], in_=pst[:, 0:64]), reads=[rpst], writes=[rst_])
            return (cc, rcc, m, rm, csb, rcs, xd, rxd, st_, rst_)

        def p2(d, c, hnd):
            cc, rcc, m, rm, csb, rcs, xd, rxd, st_, rst_ = hnd
            sl = slice(c * 128, (c + 1) * 128)
            hb, rhb = hcur[d] if hcur[d] is not None else (hsb[d], rhsb[d])
            py, rpy, _ = gen.next()
            P.op("pe", lambda: nc.tensor.matmul(py[0:64, 0:128], lhsT=xd[:], rhs=m[:], start=True, stop=False), reads=[rxd, rm], writes=[rpy])
            P.op("pe", lambda: nc.tensor.matmul(py[0:64, 0:128], lhsT=hb[:], rhs=csb[:], start=False, stop=True), reads=[rhb, rcs], writes=[rpy])
            P.op("dve", lambda: nc.vector.scalar_tensor_tensor(out=hs[d][:], in0=hs[d][:], scalar=cc[:, 3:4], in1=st_[:], op0=ALU.mult, op1=ALU.add),
                 reads=[rhs_[d], rcc, rst_], writes=[rhs_[d]])
            hn, rhn, _ = hsr.next()
            P.op("dve", lambda: nc.vector.tensor_copy(out=hn[:], in_=hs[d][:]), reads=[rhs_[d]], writes=[rhn])
            hcur[d] = (hn, rhn)
            P.op("dve", lambda: nc.vector.tensor_tensor(out=yb[:, sl], in0=yb[:, sl], in1=py[0:64, 0:128], op=ALU.add), reads=[ryc[c], rpy], writes=[ryc[c]])

        pend = None
        for s_ in range(NCH + 1):
            cur = None
            if s_ < NCH:
                cur = [(d, orders[d][s_], p1(d, orders[d][s_])) for d in range(2)]
            if pend is not None:
                for (d, c, hnd) in pend:
                    p2(d, c, hnd)
            pend = cur
        ry_all = ryc
        for i in range(3):
            P.dma("sp", YSc[i][:, :], yb[:, i * YCH:(i + 1) * YCH], ds_, reads=ry_all, writes=[r_YS])
        P.pop_scope()
        for i in range(3):
            allgather(P, YSc[i], YGc[i], [r_YS], [r_YG])

        P.push_scope()
        oTs = P.sb("oTs", [128, 6, NTOK], BF16)
        P.push_scope()
        ost = Rot(P, "tost", 2, [128, 4, 256], F32)

        def emit_oT(o, ro, ncol, tile0, chunk0, mla):
            for c in range(ncol):
                for half in range(2):
                    pt, rpt, _ = A.misc.next()
                    P.op("pe", lambda c=c, half=half, pt=pt: nc.tensor.transpose(pt[:, 0:128], o[:, c, half * 128:(half + 1) * 128], ident[:]), reads=[ro, rid], writes=[rpt])
                    evac(oTs[:, chunk0 + half, (tile0 + c) * 128:(tile0 + c + 1) * 128], pt[:, 0:128], [rpt], [roT])

        P.push_scope()
        EXT = 2816
        dn = P.dsem(f"na{l}")
        q = P.sb("naq", [64, 4, NQ], BF16); rq = P.res()
        k = P.sb("nak", [64, 4, EXT], BF16); rk = P.res()
        v = P.sb("nav", [128, EXT // 128, 4, 65], BF16); rv = P.res()
        qc = P.sb("naqc", [64, 4, NCX], BF16); kc_ = P.sb("nakc", [64, 4, NCX], BF16); vc = P.sb("navc", [128, 2, 4, 65], BF16); rc = P.res()
        P.op("dve", lambda: nc.vector.memset(v[:], 1.0), writes=[rv])
        P.op("dve", lambda: nc.vector.memset(vc[:], 1.0), writes=[rc])
        hk = Rot(P, "hk", 2, [64, 4, 384], F32, dma=True)
        hacc = Rot(P, "hacc", 2, [64, 384], F32)
        hv = Rot(P, "hv", 2, [128, 4, 768], F32, dma=True)
        hvacc = Rot(P, "hvacc", 2, [128, 768], F32)

        def halo_k(dst, row0, npart, width, side, hkr, haccr, rowperm=None):
            t_, rt_, dt__ = hkr.next()
            c0 = NQ - width if side == 0 else 0
            for r in range(4):
                for (d0_, s0_, nr_) in (rowperm or ((0, 0, npart),)):
                    P.dma("sp", t_[d0_:d0_ + nr_, r, :width], rbuf(r, row0 + s0_, nr_, c0, c0 + width), dt__, reads=[r_RB2], writes=[rt_])
            a_, ra_, _ = haccr.next()
            f0 = 4 if side == 0 else 8
            P.op("dve", lambda: nc.vector.tensor_scalar(out=a_[:npart, :width], in0=t_[:npart, 0, :width], scalar1=flg[:npart, f0:f0 + 1], scalar2=None, op0=ALU.mult),
                 reads=[rt_, rflg], writes=[ra_])
            for r in range(1, 4):
                P.op("dve", lambda r=r: nc.vector.scalar_tensor_tensor(out=a_[:npart, :width], in0=t_[:npart, r, :width], scalar=flg[:npart, f0 + r:f0 + r + 1], in1=a_[:npart, :width],
                                                                     op0=ALU.mult, op1=ALU.add), reads=[rt_, rflg, ra_], writes=[ra_])
            return a_, ra_

        def halo_v(col0, ncolv, ntile, side, hvr, hvaccr):
            t_, rt_, dt__ = hvr.next()
            r0 = NQ - ntile * 128 if side == 0 else 0
            for r in range(4):
                P.dma("sp", t_[:, r, 0:ntile * ncolv].rearrange("p (t c) -> p t c", c=ncolv),
                      vgbuf(r, r0, ntile * 128, col0, col0 + ncolv).rearrange("(t p) c -> p t c", p=128), dt__, reads=[r_VG], writes=[rt_])
            a_, ra_, _ = hvaccr.next()
            f0 = 4 if side == 0 else 8
            w_ = ntile * ncolv
            P.op("dve", lambda: nc.vector.tensor_scalar(out=a_[:, :w_], in0=t_[:, 0, :w_], scalar1=flg[:, f0:f0 + 1], scalar2=None, op0=ALU.mult), reads=[rt_, rflg], writes=[ra_])
            for r in range(1, 4):
                P.op("dve", lambda r=r: nc.vector.scalar_tensor_tensor(out=a_[:, :w_], in0=t_[:, r, :w_], scalar=flg[:, f0 + r:f0 + r + 1], in1=a_[:, :w_], op0=ALU.mult, op1=ALU.add),
                     reads=[rt_, rflg, ra_], writes=[ra_])
            return a_, ra_

        for h in range(4):
            P.dma("pool", q[:, h, :], pT_d[h * 64:(h + 1) * 64, 0:NQ], dn, reads=[r_pT], writes=[rq])
            P.dma("pool", k[:, h, 384:384 + NQ], pT_d[256 + h * 64:256 + (h + 1) * 64, 0:NQ], dn, reads=[r_pT], writes=[rk])
            P.dma("pool", qc[:, h, :], pT_d[h * 64:(h + 1) * 64, NQ:NTOK], dn, reads=[r_pT], writes=[rc])
            P.dma("pool", kc_[:, h, :], pT_d[256 + h * 64:256 + (h + 1) * 64, NQ:NTOK], dn, reads=[r_pT], writes=[rc])
            for side in range(2):
                a_, ra_ = halo_k(None, R_NAK + h * 64, 64, 384, side, hk, hacc)
                off = 0 if side == 0 else 384 + NQ
                P.op("act", lambda a_=a_, off=off, h=h: nc.scalar.copy(out=k[:, h, off:off + 384], in_=a_[:64, :384]), reads=[ra_], writes=[rk])
        vsrc = vtok_d[:, 0:256].rearrange("(t p) (h d) -> p t h d", p=128, h=4)
        for t in range(16):
            P.dma("pool", v[:, 3 + t, :, 0:64], vsrc[:, t, :, :], dn, reads=[r_vtok], writes=[rv])
        for t in range(2):
            P.dma("pool", vc[:, t, :, 0:64], vsrc[:, 16 + t, :, :], dn, reads=[r_vtok], writes=[rc])
        for side in range(2):
            a_, ra_ = halo_v(0, 256, 3, side, hv, hvacc)
            t0_ = 0 if side == 0 else 19
            for t in range(3):
                P.op("act", lambda a_=a_, t=t, t0_=t0_: nc.scalar.copy(out=v[:, t0_ + t, :, 0:64], in_=a_[:, t * 256:(t + 1) * 256].rearrange("p (h d) -> p h d", h=4)),
                     reads=[ra_], writes=[rv])
        bias = Rot(P, "nab", 2, [128, 7, 512], F32, dma=True)
        scale = 64 ** -0.5
        for t in range(18):
            items = []
            if t < 16:
                pat = 0 if t == 0 else 1 if t == 1 else 3 if t == 14 else 4 if t == 15 else 2
                bt_, rb, db = bias.next()
                P.dma("sp", bt_[:], nab_d[l][pat], db, writes=[rb])
                qa = [[q[:, h, t * 128:(t + 1) * 128]] for h in range(4)]
                for kt in (range(1, 6) if pat == 2 else range(7)):
                    et = t + kt
                    items.append(dict(k=[[k[:, h, et * 128:(et + 1) * 128]] for h in range(4)], v=[v[:, et, h, :] for h in range(4)], bias=bt_[:, kt, :], bres=[rb], res=[rk, rv]))
                qres = [rq]
            else:
                tc = t - 16
                qa = [[qc[:, h, tc * 128:(tc + 1) * 128]] for h in range(4)]
                qres = [rc]
            for kt in range(2):
                items.append(dict(k=[[kc_[:, h, kt * 128:(kt + 1) * 128]] for h in range(4)], v=[vc[:, kt, h, :] for h in range(4)], bias=None, res=[rc]))
            o, ro, _ = ost.next()
            attend(A, 4, False, qa, qres, items, scale, [o[:, 0, h * 64:(h + 1) * 64] for h in range(4)], ro)
            emit_oT(o, ro, 1, t, 0, False)
        P.pop_scope()

        P.push_scope()
        EXT = 2304
        dn = P.dsem(f"sw{l}")
        q = P.sb("swq", [64, 4, NQ], BF16); rq = P.res()
        k = P.sb("swk", [64, 2, EXT], BF16); rk = P.res()
        v = P.sb("swv", [128, EXT // 128, 2, 65], BF16); rv = P.res()
        qc = P.sb("swqc", [64, 4, NCX], BF16); kc_ = P.sb("swkc", [64, 2, NCX], BF16); vc = P.sb("swvc", [128, 2, 2, 65], BF16); rc = P.res()
        P.op("dve", lambda: nc.vector.memset(v[:], 1.0), writes=[rv])
        P.op("dve", lambda: nc.vector.memset(vc[:], 1.0), writes=[rc])
        cs_ = P.sb("swcs", [64, 2, EXT], F32); rcs_ = P.res()
        bs = P.sb("swb", [128, 4, 128], F32); rbs = P.res()
        sk = P.sb("swsk", [128, 4], F32); rsk = P.res()
        P.dma("sp", cs_[:, 0, :], swcs_d[0], dn, writes=[rcs_]); P.dma("sp", cs_[:, 1, :], swcs_d[1], dn, writes=[rcs_])
        P.dma("sp", bs[:], swb_d, dn, writes=[rbs]); P.dma("sp", sk[:], swk_d[l], dn, writes=[rsk])
        P.op("act", lambda: nc.scalar.activation(out=sk[:], in_=sk[:], func=AF.Exp), reads=[rsk], writes=[rsk])
        stg = Rot(P, "swstg", 4, [64, EXT], F32, dma=True)
        hk2 = Rot(P, "hk2", 2, [64, 4, 128], F32, dma=True)
        hacc2 = Rot(P, "hacc2", 4, [64, 128], F32)
        hv2 = Rot(P, "hv2", 2, [128, 4, 128], F32, dma=True)
        hvacc2 = Rot(P, "hvacc2", 2, [128, 128], F32)
        perm = ((0, 16), (16, 0), (32, 48), (48, 32))

        def rope_rows(dst, res, row0, col0, n, tab0, extra_writer=None):
            a, ra, da = stg.next(); b, rb_, db = stg.next()
            P.dma("sp", a[:, :n], pT_d[row0:row0 + 64, col0:col0 + n], da, reads=[r_pT], writes=[ra])
            for (d0, s0) in perm:
                P.dma("sp", b[d0:d0 + 16, :n], pT_d[row0 + s0:row0 + s0 + 16, col0:col0 + n], db, reads=[r_pT], writes=[rb_])
            P.op("dve", lambda: nc.vector.tensor_tensor(out=a[:, :n], in0=a[:, :n], in1=cs_[:, 0, tab0:tab0 + n], op=ALU.mult), reads=[ra, rcs_], writes=[ra])
            P.op("pool", lambda: nc.gpsimd.tensor_tensor(out=b[:, :n], in0=b[:, :n], in1=cs_[:, 1, tab0:tab0 + n], op=ALU.mult), reads=[rb_, rcs_], writes=[rb_])
            P.op("dve", lambda: nc.vector.tensor_tensor(out=dst, in0=a[:, :n], in1=b[:, :n], op=ALU.add), reads=[ra, rb_], writes=[res])

        for h in range(4):
            rope_rows(q[:, h, :], rq, SW0 + h * 64, 0, NQ, 128)
            P.dma("pool", qc[:, h, :], pT_d[SW0 + h * 64:SW0 + (h + 1) * 64, NQ:NTOK], dn, reads=[r_pT], writes=[rc])
        for g in range(2):
            rope_rows(k[:, g, 128:128 + NQ], rk, SW0 + 256 + g * 64, 0, NQ, 128)
            P.dma("pool", kc_[:, g, :], pT_d[SW0 + 256 + g * 64:SW0 + 256 + (g + 1) * 64, NQ:NTOK], dn, reads=[r_pT], writes=[rc])
            for side in range(2):
                a_, ra_ = halo_k(None, R_SWK + g * 64, 64, 128, side, hk2, hacc2)
                b_, rb2 = halo_k(None, R_SWK + g * 64, 64, 128, side, hk2, hacc2, rowperm=[(d0, s0, 16) for (d0, s0) in perm])
                tab0 = 0 if side == 0 else 128 + NQ
                P.op("dve", lambda a_=a_, tab0=tab0: nc.vector.tensor_tensor(out=a_[:64, :128], in0=a_[:64, :128], in1=cs_[:, 0, tab0:tab0 + 128], op=ALU.mult), reads=[ra_, rcs_], writes=[ra_])
                P.op("dve", lambda b_=b_, tab0=tab0: nc.vector.tensor_tensor(out=b_[:64, :128], in0=b_[:64, :128], in1=cs_[:, 1, tab0:tab0 + 128], op=ALU.mult), reads=[rb2, rcs_], writes=[rb2])
                P.op("dve", lambda a_=a_, b_=b_, g=g, tab0=tab0: nc.vector.tensor_tensor(out=k[:, g, tab0:tab0 + 128], in0=a_[:64, :128], in1=b_[:64, :128], op=ALU.add),
                     reads=[ra_, rb2], writes=[rk])
        vsrc = vtok_d[:, 256:384].rearrange("(t p) (h d) -> p t h d", p=128, h=2)
        for t in range(16):
            P.dma("pool", v[:, 1 + t, :, 0:64], vsrc[:, t, :, :], dn, reads=[r_vtok], writes=[rv])
        for t in range(2):
            P.dma("pool", vc[:, t, :, 0:64], vsrc[:, 16 + t, :, :], dn, reads=[r_vtok], writes=[rc])
        for side in range(2):
            a_, ra_ = halo_v(256, 128, 1, side, hv2, hvacc2)
            t0_ = 0 if side == 0 else 17
            P.op("act", lambda a_=a_, t0_=t0_: nc.scalar.copy(out=v[:, t0_, :, 0:64], in_=a_[:, 0:128].rearrange("p (h d) -> p h d", h=2)), reads=[ra_], writes=[rv])
        bp = P.sb("swbp", [128, 4, 4, 128], F32); rbp = P.res()
        for kind in range(4):
            for h in range(4):
                P.op("dve", lambda kind=kind, h=h: nc.vector.tensor_copy(out=bp[:, kind, h, :], in_=bs[:, kind, :]), reads=[rbs], writes=[rbp])
        for t in range(18):
            items = []
            if t < 16:
                qa = [[q[:, h, t * 128:(t + 1) * 128]] for h in range(4)]
                qres = [rq]
                for kt in range(3):
                    et = t + kt
                    if kt == 0:
                        b = bp[:, 0 if t == 0 else 1, :, :].rearrange('p h q -> p (h q)')
                    elif kt == 2:
                        b = bp[:, 3 if t == 15 else 2, :, :].rearrange('p h q -> p (h q)')
                    else:
                        b = None
                    items.append(dict(k=[[k[:, h // 2, et * 128:(et + 1) * 128]] for h in range(4)], v=[v[:, et, h // 2, :] for h in range(4)], bias=b, bres=[rbp], res=[rk, rv]))
            else:
                tc = t - 16
                qa = [[qc[:, h, tc * 128:(tc + 1) * 128]] for h in range(4)]
                qres = [rc]
            for kt in range(2):
                items.append(dict(k=[[kc_[:, h // 2, kt * 128:(kt + 1) * 128]] for h in range(4)], v=[vc[:, kt, h // 2, :] for h in range(4)], bias=None, res=[rc]))
            o, ro, _ = ost.next()
            attend(A, 4, False, qa, qres, items, scale, [o[:, 0, h * 64:(h + 1) * 64] for h in range(4)], ro, sinkexp=sk[:, 0:4], sink_res=rsk)
            emit_oT(o, ro, 1, t, 2, False)
        P.pop_scope()

        P.push_scope()
        NK = LSEQ
        dsm = P.dsem(f"ml{l}")
        gkv = P.sb("gkv", [128, 1], F32); gq = P.sb("gq", [128, 2], F32); rg = P.res()
        P.dma("sp", gkv[:], mgkv_d[l], dsm, writes=[rg]); P.dma("sp", gq[:], mgq_d[l], dsm, writes=[rg])
        wkn = P.sb("wkn", [128, 256], BF16); wkv = P.sb("wkv", [128, 256], BF16)
        wqn = P.sb("wqn", [128, 2, 256], BF16); wqr = P.sb("wqr", [128, 2, 128], BF16); wqrs = P.sb("wqrs", [128, 2, 128], BF16)
        rw = P.res(); dw = P.dsem(f"mw{l}")
        P.dma("pool", wkn[:], mwkn_d[l], dw, writes=[rw]); P.dma("pool", wkv[:], mwkv_d[l], dw, writes=[rw])
        for c in range(2):
            P.dma("pool", wqn[:, c, :], mwqn_d[l][c * 128:(c + 1) * 128, :], dw, writes=[rw])
            P.dma("pool", wqr[:, c, :], mwqr_d[l][c * 128:(c + 1) * 128, :], dw, writes=[rw])
            P.dma("pool", wqrs[:, c, :], mwqrs_d[l][c * 128:(c + 1) * 128, :], dw, writes=[rw])
        K96 = P.sb("K96", [96, 4, NK], BF16); rkn = P.res(); rkr = P.res()
        vm = P.sb("vm", [128, NK // 128, 4, 65], BF16); rvm = P.res()
        Q96 = P.sb("Q96", [96, 4, NTOK], BF16); rqn = P.res(); rqr = P.res()
        rrow = P.sb("rrow", [65, 512], F32); rrr_ = P.res()
        bcs = P.sb("bcs", [64, 512], F32); rbcs = P.res()
        P.op("dve", lambda: nc.vector.memset(vm[:], 1.0), writes=[rvm])
        xin = Rot(P, "mx", 2, [128, 2, 512], F32, dma=True)
        sq = Rot(P, "msq", 1, [128, 2, 512], BF16)
        rms = Rot(P, "mrms", 1, [128, 512], F32)
        xn = Rot(P, "mxn", 2, [128, 2, 512], BF16)
        tab = Rot(P, "mtab", 2, [32, 2, 512], F32, dma=True)
        rr4 = Rot(P, "mrr", 4, [32, 512], F32, dma=True)
        perm32 = ((0, 8), (8, 0), (16, 24), (24, 16))

        def key_src(row0, nrows, t0, n):
            out = []
            if t0 < 256:
                ln = min(n, 256 - t0)
                out.append((rbuf(0, row0, nrows, NQ + t0, NQ + t0 + ln), 0))
                if ln < n:
                    for (r, ls, l2, do_) in seg_lat(0, n - ln):
                        out.append((rbuf(r, row0, nrows, ls, ls + l2), ln + do_))
            else:
                for (r, ls, l2, do_) in seg_lat(t0 - 256, n):
                    out.append((rbuf(r, row0, nrows, ls, ls + l2), do_))
            return out

        def norm_blk(loads, kc, n, g):
            x, rx, dx = xin.next()
            for (c, p0, ap, off) in loads:
                P.dma("sp", x[p0:p0 + ap.shape[0], c, off:off + ap.shape[1]], ap, dx, reads=[r_RB2, r_pT], writes=[rx])
            s, rs, _ = sq.next(); r, rr, _ = rms.next()
            rstd_of(x, rx, kc, n, r, rr, s, rs, kc * 128)
            y, ry, _ = xn.next()
            for c in range(kc):
                P.op("dve", lambda c=c: nc.vector.scalar_tensor_tensor(out=y[:, c, :n], in0=x[:, c, :n], scalar=g[:, c:c + 1], in1=r[:, :n], op0=ALU.mult, op1=ALU.mult),
                     reads=[rx, rr, rg], writes=[ry])
            return y, ry

        for t0 in range(0, NK, 512):
            n = min(512, NK - t0)
            y, ry = norm_blk([(0, hf * 64, ap, off) for hf in range(2) for (ap, off) in key_src(R_CKV + hf * 64, 64, t0, n)], 1, n, gkv)
            for pr in range(2):
                pt, rpt, _ = A.misc.next()
                P.op("pe", lambda pr=pr, pt=pt: nc.tensor.matmul(pt[:, :n], lhsT=wkn[:, pr * 128:(pr + 1) * 128], rhs=y[:, 0, :n], start=True, stop=True), reads=[rw, ry], writes=[rpt])
                evac(K96[0:64, 2 * pr, t0:t0 + n], pt[0:64, :n], [rpt], [rkn])
                evac(K96[0:64, 2 * pr + 1, t0:t0 + n], pt[64:128, :n], [rpt], [rkn])
            for tt_ in range(n // 128):
                kt = t0 // 128 + tt_
                pt, rpt, _ = A.misc.next()
                P.op("pe", lambda tt_=tt_, pt=pt: nc.tensor.matmul(pt[:, 0:256], lhsT=y[:, 0, tt_ * 128:(tt_ + 1) * 128], rhs=wkv[:], start=True, stop=True), reads=[rw, ry], writes=[rpt])
                evac(vm[:, kt, :, 0:64], pt[:, 0:256].rearrange("p (h d) -> p h d", h=4), [rpt], [rvm])
            tb, rtb, dtb = tab.next()
            P.dma("sp", tb[:, 0, :n], mkcs_d[0][:, t0:t0 + n], dtb, writes=[rtb]); P.dma("sp", tb[:, 1, :n], mkcs_d[1][:, t0:t0 + n], dtb, writes=[rtb])
            a, ra, da = rr4.next(); b, rb_, db = rr4.next()
            for (ap, off) in key_src(R_KR, 32, t0, n):
                P.dma("sp", a[:, off:off + ap.shape[1]], ap, da, reads=[r_RB2], writes=[ra])
            for (d0, s0) in perm32:
                for (ap, off) in key_src(R_KR + s0, 8, t0, n):
                    P.dma("sp", b[d0:d0 + 8, off:off + ap.shape[1]], ap, db, reads=[r_RB2], writes=[rb_])
            P.op("dve", lambda: nc.vector.tensor_tensor(out=a[:, :n], in0=a[:, :n], in1=tb[:, 0, :n], op=ALU.mult), reads=[ra, rtb], writes=[ra])
            P.op("pool", lambda: nc.gpsimd.tensor_tensor(out=b[:, :n], in0=b[:, :n], in1=tb[:, 1, :n], op=ALU.mult), reads=[rb_, rtb], writes=[rb_])
            for h4 in range(4):
                P.op("dve" if h4 % 2 == 0 else "pool", lambda h4=h4: (nc.vector if h4 % 2 == 0 else nc.gpsimd).tensor_tensor(out=K96[64:96, h4, t0:t0 + n], in0=a[:, :n], in1=b[:, :n], op=ALU.add),
                     reads=[ra, rb_], writes=[rkr])
        for t0 in range(0, NTOK, 512):
            n = min(512, NTOK - t0)
            y, ry = norm_blk([(c, 0, pT_d[ML0 + c * 128:ML0 + (c + 1) * 128, t0:t0 + n], 0) for c in range(2)], 2, n, gq)
            for pr in range(2):
                pt, rpt, _ = A.misc.next()
                for c in range(2):
                    P.op("pe", lambda pr=pr, pt=pt, c=c: nc.tensor.matmul(pt[:, :n], lhsT=wqn[:, c, pr * 128:(pr + 1) * 128], rhs=y[:, c, :n], start=(c == 0), stop=(c == 1)),
                         reads=[rw, ry], writes=[rpt])
                evac(Q96[0:64, 2 * pr, t0:t0 + n], pt[0:64, :n], [rpt], [rqn])
                evac(Q96[0:64, 2 * pr + 1, t0:t0 + n], pt[64:128, :n], [rpt], [rqn])
            tb, rtb, dtb = tab.next()
            P.dma("sp", tb[:, 0, :n], mqcs_d[0][:, t0:t0 + n], dtb, writes=[rtb]); P.dma("sp", tb[:, 1, :n], mqcs_d[1][:, t0:t0 + n], dtb, writes=[rtb])
            for h in range(4):
                pa, rpa, _ = A.misc.next()
                for c in range(2):
                    P.op("pe", lambda pa=pa, c=c, h=h: nc.tensor.matmul(pa[0:32, :n], lhsT=wqr[:, c, h * 32:(h + 1) * 32], rhs=y[:, c, :n], start=(c == 0), stop=(c == 1)),
                         reads=[rw, ry], writes=[rpa])
                a, ra, _ = rr4.next()
                P.op("dve", lambda a=a, pa=pa: nc.vector.tensor_tensor(out=a[:, :n], in0=pa[0:32, :n], in1=tb[:, 0, :n], op=ALU.mult), reads=[rpa, rtb], writes=[ra])
                pb, rpb, _ = A.misc.next()
                for c in range(2):
                    P.op("pe", lambda pb=pb, c=c, h=h: nc.tensor.matmul(pb[0:32, :n], lhsT=wqrs[:, c, h * 32:(h + 1) * 32], rhs=y[:, c, :n], start=(c == 0), stop=(c == 1)),
                         reads=[rw, ry], writes=[rpb])
                b, rb_, _ = rr4.next()
                P.op("dve", lambda b=b, pb=pb: nc.vector.tensor_tensor(out=b[:, :n], in0=pb[0:32, :n], in1=tb[:, 1, :n], op=ALU.mult), reads=[rpb, rtb], writes=[rb_])
                P.op("pool", lambda a=a, b=b, h=h: nc.gpsimd.tensor_tensor(out=Q96[64:96, h, t0:t0 + n], in0=a[:, :n], in1=b[:, :n], op=ALU.add), reads=[ra, rb_], writes=[rqr])
        scale = 96 ** -0.5
        groups = [(g * 512, 4, NK // 128) for g in range(4)] + [(2048, 2, 2)]
        for (q0, ncol, nkt) in groups:
            W_ = ncol * 128
            for h in range(4):
                acc, racc, _ = A.accs.next()
                def score(kt):
                    pan, rpan, _ = A.panels.next()
                    P.op("pe", lambda: nc.tensor.matmul(pan[:, 0:W_], lhsT=K96[0:96, h, kt * 128:(kt + 1) * 128], rhs=Q96[0:96, h, q0:q0 + W_], start=True, stop=True),
                         reads=[rkn, rkr, rqn, rqr], writes=[rpan])
                    pt_, rpt_, _ = A.pT.next()
                    P.op("act", lambda: nc.scalar.activation(out=pt_[:, 0:W_], in_=pan[:, 0:W_], func=AF.Exp, scale=scale), reads=[rpan], writes=[rpt_])
                    return pt_, rpt_
                nxt = score(0)
                for kt in range(nkt):
                    pt_, rpt_ = nxt
                    if kt + 1 < nkt:
                        nxt = score(kt + 1)
                    P.op("pe", lambda: nc.tensor.matmul(acc[0:65, 0:W_], lhsT=vm[:, kt, h, :], rhs=pt_[:, 0:W_], start=(kt == 0), stop=(kt == nkt - 1)),
                         reads=[rvm, rpt_], writes=[racc])
                P.op("dve", lambda: nc.vector.reciprocal(out=rrow[64:65, 0:W_], in_=acc[64:65, 0:W_]), reads=[racc], writes=[rrr_])
                pb_, rpb_, _ = A.misc.next()
                P.op("pe", lambda: nc.tensor.matmul(pb_[0:64, 0:W_], lhsT=onesf[64:65, 0:64], rhs=rrow[64:65, 0:W_], start=True, stop=True), reads=[ronesf, rrr_], writes=[rpb_])
                P.op("act", lambda: nc.scalar.copy(out=bcs[:, 0:W_], in_=pb_[0:64, 0:W_]), reads=[rpb_], writes=[rbcs])
                hp_ = (h % 2) * 64
                P.op("dve", lambda: nc.vector.tensor_tensor(out=oTs[hp_:hp_ + 64, 4 + h // 2, q0:q0 + W_], in0=acc[0:64, 0:W_], in1=bcs[:, 0:W_], op=ALU.mult),
                     reads=[racc, rbcs], writes=[roT])
        P.pop_scope()
        P.pop_scope()

        NB = 256
        fblocks = [(t0, NB, 0 if t0 < NQ else 1) for t0 in range(0, NTOK, NB)]
        P.push_scope()
        wo = P.sb("wo", [128, KC, D], BF16); rwo = P.res(); dwo = P.dsem(f"wo{l}")
        for kk in range(KC):
            P.dma("pool", wo[:, kk, :], wo_d[l][kk * 128:(kk + 1) * 128, :], dwo, writes=[rwo])
        hin = Rot(P, "fhin", 3, [128, KC, NB], F32, dma=True)
        ycand = Rot(P, "ycand", 3, [128, 2, 4, NB], F32, dma=True)
        yin = Rot(P, "yin", 3, [128, 2, NB], F32)
        zin = Rot(P, "zin", 3, [128, 2, NB], F32, dma=True)
        sq = Rot(P, "fsq", 2, [128, 2, NB], BF16)
        rms = Rot(P, "frms", 2, [128, NB], F32)
        od = Rot(P, "fod", 2, [128, 2, NB], BF16)
        for (t0, n, j) in fblocks:
            h, rh, dh = hin.next()
            P.dma("sp", h[:, :, :n], hT_d.rearrange("(k p) t -> p k t", p=128)[:, :, t0:t0 + n], dh, reads=[r_hin], writes=[rh])
            y, ry, _ = yin.next()
            if j == 0:
                yc, ryc, dyc = ycand.next()
                for kk in range(2):
                    for r in range(4):
                        col = 256 + r * NQ + t0
                        P.dma("sp", yc[:, kk, r, :n], YGc[col // YCH][kk * 128:(kk + 1) * 128, col % YCH:col % YCH + n], dyc, reads=[r_YG], writes=[ryc])
                for kk in range(2):
                    P.op("dve", lambda kk=kk: nc.vector.tensor_scalar(out=y[:, kk, :n], in0=yc[:, kk, 0, :n], scalar1=flg[:, 0:1], scalar2=None, op0=ALU.mult),
                         reads=[ryc, rflg], writes=[ry])
                    for r in range(1, 4):
                        P.op("dve", lambda kk=kk, r=r: nc.vector.scalar_tensor_tensor(out=y[:, kk, :n], in0=yc[:, kk, r, :n], scalar=flg[:, r:r + 1], in1=y[:, kk, :n],
                                                                                    op0=ALU.mult, op1=ALU.add), reads=[ryc, rflg, ry], writes=[ry])
            else:
                yc, ryc, dyc = ycand.next()
                for kk in range(2):
                    P.dma("sp", yc[:, kk, 0, :n], YGc[0][kk * 128:(kk + 1) * 128, 0:256], dyc, reads=[r_YG], writes=[ryc])
                P.op("dve", lambda: nc.vector.tensor_copy(out=y[:, :, :n], in_=yc[:, :, 0, :n]), reads=[ryc], writes=[ry])
            z, rz, dz = zin.next()
            P.dma("sp", z[:, :, :n], pT_d[SS0:SS0 + 256, :].rearrange("(k p) t -> p k t", p=128)[:, :, t0:t0 + n], dz, reads=[r_pT], writes=[rz])
            P.op("act", lambda: nc.scalar.activation(out=z[:, :, :n], in_=z[:, :, :n], func=AF.Silu), reads=[rz], writes=[rz])
            P.op("dve", lambda: nc.vector.tensor_tensor(out=y[:, :, :n], in0=y[:, :, :n], in1=z[:, :, :n], op=ALU.mult), reads=[ry, rz], writes=[ry])
            s, rs, _ = sq.next(); r, rr, _ = rms.next()
            rstd_of(y, ry, 2, n, r, rr, s, rs, 256)
            odt, rod, _ = od.next()
            for kk in range(2):
                P.op("dve", lambda kk=kk: nc.vector.scalar_tensor_tensor(out=odt[:, kk, :n], in0=y[:, kk, :n], scalar=gvs[:, 16 + kk:17 + kk], in1=r[:, :n], op0=ALU.mult, op1=ALU.mult),
                     reads=[ry, rgv, rr], writes=[rod])
            for cb in range(KC):
                pt, rpt, _ = gen.next()
                for kk in range(KC):
                    rhs = oTs[:, kk, t0:t0 + n] if kk < 6 else odt[:, kk - 6, :n]
                    P.op("pe", lambda kk=kk, rhs=rhs, pt=pt: nc.tensor.matmul(pt[:, :n], lhsT=wo[:, kk, cb * 128:(cb + 1) * 128], rhs=rhs, start=(kk == 0), stop=(kk == KC - 1)),
                         reads=[rwo, roT, rod], writes=[rpt])
                P.op("dve", lambda cb=cb, pt=pt: nc.vector.scalar_tensor_tensor(out=h[:, cb, :n], in0=pt[:, :n], scalar=mo[:, 16 + cb, j:j + 1], in1=h[:, cb, :n], op0=ALU.mult, op1=ALU.add),
                     reads=[rpt, rmo, rh], writes=[rh])
            P.dma("sp", h2_d.rearrange("(k p) t -> p k t", p=128)[:, :, t0:t0 + n], h[:, :, :n], dh, reads=[rh], writes=[r_h2])
        P.pop_scope()
        P.pop_scope()
        P.push_scope()
        w1 = P.sb("w1", [128, KC, 4 * D], BF16); rw1 = P.res(); dw1 = P.dsem(f"w1{l}")
        w2 = P.sb("w2", [128, 32, D], BF16); rw2 = P.res(); dw2 = P.dsem(f"w2{l}")
        for kk in range(KC):
            for c0 in range(0, 4 * D, 2048):
                P.dma("pool", w1[:, kk, c0:c0 + 2048], w1_d[l][kk * 128:(kk + 1) * 128, c0:c0 + 2048], dw1, writes=[rw1])
        for kk in range(32):
            P.dma("pool", w2[:, kk, :], w2_d[l][kk * 128:(kk + 1) * 128, :], dw2, writes=[rw2])
        NB2 = 512
        hin = Rot(P, "h2in", 1, [128, KC, NB2], F32, dma=True)
        rms = Rot(P, "rms2", 1, [128, NB2], F32)
        tt = Rot(P, "tt2", 2, [128, NB2], F32)
        xm = Rot(P, "xm2", 1, [128, KC, NB2], BF16)
        at = Rot(P, "at", 1, [128, 32, NB2], BF16)
        rl = Rot(P, "rl", 2, [128, NB2], F32)
        for (t0, n, j) in [(t0, NB2, 0) for t0 in range(0, NQ, NB2)] + [(NQ, NCX, 1)]:
            if final and j == 1:
                continue
            h, rh, dh = hin.next()
            P.dma("sp", h[:, :, :n], h2_d.rearrange("(k p) t -> p k t", p=128)[:, :, t0:t0 + n], dh, reads=[r_h2], writes=[rh])
            a, ra, _ = at.next()
            s, rs = a, ra
            r, rr, _ = rms.next()
            rstd_of(h, rh, KC, n, r, rr, s, rs, D)
            x, rx, _ = xm.next()
            for kk in range(KC):
                t, rt, _ = tt.next()
                P.op("dve", lambda kk=kk, t=t: nc.vector.tensor_tensor(out=t[:, :n], in0=h[:, kk, :n], in1=r[:, :n], op=ALU.mult), reads=[rh, rr], writes=[rt])
                P.op("act", lambda kk=kk, t=t: nc.scalar.activation(out=x[:, kk, :n], in_=t[:, :n], func=AF.Identity, scale=gs2[:, kk, j:j + 1], bias=mo[:, 24 + kk, j:j + 1]),
                     reads=[rt, rgs, rmo], writes=[rx])
            for cb in range(32):
                pt, rpt, _ = gen.next()
                for kk in range(KC):
                    P.op("pe", lambda kk=kk, pt=pt: nc.tensor.matmul(pt[:, :n], lhsT=w1[:, kk, cb * 128:(cb + 1) * 128], rhs=x[:, kk, :n], start=(kk == 0), stop=(kk == KC - 1)),
                         reads=[rw1, rx], writes=[rpt])
                qq, rq_, _ = rl.next()
                P.op("act", lambda pt=pt, qq=qq: nc.scalar.activation(out=qq[:, :n], in_=pt[:, :n], func=AF.Relu), reads=[rpt], writes=[rq_])
                P.op("pool", lambda cb=cb, qq=qq: nc.gpsimd.tensor_tensor(out=a[:, cb, :n], in0=qq[:, :n], in1=qq[:, :n], op=ALU.mult), reads=[rq_], writes=[ra])
            for cb in range(KC):
                pt, rpt, _ = gen.next()
                for kk in range(32):
                    P.op("pe", lambda kk=kk, pt=pt: nc.tensor.matmul(pt[:, :n], lhsT=w2[:, kk, cb * 128:(cb + 1) * 128], rhs=a[:, kk, :n], start=(kk == 0), stop=(kk == 31)),
                         reads=[rw2, ra], writes=[rpt])
                P.op("dve", lambda cb=cb, pt=pt: nc.vector.scalar_tensor_tensor(out=h[:, cb, :n], in0=pt[:, :n], scalar=mo[:, 40 + cb, j:j + 1], in1=h[:, cb, :n], op0=ALU.mult, op1=ALU.add),
                     reads=[rpt, rmo, rh], writes=[rh])
            if final:
                r, rr, _ = rms.next()
                rstd_of(h, rh, KC, n, r, rr, a, ra, D)
                for kk in range(KC):
                    P.op("dve", lambda kk=kk: nc.vector.scalar_tensor_tensor(out=h[:, kk, :n], in0=h[:, kk, :n], scalar=gvs[:, 8 + kk:9 + kk], in1=r[:, :n], op0=ALU.mult, op1=ALU.mult),
                         reads=[rh, rgv, rr], writes=[rh])
                P.dma("sp", out_d.rearrange("(k p) t -> p k t", p=128)[:, :, t0:t0 + n], h[:, :, :n], dh, reads=[rh], writes=[P.res()], final=True)
            else:
                last = (l == nlayers - 1)
                if last and j == 0:
                    P.dma("sp", out_d.rearrange("(k p) t -> p k t", p=128)[:, :, t0:t0 + n], h[:, :, :n], dh, reads=[rh], writes=[P.res()], final=True)
                P.dma("sp", hTn_d.rearrange("(k p) t -> p k t", p=128)[:, :, t0:t0 + n], h[:, :, :n], dh, reads=[rh], writes=[r_hn])
        P.pop_scope()
    P.finish()
    P.close()
    return P


NEG = -1e30
NA0, SW0, ML0, SS0 = 0, 768, 1280, 1696


def cvec(v, n):
    return np.ascontiguousarray(v.reshape(n, 128).T)


def swap_idx(dim):
    q = dim // 4
    idx = np.arange(dim)
    blk = (idx // q) % 2
    return np.where(blk == 0, idx + q, idx - q)


def rope_tables(pos, dim):
    nf = dim // 4
    inv = (1.0 / (10000.0 ** (np.arange(nf, dtype=np.float32) / nf))).astype(np.float32)
    row = (pos // 64).astype(np.float32)
    col = (pos % 64).astype(np.float32)
    d = np.arange(dim)
    f = d % nf
    p = np.where((d < dim // 2)[:, None], row[None, :], col[None, :]).astype(np.float32)
    ang = (p * inv[f][:, None]).astype(np.float32)
    sign = np.where(((d // nf) % 2) == 0, -1.0, 1.0).astype(np.float32)
    return np.cos(ang).astype(np.float32), (np.sin(ang) * sign[:, None]).astype(np.float32)


def ext_rows(a, lo, hi):
    S = a.shape[0]
    out = np.zeros((hi - lo,) + a.shape[1:], a.dtype)
    l2, h2 = max(lo, 0), min(hi, S)
    out[l2 - lo:h2 - lo] = a[l2:h2]
    return out


def na_bias_tile(rpb, T):
    q = np.arange(128)
    r = 2 * T + q // 64
    qc = q % 64
    rs = np.clip(r - 4, 0, 120)
    cst = np.clip(qc - 8, 0, 48)
    out = np.full((128, 7, 4, 128), NEG, np.float32)
    i = np.arange(128)
    for kt in range(7):
        krow = 2 * T - 6 + 2 * kt + i // 64
        kcol = i % 64
        valid = ((krow[:, None] >= rs[None, :]) & (krow[:, None] < rs[None, :] + 8) & (krow[:, None] >= 0) & (krow[:, None] < 128)
                 & (kcol[:, None] >= cst[None, :]) & (kcol[:, None] < cst[None, :] + 16))
        dr = np.clip(krow[:, None] - r[None, :] + 7, 0, 14)
        dc = np.clip(kcol[:, None] - qc[None, :] + 15, 0, 30)
        for h in range(4):
            out[:, kt, h, :] = np.where(valid, rpb[h][dr, dc], NEG)
    return out


def prep_T(p_b, pc_b, j, W, l):
    o0 = 2048 * j
    T = lambda a: np.ascontiguousarray(a.T)
    m = {}
    own = p_b[o0:o0 + 2048]
    m["na_q"] = T(own[:, 0:256])
    e = ext_rows(p_b[:, 256:768], o0 - 384, o0 + 2048 + 384)
    m["na_k"] = T(e[:, 0:256]); m["na_v"] = np.ascontiguousarray(e[:, 256:512])
    m["na_qc"] = T(pc_b[:, 0:256]); m["na_kc"] = T(pc_b[:, 256:512]); m["na_vc"] = np.ascontiguousarray(pc_b[:, 512:768])
    rpb = W["na_rpb"][l]
    m["na_bias"] = np.stack([na_bias_tile(rpb, 16 * j + t).reshape(128, 7, 512) for t in (0, 1, 8 if j in (0, 3) else 2, 14, 15)])
    sw = swap_idx(64)
    q = own[:, SW0:SW0 + 256].reshape(2048, 4, 64)
    m["sw_q"] = T(q.reshape(2048, 256)); m["sw_qs"] = T(q[:, :, sw].reshape(2048, 256))
    e = ext_rows(p_b[:, SW0 + 256:SW0 + 512], o0 - 128, o0 + 2048 + 128)
    k = e[:, 0:128].reshape(2304, 2, 64)
    m["sw_k"] = T(k.reshape(2304, 128)); m["sw_ks"] = T(k[:, :, sw].reshape(2304, 128))
    m["sw_v"] = np.ascontiguousarray(e[:, 128:256])
    pos = np.arange(o0 - 128, o0 + 2048 + 128)
    c, s = rope_tables(np.clip(pos, 0, 8191), 64)
    m["sw_cs"] = np.stack([c, s])
    m["sw_qc"] = T(pc_b[:, SW0:SW0 + 256]); m["sw_kc"] = T(pc_b[:, SW0 + 256:SW0 + 384]); m["sw_vc"] = np.ascontiguousarray(pc_b[:, SW0 + 384:SW0 + 512])
    i = np.arange(128)
    prev = np.where(i[:, None] >= i[None, :], 0.0, NEG).astype(np.float32)
    nxt = np.where(i[:, None] <= i[None, :], 0.0, NEG).astype(np.float32)
    allneg = np.full((128, 128), NEG, np.float32)
    m["sw_bias"] = np.ascontiguousarray(np.stack([allneg if j == 0 else prev, prev, nxt, allneg if j == 3 else nxt], 1))
    m["sw_sink"] = np.ascontiguousarray(np.broadcast_to(W["swa_sink"][l][None, :], (128, 4)))
    sw32 = swap_idx(32)
    allk = np.concatenate([pc_b[:, ML0 + 256:ML0 + 416], p_b[:, ML0 + 256:ML0 + 416]], 0)
    m["m_ckv"] = T(allk[:, 0:128]); m["m_kr"] = T(allk[:, 128:160]); m["m_krs"] = T(allk[:, 128:160][:, sw32])
    c, s = rope_tables(np.arange(8192), 32)
    c = np.concatenate([np.ones((32, 256), np.float32), c], 1); s = np.concatenate([np.zeros((32, 256), np.float32), s], 1)
    m["m_kcs"] = np.stack([c, s])
    m["m_cq"] = T(np.concatenate([own[:, ML0:ML0 + 256], pc_b[:, ML0:ML0 + 256]], 0))
    c, s = rope_tables(np.arange(o0, o0 + 2048), 32)
    c = np.concatenate([c, np.ones((32, 256), np.float32)], 1); s = np.concatenate([s, np.zeros((32, 256), np.float32)], 1)
    m["m_qcs"] = np.stack([c, s])
    m["m_gkv"] = np.ascontiguousarray(W["mla_g_kv"][l].reshape(128, 1)); m["m_gq"] = cvec(W["mla_g_q"][l], 2)
    wkv = W["mla_w_ukv"][l].reshape(128, 4, 128)
    m["m_wkn"] = np.ascontiguousarray(wkv[:, :, 0:64].reshape(128, 256)); m["m_wkv"] = np.ascontiguousarray(wkv[:, :, 64:128].reshape(128, 256))
    wq = W["mla_w_uq"][l].reshape(256, 4, 96)
    m["m_wqn"] = np.ascontiguousarray(wq[:, :, 0:64].reshape(256, 256))
    m["m_wqr"] = np.ascontiguousarray(wq[:, :, 64:96].reshape(256, 128))
    m["m_wqrs"] = np.ascontiguousarray(wq[:, :, 64:96][:, :, sw32].reshape(256, 128))
    return m


def prep_S(p_b, pc_b, hd, W, l):
    g = hd // 2
    T = lambda a: np.ascontiguousarray(a.T)
    allp = np.concatenate([pc_b[:, SS0:], p_b[:, SS0:]], 0)
    xbc = allp[:, 256:1024]
    m = {}
    m["s_x"] = T(xbc[:, hd * 64:(hd + 1) * 64])
    m["s_b"] = T(xbc[:, 256 + g * 128:256 + (g + 1) * 128])
    m["s_c"] = T(xbc[:, 512 + g * 128:512 + (g + 1) * 128])
    cwv = np.concatenate([W["ssd_conv_w"][l], W["ssd_conv_b"][l][None, :]], 0)
    cw = np.zeros((128, 3, 6), np.float32)
    cw[:64, 0, :] = cwv[:, hd * 64:(hd + 1) * 64].T
    cw[:, 1, :] = cwv[:, 256 + g * 128:256 + (g + 1) * 128].T
    cw[:, 2, :] = cwv[:, 512 + g * 128:512 + (g + 1) * 128].T
    m["s_cw"] = cw
    dt = allp[:, 1024:1032].reshape(-1, 2, 4)[:, :, hd]
    m["s_dt"] = np.ascontiguousarray(dt.reshape(66, 128, 2).transpose(1, 0, 2))
    par = np.zeros((128, 8), np.float32)
    par[:, 0:2] = W["ssd_dt_bias"][l][:, hd]; par[:, 2:4] = W["ssd_a_log"][l][:, hd]; par[:, 4] = W["ssd_d"][l][hd]
    m["s_par"] = par
    i = np.arange(128)
    m["s_u"] = np.stack([(i[:, None] <= i[None, :]), (i[:, None] >= i[None, :])]).astype(np.float32)
    m["s_id"] = np.eye(128, dtype=np.float32)
    return m


def prep_M(I, core):
    b, j = core // 4, core % 4
    T = lambda a: np.ascontiguousarray(a.T)
    L = 4
    m = {}
    m["hT0"] = T(np.concatenate([I['x'][b, 2048 * j:2048 * (j + 1)], I['ctx'][b]], 0))
    m["cv"] = np.ascontiguousarray(np.stack([cvec(I['c'][b], 8), cvec(I['c_ctx'], 8)], -1))
    flg = np.zeros((128, 16), np.float32)
    flg[:, j] = 1.0
    if j > 0: flg[:, 4 + j - 1] = 1.0
    if j < 3: flg[:, 8 + j + 1] = 1.0
    m["flg"] = flg
    selx = np.zeros((128, 2, 64), np.float32); selb = np.zeros((128, 2, 128), np.float32)
    for d in range(64): selx[(j % 2) * 64 + d, j // 2, d] = 1.0
    for n in range(128): selb[n, j // 2, n] = 1.0
    m["selx"] = selx; m["selb"] = selb
    m["w_in"] = I['w_in']; m["w_mod"] = I['w_mod']; m["w_out"] = I['w_out']; m["w1"] = I['w_mlp1']; m["w2"] = I['w_mlp2']
    m["bmod"] = np.stack([cvec(I['b_mod'][l], 48) for l in range(L)])
    m["g1"] = np.stack([cvec(I['g_norm1'][l], 8) for l in range(L)])
    gv = np.zeros((L, 128, 20), np.float32)
    for l in range(L):
        gv[l, :, 0:8] = cvec(I['g_norm2'][l], 8); gv[l, :, 8:16] = cvec(I['g_final'], 8); gv[l, :, 16:18] = cvec(I['ssd_g_norm'][l], 2)
    m["gv"] = gv
    m["na_bias"] = np.stack([np.stack([na_bias_tile(I['na_rpb'][l], 16 * j + t).reshape(128, 7, 512) for t in (0, 1, 8 if j in (0, 3) else 2, 14, 15)]) for l in range(L)])
    i = np.arange(128)
    prev = np.where(i[:, None] >= i[None, :], 0.0, NEG).astype(np.float32)
    nxt = np.where(i[:, None] <= i[None, :], 0.0, NEG).astype(np.float32)
    allneg = np.full((128, 128), NEG, np.float32)
    m["sw_bias"] = np.ascontiguousarray(np.stack([allneg if j == 0 else prev, prev, nxt, allneg if j == 3 else nxt], 1))
    m["sw_sink"] = np.stack([np.ascontiguousarray(np.broadcast_to(I['swa_sink'][l][None, :], (128, 4))) for l in range(L)])
    o0 = 2048 * j
    c, s = rope_tables(np.clip(np.arange(o0 - 128, o0 + 2048 + 128), 0, 8191), 64)
    m["sw_cs"] = np.stack([c, s])
    c, s = rope_tables(np.arange(8192), 32)
    m["m_kcs"] = np.stack([np.concatenate([np.ones((32, 256), np.float32), c], 1), np.concatenate([np.zeros((32, 256), np.float32), s], 1)])
    c, s = rope_tables(np.arange(o0, o0 + 2048), 32)
    m["m_qcs"] = np.stack([np.concatenate([c, np.ones((32, 256), np.float32)], 1), np.concatenate([s, np.zeros((32, 256), np.float32)], 1)])
    sw32 = swap_idx(32)
    m["m_gkv"] = np.stack([I['mla_g_kv'][l].reshape(128, 1) for l in range(L)])
    m["m_gq"] = np.stack([cvec(I['mla_g_q'][l], 2) for l in range(L)])
    wkv = I['mla_w_ukv'].reshape(L, 128, 4, 128)
    m["m_wkn"] = np.ascontiguousarray(wkv[:, :, :, 0:64].reshape(L, 128, 256)); m["m_wkv"] = np.ascontiguousarray(wkv[:, :, :, 64:128].reshape(L, 128, 256))
    wq = I['mla_w_uq'].reshape(L, 256, 4, 96)
    m["m_wqn"] = np.ascontiguousarray(wq[:, :, :, 0:64].reshape(L, 256, 256))
    m["m_wqr"] = np.ascontiguousarray(wq[:, :, :, 64:96].reshape(L, 256, 128))
    m["m_wqrs"] = np.ascontiguousarray(wq[:, :, :, 64:96][:, :, :, sw32].reshape(L, 256, 128))
    hd, g = j, j // 2
    cw = np.zeros((L, 128, 3, 6), np.float32); par = np.zeros((L, 128, 8), np.float32)
    for l in range(L):
        cwv = np.concatenate([I['ssd_conv_w'][l], I['ssd_conv_b'][l][None, :]], 0)
        cw[l, :64, 0, :] = cwv[:, hd * 64:(hd + 1) * 64].T
        cw[l, :, 1, :] = cwv[:, 256 + g * 128:256 + (g + 1) * 128].T
        cw[l, :, 2, :] = cwv[:, 512 + g * 128:512 + (g + 1) * 128].T
        par[l, :, 0:2] = I['ssd_dt_bias'][l][:, hd]; par[l, :, 2:4] = I['ssd_a_log'][l][:, hd]; par[l, :, 4] = I['ssd_d'][l][hd]
    m["s_cw"] = cw; m["s_par"] = par
    m["s_u"] = np.stack([(i[:, None] <= i[None, :]), (i[:, None] >= i[None, :])]).astype(np.float32)
    m["ident"] = np.eye(128, dtype=np.float32)
    return m


from concourse.bass_utils import run_bass_kernel_spmd

_PROG = {}


def kernel(**inputs):
    I = {k: np.asarray(v, dtype=np.float32) for k, v in inputs.items()}
    if "M" not in _PROG:
        _PROG["M"] = build_M(4, 3, 4)
    P = _PROG["M"]
    in_maps = [prep_M(I, c) for c in range(8)]
    res = run_bass_kernel_spmd(P.nc, in_maps, core_ids=list(range(8)))
    out = np.stack([np.concatenate([res.results[b * 4 + j]["out"].T for j in range(4)], 0) for b in range(2)])
    return np.ascontiguousarray(out.astype(np.float32))
```

```python
import numpy as np
from contextlib import ExitStack
import concourse.bass as bass
import concourse.mybir as mybir

F32 = mybir.dt.float32
BF16 = mybir.dt.bfloat16
AF = mybir.ActivationFunctionType
ALU = mybir.AluOpType
AX = mybir.AxisListType


class Res:
    __slots__ = ("name", "w", "r")

    def __init__(self, name):
        self.name = name
        self.w = None
        self.r = []


class Prog:
    ENG = ("pe", "act", "dve", "pool", "sp")

    def __init__(self):
        self.nc = bass.Bass("TRN2", target_bir_lowering=False)
        self.es = ExitStack()
        self.root = self.es
        nc = self.nc
        self.e = {"pe": nc.tensor, "act": nc.scalar, "dve": nc.vector, "pool": nc.gpsimd, "sp": nc.sync}
        self.sem = {}
        self.cnt = {}
        for k in self.ENG:
            self.sem[k] = self.es.enter_context(nc.semaphore("s_" + k))
            self.cnt[k] = 0
        self.seen = {k: {} for k in self.ENG}
        self.ndma = 0
        self.out_toks = []
        self.nwait = 0
        self.nres = 0

    def dram(self, name, shape, dt, kind):
        return self.nc.dram_tensor(name, list(shape), dt, kind=kind).ap()

    def _u(self, name):
        self.nuniq = getattr(self, "nuniq", 0) + 1
        return f"{name}_u{self.nuniq}"

    def sb(self, name, shape, dt):
        return self.es.enter_context(self.nc.sbuf_tensor(self._u(name), list(shape), dt))

    def ps(self, name, shape, dt=F32):
        return self.es.enter_context(self.nc.psum_tensor(self._u(name), list(shape), dt))

    def res(self, name=None):
        self.nres += 1
        return Res(name or f"r{self.nres}")

    def dsem(self, name):
        free = getattr(self, "free_dsems", None)
        if free:
            key = free.pop()
        else:
            name = self._u(name)
            s = self.root.enter_context(self.nc.semaphore("d_" + name))
            key = ("d", name)
            self.sem[key] = s
            self.cnt[key] = 0
        if getattr(self, "_scope_dsems", None):
            self._scope_dsems[-1].append(key)
        return key

    def _deps(self, eng, reads, writes):
        need = {}

        def add(t, same_ok):
            if t is None:
                return
            sk, val, src = t
            if src == eng and not same_ok:
                return
            if need.get(sk, 0) < val:
                need[sk] = val

        for r in reads:
            add(r.w, True)
        for w in writes:
            add(w.w, False)
            for t in w.r:
                add(t, False)
        out = []
        seen = self.seen[eng]
        for sk, val in need.items():
            if seen.get(sk, 0) >= val:
                continue
            seen[sk] = val
            out.append((sk, val))
        return out

    def _emit_waits(self, eng, waits):
        e = self.e[eng]
        for sk, val in waits:
            e.wait_ge(self.sem[sk], val)
            self.nwait += 1

    def _record(self, tok, reads, writes):
        for r in reads:
            r.r.append(tok)
        for w in writes:
            w.w = tok
            w.r = []

    def op(self, eng, fn, reads=(), writes=(), pe_chain=False):
        reads = [r for r in reads if r is not None]
        writes = [w for w in writes if w is not None]
        if eng == "pe":
            waits = self._deps_pe(reads, writes)
        else:
            waits = self._deps(eng, reads, writes)
        self._emit_waits(eng, waits)
        ins = fn()
        self.cnt[eng] += 1
        ins.then_inc(self.sem[eng], 1)
        tok = (eng, self.cnt[eng], eng)
        self._record(tok, reads, writes)
        return ins

    def _deps_pe(self, reads, writes):
        need = {}

        def add(t):
            if t is None:
                return
            sk, val, src = t
            if src == "pe":
                return
            if need.get(sk, 0) < val:
                need[sk] = val

        for r in reads:
            add(r.w)
        for w in writes:
            add(w.w)
            for t in w.r:
                add(t)
        out = []
        seen = self.seen["pe"]
        for sk, val in need.items():
            if seen.get(sk, 0) >= val:
                continue
            seen[sk] = val
            out.append((sk, val))
        return out

    def dma(self, q, dst, src, dsem, reads=(), writes=(), final=False, **kw):
        reads = [r for r in reads if r is not None]
        writes = [w for w in writes if w is not None]
        waits = self._deps(q, reads, writes)
        self._emit_waits(q, waits)
        ins = self.e[q].dma_start(out=dst, in_=src, **kw)
        self.cnt[dsem] += 16
        ins.then_inc(self.sem[dsem], 16)
        tok = (dsem, self.cnt[dsem], "dma")
        self._record(tok, reads, writes)
        self.ndma += 1
        if final:
            self.out_toks.append(tok)
        return ins

    def barrier(self):
        for eng in self.ENG:
            for sk, val in self.cnt.items():
                if val > 0 and self.seen[eng].get(sk, 0) < val:
                    self.e[eng].wait_ge(self.sem[sk], val)
                    self.seen[eng][sk] = val

    def push_scope(self):
        self._scopes = getattr(self, "_scopes", [])
        self._scopes.append(self.es)
        self.es = ExitStack()
        self._scope_dsems = getattr(self, "_scope_dsems", [])
        self._scope_dsems.append([])

    def pop_scope(self):
        self.barrier()
        self.es.close()
        self.es = self._scopes.pop()
        self.free_dsems = getattr(self, "free_dsems", [])
        self.free_dsems.extend(self._scope_dsems.pop())

    def finish(self):
        toks = list(self.out_toks)
        need = {}
        for sk, val, _ in toks:
            need[sk] = max(need.get(sk, 0), val)
        for sk, val in need.items():
            self.e["sp"].wait_ge(self.sem[sk], val)

    def close(self):
        self.es.close()


class Rot:
    def __init__(self, P, name, n, shape, dt, psum=False, dma=False):
        self.t, self.r, self.d = [], [], []
        for i in range(n):
            self.t.append(P.ps(f"{name}{i}", shape, dt) if psum else P.sb(f"{name}{i}", shape, dt))
            self.r.append(P.res(f"{name}{i}"))
            self.d.append(P.dsem(f"{name}{i}") if dma else None)
        self.i = -1
        self.n = n

    def next(self):
        self.i = (self.i + 1) % self.n
        return self.t[self.i], self.r[self.i], self.d[self.i]


EPS = 1e-6


class AttnCtx:
    def __init__(self, P):
        self.P = P
        nc = P.nc
        self.panels = Rot(P, "pan", 3, [128, 512], F32, psum=True)
        self.accs = Rot(P, "acc", 2, [128, 512], F32, psum=True)
        self.misc = Rot(P, "mps", 2, [128, 512], F32, psum=True)
        self.pT = Rot(P, "pT", 3, [128, 512], BF16)
        self.sT = Rot(P, "sT", 2, [128, 512], F32)
        self.rec = Rot(P, "rec", 2, [128, 4], F32)
        self.zeros = P.sb("zeros", [128, 512], BF16)
        self.rz = P.res()
        P.op("dve", lambda: nc.vector.memset(self.zeros[:], 0.0), writes=[self.rz])
        self.ev = 0


def attend(A, ncol, merged, q_aps, q_res, key_items, scale, out_aps, out_res, sinkexp=None, sink_res=None):
    P = A.P
    nc = P.nc
    W = ncol * 128
    acc, racc, _ = A.accs.next()
    P.op("pe", lambda: nc.tensor.matmul(acc[:, 0:ncol * 65], lhsT=A.zeros[:, 0:128], rhs=A.zeros[:, 0:ncol * 65], start=True, stop=False),
         reads=[A.rz], writes=[racc])
    nk = len(key_items)

    def score(it):
        pan, rpan, _ = A.panels.next()
        if merged:
            parts_k = it["k"][0]
            parts_q = q_aps[0]
            for pi, (kp, qp) in enumerate(zip(parts_k, parts_q)):
                P.op("pe", lambda kp=kp, qp=qp, pi=pi: nc.tensor.matmul(pan[:, 0:W], lhsT=kp, rhs=qp, start=(pi == 0), stop=(pi == len(parts_k) - 1)),
                     reads=list(it["res"]) + list(q_res), writes=[rpan])
        else:
            for c in range(ncol):
                parts_k = it["k"][c]
                parts_q = q_aps[c]
                for pi, (kp, qp) in enumerate(zip(parts_k, parts_q)):
                    P.op("pe", lambda kp=kp, qp=qp, pi=pi, c=c, n=len(parts_k): nc.tensor.matmul(
                        pan[:, c * 128:(c + 1) * 128], lhsT=kp, rhs=qp, start=(pi == 0), stop=(pi == n - 1)),
                        reads=list(it["res"]) + list(q_res), writes=[rpan])
        pt, rpt, _ = A.pT.next()
        if it.get("bias") is not None:
            st, rst, _ = A.sT.next()
            P.op("dve", lambda: nc.vector.scalar_tensor_tensor(out=st[:, 0:W], in0=pan[:, 0:W], scalar=scale, in1=it["bias"], op0=ALU.mult, op1=ALU.add),
                 reads=[rpan] + list(it.get("bres", [])), writes=[rst])
            P.op("act", lambda: nc.scalar.activation(out=pt[:, 0:W], in_=st[:, 0:W], func=AF.Exp), reads=[rst], writes=[rpt])
        else:
            P.op("act", lambda: nc.scalar.activation(out=pt[:, 0:W], in_=pan[:, 0:W], func=AF.Exp, scale=scale), reads=[rpan], writes=[rpt])
        return pt, rpt

    nxt = score(key_items[0])
    for ki, it in enumerate(key_items):
        pt, rpt = nxt
        if ki + 1 < nk:
            nxt = score(key_items[ki + 1])
        for c in range(ncol):
            P.op("pe", lambda c=c: nc.tensor.matmul(acc[:, c * 65:(c + 1) * 65], lhsT=pt[:, c * 128:(c + 1) * 128], rhs=it["v"][c], start=False, stop=(ki == nk - 1)),
                 reads=[rpt] + list(it["res"]), writes=[racc])
    rec, rrec, _ = A.rec.next()
    den = acc[:, 64:64 + 65 * (ncol - 1) + 1:65]
    if sinkexp is not None:
        P.op("dve", lambda: nc.vector.tensor_tensor(out=rec[:, 0:ncol], in0=den, in1=sinkexp, op=ALU.add), reads=[racc, sink_res], writes=[rrec])
        P.op("dve", lambda: nc.vector.reciprocal(out=rec[:, 0:ncol], in_=rec[:, 0:ncol]), reads=[rrec], writes=[rrec])
    else:
        P.op("dve", lambda: nc.vector.reciprocal(out=rec[:, 0:ncol], in_=den), reads=[racc], writes=[rrec])
    for c in range(ncol):
        A.ev += 1
        if A.ev % 2 == 0:
            P.op("act", lambda c=c: nc.scalar.activation(out=out_aps[c], in_=acc[:, c * 65:c * 65 + 64], func=AF.Copy, scale=rec[:, c:c + 1]),
                 reads=[racc, rrec], writes=[out_res])
        else:
            P.op("dve", lambda c=c: nc.vector.tensor_scalar(out=out_aps[c], in0=acc[:, c * 65:c * 65 + 64], scalar1=rec[:, c:c + 1], scalar2=None,
                                                            op0=ALU.mult), reads=[racc, rrec], writes=[out_res])


def load_cast(P, dst, src, res, dsem):
    P.dma("pool", dst, src, dsem, writes=[res])


def load_v(P, vt, v_d, ntile, nh, res, dsem):
    nc = P.nc
    P.op("dve", lambda: nc.vector.memset(vt[:], 1.0), writes=[res])
    src = v_d.rearrange("(t p) (h d) -> p t h d", p=128, h=nh)
    for t in range(ntile):
        P.dma("pool", vt[:, t, :, 0:64], src[:, t, :, :], dsem, writes=[res])


def rope_to(P, dst, x_d, xs_d, cos_t, sin_t, rtab, n, npart, stg, res):
    nc = P.nc
    a, ra, da = stg.next()
    b, rb, db = stg.next()
    P.dma("sp", a[:npart, :n], x_d, da, writes=[ra])
    P.dma("sp", b[:npart, :n], xs_d, db, writes=[rb])
    P.op("dve", lambda: nc.vector.tensor_tensor(out=a[:npart, :n], in0=a[:npart, :n], in1=cos_t, op=ALU.mult), reads=[ra, rtab], writes=[ra])
    P.op("pool", lambda: nc.gpsimd.tensor_tensor(out=b[:npart, :n], in0=b[:npart, :n], in1=sin_t, op=ALU.mult), reads=[rb, rtab], writes=[rb])
    P.op("dve", lambda: nc.vector.tensor_tensor(out=dst, in0=a[:npart, :n], in1=b[:npart, :n], op=ALU.add), reads=[ra, rb], writes=[res])


def build_T(do_na=True, do_swa=True, do_mla=True):
    P = Prog()
    nc = P.nc
    A = AttnCtx(P)
    NQ = 2048
    NC = 256
    rout = P.res("out")
    ost = Rot(P, "ost", 2, [128, 4, 256], F32, dma=True)

    if do_na:
        P.push_scope()
        EXT = 2816
        q_d = P.dram("na_q", [256, NQ], F32, "ExternalInput")
        k_d = P.dram("na_k", [256, EXT], F32, "ExternalInput")
        v_d = P.dram("na_v", [EXT, 256], F32, "ExternalInput")
        qc_d = P.dram("na_qc", [256, NC], F32, "ExternalInput")
        kc_d = P.dram("na_kc", [256, NC], F32, "ExternalInput")
        vc_d = P.dram("na_vc", [NC, 256], F32, "ExternalInput")
        b_d = P.dram("na_bias", [5, 128, 7, 512], F32, "ExternalInput")
        o_d = P.dram("o_na", [NQ + NC, 256], F32, "ExternalOutput")
        q = P.sb("naq", [64, 4, NQ], BF16); rq = P.res(); d1 = P.dsem("naq")
        k = P.sb("nak", [64, 4, EXT], BF16); rk = P.res(); d2 = P.dsem("nak")
        v = P.sb("nav", [128, EXT // 128, 4, 65], BF16); rv = P.res(); d3 = P.dsem("nav")
        qc = P.sb("naqc", [64, 4, NC], BF16); kc = P.sb("nakc", [64, 4, NC], BF16); vc = P.sb("navc", [128, 2, 4, 65], BF16)
        rc = P.res(); d4 = P.dsem("nac")
        for h in range(4):
            load_cast(P, q[:, h, :], q_d[h * 64:(h + 1) * 64, :], rq, d1)
            load_cast(P, k[:, h, :], k_d[h * 64:(h + 1) * 64, :], rk, d2)
            load_cast(P, qc[:, h, :], qc_d[h * 64:(h + 1) * 64, :], rc, d4)
            load_cast(P, kc[:, h, :], kc_d[h * 64:(h + 1) * 64, :], rc, d4)
        load_v(P, v, v_d, EXT // 128, 4, rv, d3)
        load_v(P, vc, vc_d, 2, 4, rc, d4)
        bias = Rot(P, "nab", 2, [128, 7, 512], F32, dma=True)
        scale = 64 ** -0.5
        for t in range(16 + 2):
            items = []
            if t < 16:
                pat = 0 if t == 0 else 1 if t == 1 else 3 if t == 14 else 4 if t == 15 else 2
                bt, rb, db = bias.next()
                P.dma("sp", bt[:], b_d[pat], db, writes=[rb])
                qa = [[q[:, h, t * 128:(t + 1) * 128]] for h in range(4)]
                for kt in range(7):
                    et = t + kt
                    items.append(dict(k=[[k[:, h, et * 128:(et + 1) * 128]] for h in range(4)], v=[v[:, et, h, :] for h in range(4)],
                                      bias=bt[:, kt, :], bres=[rb], res=[rk, rv]))
                qres = [rq]
            else:
                tc = t - 16
                qa = [[qc[:, h, tc * 128:(tc + 1) * 128]] for h in range(4)]
                qres = [rc]
            for kt in range(2):
                items.append(dict(k=[[kc[:, h, kt * 128:(kt + 1) * 128]] for h in range(4)], v=[vc[:, kt, h, :] for h in range(4)], bias=None, res=[rc]))
            o, ro, do = ost.next()
            attend(A, 4, False, qa, qres, items, scale, [o[:, 0, h * 64:(h + 1) * 64] for h in range(4)], ro)
            P.dma("sp", o_d[t * 128:(t + 1) * 128, :], o[:, 0, :], do, reads=[ro], writes=[rout], final=True)
        P.pop_scope()

    if do_swa:
        P.push_scope()
        EXT = 2304
        q_d = P.dram("sw_q", [256, NQ], F32, "ExternalInput")
        qs_d = P.dram("sw_qs", [256, NQ], F32, "ExternalInput")
        k_d = P.dram("sw_k", [128, EXT], F32, "ExternalInput")
        ks_d = P.dram("sw_ks", [128, EXT], F32, "ExternalInput")
        cs_d = P.dram("sw_cs", [2, 64, EXT], F32, "ExternalInput")
        v_d = P.dram("sw_v", [EXT, 128], F32, "ExternalInput")
        qc_d = P.dram("sw_qc", [256, NC], F32, "ExternalInput")
        kc_d = P.dram("sw_kc", [128, NC], F32, "ExternalInput")
        vc_d = P.dram("sw_vc", [NC, 128], F32, "ExternalInput")
        b_d = P.dram("sw_bias", [128, 4, 128], F32, "ExternalInput")
        sk_d = P.dram("sw_sink", [128, 4], F32, "ExternalInput")
        o_d = P.dram("o_sw", [NQ + NC, 256], F32, "ExternalOutput")
        q = P.sb("swq", [64, 4, NQ], BF16); rq = P.res()
        k = P.sb("swk", [64, 2, EXT], BF16); rk = P.res()
        v = P.sb("swv", [128, EXT // 128, 2, 65], BF16); rv = P.res(); d3 = P.dsem("swv")
        qc = P.sb("swqc", [64, 4, NC], BF16); kc = P.sb("swkc", [64, 2, NC], BF16); vc = P.sb("swvc", [128, 2, 2, 65], BF16)
        rc = P.res(); d4 = P.dsem("swc")
        cs = P.sb("swcs", [64, 2, EXT], F32); rcs = P.res(); d5 = P.dsem("swcs")
        bs = P.sb("swb", [128, 4, 128], F32); rbs = P.res()
        sk = P.sb("swsk", [128, 4], F32); rsk = P.res()
        P.dma("sp", cs[:, 0, :], cs_d[0], d5, writes=[rcs])
        P.dma("sp", cs[:, 1, :], cs_d[1], d5, writes=[rcs])
        P.dma("sp", bs[:], b_d, d5, writes=[rbs])
        P.dma("sp", sk[:], sk_d, d5, writes=[rsk])
        P.op("act", lambda: nc.scalar.activation(out=sk[:], in_=sk[:], func=AF.Exp), reads=[rsk], writes=[rsk])
        stg = Rot(P, "swstg", 4, [64, EXT], F32, dma=True)
        for h in range(4):
            rope_to(P, q[:, h, :], q_d[h * 64:(h + 1) * 64, :], qs_d[h * 64:(h + 1) * 64, :], cs[:, 0, 128:128 + NQ], cs[:, 1, 128:128 + NQ], rcs, NQ, 64, stg, rq)
            load_cast(P, qc[:, h, :], qc_d[h * 64:(h + 1) * 64, :], rc, d4)
        for g in range(2):
            rope_to(P, k[:, g, :], k_d[g * 64:(g + 1) * 64, :], ks_d[g * 64:(g + 1) * 64, :], cs[:, 0, :], cs[:, 1, :], rcs, EXT, 64, stg, rk)
            load_cast(P, kc[:, g, :], kc_d[g * 64:(g + 1) * 64, :], rc, d4)
        load_v(P, v, v_d, EXT // 128, 2, rv, d3)
        load_v(P, vc, vc_d, 2, 2, rc, d4)
        bp = P.sb("swbp", [128, 4, 4, 128], F32); rbp = P.res()
        for kind in range(4):
            for h in range(4):
                P.op("dve", lambda kind=kind, h=h: nc.vector.tensor_copy(out=bp[:, kind, h, :], in_=bs[:, kind, :]), reads=[rbs], writes=[rbp])
        scale = 64 ** -0.5
        for t in range(16 + 2):
            items = []
            if t < 16:
                qa = [[q[:, h, t * 128:(t + 1) * 128]] for h in range(4)]
                qres = [rq]
                for kt in range(3):
                    et = t + kt
                    if kt == 0:
                        b = bp[:, 0 if t == 0 else 1, :, :].rearrange('p h q -> p (h q)')
                    elif kt == 2:
                        b = bp[:, 3 if t == 15 else 2, :, :].rearrange('p h q -> p (h q)')
                    else:
                        b = None
                    items.append(dict(k=[[k[:, h // 2, et * 128:(et + 1) * 128]] for h in range(4)], v=[v[:, et, h // 2, :] for h in range(4)],
                                      bias=b, bres=[rbp], res=[rk, rv]))
            else:
                tc = t - 16
                qa = [[qc[:, h, tc * 128:(tc + 1) * 128]] for h in range(4)]
                qres = [rc]
            for kt in range(2):
                items.append(dict(k=[[kc[:, h // 2, kt * 128:(kt + 1) * 128]] for h in range(4)], v=[vc[:, kt, h // 2, :] for h in range(4)], bias=None, res=[rc]))
            o, ro, do = ost.next()
            attend(A, 4, False, qa, qres, items, scale, [o[:, 0, h * 64:(h + 1) * 64] for h in range(4)], ro, sinkexp=sk[:, 0:4], sink_res=rsk)
            P.dma("sp", o_d[t * 128:(t + 1) * 128, :], o[:, 0, :], do, reads=[ro], writes=[rout], final=True)
        P.pop_scope()

    if do_mla:
        P.push_scope()
        NK = 8448
        NQA = NQ + NC
        ckv_d = P.dram("m_ckv", [128, NK], F32, "ExternalInput")
        kr_d = P.dram("m_kr", [32, NK], F32, "ExternalInput")
        krs_d = P.dram("m_krs", [32, NK], F32, "ExternalInput")
        kcs_d = P.dram("m_kcs", [2, 32, NK], F32, "ExternalInput")
        cq_d = P.dram("m_cq", [256, NQA], F32, "ExternalInput")
        qcs_d = P.dram("m_qcs", [2, 32, NQA], F32, "ExternalInput")
        gkv_d = P.dram("m_gkv", [128, 1], F32, "ExternalInput")
        gq_d = P.dram("m_gq", [128, 2], F32, "ExternalInput")
        wkn_d = P.dram("m_wkn", [128, 256], F32, "ExternalInput")
        wkv_d = P.dram("m_wkv", [128, 256], F32, "ExternalInput")
        wqn_d = P.dram("m_wqn", [256, 256], F32, "ExternalInput")
        wqr_d = P.dram("m_wqr", [256, 128], F32, "ExternalInput")
        wqrs_d = P.dram("m_wqrs", [256, 128], F32, "ExternalInput")
        o_d = P.dram("o_ml", [NQA, 256], F32, "ExternalOutput")

        ones = P.sb("mones", [128, 128], BF16); rones = P.res()
        P.op("dve", lambda: nc.vector.memset(ones[:], 1.0), writes=[rones])
        dsm = P.dsem("msmall")
        gkv = P.sb("gkv", [128, 1], F32); gq = P.sb("gq", [128, 2], F32); rg = P.res()
        P.dma("sp", gkv[:], gkv_d, dsm, writes=[rg]); P.dma("sp", gq[:], gq_d, dsm, writes=[rg])
        wkn = P.sb("wkn", [128, 256], BF16); wkv = P.sb("wkv", [128, 256], BF16)
        wqn = P.sb("wqn", [128, 2, 256], BF16); wqr = P.sb("wqr", [128, 2, 128], BF16); wqrs = P.sb("wqrs", [128, 2, 128], BF16)
        rw = P.res(); dw = P.dsem("mw")
        load_cast(P, wkn[:], wkn_d, rw, dw); load_cast(P, wkv[:], wkv_d, rw, dw)
        for c in range(2):
            load_cast(P, wqn[:, c, :], wqn_d[c * 128:(c + 1) * 128, :], rw, dw)
            load_cast(P, wqr[:, c, :], wqr_d[c * 128:(c + 1) * 128, :], rw, dw)
            load_cast(P, wqrs[:, c, :], wqrs_d[c * 128:(c + 1) * 128, :], rw, dw)

        kn = P.sb("kn", [128, 2, NK], BF16); rkn = P.res()
        kr = P.sb("kr", [32, NK], BF16); rkr = P.res()
        vm = P.sb("vm", [128, NK // 128, 4, 65], BF16); rvm = P.res()
        qn = P.sb("qn", [128, 2, NQA], BF16); rqn = P.res()
        qr = P.sb("qr", [32, 4, NQA], BF16); rqr = P.res()
        P.op("dve", lambda: nc.vector.memset(vm[:], 1.0), writes=[rvm])

        xin = Rot(P, "mx", 2, [128, 2, 512], F32, dma=True)
        sq = Rot(P, "msq", 2, [128, 2, 512], BF16)
        rms = Rot(P, "mrms", 2, [128, 512], F32)
        tt = Rot(P, "mtt", 2, [128, 512], F32)
        xn = Rot(P, "mxn", 2, [128, 2, 512], BF16)
        tab = Rot(P, "mtab", 2, [32, 2, 512], F32, dma=True)
        rr = Rot(P, "mrr", 4, [32, 512], F32, dma=True)
        evi = [0]

        def evac(dst, src, reads, writes):
            evi[0] += 1
            if evi[0] % 2 == 0:
                P.op("act", lambda: nc.scalar.copy(out=dst, in_=src), reads=reads, writes=writes)
            else:
                P.op("dve", lambda: nc.vector.tensor_copy(out=dst, in_=src), reads=reads, writes=writes)

        def norm_block(src_d, kc, t0, n, g):
            x, rx, dx = xin.next()
            for c in range(kc):
                P.dma("sp", x[:, c, :n], src_d[c * 128:(c + 1) * 128, t0:t0 + n], dx, writes=[rx])
            s, rs, _ = sq.next()
            for c in range(kc):
                P.op("act", lambda c=c: nc.scalar.activation(out=s[:, c, :n], in_=x[:, c, :n], func=AF.Square), reads=[rx], writes=[rs])
            pt, rpt, _ = A.misc.next()
            for c in range(kc):
                P.op("pe", lambda c=c: nc.tensor.matmul(pt[:, :n], lhsT=ones[:], rhs=s[:, c, :n], start=(c == 0), stop=(c == kc - 1)), reads=[rones, rs], writes=[rpt])
            r, rrr, _ = rms.next()
            P.op("act", lambda: nc.scalar.activation(out=r[:, :n], in_=pt[:, :n], func=AF.Sqrt, scale=1.0 / (kc * 128), bias=EPS), reads=[rpt], writes=[rrr])
            P.op("dve", lambda: nc.vector.reciprocal(out=r[:, :n], in_=r[:, :n]), reads=[rrr], writes=[rrr])
            y, ry, _ = xn.next()
            for c in range(kc):
                P.op("dve", lambda c=c: nc.vector.scalar_tensor_tensor(out=y[:, c, :n], in0=x[:, c, :n], scalar=g[:, c:c + 1], in1=r[:, :n],
                                                                     op0=ALU.mult, op1=ALU.mult), reads=[rx, rrr, rg], writes=[ry])
            return y, ry

        def rope_block(dst, x_d, xs_d, cs_d, t0, n, res):
            tb, rtb, dtb = tab.next()
            P.dma("sp", tb[:, 0, :n], cs_d[0][:, t0:t0 + n], dtb, writes=[rtb])
            P.dma("sp", tb[:, 1, :n], cs_d[1][:, t0:t0 + n], dtb, writes=[rtb])
            a, ra, da = rr.next(); b, rb, db = rr.next()
            P.dma("sp", a[:, :n], x_d[:, t0:t0 + n], da, writes=[ra])
            P.dma("sp", b[:, :n], xs_d[:, t0:t0 + n], db, writes=[rb])
            P.op("dve", lambda: nc.vector.tensor_tensor(out=a[:, :n], in0=a[:, :n], in1=tb[:, 0, :n], op=ALU.mult), reads=[ra, rtb], writes=[ra])
            P.op("pool", lambda: nc.gpsimd.tensor_tensor(out=b[:, :n], in0=b[:, :n], in1=tb[:, 1, :n], op=ALU.mult), reads=[rb, rtb], writes=[rb])
            P.op("dve", lambda: nc.vector.tensor_tensor(out=dst, in0=a[:, :n], in1=b[:, :n], op=ALU.add), reads=[ra, rb], writes=[res])

        for t0 in range(0, NK, 512):
            n = min(512, NK - t0)
            y, ry = norm_block(ckv_d, 1, t0, n, gkv)
            for pr in range(2):
                pt, rpt, _ = A.misc.next()
                P.op("pe", lambda pr=pr, pt=pt: nc.tensor.matmul(pt[:, :n], lhsT=wkn[:, pr * 128:(pr + 1) * 128], rhs=y[:, 0, :n], start=True, stop=True),
                     reads=[rw, ry], writes=[rpt])
                evac(kn[:, pr, t0:t0 + n], pt[:, :n], [rpt], [rkn])
            for tt_ in range(n // 128):
                kt = t0 // 128 + tt_
                pt, rpt, _ = A.misc.next()
                P.op("pe", lambda tt_=tt_, pt=pt: nc.tensor.matmul(pt[:, 0:256], lhsT=y[:, 0, tt_ * 128:(tt_ + 1) * 128], rhs=wkv[:], start=True, stop=True),
                     reads=[rw, ry], writes=[rpt])
                evac(vm[:, kt, :, 0:64], pt[:, 0:256].rearrange("p (h d) -> p h d", h=4), [rpt], [rvm])
            rope_block(kr[:, t0:t0 + n], kr_d, krs_d, kcs_d, t0, n, rkr)
        for t0 in range(0, NQA, 512):
            n = min(512, NQA - t0)
            y, ry = norm_block(cq_d, 2, t0, n, gq)
            for pr in range(2):
                pt, rpt, _ = A.misc.next()
                for c in range(2):
                    P.op("pe", lambda pr=pr, pt=pt, c=c: nc.tensor.matmul(pt[:, :n], lhsT=wqn[:, c, pr * 128:(pr + 1) * 128], rhs=y[:, c, :n],
                                                                         start=(c == 0), stop=(c == 1)), reads=[rw, ry], writes=[rpt])
                evac(qn[:, pr, t0:t0 + n], pt[:, :n], [rpt], [rqn])
            tb, rtb, dtb = tab.next()
            P.dma("sp", tb[:, 0, :n], qcs_d[0][:, t0:t0 + n], dtb, writes=[rtb])
            P.dma("sp", tb[:, 1, :n], qcs_d[1][:, t0:t0 + n], dtb, writes=[rtb])
            for h in range(4):
                pa, rpa, _ = A.misc.next()
                for c in range(2):
                    P.op("pe", lambda pa=pa, c=c, h=h: nc.tensor.matmul(pa[0:32, :n], lhsT=wqr[:, c, h * 32:(h + 1) * 32], rhs=y[:, c, :n],
                                                                       start=(c == 0), stop=(c == 1)), reads=[rw, ry], writes=[rpa])
                a, ra, _ = rr.next()
                P.op("dve", lambda a=a, pa=pa: nc.vector.tensor_tensor(out=a[:, :n], in0=pa[0:32, :n], in1=tb[:, 0, :n], op=ALU.mult), reads=[rpa, rtb], writes=[ra])
                pb, rpb, _ = A.misc.next()
                for c in range(2):
                    P.op("pe", lambda pb=pb, c=c, h=h: nc.tensor.matmul(pb[0:32, :n], lhsT=wqrs[:, c, h * 32:(h + 1) * 32], rhs=y[:, c, :n],
                                                                       start=(c == 0), stop=(c == 1)), reads=[rw, ry], writes=[rpb])
                b, rb, _ = rr.next()
                P.op("dve", lambda b=b, pb=pb: nc.vector.tensor_tensor(out=b[:, :n], in0=pb[0:32, :n], in1=tb[:, 1, :n], op=ALU.mult), reads=[rpb, rtb], writes=[rb])
                P.op("pool", lambda a=a, b=b, h=h: nc.gpsimd.tensor_tensor(out=qr[:, h, t0:t0 + n], in0=a[:, :n], in1=b[:, :n], op=ALU.add), reads=[ra, rb], writes=[rqr])
        scale = 96 ** -0.5
        groups = [(g * 512, 4, NK // 128) for g in range(4)] + [(2048, 2, 2)]
        for (q0, ncol, nkt) in groups:
            o, ro, do = ost.next()
            for h in range(4):
                hp, pr = (h % 2) * 64, h // 2
                qa = [[qn[hp:hp + 64, pr, q0:q0 + ncol * 128], qr[:, h, q0:q0 + ncol * 128]]]
                items = []
                for kt in range(nkt):
                    items.append(dict(k=[[kn[hp:hp + 64, pr, kt * 128:(kt + 1) * 128], kr[:, kt * 128:(kt + 1) * 128]]],
                                      v=[vm[:, kt, h, :]] * ncol, bias=None, res=[rkn, rkr, rvm]))
                attend(A, ncol, True, qa, [rqn, rqr], items, scale, [o[:, c, h * 64:(h + 1) * 64] for c in range(ncol)], ro)
            P.dma("sp", o_d[q0:q0 + ncol * 128, :].rearrange("(c p) f -> p c f", p=128), o[:, 0:ncol, :], do, reads=[ro], writes=[rout], final=True)
        P.pop_scope()
    P.finish()
    P.close()
    return P


D = 1024
KC = 8
EPS = 1e-6
NTOK = 2304
NQ = 2048
NCX = 256
NCOL = 2728
NA0, SW0, ML0, SS0 = 0, 768, 1280, 1696
SBROWS = 1312
R_NAK, R_SWK, R_CKV, R_XBC, R_KR = 0, 256, 384, 512, 1280
YCH = 2816
G4 = [[0, 1, 2, 3], [4, 5, 6, 7]]
LSEQ = 8448
NCH = LSEQ // 128


class RotView:
    def __init__(self, rots):
        self.t, self.r, self.d = [], [], []
        for ro in rots:
            self.t += ro.t; self.r += ro.r; self.d += ro.d
        self.i = -1
        self.n = len(self.t)

    def next(self):
        self.i = (self.i + 1) % self.n
        return self.t[self.i], self.r[self.i], self.d[self.i]


def allgather(P, src, dst, reads, writes):
    nc = P.nc
    if "cc" not in P.sem:
        P.sem["cc"] = P.root.enter_context(nc.semaphore("s_cc"))
        P.cnt["cc"] = 0
    P._emit_waits("pool", P._deps("pool", reads, writes))
    ins = nc.gpsimd.collective_compute("AllGather", mybir.AluOpType.bypass, replica_groups=G4, ins=[src.opt()], outs=[dst.opt()])
    P.cnt["cc"] += 1
    ins.then_inc(P.sem["cc"])
    P._record(("cc", P.cnt["cc"], "dma"), reads, writes)
    nc.gpsimd.wait_ge(P.sem["cc"], P.cnt["cc"])
    P.seen["pool"]["cc"] = P.cnt["cc"]


def seg_lat(t0, n):
    out = []
    t = t0
    while t < t0 + n:
        r = t // 2048
        ln = min(t0 + n, (r + 1) * 2048) - t
        out.append((r, t - r * 2048, ln, t - t0))
        t += ln
    return out


def build_M(nlayers=4, final_layer=3, NLW=4):
    P = Prog()
    nc = P.nc
    X = lambda name, shape: P.dram(name, shape, F32, "ExternalInput")
    hT0_d = X("hT0", [D, NTOK]); cv_d = X("cv", [128, KC, 2]); flg_d = X("flg", [128, 16])
    selx_d = X("selx", [128, 2, 64]); selb_d = X("selb", [128, 2, 128])
    w_in_d = X("w_in", [NLW, D, NCOL]); w_mod_d = X("w_mod", [NLW, D, 6 * D]); bmod_d = X("bmod", [NLW, 128, 48])
    g1_d = X("g1", [NLW, 128, KC]); gv_d = X("gv", [NLW, 128, 20])
    wo_d = X("w_out", [NLW, D, D]); w1_d = X("w1", [NLW, D, 4 * D]); w2_d = X("w2", [NLW, 4 * D, D])
    nab_d = X("na_bias", [NLW, 5, 128, 7, 512]); swb_d = X("sw_bias", [128, 4, 128]); swk_d = X("sw_sink", [NLW, 128, 4])
    swcs_d = X("sw_cs", [2, 64, 2304]); mkcs_d = X("m_kcs", [2, 32, LSEQ]); mqcs_d = X("m_qcs", [2, 32, NTOK])
    mgkv_d = X("m_gkv", [NLW, 128, 1]); mgq_d = X("m_gq", [NLW, 128, 2])
    mwkn_d = X("m_wkn", [NLW, 128, 256]); mwkv_d = X("m_wkv", [NLW, 128, 256])
    mwqn_d = X("m_wqn", [NLW, 256, 256]); mwqr_d = X("m_wqr", [NLW, 256, 128]); mwqrs_d = X("m_wqrs", [NLW, 256, 128])
    scw_d = X("s_cw", [NLW, 128, 3, 6]); spar_d = X("s_par", [NLW, 128, 8]); su_d = X("s_u", [2, 128, 128]); id_d = X("ident", [128, 128])
    out_d = P.dram("out", [D, NQ], F32, "ExternalOutput")
    S_ = lambda name, shape: nc.dram_tensor(name, list(shape), F32).ap()
    hT_s = [S_("hTa", [D, NTOK]), S_("hTb", [D, NTOK])]
    pT_d = S_("pT", [NCOL, NTOK]); vtok_d = S_("vtok", [NTOK, 392])
    sb_nr = [64] * 20 + [32]
    SBc = [S_(f"SBc{c}", [sb_nr[c], NTOK]) for c in range(21)]
    RBc = [S_(f"RBc{c}", [4 * sb_nr[c], NTOK]) for c in range(21)]
    vg_nr = [512] * 4 + [256]
    VSc = [S_(f"VSc{i}", [vg_nr[i], 392]) for i in range(5)]
    VGc = [S_(f"VGc{i}", [4 * vg_nr[i], 392]) for i in range(5)]
    YSc = [S_(f"YSc{i}", [64, YCH]) for i in range(3)]
    YGc = [S_(f"YGc{i}", [256, YCH]) for i in range(3)]
    h2_d = S_("h2", [D, NTOK])

    def rbuf(r, row0, nrows, c0, c1):
        c = row0 // 64
        assert row0 + nrows <= 64 * c + sb_nr[c], (row0, nrows)
        o = r * sb_nr[c] + row0 - 64 * c
        return RBc[c][o:o + nrows, c0:c1]

    def vgbuf(r, row0, nrows, c0, c1):
        i = row0 // 512
        assert row0 + nrows <= 512 * i + vg_nr[i], (row0, nrows)
        o = r * vg_nr[i] + row0 - 512 * i
        return VGc[i][o:o + nrows, c0:c1]
    r_pT, r_vtok, r_SB, r_RB1, r_VG, r_YS, r_YG, r_h2, r_RB2 = (P.res() for _ in range(9))
    r_hT = [P.res(), P.res()]

    A = AttnCtx(P)
    extra = Rot(P, "xps", 1, [128, 512], F32, psum=True)
    gen = RotView([A.misc, extra, A.panels, A.accs])
    ones = P.sb("ones", [128, 128], BF16); rones = P.res()
    P.op("dve", lambda: nc.vector.memset(ones[:], 1.0), writes=[rones])
    onesf = P.sb("onesf", [128, 128], F32); ronesf = P.res()
    P.op("pool", lambda: nc.gpsimd.memset(onesf[:], 1.0), writes=[ronesf])
    dc = P.dsem("const")
    ident = P.sb("ident", [128, 128], F32); rid = P.res(); P.dma("sp", ident[:], id_d, dc, writes=[rid])
    flg = P.sb("flg", [128, 16], F32); rflg = P.res(); P.dma("sp", flg[:], flg_d, dc, writes=[rflg])
    U = P.sb("U", [128, 2, 128], F32); rU = P.res()
    P.dma("sp", U[:, 0, :], su_d[0], dc, writes=[rU]); P.dma("sp", U[:, 1, :], su_d[1], dc, writes=[rU])
    selx = P.sb("selx", [128, 2, 64], F32); selb = P.sb("selb", [128, 2, 128], F32); rsel = P.res()
    P.dma("sp", selx[:], selx_d, dc, writes=[rsel]); P.dma("sp", selb[:], selb_d, dc, writes=[rsel])
    cvs = P.sb("cvs", [128, KC, 2], F32); rcv = P.res(); P.dma("sp", cvs[:], cv_d, dc, writes=[rcv])
    ca = P.sb("ca", [128, KC, 2], F32); rca = P.res()
    P.op("act", lambda: nc.scalar.activation(out=ca[:], in_=cvs[:], func=AF.Silu), reads=[rcv], writes=[rca])
    moL = [P.sb("mo", [128, 48, 2], F32) for _ in range(2)]; rmoL = [P.res(), P.res()]
    gs1L = [P.sb("gs1", [128, KC, 2], F32) for _ in range(2)]; gs2L = [P.sb("gs2", [128, KC, 2], F32) for _ in range(2)]; rgsL = [P.res(), P.res()]
    gvsL = [P.sb("gvs", [128, 28], F32) for _ in range(2)]; rgvL = [P.res(), P.res()]
    roT = P.res()
    evi = [0]

    def evac(dst, src, reads, writes):
        evi[0] += 1
        if evi[0] % 2 == 0:
            P.op("act", lambda: nc.scalar.copy(out=dst, in_=src), reads=reads, writes=writes)
        else:
            P.op("dve", lambda: nc.vector.tensor_copy(out=dst, in_=src), reads=reads, writes=writes)

    def rstd_of(x, rx, kc, n, r, rr, s, rs, feat):
        for k in range(kc):
            P.op("act", lambda k=k: nc.scalar.activation(out=s[:, k, :n], in_=x[:, k, :n], func=AF.Square), reads=[rx], writes=[rs])
        pt, rpt, _ = gen.next()
        for k in range(kc):
            P.op("pe", lambda k=k: nc.tensor.matmul(pt[:, :n], lhsT=ones[:], rhs=s[:, k, :n], start=(k == 0), stop=(k == kc - 1)), reads=[rones, rs], writes=[rpt])
        P.op("act", lambda: nc.scalar.activation(out=r[:, :n], in_=pt[:, :n], func=AF.Sqrt, scale=1.0 / feat, bias=EPS), reads=[rpt], writes=[rr])
        P.op("dve", lambda: nc.vector.reciprocal(out=r[:, :n], in_=r[:, :n]), reads=[rr], writes=[rr])

    def emit_mod(l):
        mo, rmo, gs1, gs2, rgs, gvs, rgv = moL[l % 2], rmoL[l % 2], gs1L[l % 2], gs2L[l % 2], rgsL[l % 2], gvsL[l % 2], rgvL[l % 2]
        P.push_scope()
        dm = P.dsem(f"mod{l}")
        bm = P.sb("bm", [128, 48], F32); rbm = P.res()
        P.dma("sp", bm[:], bmod_d[l], dm, writes=[rbm])
        P.dma("sp", gvs[:, 0:20], gv_d[l], dm, writes=[rgv]); P.dma("sp", gvs[:, 20:28], g1_d[l], dm, writes=[rgv])
        wm = Rot(P, "wm", 2, [128, KC, 1024], F32, dma=True)
        for grp in range(6):
            w, rw_, dw_ = wm.next()
            for k in range(KC):
                P.dma("sp", w[:, k, :], w_mod_d[l][k * 128:(k + 1) * 128, grp * 1024:(grp + 1) * 1024], dw_, writes=[rw_])
            pt, rpt, _ = gen.next()
            for cb in range(8):
                for k in range(KC):
                    P.op("pe", lambda cb=cb, k=k: nc.tensor.matmul(pt[:, cb * 2:cb * 2 + 2], lhsT=w[:, k, cb * 128:(cb + 1) * 128], rhs=ca[:, k, :],
                                                                 start=(k == 0), stop=(k == KC - 1)), reads=[rw_, rca], writes=[rpt])
            for j in range(2):
                P.op("dve", lambda j=j: nc.vector.tensor_tensor(out=mo[:, grp * 8:(grp + 1) * 8, j], in0=pt[:, j:16:2], in1=bm[:, grp * 8:(grp + 1) * 8], op=ALU.add),
                     reads=[rpt, rbm], writes=[rmo])
        for j in range(2):
            P.op("dve", lambda j=j: nc.vector.scalar_tensor_tensor(out=gs1[:, :, j], in0=mo[:, 8:16, j], scalar=1.0, in1=gvs[:, 20:28], op0=ALU.add, op1=ALU.mult),
                 reads=[rmo, rgv], writes=[rgs])
            P.op("dve", lambda j=j: nc.vector.scalar_tensor_tensor(out=gs2[:, :, j], in0=mo[:, 32:40, j], scalar=1.0, in1=gvs[:, 0:8], op0=ALU.add, op1=ALU.mult),
                 reads=[rmo, rgv], writes=[rgs])
        P.pop_scope()


    for l in range(nlayers):
        final = (l == final_layer)
        hT_d, r_hin = (hT0_d, None) if l == 0 else (hT_s[(l - 1) % 2], r_hT[(l - 1) % 2])
        hTn_d, r_hn = hT_s[l % 2], r_hT[l % 2]

        mo, rmo, gs1, gs2, rgs, gvs, rgv = moL[l % 2], rmoL[l % 2], gs1L[l % 2], gs2L[l % 2], rgsL[l % 2], gvsL[l % 2], rgvL[l % 2]
        if l == 0:
            emit_mod(0)
        P.push_scope()
        wsb = P.sb("wsb", [128, KC, NCOL], BF16); rw = P.res(); dw = P.dsem(f"w{l}")
        wv = P.sb("wv", [128, KC, 392], BF16)
        for k in range(KC):
            for c0 in range(0, NCOL, 2048):
                c1 = min(NCOL, c0 + 2048)
                P.dma("pool", wsb[:, k, c0:c1], w_in_d[l][k * 128:(k + 1) * 128, c0:c1], dw, writes=[rw])
            P.dma("pool", wv[:, k, 0:256], w_in_d[l][k * 128:(k + 1) * 128, 512:768], dw, writes=[rw])
            P.dma("pool", wv[:, k, 256:384], w_in_d[l][k * 128:(k + 1) * 128, SW0 + 384:SW0 + 512], dw, writes=[rw])
            P.dma("pool", wv[:, k, 384:392], w_in_d[l][k * 128:(k + 1) * 128, SS0 + 1024:SS0 + 1032], dw, writes=[rw])
        hin = Rot(P, "hin", 2, [128, KC, 512], F32, dma=True)
        sq = Rot(P, "sq", 2, [128, KC, 512], BF16)
        xm = Rot(P, "xm", 2, [128, KC, 512], BF16)
        tt = Rot(P, "tt", 2, [128, 512], F32)
        rms = Rot(P, "rms", 2, [128, 512], F32)
        ost = Rot(P, "ost", 4, [128, 512], F32, dma=True)
        blocks = [(t0, 512, 0) for t0 in range(0, NQ, 512)] + [(NQ, 256, 1)]
        ncb = (NCOL + 127) // 128
        for (t0, n, j) in blocks:
            h, rh, dh = hin.next()
            P.dma("sp", h[:, :, :n], hT_d.rearrange("(k p) t -> p k t", p=128)[:, :, t0:t0 + n], dh, reads=[r_hin], writes=[rh])
            s, rs, _ = sq.next(); r, rr, _ = rms.next()
            rstd_of(h, rh, KC, n, r, rr, s, rs, D)
            x, rx, _ = xm.next()
            for k in range(KC):
                t, rt, _ = tt.next()
                P.op("dve", lambda k=k, t=t: nc.vector.tensor_tensor(out=t[:, :n], in0=h[:, k, :n], in1=r[:, :n], op=ALU.mult), reads=[rh, rr], writes=[rt])
                P.op("act", lambda k=k, t=t: nc.scalar.activation(out=x[:, k, :n], in_=t[:, :n], func=AF.Identity, scale=gs1[:, k, j:j + 1], bias=mo[:, k, j:j + 1]),
                     reads=[rt, rgs, rmo], writes=[rx])
            for cb in range(ncb):
                c0 = cb * 128
                m = min(128, NCOL - c0)
                pt, rpt, _ = gen.next()
                for k in range(KC):
                    P.op("pe", lambda k=k, pt=pt: nc.tensor.matmul(pt[:m, :n], lhsT=wsb[:, k, c0:c0 + m], rhs=x[:, k, :n], start=(k == 0), stop=(k == KC - 1)),
                         reads=[rw, rx], writes=[rpt])
                o, ro, do = ost.next()
                evac(o[:m, :n], pt[:m, :n], [rpt], [ro])
                P.dma("sp", pT_d[c0:c0 + m, t0:t0 + n], o[:m, :n], do, reads=[ro], writes=[r_pT])
            for ti in range(n // 128):
                pt, rpt, _ = gen.next()
                for k in range(KC):
                    P.op("pe", lambda k=k, pt=pt: nc.tensor.matmul(pt[:, 0:392], lhsT=x[:, k, ti * 128:(ti + 1) * 128], rhs=wv[:, k, :], start=(k == 0), stop=(k == KC - 1)),
                         reads=[rw, rx], writes=[rpt])
                o, ro, do = ost.next()
                evac(o[:, 0:392], pt[:, 0:392], [rpt], [ro])
                P.dma("sp", vtok_d[t0 + ti * 128:t0 + (ti + 1) * 128, :], o[:, 0:392], do, reads=[ro], writes=[r_vtok])
        P.pop_scope()

        dx = P.dsem(f"x1_{l}")
        for (d0, s0, nr) in ((R_NAK, 256, 256), (R_SWK, SW0 + 256, 128), (R_CKV, ML0 + 256, 128), (R_XBC, SS0 + 256, 768), (R_KR, ML0 + 384, 32)):
            for rr0 in range(0, nr, 64):
                n_ = min(64, nr - rr0)
                P.dma("sp", SBc[(d0 + rr0) // 64][0:n_, :], pT_d[s0 + rr0:s0 + rr0 + n_, :], dx, reads=[r_pT], writes=[r_SB])
        for i in range(5):
            P.dma("sp", VSc[i][:, :], vtok_d[512 * i:512 * i + vg_nr[i], :], dx, reads=[r_vtok], writes=[r_SB])
        P.barrier()
        for c in range(8, 20):
            allgather(P, SBc[c], RBc[c], [r_SB], [r_RB1])
        for i in range(5):
            allgather(P, VSc[i], VGc[i], [r_SB], [r_VG])
        if l + 1 < nlayers:
            emit_mod(l + 1)
        for c in list(range(8)) + [20]:
            allgather(P, SBc[c], RBc[c], [r_SB], [r_RB2])

        P.push_scope()
        ds_ = P.dsem(f"ssm{l}")
        cw = P.sb("cw", [128, 3, 6], F32); rcw = P.res(); P.dma("sp", cw[:], scw_d[l], ds_, writes=[rcw])
        par = P.sb("par", [128, 8], F32); rpar = P.res(); P.dma("sp", par[:], spar_d[l], ds_, writes=[rpar])
        dtall = P.sb("dtall", [128, NCH, 8], F32); rdta_ = P.res()
        P.dma("sp", dtall[:, 0:2, :], vgbuf(0, NQ, 256, 384, 392).rearrange("(c p) e -> p c e", p=128), ds_, reads=[r_VG], writes=[rdta_])
        for r in range(4):
            for i in range(4):
                P.dma("sp", dtall[:, 2 + 16 * r + 4 * i:2 + 16 * r + 4 * (i + 1), :], vgbuf(r, 512 * i, 512, 384, 392).rearrange("(c p) e -> p c e", p=128), ds_,
                      reads=[r_VG], writes=[rdta_])
        dt = P.sb("dt", [128, NCH, 2], F32); rdt = P.res()
        dta = P.sb("dta", [128, NCH, 2], F32); rdta = P.res()
        for d in range(2):
            P.op("dve", lambda d=d: nc.vector.tensor_scalar(out=dt[:, :, d], in0=dtall[:, :, d * 4], scalar1=flg[:, 0:1], scalar2=None, op0=ALU.mult),
                 reads=[rdta_, rflg], writes=[rdt])
            for hh in range(1, 4):
                P.op("dve", lambda d=d, hh=hh: nc.vector.scalar_tensor_tensor(out=dt[:, :, d], in0=dtall[:, :, d * 4 + hh], scalar=flg[:, hh:hh + 1], in1=dt[:, :, d],
                                                                            op0=ALU.mult, op1=ALU.add), reads=[rdta_, rflg, rdt], writes=[rdt])
        av = P.sb("av", [128, 2], F32); rav = P.res()
        P.op("act", lambda: nc.scalar.activation(out=av[:], in_=par[:, 2:4], func=AF.Exp), reads=[rpar], writes=[rav])
        P.op("dve", lambda: nc.vector.tensor_scalar(out=av[:], in0=av[:], scalar1=-1.0, scalar2=None, op0=ALU.mult), reads=[rav], writes=[rav])
        for d in range(2):
            P.op("act", lambda d=d: nc.scalar.activation(out=dt[:, :, d], in_=dt[:, :, d], func=AF.Exp, bias=par[:, d:d + 1]), reads=[rdt, rpar], writes=[rdt])
        for d in range(2):
            P.op("act", lambda d=d: nc.scalar.activation(out=dt[:, :, d], in_=dt[:, :, d], func=AF.Ln, bias=1.0), reads=[rdt], writes=[rdt])
        for d in range(2):
            P.op("dve", lambda d=d: nc.vector.tensor_scalar(out=dta[:, :, d], in0=dt[:, :, d], scalar1=av[:, d:d + 1], scalar2=None, op0=ALU.mult),
                 reads=[rdt, rav], writes=[rdta])
        xT = P.sb("xT", [64, LSEQ], F32); rx_ = P.res()
        bT = P.sb("bT", [128, LSEQ], BF16); rb_ = P.res()
        cT = P.sb("cT", [128, LSEQ], BF16); rc_ = P.res()
        yb = P.sb("yb", [64, LSEQ], F32); ry_ = P.res()
        P.push_scope()
        raw = Rot(P, "raw", 6, [128, 2, 512], F32, dma=True)
        cacc = Rot(P, "cacc", 3, [128, 508], F32)
        CB = 508
        for gi, (row0, npart, sel, dst, rdst) in enumerate(((R_XBC, 64, selx, xT, rx_), (R_XBC + 256, 128, selb, bT, rb_), (R_XBC + 512, 128, selb, cT, rc_))):
            for (s0, sl, is_ctx) in ((0, 256, True), (256, 8192, False)):
                for t0 in range(0, sl, CB):
                    n = min(CB, sl - t0)
                    rt_, rr_, dr_ = raw.next()
                    lo = max(t0 - 2, 0); hi = min(t0 + n + 2, sl)
                    if lo > t0 - 2 or hi < t0 + n + 2:
                        P.op("dve", lambda rt_=rt_: nc.vector.memset(rt_[:], 0.0), writes=[rr_])
                    if is_ctx:
                        segs = [(0, NQ + lo, hi - lo, lo - (t0 - 2))]
                    else:
                        segs = [(r, ls, ln, do_ + lo - (t0 - 2)) for (r, ls, ln, do_) in seg_lat(lo, hi - lo)]
                    for (r, ls, ln, do_) in segs:
                        for kc in range(2):
                            for hf in range(2):
                                P.dma("sp", rt_[hf * 64:(hf + 1) * 64, kc, do_:do_ + ln], rbuf(r, row0 + kc * 128 + hf * 64, 64, ls, ls + ln), dr_,
                                      reads=[r_RB1], writes=[rr_])
                    pt, rpt, _ = gen.next()
                    for kc in range(2):
                        P.op("pe", lambda kc=kc, pt=pt: nc.tensor.matmul(pt[:npart, 0:n + 4], lhsT=sel[:, kc, :], rhs=rt_[:, kc, 0:n + 4], start=(kc == 0), stop=(kc == 1)),
                             reads=[rsel, rr_], writes=[rpt])
                    a, ra, _ = cacc.next()
                    P.op("dve", lambda: nc.vector.tensor_scalar(out=a[:npart, :n], in0=pt[:npart, 0:n], scalar1=cw[:npart, gi, 0:1], scalar2=None, op0=ALU.mult),
                         reads=[rpt, rcw], writes=[ra])
                    for k in range(1, 5):
                        P.op("dve", lambda k=k: nc.vector.scalar_tensor_tensor(out=a[:npart, :n], in0=pt[:npart, k:k + n], scalar=cw[:npart, gi, k:k + 1], in1=a[:npart, :n],
                                                                             op0=ALU.mult, op1=ALU.add), reads=[rpt, rcw, ra], writes=[ra])
                    P.op("act", lambda: nc.scalar.activation(out=dst[:npart, s0 + t0:s0 + t0 + n], in_=a[:npart, :n], func=AF.Silu, bias=cw[:npart, gi, 5:6]),
                         reads=[ra, rcw], writes=[rdst])
        P.pop_scope()
        xtokA = P.sb("xtokA", [128, NCH, 64], F32); rxt = P.res()
        btokA = P.sb("btokA", [128, NCH, 128], BF16); rbtk = P.res()
        gt = Rot(P, "gt", 2, [128, 128], F32)
        for c in range(NCH):
            sl = slice(c * 128, (c + 1) * 128)
            px, rpx, _ = gen.next()
            P.op("pe", lambda: nc.tensor.transpose(px[:, 0:64], xT[:, sl], ident[0:64, 0:64]), reads=[rx_, rid], writes=[rpx])
            evac(xtokA[:, c, :], px[:, 0:64], [rpx], [rxt])
            btf, rbtf, _ = gt.next()
            P.op("act", lambda: nc.scalar.copy(out=btf[:], in_=bT[:, sl]), reads=[rb_], writes=[rbtf])
            pb, rpb, _ = gen.next()
            P.op("pe", lambda: nc.tensor.transpose(pb[:, 0:128], btf[:], ident[:]), reads=[rbtf, rid], writes=[rpb])
            evac(btokA[:, c, :], pb[:, 0:128], [rpb], [rbtk])
        ryc = [P.res() for _ in range(NCH)]
        for c in range(NCH):
            sl = slice(c * 128, (c + 1) * 128)
            P.op("pool", lambda: nc.gpsimd.tensor_scalar(out=yb[:, sl], in0=xT[:, sl], scalar1=par[0:64, 4:5], scalar2=None, op0=ALU.mult), reads=[rx_, rpar], writes=[ryc[c]])
        hs = [P.sb("hs", [128, 64], F32) for _ in range(2)]; rhs_ = [P.res(), P.res()]
        hsb = [P.sb("hsb", [128, 64], BF16) for _ in range(2)]; rhsb = [P.res(), P.res()]
        NS = 4
        dtab = Rot(P, "dtab", NS, [128, 128], F32); ccol = Rot(P, "ccol", 2 * NS, [128, 4], F32)
        dd = Rot(P, "dd", NS, [128, 128], F32); ee = Rot(P, "ee", NS, [128, 128], F32)
        gm = Rot(P, "gm", NS, [128, 128], F32); mt = Rot(P, "mt", NS, [128, 128], BF16)
        ecr = Rot(P, "ecr", NS, [128, 128], F32); cs = Rot(P, "cs", NS, [128, 128], BF16)
        xdt = Rot(P, "xdt", NS, [128, 64], BF16); xdd = Rot(P, "xdd", NS, [128, 64], BF16)
        sst = Rot(P, "sst", NS, [128, 64], F32); hsr = Rot(P, "hsr", NS, [128, 64], BF16)
        hcur = [None, None]
        for d in range(2):
            P.op("dve", lambda d=d: nc.vector.memset(hs[d][:], 0.0), writes=[rhs_[d]])
            P.op("dve", lambda d=d: nc.vector.memset(hsb[d][:], 0.0), writes=[rhsb[d]])
        orders = [list(range(NCH)), [1, 0] + list(range(NCH - 1, 1, -1))]

        def p1(d, c):
            sl = slice(c * 128, (c + 1) * 128)
            tot_col = 127 if d == 0 else 0
            da, rda, _ = dtab.next()
            P.op("pool", lambda: nc.gpsimd.tensor_scalar(out=da[:], in0=onesf[:], scalar1=dta[:, c, d:d + 1], scalar2=None, op0=ALU.mult), reads=[ronesf, rdta], writes=[rda])
            pcr, rpcr, _ = gen.next()
            P.op("pe", lambda: nc.tensor.matmul(pcr[:, 0:128], lhsT=da[:], rhs=U[:, d, :], start=True, stop=True), reads=[rda, rU], writes=[rpcr])
            P.op("pe", lambda: nc.tensor.matmul(pcr[:, 128:130], lhsT=U[:, d, :], rhs=dta[:, c, :], start=True, stop=True), reads=[rU, rdta], writes=[rpcr])
            cc, rcc, _ = ccol.next()
            P.op("dve", lambda: nc.vector.tensor_copy(out=cc[:, 0:1], in_=pcr[:, 128 + d:129 + d]), reads=[rpcr], writes=[rcc])
            P.op("dve", lambda: nc.vector.tensor_copy(out=cc[:, 1:2], in_=pcr[:, tot_col:tot_col + 1]), reads=[rpcr], writes=[rcc])
            P.op("act", lambda: nc.scalar.activation(out=cc[:, 3:4], in_=cc[:, 1:2], func=AF.Exp), reads=[rcc], writes=[rcc])
            dt_, rdd, _ = dd.next()
            P.op("dve", lambda: nc.vector.tensor_scalar(out=dt_[:], in0=pcr[:, 0:128], scalar1=cc[:, 0:1], scalar2=0.0, op0=ALU.subtract, op1=ALU.min), reads=[rpcr, rcc], writes=[rdd])
            e, re_, _ = ee.next()
            P.op("act", lambda: nc.scalar.activation(out=e[:], in_=dt_[:], func=AF.Exp), reads=[rdd], writes=[re_])
            er, rer, _ = ecr.next()
            P.op("act", lambda: nc.scalar.activation(out=er[:], in_=pcr[:, 0:128], func=AF.Exp), reads=[rpcr], writes=[rer])
            pg, rpg, _ = gen.next()
            P.op("pe", lambda: nc.tensor.matmul(pg[:, 0:128], lhsT=bT[:, sl], rhs=cT[:, sl], start=True, stop=True), reads=[rb_, rc_], writes=[rpg])
            g, rg_, _ = gm.next()
            P.op("dve", lambda: nc.vector.tensor_tensor(out=g[:], in0=pg[:, 0:128], in1=U[:, d, :], op=ALU.mult), reads=[rpg, rU], writes=[rg_])
            m, rm, _ = mt.next()
            P.op("pool", lambda: nc.gpsimd.tensor_tensor(out=m[:], in0=g[:], in1=e[:], op=ALU.mult), reads=[rg_, re_], writes=[rm])
            csb, rcs, _ = cs.next()
            P.op("pool", lambda: nc.gpsimd.tensor_tensor(out=csb[:], in0=cT[:, sl], in1=er[:], op=ALU.mult), reads=[rc_, rer], writes=[rcs])
            xd, rxd, _ = xdt.next()
            P.op("dve", lambda: nc.vector.tensor_scalar(out=xd[:], in0=xtokA[:, c, :], scalar1=dt[:, c, d:d + 1], scalar2=None, op0=ALU.mult), reads=[rxt, rdt], writes=[rxd])
            de, rde, _ = ccol.next()
            P.op("act", lambda: nc.scalar.activation(out=de[:, 0:1], in_=cc[:, 0:1], func=AF.Exp, scale=-1.0, bias=cc[:, 1:2]), reads=[rcc], writes=[rde])
            P.op("dve", lambda: nc.vector.tensor_tensor(out=de[:, 1:2], in0=de[:, 0:1], in1=dt[:, c, d:d + 1], op=ALU.mult), reads=[rde, rdt], writes=[rde])
            xe, rxe, _ = xdd.next()
            P.op("dve", lambda: nc.vector.tensor_scalar(out=xe[:], in0=xtokA[:, c, :], scalar1=de[:, 1:2], scalar2=None, op0=ALU.mult), reads=[rxt, rde], writes=[rxe])
            pst, rpst, _ = gen.next()
            P.op("pe", lambda: nc.tensor.matmul(pst[:, 0:64], lhsT=btokA[:, c, :], rhs=xe[:], start=True, stop=True), reads=[rbtk, rxe], writes=[rpst])
            st_, rst_, _ = sst.next()
            P.op("act", lambda: nc.scalar.copy(out=st_[:], in_=pst[:, 0:64]), reads=[rpst], writes=[rst_])
            return (cc, rcc, m, rm, csb, rcs, xd, rxd, st_, rst_)

        def p2(d, c, hnd):
            cc, rcc, m, rm, csb, rcs, xd, rxd, st_, rst_ = hnd
            sl = slice(c * 128, (c + 1) * 128)
            hb, rhb = hcur[d] if hcur[d] is not None else (hsb[d], rhsb[d])
            py, rpy, _ = gen.next()
            P.op("pe", lambda: nc.tensor.matmul(py[0:64, 0:128], lhsT=xd[:], rhs=m[:], start=True, stop=False), reads=[rxd, rm], writes=[rpy])
            P.op("pe", lambda: nc.tensor.matmul(py[0:64, 0:128], lhsT=hb[:], rhs=csb[:], start=False, stop=True), reads=[rhb, rcs], writes=[rpy])
            P.op("dve", lambda: nc.vector.scalar_tensor_tensor(out=hs[d][:], in0=hs[d][:], scalar=cc[:, 3:4], in1=st_[:], op0=ALU.mult, op1=ALU.add),
                 reads=[rhs_[d], rcc, rst_], writes=[rhs_[d]])
            hn, rhn, _ = hsr.next()
            P.op("dve", lambda: nc.vector.tensor_copy(out=hn[:], in_=hs[d][:]), reads=[rhs_[d]], writes=[rhn])
            hcur[d] = (hn, rhn)
            P.op("dve", lambda: nc.vector.tensor_tensor(out=yb[:, sl], in0=yb[:, sl], in1=py[0:64, 0:128], op=ALU.add), reads=[ryc[c], rpy], writes=[ryc[c]])

        pend = None
        for s_ in range(NCH + 1):
            cur = None
            if s_ < NCH:
                cur = [(d, orders[d][s_], p1(d, orders[d][s_])) for d in range(2)]
            if pend is not None:
                for (d, c, hnd) in pend:
                    p2(d, c, hnd)
            pend = cur
        ry_all = ryc
        for i in range(3):
            P.dma("sp", YSc[i][:, :], yb[:, i * YCH:(i + 1) * YCH], ds_, reads=ry_all, writes=[r_YS])
        P.pop_scope()
        for i in range(3):
            allgather(P, YSc[i], YGc[i], [r_YS], [r_YG])

        P.push_scope()
        oTs = P.sb("oTs", [128, 6, NTOK], BF16)
        P.push_scope()
        ost = Rot(P, "tost", 2, [128, 4, 256], F32)

        def emit_oT(o, ro, ncol, tile0, chunk0, mla):
            for c in range(ncol):
                for half in range(2):
                    pt, rpt, _ = A.misc.next()
                    P.op("pe", lambda c=c, half=half, pt=pt: nc.tensor.transpose(pt[:, 0:128], o[:, c, half * 128:(half + 1) * 128], ident[:]), reads=[ro, rid], writes=[rpt])
                    evac(oTs[:, chunk0 + half, (tile0 + c) * 128:(tile0 + c + 1) * 128], pt[:, 0:128], [rpt], [roT])

        P.push_scope()
        EXT = 2816
        dn = P.dsem(f"na{l}")
        q = P.sb("naq", [64, 4, NQ], BF16); rq = P.res()
        k = P.sb("nak", [64, 4, EXT], BF16); rk = P.res()
        v = P.sb("nav", [128, EXT // 128, 4, 65], BF16); rv = P.res()
        qc = P.sb("naqc", [64, 4, NCX], BF16); kc_ = P.sb("nakc", [64, 4, NCX], BF16); vc = P.sb("navc", [128, 2, 4, 65], BF16); rc = P.res()
        P.op("dve", lambda: nc.vector.memset(v[:], 1.0), writes=[rv])
        P.op("dve", lambda: nc.vector.memset(vc[:], 1.0), writes=[rc])
        hk = Rot(P, "hk", 2, [64, 4, 384], F32, dma=True)
        hacc = Rot(P, "hacc", 2, [64, 384], F32)
        hv = Rot(P, "hv", 2, [128, 4, 768], F32, dma=True)
        hvacc = Rot(P, "hvacc", 2, [128, 768], F32)

        def halo_k(dst, row0, npart, width, side, hkr, haccr, rowperm=None):
            t_, rt_, dt__ = hkr.next()
            c0 = NQ - width if side == 0 else 0
            for r in range(4):
                for (d0_, s0_, nr_) in (rowperm or ((0, 0, npart),)):
                    P.dma("sp", t_[d0_:d0_ + nr_, r, :width], rbuf(r, row0 + s0_, nr_, c0, c0 + width), dt__, reads=[r_RB2], writes=[rt_])
            a_, ra_, _ = haccr.next()
            f0 = 4 if side == 0 else 8
            P.op("dve", lambda: nc.vector.tensor_scalar(out=a_[:npart, :width], in0=t_[:npart, 0, :width], scalar1=flg[:npart, f0:f0 + 1], scalar2=None, op0=ALU.mult),
                 reads=[rt_, rflg], writes=[ra_])
            for r in range(1, 4):
                P.op("dve", lambda r=r: nc.vector.scalar_tensor_tensor(out=a_[:npart, :width], in0=t_[:npart, r, :width], scalar=flg[:npart, f0 + r:f0 + r + 1], in1=a_[:npart, :width],
                                                                     op0=ALU.mult, op1=ALU.add), reads=[rt_, rflg, ra_], writes=[ra_])
            return a_, ra_

        def halo_v(col0, ncolv, ntile, side, hvr, hvaccr):
            t_, rt_, dt__ = hvr.next()
            r0 = NQ - ntile * 128 if side == 0 else 0
            for r in range(4):
                P.dma("sp", t_[:, r, 0:ntile * ncolv].rearrange("p (t c) -> p t c", c=ncolv),
                      vgbuf(r, r0, ntile * 128, col0, col0 + ncolv).rearrange("(t p) c -> p t c", p=128), dt__, reads=[r_VG], writes=[rt_])
            a_, ra_, _ = hvaccr.next()
            f0 = 4 if side == 0 else 8
            w_ = ntile * ncolv
            P.op("dve", lambda: nc.vector.tensor_scalar(out=a_[:, :w_], in0=t_[:, 0, :w_], scalar1=flg[:, f0:f0 + 1], scalar2=None, op0=ALU.mult), reads=[rt_, rflg], writes=[ra_])
            for r in range(1, 4):
                P.op("dve", lambda r=r: nc.vector.scalar_tensor_tensor(out=a_[:, :w_], in0=t_[:, r, :w_], scalar=flg[:, f0 + r:f0 + r + 1], in1=a_[:, :w_], op0=ALU.mult, op1=ALU.add),
                     reads=[rt_, rflg, ra_], writes=[ra_])
            return a_, ra_

        for h in range(4):
            P.dma("pool", q[:, h, :], pT_d[h * 64:(h + 1) * 64, 0:NQ], dn, reads=[r_pT], writes=[rq])
            P.dma("pool", k[:, h, 384:384 + NQ], pT_d[256 + h * 64:256 + (h + 1) * 64, 0:NQ], dn, reads=[r_pT], writes=[rk])
            P.dma("pool", qc[:, h, :], pT_d[h * 64:(h + 1) * 64, NQ:NTOK], dn, reads=[r_pT], writes=[rc])
            P.dma("pool", kc_[:, h, :], pT_d[256 + h * 64:256 + (h + 1) * 64, NQ:NTOK], dn, reads=[r_pT], writes=[rc])
            for side in range(2):
                a_, ra_ = halo_k(None, R_NAK + h * 64, 64, 384, side, hk, hacc)
                off = 0 if side == 0 else 384 + NQ
                P.op("act", lambda a_=a_, off=off, h=h: nc.scalar.copy(out=k[:, h, off:off + 384], in_=a_[:64, :384]), reads=[ra_], writes=[rk])
        vsrc = vtok_d[:, 0:256].rearrange("(t p) (h d) -> p t h d", p=128, h=4)
        for t in range(16):
            P.dma("pool", v[:, 3 + t, :, 0:64], vsrc[:, t, :, :], dn, reads=[r_vtok], writes=[rv])
        for t in range(2):
            P.dma("pool", vc[:, t, :, 0:64], vsrc[:, 16 + t, :, :], dn, reads=[r_vtok], writes=[rc])
        for side in range(2):
            a_, ra_ = halo_v(0, 256, 3, side, hv, hvacc)
            t0_ = 0 if side == 0 else 19
            for t in range(3):
                P.op("act", lambda a_=a_, t=t, t0_=t0_: nc.scalar.copy(out=v[:, t0_ + t, :, 0:64], in_=a_[:, t * 256:(t + 1) * 256].rearrange("p (h d) -> p h d", h=4)),
                     reads=[ra_], writes=[rv])
        bias = Rot(P, "nab", 2, [128, 7, 512], F32, dma=True)
        scale = 64 ** -0.5
        for t in range(16 if final else 18):
            items = []
            if t < 16:
                pat = 0 if t == 0 else 1 if t == 1 else 3 if t == 14 else 4 if t == 15 else 2
                bt_, rb, db = bias.next()
                P.dma("sp", bt_[:], nab_d[l][pat], db, writes=[rb])
                qa = [[q[:, h, t * 128:(t + 1) * 128]] for h in range(4)]
                for kt in (range(1, 6) if pat == 2 else range(7)):
                    et = t + kt
                    items.append(dict(k=[[k[:, h, et * 128:(et + 1) * 128]] for h in range(4)], v=[v[:, et, h, :] for h in range(4)], bias=bt_[:, kt, :], bres=[rb], res=[rk, rv]))
                qres = [rq]
            else:
                tc = t - 16
                qa = [[qc[:, h, tc * 128:(tc + 1) * 128]] for h in range(4)]
                qres = [rc]
            for kt in range(2):
                items.append(dict(k=[[kc_[:, h, kt * 128:(kt + 1) * 128]] for h in range(4)], v=[vc[:, kt, h, :] for h in range(4)], bias=None, res=[rc]))
            o, ro, _ = ost.next()
            attend(A, 4, False, qa, qres, items, scale, [o[:, 0, h * 64:(h + 1) * 64] for h in range(4)], ro)
            emit_oT(o, ro, 1, t, 0, False)
        P.pop_scope()

        P.push_scope()
        EXT = 2304
        dn = P.dsem(f"sw{l}")
        q = P.sb("swq", [64, 4, NQ], BF16); rq = P.res()
        k = P.sb("swk", [64, 2, EXT], BF16); rk = P.res()
        v = P.sb("swv", [128, EXT // 128, 2, 65], BF16); rv = P.res()
        qc = P.sb("swqc", [64, 4, NCX], BF16); kc_ = P.sb("swkc", [64, 2, NCX], BF16); vc = P.sb("swvc", [128, 2, 2, 65], BF16); rc = P.res()
        P.op("dve", lambda: nc.vector.memset(v[:], 1.0), writes=[rv])
        P.op("dve", lambda: nc.vector.memset(vc[:], 1.0), writes=[rc])
        cs_ = P.sb("swcs", [64, 2, EXT], F32); rcs_ = P.res()
        bs = P.sb("swb", [128, 4, 128], F32); rbs = P.res()
        sk = P.sb("swsk", [128, 4], F32); rsk = P.res()
        P.dma("sp", cs_[:, 0, :], swcs_d[0], dn, writes=[rcs_]); P.dma("sp", cs_[:, 1, :], swcs_d[1], dn, writes=[rcs_])
        P.dma("sp", bs[:], swb_d, dn, writes=[rbs]); P.dma("sp", sk[:], swk_d[l], dn, writes=[rsk])
        P.op("act", lambda: nc.scalar.activation(out=sk[:], in_=sk[:], func=AF.Exp), reads=[rsk], writes=[rsk])
        stg = Rot(P, "swstg", 4, [64, EXT], F32, dma=True)
        hk2 = Rot(P, "hk2", 2, [64, 4, 128], F32, dma=True)
        hacc2 = Rot(P, "hacc2", 4, [64, 128], F32)
        hv2 = Rot(P, "hv2", 2, [128, 4, 128], F32, dma=True)
        hvacc2 = Rot(P, "hvacc2", 2, [128, 128], F32)
        perm = ((0, 16), (16, 0), (32, 48), (48, 32))

        def rope_rows(dst, res, row0, col0, n, tab0, extra_writer=None):
            a, ra, da = stg.next(); b, rb_, db = stg.next()
            P.dma("sp", a[:, :n], pT_d[row0:row0 + 64, col0:col0 + n], da, reads=[r_pT], writes=[ra])
            for (d0, s0) in perm:
                P.dma("sp", b[d0:d0 + 16, :n], pT_d[row0 + s0:row0 + s0 + 16, col0:col0 + n], db, reads=[r_pT], writes=[rb_])
            P.op("dve", lambda: nc.vector.tensor_tensor(out=a[:, :n], in0=a[:, :n], in1=cs_[:, 0, tab0:tab0 + n], op=ALU.mult), reads=[ra, rcs_], writes=[ra])
            P.op("pool", lambda: nc.gpsimd.tensor_tensor(out=b[:, :n], in0=b[:, :n], in1=cs_[:, 1, tab0:tab0 + n], op=ALU.mult), reads=[rb_, rcs_], writes=[rb_])
            P.op("dve", lambda: nc.vector.tensor_tensor(out=dst, in0=a[:, :n], in1=b[:, :n], op=ALU.add), reads=[ra, rb_], writes=[res])

        for h in range(4):
            rope_rows(q[:, h, :], rq, SW0 + h * 64, 0, NQ, 128)
            P.dma("pool", qc[:, h, :], pT_d[SW0 + h * 64:SW0 + (h + 1) * 64, NQ:NTOK], dn, reads=[r_pT], writes=[rc])
        for g in range(2):
            rope_rows(k[:, g, 128:128 + NQ], rk, SW0 + 256 + g * 64, 0, NQ, 128)
            P.dma("pool", kc_[:, g, :], pT_d[SW0 + 256 + g * 64:SW0 + 256 + (g + 1) * 64, NQ:NTOK], dn, reads=[r_pT], writes=[rc])
            for side in range(2):
                a_, ra_ = halo_k(None, R_SWK + g * 64, 64, 128, side, hk2, hacc2)
                b_, rb2 = halo_k(None, R_SWK + g * 64, 64, 128, side, hk2, hacc2, rowperm=[(d0, s0, 16) for (d0, s0) in perm])
                tab0 = 0 if side == 0 else 128 + NQ
                P.op("dve", lambda a_=a_, tab0=tab0: nc.vector.tensor_tensor(out=a_[:64, :128], in0=a_[:64, :128], in1=cs_[:, 0, tab0:tab0 + 128], op=ALU.mult), reads=[ra_, rcs_], writes=[ra_])
                P.op("dve", lambda b_=b_, tab0=tab0: nc.vector.tensor_tensor(out=b_[:64, :128], in0=b_[:64, :128], in1=cs_[:, 1, tab0:tab0 + 128], op=ALU.mult), reads=[rb2, rcs_], writes=[rb2])
                P.op("dve", lambda a_=a_, b_=b_, g=g, tab0=tab0: nc.vector.tensor_tensor(out=k[:, g, tab0:tab0 + 128], in0=a_[:64, :128], in1=b_[:64, :128], op=ALU.add),
                     reads=[ra_, rb2], writes=[rk])
        vsrc = vtok_d[:, 256:384].rearrange("(t p) (h d) -> p t h d", p=128, h=2)
        for t in range(16):
            P.dma("pool", v[:, 1 + t, :, 0:64], vsrc[:, t, :, :], dn, reads=[r_vtok], writes=[rv])
        for t in range(2):
            P.dma("pool", vc[:, t, :, 0:64], vsrc[:, 16 + t, :, :], dn, reads=[r_vtok], writes=[rc])
        for side in range(2):
            a_, ra_ = halo_v(256, 128, 1, side, hv2, hvacc2)
            t0_ = 0 if side == 0 else 17
            P.op("act", lambda a_=a_, t0_=t0_: nc.scalar.copy(out=v[:, t0_, :, 0:64], in_=a_[:, 0:128].rearrange("p (h d) -> p h d", h=2)), reads=[ra_], writes=[rv])
        bp = P.sb("swbp", [128, 4, 4, 128], F32); rbp = P.res()
        for kind in range(4):
            for h in range(4):
                P.op("dve", lambda kind=kind, h=h: nc.vector.tensor_copy(out=bp[:, kind, h, :], in_=bs[:, kind, :]), reads=[rbs], writes=[rbp])
        for t in range(16 if final else 18):
            items = []
            if t < 16:
                qa = [[q[:, h, t * 128:(t + 1) * 128]] for h in range(4)]
                qres = [rq]
                for kt in range(3):
                    et = t + kt
                    if kt == 0:
                        b = bp[:, 0 if t == 0 else 1, :, :].rearrange('p h q -> p (h q)')
                    elif kt == 2:
                        b = bp[:, 3 if t == 15 else 2, :, :].rearrange('p h q -> p (h q)')
                    else:
                        b = None
                    items.append(dict(k=[[k[:, h // 2, et * 128:(et + 1) * 128]] for h in range(4)], v=[v[:, et, h // 2, :] for h in range(4)], bias=b, bres=[rbp], res=[rk, rv]))
            else:
                tc = t - 16
                qa = [[qc[:, h, tc * 128:(tc + 1) * 128]] for h in range(4)]
                qres = [rc]
            for kt in range(2):
                items.append(dict(k=[[kc_[:, h // 2, kt * 128:(kt + 1) * 128]] for h in range(4)], v=[vc[:, kt, h // 2, :] for h in range(4)], bias=None, res=[rc]))
            o, ro, _ = ost.next()
            attend(A, 4, False, qa, qres, items, scale, [o[:, 0, h * 64:(h + 1) * 64] for h in range(4)], ro, sinkexp=sk[:, 0:4], sink_res=rsk)
            emit_oT(o, ro, 1, t, 2, False)
        P.pop_scope()

        P.push_scope()
        NK = LSEQ
        dsm = P.dsem(f"ml{l}")
        gkv = P.sb("gkv", [128, 1], F32); gq = P.sb("gq", [128, 2], F32); rg = P.res()
        P.dma("sp", gkv[:], mgkv_d[l], dsm, writes=[rg]); P.dma("sp", gq[:], mgq_d[l], dsm, writes=[rg])
        wkn = P.sb("wkn", [128, 256], BF16); wkv = P.sb("wkv", [128, 256], BF16)
        wqn = P.sb("wqn", [128, 2, 256], BF16); wqr = P.sb("wqr", [128, 2, 128], BF16); wqrs = P.sb("wqrs", [128, 2, 128], BF16)
        rw = P.res(); dw = P.dsem(f"mw{l}")
        P.dma("pool", wkn[:], mwkn_d[l], dw, writes=[rw]); P.dma("pool", wkv[:], mwkv_d[l], dw, writes=[rw])
        for c in range(2):
            P.dma("pool", wqn[:, c, :], mwqn_d[l][c * 128:(c + 1) * 128, :], dw, writes=[rw])
            P.dma("pool", wqr[:, c, :], mwqr_d[l][c * 128:(c + 1) * 128, :], dw, writes=[rw])
            P.dma("pool", wqrs[:, c, :], mwqrs_d[l][c * 128:(c + 1) * 128, :], dw, writes=[rw])
        K96 = P.sb("K96", [96, 4, NK], BF16); rkn = P.res(); rkr = P.res()
        vm = P.sb("vm", [128, NK // 128, 4, 65], BF16); rvm = P.res()
        Q96 = P.sb("Q96", [96, 4, NTOK], BF16); rqn = P.res(); rqr = P.res()
        rrow = P.sb("rrow", [65, 512], F32); rrr_ = P.res()
        bcs = P.sb("bcs", [64, 512], F32); rbcs = P.res()
        P.op("dve", lambda: nc.vector.memset(vm[:], 1.0), writes=[rvm])
        xin = Rot(P, "mx", 2, [128, 2, 512], F32, dma=True)
        sq = Rot(P, "msq", 1, [128, 2, 512], BF16)
        rms = Rot(P, "mrms", 1, [128, 512], F32)
        xn = Rot(P, "mxn", 2, [128, 2, 512], BF16)
        tab = Rot(P, "mtab", 2, [32, 2, 512], F32, dma=True)
        rr4 = Rot(P, "mrr", 4, [32, 512], F32, dma=True)
        perm32 = ((0, 8), (8, 0), (16, 24), (24, 16))

        def key_src(row0, nrows, t0, n):
            out = []
            if t0 < 256:
                ln = min(n, 256 - t0)
                out.append((rbuf(0, row0, nrows, NQ + t0, NQ + t0 + ln), 0))
                if ln < n:
                    for (r, ls, l2, do_) in seg_lat(0, n - ln):
                        out.append((rbuf(r, row0, nrows, ls, ls + l2), ln + do_))
            else:
                for (r, ls, l2, do_) in seg_lat(t0 - 256, n):
                    out.append((rbuf(r, row0, nrows, ls, ls + l2), do_))
            return out

        def norm_blk(loads, kc, n, g):
            x, rx, dx = xin.next()
            for (c, p0, ap, off) in loads:
                P.dma("sp", x[p0:p0 + ap.shape[0], c, off:off + ap.shape[1]], ap, dx, reads=[r_RB2, r_pT], writes=[rx])
            s, rs, _ = sq.next(); r, rr, _ = rms.next()
            rstd_of(x, rx, kc, n, r, rr, s, rs, kc * 128)
            y, ry, _ = xn.next()
            for c in range(kc):
                P.op("dve", lambda c=c: nc.vector.scalar_tensor_tensor(out=y[:, c, :n], in0=x[:, c, :n], scalar=g[:, c:c + 1], in1=r[:, :n], op0=ALU.mult, op1=ALU.mult),
                     reads=[rx, rr, rg], writes=[ry])
            return y, ry

        for t0 in range(0, NK, 512):
            n = min(512, NK - t0)
            y, ry = norm_blk([(0, hf * 64, ap, off) for hf in range(2) for (ap, off) in key_src(R_CKV + hf * 64, 64, t0, n)], 1, n, gkv)
            for pr in range(2):
                pt, rpt, _ = A.misc.next()
                P.op("pe", lambda pr=pr, pt=pt: nc.tensor.matmul(pt[:, :n], lhsT=wkn[:, pr * 128:(pr + 1) * 128], rhs=y[:, 0, :n], start=True, stop=True), reads=[rw, ry], writes=[rpt])
                evac(K96[0:64, 2 * pr, t0:t0 + n], pt[0:64, :n], [rpt], [rkn])
                evac(K96[0:64, 2 * pr + 1, t0:t0 + n], pt[64:128, :n], [rpt], [rkn])
            for tt_ in range(n // 128):
                kt = t0 // 128 + tt_
                pt, rpt, _ = A.misc.next()
                P.op("pe", lambda tt_=tt_, pt=pt: nc.tensor.matmul(pt[:, 0:256], lhsT=y[:, 0, tt_ * 128:(tt_ + 1) * 128], rhs=wkv[:], start=True, stop=True), reads=[rw, ry], writes=[rpt])
                evac(vm[:, kt, :, 0:64], pt[:, 0:256].rearrange("p (h d) -> p h d", h=4), [rpt], [rvm])
            tb, rtb, dtb = tab.next()
            P.dma("sp", tb[:, 0, :n], mkcs_d[0][:, t0:t0 + n], dtb, writes=[rtb]); P.dma("sp", tb[:, 1, :n], mkcs_d[1][:, t0:t0 + n], dtb, writes=[rtb])
            a, ra, da = rr4.next(); b, rb_, db = rr4.next()
            for (ap, off) in key_src(R_KR, 32, t0, n):
                P.dma("sp", a[:, off:off + ap.shape[1]], ap, da, reads=[r_RB2], writes=[ra])
            for (d0, s0) in perm32:
                for (ap, off) in key_src(R_KR + s0, 8, t0, n):
                    P.dma("sp", b[d0:d0 + 8, off:off + ap.shape[1]], ap, db, reads=[r_RB2], writes=[rb_])
            P.op("dve", lambda: nc.vector.tensor_tensor(out=a[:, :n], in0=a[:, :n], in1=tb[:, 0, :n], op=ALU.mult), reads=[ra, rtb], writes=[ra])
            P.op("pool", lambda: nc.gpsimd.tensor_tensor(out=b[:, :n], in0=b[:, :n], in1=tb[:, 1, :n], op=ALU.mult), reads=[rb_, rtb], writes=[rb_])
            for h4 in range(4):
                P.op("dve" if h4 % 2 == 0 else "pool", lambda h4=h4: (nc.vector if h4 % 2 == 0 else nc.gpsimd).tensor_tensor(out=K96[64:96, h4, t0:t0 + n], in0=a[:, :n], in1=b[:, :n], op=ALU.add),
                     reads=[ra, rb_], writes=[rkr])
        for t0 in range(0, NTOK, 512):
            n = min(512, NTOK - t0)
            y, ry = norm_blk([(c, 0, pT_d[ML0 + c * 128:ML0 + (c + 1) * 128, t0:t0 + n], 0) for c in range(2)], 2, n, gq)
            for pr in range(2):
                pt, rpt, _ = A.misc.next()
                for c in range(2):
                    P.op("pe", lambda pr=pr, pt=pt, c=c: nc.tensor.matmul(pt[:, :n], lhsT=wqn[:, c, pr * 128:(pr + 1) * 128], rhs=y[:, c, :n], start=(c == 0), stop=(c == 1)),
                         reads=[rw, ry], writes=[rpt])
                evac(Q96[0:64, 2 * pr, t0:t0 + n], pt[0:64, :n], [rpt], [rqn])
                evac(Q96[0:64, 2 * pr + 1, t0:t0 + n], pt[64:128, :n], [rpt], [rqn])
            tb, rtb, dtb = tab.next()
            P.dma("sp", tb[:, 0, :n], mqcs_d[0][:, t0:t0 + n], dtb, writes=[rtb]); P.dma("sp", tb[:, 1, :n], mqcs_d[1][:, t0:t0 + n], dtb, writes=[rtb])
            for h in range(4):
                pa, rpa, _ = A.misc.next()
                for c in range(2):
                    P.op("pe", lambda pa=pa, c=c, h=h: nc.tensor.matmul(pa[0:32, :n], lhsT=wqr[:, c, h * 32:(h + 1) * 32], rhs=y[:, c, :n], start=(c == 0), stop=(c == 1)),
                         reads=[rw, ry], writes=[rpa])
                a, ra, _ = rr4.next()
                P.op("dve", lambda a=a, pa=pa: nc.vector.tensor_tensor(out=a[:, :n], in0=pa[0:32, :n], in1=tb[:, 0, :n], op=ALU.mult), reads=[rpa, rtb], writes=[ra])
                pb, rpb, _ = A.misc.next()
                for c in range(2):
                    P.op("pe", lambda pb=pb, c=c, h=h: nc.tensor.matmul(pb[0:32, :n], lhsT=wqrs[:, c, h * 32:(h + 1) * 32], rhs=y[:, c, :n], start=(c == 0), stop=(c == 1)),
                         reads=[rw, ry], writes=[rpb])
                b, rb_, _ = rr4.next()
                P.op("dve", lambda b=b, pb=pb: nc.vector.tensor_tensor(out=b[:, :n], in0=pb[0:32, :n], in1=tb[:, 1, :n], op=ALU.mult), reads=[rpb, rtb], writes=[rb_])
                P.op("pool", lambda a=a, b=b, h=h: nc.gpsimd.tensor_tensor(out=Q96[64:96, h, t0:t0 + n], in0=a[:, :n], in1=b[:, :n], op=ALU.add), reads=[ra, rb_], writes=[rqr])
        scale = 96 ** -0.5
        groups = [(g * 512, 4, NK // 128) for g in range(4)] + ([] if final else [(2048, 2, 2)])
        for (q0, ncol, nkt) in groups:
            W_ = ncol * 128
            for h in range(4):
                acc, racc, _ = A.accs.next()
                def score(kt):
                    pan, rpan, _ = A.panels.next()
                    P.op("pe", lambda: nc.tensor.matmul(pan[:, 0:W_], lhsT=K96[0:96, h, kt * 128:(kt + 1) * 128], rhs=Q96[0:96, h, q0:q0 + W_], start=True, stop=True),
                         reads=[rkn, rkr, rqn, rqr], writes=[rpan])
                    pt_, rpt_, _ = A.pT.next()
                    P.op("act", lambda: nc.scalar.activation(out=pt_[:, 0:W_], in_=pan[:, 0:W_], func=AF.Exp, scale=scale), reads=[rpan], writes=[rpt_])
                    return pt_, rpt_
                nxt = score(0)
                for kt in range(nkt):
                    pt_, rpt_ = nxt
                    if kt + 1 < nkt:
                        nxt = score(kt + 1)
                    P.op("pe", lambda: nc.tensor.matmul(acc[0:65, 0:W_], lhsT=vm[:, kt, h, :], rhs=pt_[:, 0:W_], start=(kt == 0), stop=(kt == nkt - 1)),
                         reads=[rvm, rpt_], writes=[racc])
                P.op("dve", lambda: nc.vector.reciprocal(out=rrow[64:65, 0:W_], in_=acc[64:65, 0:W_]), reads=[racc], writes=[rrr_])
                pb_, rpb_, _ = A.misc.next()
                P.op("pe", lambda: nc.tensor.matmul(pb_[0:64, 0:W_], lhsT=onesf[64:65, 0:64], rhs=rrow[64:65, 0:W_], start=True, stop=True), reads=[ronesf, rrr_], writes=[rpb_])
                P.op("act", lambda: nc.scalar.copy(out=bcs[:, 0:W_], in_=pb_[0:64, 0:W_]), reads=[rpb_], writes=[rbcs])
                hp_ = (h % 2) * 64
                P.op("dve", lambda: nc.vector.tensor_tensor(out=oTs[hp_:hp_ + 64, 4 + h // 2, q0:q0 + W_], in0=acc[0:64, 0:W_], in1=bcs[:, 0:W_], op=ALU.mult),
                     reads=[racc, rbcs], writes=[roT])
        P.pop_scope()
        P.pop_scope()

        NB = 256
        fblocks = [(t0, NB, 0 if t0 < NQ else 1) for t0 in range(0, NTOK, NB)]
        P.push_scope()
        wo = P.sb("wo", [128, KC, D], BF16); rwo = P.res(); dwo = P.dsem(f"wo{l}")
        for kk in range(KC):
            P.dma("pool", wo[:, kk, :], wo_d[l][kk * 128:(kk + 1) * 128, :], dwo, writes=[rwo])
        hin = Rot(P, "fhin", 3, [128, KC, NB], F32, dma=True)
        ycand = Rot(P, "ycand", 3, [128, 2, 4, NB], F32, dma=True)
        yin = Rot(P, "yin", 3, [128, 2, NB], F32)
        zin = Rot(P, "zin", 3, [128, 2, NB], F32, dma=True)
        sq = Rot(P, "fsq", 2, [128, 2, NB], BF16)
        rms = Rot(P, "frms", 2, [128, NB], F32)
        od = Rot(P, "fod", 2, [128, 2, NB], BF16)
        for (t0, n, j) in fblocks:
            if final and j == 1:
                continue
            h, rh, dh = hin.next()
            P.dma("sp", h[:, :, :n], hT_d.rearrange("(k p) t -> p k t", p=128)[:, :, t0:t0 + n], dh, reads=[r_hin], writes=[rh])
            y, ry, _ = yin.next()
            if j == 0:
                yc, ryc, dyc = ycand.next()
                for kk in range(2):
                    for r in range(4):
                        col = 256 + r * NQ + t0
                        P.dma("sp", yc[:, kk, r, :n], YGc[col // YCH][kk * 128:(kk + 1) * 128, col % YCH:col % YCH + n], dyc, reads=[r_YG], writes=[ryc])
                for kk in range(2):
                    P.op("dve", lambda kk=kk: nc.vector.tensor_scalar(out=y[:, kk, :n], in0=yc[:, kk, 0, :n], scalar1=flg[:, 0:1], scalar2=None, op0=ALU.mult),
                         reads=[ryc, rflg], writes=[ry])
                    for r in range(1, 4):
                        P.op("dve", lambda kk=kk, r=r: nc.vector.scalar_tensor_tensor(out=y[:, kk, :n], in0=yc[:, kk, r, :n], scalar=flg[:, r:r + 1], in1=y[:, kk, :n],
                                                                                    op0=ALU.mult, op1=ALU.add), reads=[ryc, rflg, ry], writes=[ry])
            else:
                yc, ryc, dyc = ycand.next()
                for kk in range(2):
                    P.dma("sp", yc[:, kk, 0, :n], YGc[0][kk * 128:(kk + 1) * 128, 0:256], dyc, reads=[r_YG], writes=[ryc])
                P.op("dve", lambda: nc.vector.tensor_copy(out=y[:, :, :n], in_=yc[:, :, 0, :n]), reads=[ryc], writes=[ry])
            z, rz, dz = zin.next()
            P.dma("sp", z[:, :, :n], pT_d[SS0:SS0 + 256, :].rearrange("(k p) t -> p k t", p=128)[:, :, t0:t0 + n], dz, reads=[r_pT], writes=[rz])
            P.op("act", lambda: nc.scalar.activation(out=z[:, :, :n], in_=z[:, :, :n], func=AF.Silu), reads=[rz], writes=[rz])
            P.op("dve", lambda: nc.vector.tensor_tensor(out=y[:, :, :n], in0=y[:, :, :n], in1=z[:, :, :n], op=ALU.mult), reads=[ry, rz], writes=[ry])
            s, rs, _ = sq.next(); r, rr, _ = rms.next()
            rstd_of(y, ry, 2, n, r, rr, s, rs, 256)
            odt, rod, _ = od.next()
            for kk in range(2):
                P.op("dve", lambda kk=kk: nc.vector.scalar_tensor_tensor(out=odt[:, kk, :n], in0=y[:, kk, :n], scalar=gvs[:, 16 + kk:17 + kk], in1=r[:, :n], op0=ALU.mult, op1=ALU.mult),
                     reads=[ry, rgv, rr], writes=[rod])
            for cb in range(KC):
                pt, rpt, _ = gen.next()
                for kk in range(KC):
                    rhs = oTs[:, kk, t0:t0 + n] if kk < 6 else odt[:, kk - 6, :n]
                    P.op("pe", lambda kk=kk, rhs=rhs, pt=pt: nc.tensor.matmul(pt[:, :n], lhsT=wo[:, kk, cb * 128:(cb + 1) * 128], rhs=rhs, start=(kk == 0), stop=(kk == KC - 1)),
                         reads=[rwo, roT, rod], writes=[rpt])
                P.op("dve", lambda cb=cb, pt=pt: nc.vector.scalar_tensor_tensor(out=h[:, cb, :n], in0=pt[:, :n], scalar=mo[:, 16 + cb, j:j + 1], in1=h[:, cb, :n], op0=ALU.mult, op1=ALU.add),
                     reads=[rpt, rmo, rh], writes=[rh])
            P.dma("sp", h2_d.rearrange("(k p) t -> p k t", p=128)[:, :, t0:t0 + n], h[:, :, :n], dh, reads=[rh], writes=[r_h2])
        P.pop_scope()
        P.pop_scope()
        P.push_scope()
        w1 = P.sb("w1", [128, KC, 4 * D], BF16); rw1 = P.res(); dw1 = P.dsem(f"w1{l}")
        w2 = P.sb("w2", [128, 32, D], BF16); rw2 = P.res(); dw2 = P.dsem(f"w2{l}")
        for kk in range(KC):
            for c0 in range(0, 4 * D, 2048):
                P.dma("pool", w1[:, kk, c0:c0 + 2048], w1_d[l][kk * 128:(kk + 1) * 128, c0:c0 + 2048], dw1, writes=[rw1])
        for kk in range(32):
            P.dma("pool", w2[:, kk, :], w2_d[l][kk * 128:(kk + 1) * 128, :], dw2, writes=[rw2])
        NB2 = 512
        hin = Rot(P, "h2in", 1, [128, KC, NB2], F32, dma=True)
        rms = Rot(P, "rms2", 1, [128, NB2], F32)
        tt = Rot(P, "tt2", 2, [128, NB2], F32)
        xm = Rot(P, "xm2", 1, [128, KC, NB2], BF16)
        at = Rot(P, "at", 1, [128, 32, NB2], BF16)
        rl = Rot(P, "rl", 2, [128, NB2], F32)
        for (t0, n, j) in [(t0, NB2, 0) for t0 in range(0, NQ, NB2)] + [(NQ, NCX, 1)]:
            if final and j == 1:
                continue
            h, rh, dh = hin.next()
            P.dma("sp", h[:, :, :n], h2_d.rearrange("(k p) t -> p k t", p=128)[:, :, t0:t0 + n], dh, reads=[r_h2], writes=[rh])
            a, ra, _ = at.next()
            s, rs = a, ra
            r, rr, _ = rms.next()
            rstd_of(h, rh, KC, n, r, rr, s, rs, D)
            x, rx, _ = xm.next()
            for kk in range(KC):
                t, rt, _ = tt.next()
                P.op("dve", lambda kk=kk, t=t: nc.vector.tensor_tensor(out=t[:, :n], in0=h[:, kk, :n], in1=r[:, :n], op=ALU.mult), reads=[rh, rr], writes=[rt])
                P.op("act", lambda kk=kk, t=t: nc.scalar.activation(out=x[:, kk, :n], in_=t[:, :n], func=AF.Identity, scale=gs2[:, kk, j:j + 1], bias=mo[:, 24 + kk, j:j + 1]),
                     reads=[rt, rgs, rmo], writes=[rx])
            for cb in range(32):
                pt, rpt, _ = gen.next()
                for kk in range(KC):
                    P.op("pe", lambda kk=kk, pt=pt: nc.tensor.matmul(pt[:, :n], lhsT=w1[:, kk, cb * 128:(cb + 1) * 128], rhs=x[:, kk, :n], start=(kk == 0), stop=(kk == KC - 1)),
                         reads=[rw1, rx], writes=[rpt])
                qq, rq_, _ = rl.next()
                P.op("act", lambda pt=pt, qq=qq: nc.scalar.activation(out=qq[:, :n], in_=pt[:, :n], func=AF.Relu), reads=[rpt], writes=[rq_])
                P.op("pool", lambda cb=cb, qq=qq: nc.gpsimd.tensor_tensor(out=a[:, cb, :n], in0=qq[:, :n], in1=qq[:, :n], op=ALU.mult), reads=[rq_], writes=[ra])
            for cb in range(KC):
                pt, rpt, _ = gen.next()
                for kk in range(32):
                    P.op("pe", lambda kk=kk, pt=pt: nc.tensor.matmul(pt[:, :n], lhsT=w2[:, kk, cb * 128:(cb + 1) * 128], rhs=a[:, kk, :n], start=(kk == 0), stop=(kk == 31)),
                         reads=[rw2, ra], writes=[rpt])
                P.op("dve", lambda cb=cb, pt=pt: nc.vector.scalar_tensor_tensor(out=h[:, cb, :n], in0=pt[:, :n], scalar=mo[:, 40 + cb, j:j + 1], in1=h[:, cb, :n], op0=ALU.mult, op1=ALU.add),
                     reads=[rpt, rmo, rh], writes=[rh])
            if final:
                r, rr, _ = rms.next()
                rstd_of(h, rh, KC, n, r, rr, a, ra, D)
                for kk in range(KC):
                    P.op("dve", lambda kk=kk: nc.vector.scalar_tensor_tensor(out=h[:, kk, :n], in0=h[:, kk, :n], scalar=gvs[:, 8 + kk:9 + kk], in1=r[:, :n], op0=ALU.mult, op1=ALU.mult),
                         reads=[rh, rgv, rr], writes=[rh])
                P.dma("sp", out_d.rearrange("(k p) t -> p k t", p=128)[:, :, t0:t0 + n], h[:, :, :n], dh, reads=[rh], writes=[P.res()], final=True)
            else:
                last = (l == nlayers - 1)
                if last and j == 0:
                    P.dma("sp", out_d.rearrange("(k p) t -> p k t", p=128)[:, :, t0:t0 + n], h[:, :, :n], dh, reads=[rh], writes=[P.res()], final=True)
                P.dma("sp", hTn_d.rearrange("(k p) t -> p k t", p=128)[:, :, t0:t0 + n], h[:, :, :n], dh, reads=[rh], writes=[r_hn])
        P.pop_scope()
    P.finish()
    P.close()
    return P


NEG = -1e30
NA0, SW0, ML0, SS0 = 0, 768, 1280, 1696


def cvec(v, n):
    return np.ascontiguousarray(v.reshape(n, 128).T)


def swap_idx(dim):
    q = dim // 4
    idx = np.arange(dim)
    blk = (idx // q) % 2
    return np.where(blk == 0, idx + q, idx - q)


def rope_tables(pos, dim):
    nf = dim // 4
    inv = (1.0 / (10000.0 ** (np.arange(nf, dtype=np.float32) / nf))).astype(np.float32)
    row = (pos // 64).astype(np.float32)
    col = (pos % 64).astype(np.float32)
    d = np.arange(dim)
    f = d % nf
    p = np.where((d < dim // 2)[:, None], row[None, :], col[None, :]).astype(np.float32)
    ang = (p * inv[f][:, None]).astype(np.float32)
    sign = np.where(((d // nf) % 2) == 0, -1.0, 1.0).astype(np.float32)
    return np.cos(ang).astype(np.float32), (np.sin(ang) * sign[:, None]).astype(np.float32)


def ext_rows(a, lo, hi):
    S = a.shape[0]
    out = np.zeros((hi - lo,) + a.shape[1:], a.dtype)
    l2, h2 = max(lo, 0), min(hi, S)
    out[l2 - lo:h2 - lo] = a[l2:h2]
    return out


def na_bias_tile(rpb, T):
    q = np.arange(128)
    r = 2 * T + q // 64
    qc = q % 64
    rs = np.clip(r - 4, 0, 120)
    cst = np.clip(qc - 8, 0, 48)
    out = np.full((128, 7, 4, 128), NEG, np.float32)
    i = np.arange(128)
    for kt in range(7):
        krow = 2 * T - 6 + 2 * kt + i // 64
        kcol = i % 64
        valid = ((krow[:, None] >= rs[None, :]) & (krow[:, None] < rs[None, :] + 8) & (krow[:, None] >= 0) & (krow[:, None] < 128)
                 & (kcol[:, None] >= cst[None, :]) & (kcol[:, None] < cst[None, :] + 16))
        dr = np.clip(krow[:, None] - r[None, :] + 7, 0, 14)
        dc = np.clip(kcol[:, None] - qc[None, :] + 15, 0, 30)
        for h in range(4):
            out[:, kt, h, :] = np.where(valid, rpb[h][dr, dc], NEG)
    return out


def prep_T(p_b, pc_b, j, W, l):
    o0 = 2048 * j
    T = lambda a: np.ascontiguousarray(a.T)
    m = {}
    own = p_b[o0:o0 + 2048]
    m["na_q"] = T(own[:, 0:256])
    e = ext_rows(p_b[:, 256:768], o0 - 384, o0 + 2048 + 384)
    m["na_k"] = T(e[:, 0:256]); m["na_v"] = np.ascontiguousarray(e[:, 256:512])
    m["na_qc"] = T(pc_b[:, 0:256]); m["na_kc"] = T(pc_b[:, 256:512]); m["na_vc"] = np.ascontiguousarray(pc_b[:, 512:768])
    rpb = W["na_rpb"][l]
    m["na_bias"] = np.stack([na_bias_tile(rpb, 16 * j + t).reshape(128, 7, 512) for t in (0, 1, 8 if j in (0, 3) else 2, 14, 15)])
    sw = swap_idx(64)
    q = own[:, SW0:SW0 + 256].reshape(2048, 4, 64)
    m["sw_q"] = T(q.reshape(2048, 256)); m["sw_qs"] = T(q[:, :, sw].reshape(2048, 256))
    e = ext_rows(p_b[:, SW0 + 256:SW0 + 512], o0 - 128, o0 + 2048 + 128)
    k = e[:, 0:128].reshape(2304, 2, 64)
    m["sw_k"] = T(k.reshape(2304, 128)); m["sw_ks"] = T(k[:, :, sw].reshape(2304, 128))
    m["sw_v"] = np.ascontiguousarray(e[:, 128:256])
    pos = np.arange(o0 - 128, o0 + 2048 + 128)
    c, s = rope_tables(np.clip(pos, 0, 8191), 64)
    m["sw_cs"] = np.stack([c, s])
    m["sw_qc"] = T(pc_b[:, SW0:SW0 + 256]); m["sw_kc"] = T(pc_b[:, SW0 + 256:SW0 + 384]); m["sw_vc"] = np.ascontiguousarray(pc_b[:, SW0 + 384:SW0 + 512])
    i = np.arange(128)
    prev = np.where(i[:, None] >= i[None, :], 0.0, NEG).astype(np.float32)
    nxt = np.where(i[:, None] <= i[None, :], 0.0, NEG).astype(np.float32)
    allneg = np.full((128, 128), NEG, np.float32)
    m["sw_bias"] = np.ascontiguousarray(np.stack([allneg if j == 0 else prev, prev, nxt, allneg if j == 3 else nxt], 1))
    m["sw_sink"] = np.ascontiguousarray(np.broadcast_to(W["swa_sink"][l][None, :], (128, 4)))
    sw32 = swap_idx(32)
    allk = np.concatenate([pc_b[:, ML0 + 256:ML0 + 416], p_b[:, ML0 + 256:ML0 + 416]], 0)
    m["m_ckv"] = T(allk[:, 0:128]); m["m_kr"] = T(allk[:, 128:160]); m["m_krs"] = T(allk[:, 128:160][:, sw32])
    c, s = rope_tables(np.arange(8192), 32)
    c = np.concatenate([np.ones((32, 256), np.float32), c], 1); s = np.concatenate([np.zeros((32, 256), np.float32), s], 1)
    m["m_kcs"] = np.stack([c, s])
    m["m_cq"] = T(np.concatenate([own[:, ML0:ML0 + 256], pc_b[:, ML0:ML0 + 256]], 0))
    c, s = rope_tables(np.arange(o0, o0 + 2048), 32)
    c = np.concatenate([c, np.ones((32, 256), np.float32)], 1); s = np.concatenate([s, np.zeros((32, 256), np.float32)], 1)
    m["m_qcs"] = np.stack([c, s])
    m["m_gkv"] = np.ascontiguousarray(W["mla_g_kv"][l].reshape(128, 1)); m["m_gq"] = cvec(W["mla_g_q"][l], 2)
    wkv = W["mla_w_ukv"][l].reshape(128, 4, 128)
    m["m_wkn"] = np.ascontiguousarray(wkv[:, :, 0:64].reshape(128, 256)); m["m_wkv"] = np.ascontiguousarray(wkv[:, :, 64:128].reshape(128, 256))
    wq = W["mla_w_uq"][l].reshape(256, 4, 96)
    m["m_wqn"] = np.ascontiguousarray(wq[:, :, 0:64].reshape(256, 256))
    m["m_wqr"] = np.ascontiguousarray(wq[:, :, 64:96].reshape(256, 128))
    m["m_wqrs"] = np.ascontiguousarray(wq[:, :, 64:96][:, :, sw32].reshape(256, 128))
    return m


def prep_S(p_b, pc_b, hd, W, l):
    g = hd // 2
    T = lambda a: np.ascontiguousarray(a.T)
    allp = np.concatenate([pc_b[:, SS0:], p_b[:, SS0:]], 0)
    xbc = allp[:, 256:1024]
    m = {}
    m["s_x"] = T(xbc[:, hd * 64:(hd + 1) * 64])
    m["s_b"] = T(xbc[:, 256 + g * 128:256 + (g + 1) * 128])
    m["s_c"] = T(xbc[:, 512 + g * 128:512 + (g + 1) * 128])
    cwv = np.concatenate([W["ssd_conv_w"][l], W["ssd_conv_b"][l][None, :]], 0)
    cw = np.zeros((128, 3, 6), np.float32)
    cw[:64, 0, :] = cwv[:, hd * 64:(hd + 1) * 64].T
    cw[:, 1, :] = cwv[:, 256 + g * 128:256 + (g + 1) * 128].T
    cw[:, 2, :] = cwv[:, 512 + g * 128:512 + (g + 1) * 128].T
    m["s_cw"] = cw
    dt = allp[:, 1024:1032].reshape(-1, 2, 4)[:, :, hd]
    m["s_dt"] = np.ascontiguousarray(dt.reshape(66, 128, 2).transpose(1, 0, 2))
    par = np.zeros((128, 8), np.float32)
    par[:, 0:2] = W["ssd_dt_bias"][l][:, hd]; par[:, 2:4] = W["ssd_a_log"][l][:, hd]; par[:, 4] = W["ssd_d"][l][hd]
    m["s_par"] = par
    i = np.arange(128)
    m["s_u"] = np.stack([(i[:, None] <= i[None, :]), (i[:, None] >= i[None, :])]).astype(np.float32)
    m["s_id"] = np.eye(128, dtype=np.float32)
    return m


def prep_M(I, core):
    b, j = core // 4, core % 4
    T = lambda a: np.ascontiguousarray(a.T)
    L = 4
    m = {}
    m["hT0"] = T(np.concatenate([I['x'][b, 2048 * j:2048 * (j + 1)], I['ctx'][b]], 0))
    m["cv"] = np.ascontiguousarray(np.stack([cvec(I['c'][b], 8), cvec(I['c_ctx'], 8)], -1))
    flg = np.zeros((128, 16), np.float32)
    flg[:, j] = 1.0
    if j > 0: flg[:, 4 + j - 1] = 1.0
    if j < 3: flg[:, 8 + j + 1] = 1.0
    m["flg"] = flg
    selx = np.zeros((128, 2, 64), np.float32); selb = np.zeros((128, 2, 128), np.float32)
    for d in range(64): selx[(j % 2) * 64 + d, j // 2, d] = 1.0
    for n in range(128): selb[n, j // 2, n] = 1.0
    m["selx"] = selx; m["selb"] = selb
    m["w_in"] = I['w_in']; m["w_mod"] = I['w_mod']; m["w_out"] = I['w_out']; m["w1"] = I['w_mlp1']; m["w2"] = I['w_mlp2']
    m["bmod"] = np.stack([cvec(I['b_mod'][l], 48) for l in range(L)])
    m["g1"] = np.stack([cvec(I['g_norm1'][l], 8) for l in range(L)])
    gv = np.zeros((L, 128, 20), np.float32)
    for l in range(L):
        gv[l, :, 0:8] = cvec(I['g_norm2'][l], 8); gv[l, :, 8:16] = cvec(I['g_final'], 8); gv[l, :, 16:18] = cvec(I['ssd_g_norm'][l], 2)
    m["gv"] = gv
    m["na_bias"] = np.stack([np.stack([na_bias_tile(I['na_rpb'][l], 16 * j + t).reshape(128, 7, 512) for t in (0, 1, 8 if j in (0, 3) else 2, 14, 15)]) for l in range(L)])
    i = np.arange(128)
    prev = np.where(i[:, None] >= i[None, :], 0.0, NEG).astype(np.float32)
    nxt = np.where(i[:, None] <= i[None, :], 0.0, NEG).astype(np.float32)
    allneg = np.full((128, 128), NEG, np.float32)
    m["sw_bias"] = np.ascontiguousarray(np.stack([allneg if j == 0 else prev, prev, nxt, allneg if j == 3 else nxt], 1))
    m["sw_sink"] = np.stack([np.ascontiguousarray(np.broadcast_to(I['swa_sink'][l][None, :], (128, 4))) for l in range(L)])
    o0 = 2048 * j
    c, s = rope_tables(np.clip(np.arange(o0 - 128, o0 + 2048 + 128), 0, 8191), 64)
    m["sw_cs"] = np.stack([c, s])
    c, s = rope_tables(np.arange(8192), 32)
    m["m_kcs"] = np.stack([np.concatenate([np.ones((32, 256), np.float32), c], 1), np.concatenate([np.zeros((32, 256), np.float32), s], 1)])
    c, s = rope_tables(np.arange(o0, o0 + 2048), 32)
    m["m_qcs"] = np.stack([np.concatenate([c, np.ones((32, 256), np.float32)], 1), np.concatenate([s, np.zeros((32, 256), np.float32)], 1)])
    sw32 = swap_idx(32)
    m["m_gkv"] = np.stack([I['mla_g_kv'][l].reshape(128, 1) for l in range(L)])
    m["m_gq"] = np.stack([cvec(I['mla_g_q'][l], 2) for l in range(L)])
    wkv = I['mla_w_ukv'].reshape(L, 128, 4, 128)
    m["m_wkn"] = np.ascontiguousarray(wkv[:, :, :, 0:64].reshape(L, 128, 256)); m["m_wkv"] = np.ascontiguousarray(wkv[:, :, :, 64:128].reshape(L, 128, 256))
    wq = I['mla_w_uq'].reshape(L, 256, 4, 96)
    m["m_wqn"] = np.ascontiguousarray(wq[:, :, :, 0:64].reshape(L, 256, 256))
    m["m_wqr"] = np.ascontiguousarray(wq[:, :, :, 64:96].reshape(L, 256, 128))
    m["m_wqrs"] = np.ascontiguousarray(wq[:, :, :, 64:96][:, :, :, sw32].reshape(L, 256, 128))
    hd, g = j, j // 2
    cw = np.zeros((L, 128, 3, 6), np.float32); par = np.zeros((L, 128, 8), np.float32)
    for l in range(L):
        cwv = np.concatenate([I['ssd_conv_w'][l], I['ssd_conv_b'][l][None, :]], 0)
        cw[l, :64, 0, :] = cwv[:, hd * 64:(hd + 1) * 64].T
        cw[l, :, 1, :] = cwv[:, 256 + g * 128:256 + (g + 1) * 128].T
        cw[l, :, 2, :] = cwv[:, 512 + g * 128:512 + (g + 1) * 128].T
        par[l, :, 0:2] = I['ssd_dt_bias'][l][:, hd]; par[l, :, 2:4] = I['ssd_a_log'][l][:, hd]; par[l, :, 4] = I['ssd_d'][l][hd]
    m["s_cw"] = cw; m["s_par"] = par
    m["s_u"] = np.stack([(i[:, None] <= i[None, :]), (i[:, None] >= i[None, :])]).astype(np.float32)
    m["ident"] = np.eye(128, dtype=np.float32)
    return m


from concourse.bass_utils import run_bass_kernel_spmd

_PROG = {}


def kernel(**inputs):
    I = {k: np.asarray(v, dtype=np.float32) for k, v in inputs.items()}
    if "M" not in _PROG:
        _PROG["M"] = build_M(4, 3, 4)
    P = _PROG["M"]
    in_maps = [prep_M(I, c) for c in range(8)]
    res = run_bass_kernel_spmd(P.nc, in_maps, core_ids=list(range(8)))
    out = np.stack([np.concatenate([res.results[b * 4 + j]["out"].T for j in range(4)], 0) for b in range(2)])
    return np.ascontiguousarray(out.astype(np.float32))
```

```python
import numpy as np
from contextlib import ExitStack
import concourse.bass as bass
import concourse.mybir as mybir

F32 = mybir.dt.float32
BF16 = mybir.dt.bfloat16
AF = mybir.ActivationFunctionType
ALU = mybir.AluOpType
AX = mybir.AxisListType


class Res:
    __slots__ = ("name", "w", "r")

    def __init__(self, name):
        self.name = name
        self.w = None
        self.r = []


class Prog:
    ENG = ("pe", "act", "dve", "pool", "sp")

    def __init__(self):
        self.nc = bass.Bass("TRN2", target_bir_lowering=False)
        self.es = ExitStack()
        self.root = self.es
        nc = self.nc
        self.e = {"pe": nc.tensor, "act": nc.scalar, "dve": nc.vector, "pool": nc.gpsimd, "sp": nc.sync}
        self.sem = {}
        self.cnt = {}
        for k in self.ENG:
            self.sem[k] = self.es.enter_context(nc.semaphore("s_" + k))
            self.cnt[k] = 0
        self.seen = {k: {} for k in self.ENG}
        self.ndma = 0
        self.out_toks = []
        self.nwait = 0
        self.nres = 0

    def dram(self, name, shape, dt, kind):
        return self.nc.dram_tensor(name, list(shape), dt, kind=kind).ap()

    def _u(self, name):
        self.nuniq = getattr(self, "nuniq", 0) + 1
        return f"{name}_u{self.nuniq}"

    def sb(self, name, shape, dt):
        return self.es.enter_context(self.nc.sbuf_tensor(self._u(name), list(shape), dt))

    def ps(self, name, shape, dt=F32):
        return self.es.enter_context(self.nc.psum_tensor(self._u(name), list(shape), dt))

    def res(self, name=None):
        self.nres += 1
        return Res(name or f"r{self.nres}")

    def dsem(self, name):
        free = getattr(self, "free_dsems", None)
        if free:
            key = free.pop()
        else:
            name = self._u(name)
            s = self.root.enter_context(self.nc.semaphore("d_" + name))
            key = ("d", name)
            self.sem[key] = s
            self.cnt[key] = 0
        if getattr(self, "_scope_dsems", None):
            self._scope_dsems[-1].append(key)
        return key

    def _deps(self, eng, reads, writes):
        need = {}

        def add(t, same_ok):
            if t is None:
                return
            sk, val, src = t
            if src == eng and not same_ok:
                return
            if need.get(sk, 0) < val:
                need[sk] = val

        for r in reads:
            add(r.w, True)
        for w in writes:
            add(w.w, False)
            for t in w.r:
                add(t, False)
        out = []
        seen = self.seen[eng]
        for sk, val in need.items():
            if seen.get(sk, 0) >= val:
                continue
            seen[sk] = val
            out.append((sk, val))
        return out

    def _emit_waits(self, eng, waits):
        e = self.e[eng]
        for sk, val in waits:
            e.wait_ge(self.sem[sk], val)
            self.nwait += 1

    def _record(self, tok, reads, writes):
        for r in reads:
            r.r.append(tok)
        for w in writes:
            w.w = tok
            w.r = []

    def op(self, eng, fn, reads=(), writes=(), pe_chain=False):
        reads = [r for r in reads if r is not None]
        writes = [w for w in writes if w is not None]
        if eng == "pe":
            waits = self._deps_pe(reads, writes)
        else:
            waits = self._deps(eng, reads, writes)
        self._emit_waits(eng, waits)
        ins = fn()
        self.cnt[eng] += 1
        ins.then_inc(self.sem[eng], 1)
        tok = (eng, self.cnt[eng], eng)
        self._record(tok, reads, writes)
        return ins

    def _deps_pe(self, reads, writes):
        need = {}

        def add(t):
            if t is None:
                return
            sk, val, src = t
            if src == "pe":
                return
            if need.get(sk, 0) < val:
                need[sk] = val

        for r in reads:
            add(r.w)
        for w in writes:
            add(w.w)
            for t in w.r:
                add(t)
        out = []
        seen = self.seen["pe"]
        for sk, val in need.items():
            if seen.get(sk, 0) >= val:
                continue
            seen[sk] = val
            out.append((sk, val))
        return out

    def dma(self, q, dst, src, dsem, reads=(), writes=(), final=False, **kw):
        reads = [r for r in reads if r is not None]
        writes = [w for w in writes if w is not None]
        waits = self._deps(q, reads, writes)
        self._emit_waits(q, waits)
        ins = self.e[q].dma_start(out=dst, in_=src, **kw)
        self.cnt[dsem] += 16
        ins.then_inc(self.sem[dsem], 16)
        tok = (dsem, self.cnt[dsem], "dma")
        self._record(tok, reads, writes)
        self.ndma += 1
        if final:
            self.out_toks.append(tok)
        return ins

    def barrier(self):
        for eng in self.ENG:
            for sk, val in self.cnt.items():
                if val > 0 and self.seen[eng].get(sk, 0) < val:
                    self.e[eng].wait_ge(self.sem[sk], val)
                    self.seen[eng][sk] = val

    def push_scope(self):
        self._scopes = getattr(self, "_scopes", [])
        self._scopes.append(self.es)
        self.es = ExitStack()
        self._scope_dsems = getattr(self, "_scope_dsems", [])
        self._scope_dsems.append([])

    def pop_scope(self):
        self.barrier()
        self.es.close()
        self.es = self._scopes.pop()
        self.free_dsems = getattr(self, "free_dsems", [])
        self.free_dsems.extend(self._scope_dsems.pop())

    def finish(self):
        toks = list(self.out_toks)
        need = {}
        for sk, val, _ in toks:
            need[sk] = max(need.get(sk, 0), val)
        for sk, val in need.items():
            self.e["sp"].wait_ge(self.sem[sk], val)

    def close(self):
        self.es.close()


class Rot:
    def __init__(self, P, name, n, shape, dt, psum=False, dma=False):
        self.t, self.r, self.d = [], [], []
        for i in range(n):
            self.t.append(P.ps(f"{name}{i}", shape, dt) if psum else P.sb(f"{name}{i}", shape, dt))
            self.r.append(P.res(f"{name}{i}"))
            self.d.append(P.dsem(f"{name}{i}") if dma else None)
        self.i = -1
        self.n = n

    def next(self):
        self.i = (self.i + 1) % self.n
        return self.t[self.i], self.r[self.i], self.d[self.i]


EPS = 1e-6


class AttnCtx:
    def __init__(self, P):
        self.P = P
        nc = P.nc
        self.panels = Rot(P, "pan", 3, [128, 512], F32, psum=True)
        self.accs = Rot(P, "acc", 2, [128, 512], F32, psum=True)
        self.misc = Rot(P, "mps", 2, [128, 512], F32, psum=True)
        self.pT = Rot(P, "pT", 3, [128, 512], BF16)
        self.sT = Rot(P, "sT", 2, [128, 512], F32)
        self.rec = Rot(P, "rec", 2, [128, 4], F32)
        self.zeros = P.sb("zeros", [128, 512], BF16)
        self.rz = P.res()
        P.op("dve", lambda: nc.vector.memset(self.zeros[:], 0.0), writes=[self.rz])
        self.ev = 0


def attend(A, ncol, merged, q_aps, q_res, key_items, scale, out_aps, out_res, sinkexp=None, sink_res=None):
    P = A.P
    nc = P.nc
    W = ncol * 128
    acc, racc, _ = A.accs.next()
    P.op("pe", lambda: nc.tensor.matmul(acc[:, 0:ncol * 65], lhsT=A.zeros[:, 0:128], rhs=A.zeros[:, 0:ncol * 65], start=True, stop=False),
         reads=[A.rz], writes=[racc])
    nk = len(key_items)

    def score(it):
        pan, rpan, _ = A.panels.next()
        if merged:
            parts_k = it["k"][0]
            parts_q = q_aps[0]
            for pi, (kp, qp) in enumerate(zip(parts_k, parts_q)):
                P.op("pe", lambda kp=kp, qp=qp, pi=pi: nc.tensor.matmul(pan[:, 0:W], lhsT=kp, rhs=qp, start=(pi == 0), stop=(pi == len(parts_k) - 1)),
                     reads=list(it["res"]) + list(q_res), writes=[rpan])
        else:
            for c in range(ncol):
                parts_k = it["k"][c]
                parts_q = q_aps[c]
                for pi, (kp, qp) in enumerate(zip(parts_k, parts_q)):
                    P.op("pe", lambda kp=kp, qp=qp, pi=pi, c=c, n=len(parts_k): nc.tensor.matmul(
                        pan[:, c * 128:(c + 1) * 128], lhsT=kp, rhs=qp, start=(pi == 0), stop=(pi == n - 1)),
                        reads=list(it["res"]) + list(q_res), writes=[rpan])
        pt, rpt, _ = A.pT.next()
        if it.get("bias") is not None:
            st, rst, _ = A.sT.next()
            P.op("dve", lambda: nc.vector.scalar_tensor_tensor(out=st[:, 0:W], in0=pan[:, 0:W], scalar=scale, in1=it["bias"], op0=ALU.mult, op1=ALU.add),
                 reads=[rpan] + list(it.get("bres", [])), writes=[rst])
            P.op("act", lambda: nc.scalar.activation(out=pt[:, 0:W], in_=st[:, 0:W], func=AF.Exp), reads=[rst], writes=[rpt])
        else:
            P.op("act", lambda: nc.scalar.activation(out=pt[:, 0:W], in_=pan[:, 0:W], func=AF.Exp, scale=scale), reads=[rpan], writes=[rpt])
        return pt, rpt

    nxt = score(key_items[0])
    for ki, it in enumerate(key_items):
        pt, rpt = nxt
        if ki + 1 < nk:
            nxt = score(key_items[ki + 1])
        for c in range(ncol):
            P.op("pe", lambda c=c: nc.tensor.matmul(acc[:, c * 65:(c + 1) * 65], lhsT=pt[:, c * 128:(c + 1) * 128], rhs=it["v"][c], start=False, stop=(ki == nk - 1)),
                 reads=[rpt] + list(it["res"]), writes=[racc])
    rec, rrec, _ = A.rec.next()
    den = acc[:, 64:64 + 65 * (ncol - 1) + 1:65]
    if sinkexp is not None:
        P.op("dve", lambda: nc.vector.tensor_tensor(out=rec[:, 0:ncol], in0=den, in1=sinkexp, op=ALU.add), reads=[racc, sink_res], writes=[rrec])
        P.op("dve", lambda: nc.vector.reciprocal(out=rec[:, 0:ncol], in_=rec[:, 0:ncol]), reads=[rrec], writes=[rrec])
    else:
        P.op("dve", lambda: nc.vector.reciprocal(out=rec[:, 0:ncol], in_=den), reads=[racc], writes=[rrec])
    for c in range(ncol):
        A.ev += 1
        if A.ev % 2 == 0:
            P.op("act", lambda c=c: nc.scalar.activation(out=out_aps[c], in_=acc[:, c * 65:c * 65 + 64], func=AF.Copy, scale=rec[:, c:c + 1]),
                 reads=[racc, rrec], writes=[out_res])
        else:
            P.op("dve", lambda c=c: nc.vector.tensor_scalar(out=out_aps[c], in0=acc[:, c * 65:c * 65 + 64], scalar1=rec[:, c:c + 1], scalar2=None,
                                                            op0=ALU.mult), reads=[racc, rrec], writes=[out_res])


def load_cast(P, dst, src, res, dsem):
    P.dma("pool", dst, src, dsem, writes=[res])


def load_v(P, vt, v_d, ntile, nh, res, dsem):
    nc = P.nc
    P.op("dve", lambda: nc.vector.memset(vt[:], 1.0), writes=[res])
    src = v_d.rearrange("(t p) (h d) -> p t h d", p=128, h=nh)
    for t in range(ntile):
        P.dma("pool", vt[:, t, :, 0:64], src[:, t, :, :], dsem, writes=[res])


def rope_to(P, dst, x_d, xs_d, cos_t, sin_t, rtab, n, npart, stg, res):
    nc = P.nc
    a, ra, da = stg.next()
    b, rb, db = stg.next()
    P.dma("sp", a[:npart, :n], x_d, da, writes=[ra])
    P.dma("sp", b[:npart, :n], xs_d, db, writes=[rb])
    P.op("dve", lambda: nc.vector.tensor_tensor(out=a[:npart, :n], in0=a[:npart, :n], in1=cos_t, op=ALU.mult), reads=[ra, rtab], writes=[ra])
    P.op("pool", lambda: nc.gpsimd.tensor_tensor(out=b[:npart, :n], in0=b[:npart, :n], in1=sin_t, op=ALU.mult), reads=[rb, rtab], writes=[rb])
    P.op("dve", lambda: nc.vector.tensor_tensor(out=dst, in0=a[:npart, :n], in1=b[:npart, :n], op=ALU.add), reads=[ra, rb], writes=[res])


def build_T(do_na=True, do_swa=True, do_mla=True):
    P = Prog()
    nc = P.nc
    A = AttnCtx(P)
    NQ = 2048
    NC = 256
    rout = P.res("out")
    ost = Rot(P, "ost", 2, [128, 4, 256], F32, dma=True)

    if do_na:
        P.push_scope()
        EXT = 2816
        q_d = P.dram("na_q", [256, NQ], F32, "ExternalInput")
        k_d = P.dram("na_k", [256, EXT], F32, "ExternalInput")
        v_d = P.dram("na_v", [EXT, 256], F32, "ExternalInput")
        qc_d = P.dram("na_qc", [256, NC], F32, "ExternalInput")
        kc_d = P.dram("na_kc", [256, NC], F32, "ExternalInput")
        vc_d = P.dram("na_vc", [NC, 256], F32, "ExternalInput")
        b_d = P.dram("na_bias", [5, 128, 7, 512], F32, "ExternalInput")
        o_d = P.dram("o_na", [NQ + NC, 256], F32, "ExternalOutput")
        q = P.sb("naq", [64, 4, NQ], BF16); rq = P.res(); d1 = P.dsem("naq")
        k = P.sb("nak", [64, 4, EXT], BF16); rk = P.res(); d2 = P.dsem("nak")
        v = P.sb("nav", [128, EXT // 128, 4, 65], BF16); rv = P.res(); d3 = P.dsem("nav")
        qc = P.sb("naqc", [64, 4, NC], BF16); kc = P.sb("nakc", [64, 4, NC], BF16); vc = P.sb("navc", [128, 2, 4, 65], BF16)
        rc = P.res(); d4 = P.dsem("nac")
        for h in range(4):
            load_cast(P, q[:, h, :], q_d[h * 64:(h + 1) * 64, :], rq, d1)
            load_cast(P, k[:, h, :], k_d[h * 64:(h + 1) * 64, :], rk, d2)
            load_cast(P, qc[:, h, :], qc_d[h * 64:(h + 1) * 64, :], rc, d4)
            load_cast(P, kc[:, h, :], kc_d[h * 64:(h + 1) * 64, :], rc, d4)
        load_v(P, v, v_d, EXT // 128, 4, rv, d3)
        load_v(P, vc, vc_d, 2, 4, rc, d4)
        bias = Rot(P, "nab", 2, [128, 7, 512], F32, dma=True)
        scale = 64 ** -0.5
        for t in range(16 + 2):
            items = []
            if t < 16:
                pat = 0 if t == 0 else 1 if t == 1 else 3 if t == 14 else 4 if t == 15 else 2
                bt, rb, db = bias.next()
                P.dma("sp", bt[:], b_d[pat], db, writes=[rb])
                qa = [[q[:, h, t * 128:(t + 1) * 128]] for h in range(4)]
                for kt in range(7):
                    et = t + kt
                    items.append(dict(k=[[k[:, h, et * 128:(et + 1) * 128]] for h in range(4)], v=[v[:, et, h, :] for h in range(4)],
                                      bias=bt[:, kt, :], bres=[rb], res=[rk, rv]))
                qres = [rq]
            else:
                tc = t - 16
                qa = [[qc[:, h, tc * 128:(tc + 1) * 128]] for h in range(4)]
                qres = [rc]
            for kt in range(2):
                items.append(dict(k=[[kc[:, h, kt * 128:(kt + 1) * 128]] for h in range(4)], v=[vc[:, kt, h, :] for h in range(4)], bias=None, res=[rc]))
            o, ro, do = ost.next()
            attend(A, 4, False, qa, qres, items, scale, [o[:, 0, h * 64:(h + 1) * 64] for h in range(4)], ro)
            P.dma("sp", o_d[t * 128:(t + 1) * 128, :], o[:, 0, :], do, reads=[ro], writes=[rout], final=True)
        P.pop_scope()

    if do_swa:
        P.push_scope()
        EXT = 2304
        q_d = P.dram("sw_q", [256, NQ], F32, "ExternalInput")
        qs_d = P.dram("sw_qs", [256, NQ], F32, "ExternalInput")
        k_d = P.dram("sw_k", [128, EXT], F32, "ExternalInput")
        ks_d = P.dram("sw_ks", [128, EXT], F32, "ExternalInput")
        cs_d = P.dram("sw_cs", [2, 64, EXT], F32, "ExternalInput")
        v_d = P.dram("sw_v", [EXT, 128], F32, "ExternalInput")
        qc_d = P.dram("sw_qc", [256, NC], F32, "ExternalInput")
        kc_d = P.dram("sw_kc", [128, NC], F32, "ExternalInput")
        vc_d = P.dram("sw_vc", [NC, 128], F32, "ExternalInput")
        b_d = P.dram("sw_bias", [128, 4, 128], F32, "ExternalInput")
        sk_d = P.dram("sw_sink", [128, 4], F32, "ExternalInput")
        o_d = P.dram("o_sw", [NQ + NC, 256], F32, "ExternalOutput")
        q = P.sb("swq", [64, 4, NQ], BF16); rq = P.res()
        k = P.sb("swk", [64, 2, EXT], BF16); rk = P.res()
        v = P.sb("swv", [128, EXT // 128, 2, 65], BF16); rv = P.res(); d3 = P.dsem("swv")
        qc = P.sb("swqc", [64, 4, NC], BF16); kc = P.sb("swkc", [64, 2, NC], BF16); vc = P.sb("swvc", [128, 2, 2, 65], BF16)
        rc = P.res(); d4 = P.dsem("swc")
        cs = P.sb("swcs", [64, 2, EXT], F32); rcs = P.res(); d5 = P.dsem("swcs")
        bs = P.sb("swb", [128, 4, 128], F32); rbs = P.res()
        sk = P.sb("swsk", [128, 4], F32); rsk = P.res()
        P.dma("sp", cs[:, 0, :], cs_d[0], d5, writes=[rcs])
        P.dma("sp", cs[:, 1, :], cs_d[1], d5, writes=[rcs])
        P.dma("sp", bs[:], b_d, d5, writes=[rbs])
        P.dma("sp", sk[:], sk_d, d5, writes=[rsk])
        P.op("act", lambda: nc.scalar.activation(out=sk[:], in_=sk[:], func=AF.Exp), reads=[rsk], writes=[rsk])
        stg = Rot(P, "swstg", 4, [64, EXT], F32, dma=True)
        for h in range(4):
            rope_to(P, q[:, h, :], q_d[h * 64:(h + 1) * 64, :], qs_d[h * 64:(h + 1) * 64, :], cs[:, 0, 128:128 + NQ], cs[:, 1, 128:128 + NQ], rcs, NQ, 64, stg, rq)
            load_cast(P, qc[:, h, :], qc_d[h * 64:(h + 1) * 64, :], rc, d4)
        for g in range(2):
            rope_to(P, k[:, g, :], k_d[g * 64:(g + 1) * 64, :], ks_d[g * 64:(g + 1) * 64, :], cs[:, 0, :], cs[:, 1, :], rcs, EXT, 64, stg, rk)
            load_cast(P, kc[:, g, :], kc_d[g * 64:(g + 1) * 64, :], rc, d4)
        load_v(P, v, v_d, EXT // 128, 2, rv, d3)
        load_v(P, vc, vc_d, 2, 2, rc, d4)
        bp = P.sb("swbp", [128, 4, 4, 128], F32); rbp = P.res()
        for kind in range(4):
            for h in range(4):
                P.op("dve", lambda kind=kind, h=h: nc.vector.tensor_copy(out=bp[:, kind, h, :], in_=bs[:, kind, :]), reads=[rbs], writes=[rbp])
        scale = 64 ** -0.5
        for t in range(16 + 2):
            items = []
            if t < 16:
                qa = [[q[:, h, t * 128:(t + 1) * 128]] for h in range(4)]
                qres = [rq]
                for kt in range(3):
                    et = t + kt
                    if kt == 0:
                        b = bp[:, 0 if t == 0 else 1, :, :].rearrange('p h q -> p (h q)')
                    elif kt == 2:
                        b = bp[:, 3 if t == 15 else 2, :, :].rearrange('p h q -> p (h q)')
                    else:
                        b = None
                    items.append(dict(k=[[k[:, h // 2, et * 128:(et + 1) * 128]] for h in range(4)], v=[v[:, et, h // 2, :] for h in range(4)],
                                      bias=b, bres=[rbp], res=[rk, rv]))
            else:
                tc = t - 16
                qa = [[qc[:, h, tc * 128:(tc + 1) * 128]] for h in range(4)]
                qres = [rc]
            for kt in range(2):
                items.append(dict(k=[[kc[:, h // 2, kt * 128:(kt + 1) * 128]] for h in range(4)], v=[vc[:, kt, h // 2, :] for h in range(4)], bias=None, res=[rc]))
            o, ro, do = ost.next()
            attend(A, 4, False, qa, qres, items, scale, [o[:, 0, h * 64:(h + 1) * 64] for h in range(4)], ro, sinkexp=sk[:, 0:4], sink_res=rsk)
            P.dma("sp", o_d[t * 128:(t + 1) * 128, :], o[:, 0, :], do, reads=[ro], writes=[rout], final=True)
        P.pop_scope()

    if do_mla:
        P.push_scope()
        NK = 8448
        NQA = NQ + NC
        ckv_d = P.dram("m_ckv", [128, NK], F32, "ExternalInput")
        kr_d = P.dram("m_kr", [32, NK], F32, "ExternalInput")
        krs_d = P.dram("m_krs", [32, NK], F32, "ExternalInput")
        kcs_d = P.dram("m_kcs", [2, 32, NK], F32, "ExternalInput")
        cq_d = P.dram("m_cq", [256, NQA], F32, "ExternalInput")
        qcs_d = P.dram("m_qcs", [2, 32, NQA], F32, "ExternalInput")
        gkv_d = P.dram("m_gkv", [128, 1], F32, "ExternalInput")
        gq_d = P.dram("m_gq", [128, 2], F32, "ExternalInput")
        wkn_d = P.dram("m_wkn", [128, 256], F32, "ExternalInput")
        wkv_d = P.dram("m_wkv", [128, 256], F32, "ExternalInput")
        wqn_d = P.dram("m_wqn", [256, 256], F32, "ExternalInput")
        wqr_d = P.dram("m_wqr", [256, 128], F32, "ExternalInput")
        wqrs_d = P.dram("m_wqrs", [256, 128], F32, "ExternalInput")
        o_d = P.dram("o_ml", [NQA, 256], F32, "ExternalOutput")

        ones = P.sb("mones", [128, 128], BF16); rones = P.res()
        P.op("dve", lambda: nc.vector.memset(ones[:], 1.0), writes=[rones])
        dsm = P.dsem("msmall")
        gkv = P.sb("gkv", [128, 1], F32); gq = P.sb("gq", [128, 2], F32); rg = P.res()
        P.dma("sp", gkv[:], gkv_d, dsm, writes=[rg]); P.dma("sp", gq[:], gq_d, dsm, writes=[rg])
        wkn = P.sb("wkn", [128, 256], BF16); wkv = P.sb("wkv", [128, 256], BF16)
        wqn = P.sb("wqn", [128, 2, 256], BF16); wqr = P.sb("wqr", [128, 2, 128], BF16); wqrs = P.sb("wqrs", [128, 2, 128], BF16)
        rw = P.res(); dw = P.dsem("mw")
        load_cast(P, wkn[:], wkn_d, rw, dw); load_cast(P, wkv[:], wkv_d, rw, dw)
        for c in range(2):
            load_cast(P, wqn[:, c, :], wqn_d[c * 128:(c + 1) * 128, :], rw, dw)
            load_cast(P, wqr[:, c, :], wqr_d[c * 128:(c + 1) * 128, :], rw, dw)
            load_cast(P, wqrs[:, c, :], wqrs_d[c * 128:(c + 1) * 128, :], rw, dw)

        kn = P.sb("kn", [128, 2, NK], BF16); rkn = P.res()
        kr = P.sb("kr", [32, NK], BF16); rkr = P.res()
        vm = P.sb("vm", [128, NK // 128, 4, 65], BF16); rvm = P.res()
        qn = P.sb("qn", [128, 2, NQA], BF16); rqn = P.res()
        qr = P.sb("qr", [32, 4, NQA], BF16); rqr = P.res()
        P.op("dve", lambda: nc.vector.memset(vm[:], 1.0), writes=[rvm])

        xin = Rot(P, "mx", 2, [128, 2, 512], F32, dma=True)
        sq = Rot(P, "msq", 2, [128, 2, 512], BF16)
        rms = Rot(P, "mrms", 2, [128, 512], F32)
        tt = Rot(P, "mtt", 2, [128, 512], F32)
        xn = Rot(P, "mxn", 2, [128, 2, 512], BF16)
        tab = Rot(P, "mtab", 2, [32, 2, 512], F32, dma=True)
        rr = Rot(P, "mrr", 4, [32, 512], F32, dma=True)
        evi = [0]

        def evac(dst, src, reads, writes):
            evi[0] += 1
            if evi[0] % 2 == 0:
                P.op("act", lambda: nc.scalar.copy(out=dst, in_=src), reads=reads, writes=writes)
            else:
                P.op("dve", lambda: nc.vector.tensor_copy(out=dst, in_=src), reads=reads, writes=writes)

        def norm_block(src_d, kc, t0, n, g):
            x, rx, dx = xin.next()
            for c in range(kc):
                P.dma("sp", x[:, c, :n], src_d[c * 128:(c + 1) * 128, t0:t0 + n], dx, writes=[rx])
            s, rs, _ = sq.next()
            for c in range(kc):
                P.op("act", lambda c=c: nc.scalar.activation(out=s[:, c, :n], in_=x[:, c, :n], func=AF.Square), reads=[rx], writes=[rs])
            pt, rpt, _ = A.misc.next()
            for c in range(kc):
                P.op("pe", lambda c=c: nc.tensor.matmul(pt[:, :n], lhsT=ones[:], rhs=s[:, c, :n], start=(c == 0), stop=(c == kc - 1)), reads=[rones, rs], writes=[rpt])
            r, rrr, _ = rms.next()
            P.op("act", lambda: nc.scalar.activation(out=r[:, :n], in_=pt[:, :n], func=AF.Sqrt, scale=1.0 / (kc * 128), bias=EPS), reads=[rpt], writes=[rrr])
            P.op("dve", lambda: nc.vector.reciprocal(out=r[:, :n], in_=r[:, :n]), reads=[rrr], writes=[rrr])
            y, ry, _ = xn.next()
            for c in range(kc):
                P.op("dve", lambda c=c: nc.vector.scalar_tensor_tensor(out=y[:, c, :n], in0=x[:, c, :n], scalar=g[:, c:c + 1], in1=r[:, :n],
                                                                     op0=ALU.mult, op1=ALU.mult), reads=[rx, rrr, rg], writes=[ry])
            return y, ry

        def rope_block(dst, x_d, xs_d, cs_d, t0, n, res):
            tb, rtb, dtb = tab.next()
            P.dma("sp", tb[:, 0, :n], cs_d[0][:, t0:t0 + n], dtb, writes=[rtb])
            P.dma("sp", tb[:, 1, :n], cs_d[1][:, t0:t0 + n], dtb, writes=[rtb])
            a, ra, da = rr.next(); b, rb, db = rr.next()
            P.dma("sp", a[:, :n], x_d[:, t0:t0 + n], da, writes=[ra])
            P.dma("sp", b[:, :n], xs_d[:, t0:t0 + n], db, writes=[rb])
            P.op("dve", lambda: nc.vector.tensor_tensor(out=a[:, :n], in0=a[:, :n], in1=tb[:, 0, :n], op=ALU.mult), reads=[ra, rtb], writes=[ra])
            P.op("pool", lambda: nc.gpsimd.tensor_tensor(out=b[:, :n], in0=b[:, :n], in1=tb[:, 1, :n], op=ALU.mult), reads=[rb, rtb], writes=[rb])
            P.op("dve", lambda: nc.vector.tensor_tensor(out=dst, in0=a[:, :n], in1=b[:, :n], op=ALU.add), reads=[ra, rb], writes=[res])

        for t0 in range(0, NK, 512):
            n = min(512, NK - t0)
            y, ry = norm_block(ckv_d, 1, t0, n, gkv)
            for pr in range(2):
                pt, rpt, _ = A.misc.next()
                P.op("pe", lambda pr=pr, pt=pt: nc.tensor.matmul(pt[:, :n], lhsT=wkn[:, pr * 128:(pr + 1) * 128], rhs=y[:, 0, :n], start=True, stop=True),
                     reads=[rw, ry], writes=[rpt])
                evac(kn[:, pr, t0:t0 + n], pt[:, :n], [rpt], [rkn])
            for tt_ in range(n // 128):
                kt = t0 // 128 + tt_
                pt, rpt, _ = A.misc.next()
                P.op("pe", lambda tt_=tt_, pt=pt: nc.tensor.matmul(pt[:, 0:256], lhsT=y[:, 0, tt_ * 128:(tt_ + 1) * 128], rhs=wkv[:], start=True, stop=True),
                     reads=[rw, ry], writes=[rpt])
                evac(vm[:, kt, :, 0:64], pt[:, 0:256].rearrange("p (h d) -> p h d", h=4), [rpt], [rvm])
            rope_block(kr[:, t0:t0 + n], kr_d, krs_d, kcs_d, t0, n, rkr)
        for t0 in range(0, NQA, 512):
            n = min(512, NQA - t0)
            y, ry = norm_block(cq_d, 2, t0, n, gq)
            for pr in range(2):
                pt, rpt, _ = A.misc.next()
                for c in range(2):
                    P.op("pe", lambda pr=pr, pt=pt, c=c: nc.tensor.matmul(pt[:, :n], lhsT=wqn[:, c, pr * 128:(pr + 1) * 128], rhs=y[:, c, :n],
                                                                         start=(c == 0), stop=(c == 1)), reads=[rw, ry], writes=[rpt])
                evac(qn[:, pr, t0:t0 + n], pt[:, :n], [rpt], [rqn])
            tb, rtb, dtb = tab.next()
            P.dma("sp", tb[:, 0, :n], qcs_d[0][:, t0:t0 + n], dtb, writes=[rtb])
            P.dma("sp", tb[:, 1, :n], qcs_d[1][:, t0:t0 + n], dtb, writes=[rtb])
            for h in range(4):
                pa, rpa, _ = A.misc.next()
                for c in range(2):
                    P.op("pe", lambda pa=pa, c=c, h=h: nc.tensor.matmul(pa[0:32, :n], lhsT=wqr[:, c, h * 32:(h + 1) * 32], rhs=y[:, c, :n],
                                                                       start=(c == 0), stop=(c == 1)), reads=[rw, ry], writes=[rpa])
                a, ra, _ = rr.next()
                P.op("dve", lambda a=a, pa=pa: nc.vector.tensor_tensor(out=a[:, :n], in0=pa[0:32, :n], in1=tb[:, 0, :n], op=ALU.mult), reads=[rpa, rtb], writes=[ra])
                pb, rpb, _ = A.misc.next()
                for c in range(2):
                    P.op("pe", lambda pb=pb, c=c, h=h: nc.tensor.matmul(pb[0:32, :n], lhsT=wqrs[:, c, h * 32:(h + 1) * 32], rhs=y[:, c, :n],
                                                                       start=(c == 0), stop=(c == 1)), reads=[rw, ry], writes=[rpb])
                b, rb, _ = rr.next()
                P.op("dve", lambda b=b, pb=pb: nc.vector.tensor_tensor(out=b[:, :n], in0=pb[0:32, :n], in1=tb[:, 1, :n], op=ALU.mult), reads=[rpb, rtb], writes=[rb])
                P.op("pool", lambda a=a, b=b, h=h: nc.gpsimd.tensor_tensor(out=qr[:, h, t0:t0 + n], in0=a[:, :n], in1=b[:, :n], op=ALU.add), reads=[ra, rb], writes=[rqr])
        scale = 96 ** -0.5
        groups = [(g * 512, 4, NK // 128) for g in range(4)] + [(2048, 2, 2)]
        for (q0, ncol, nkt) in groups:
            o, ro, do = ost.next()
            for h in range(4):
                hp, pr = (h % 2) * 64, h // 2
                qa = [[qn[hp:hp + 64, pr, q0:q0 + ncol * 128], qr[:, h, q0:q0 + ncol * 128]]]
                items = []
                for kt in range(nkt):
                    items.append(dict(k=[[kn[hp:hp + 64, pr, kt * 128:(kt + 1) * 128], kr[:, kt * 128:(kt + 1) * 128]]],
                                      v=[vm[:, kt, h, :]] * ncol, bias=None, res=[rkn, rkr, rvm]))
                attend(A, ncol, True, qa, [rqn, rqr], items, scale, [o[:, c, h * 64:(h + 1) * 64] for c in range(ncol)], ro)
            P.dma("sp", o_d[q0:q0 + ncol * 128, :].rearrange("(c p) f -> p c f", p=128), o[:, 0:ncol, :], do, reads=[ro], writes=[rout], final=True)
        P.pop_scope()
    P.finish()
    P.close()
    return P


D = 1024
KC = 8
EPS = 1e-6
NTOK = 2304
NQ = 2048
NCX = 256
NCOL = 2728
NA0, SW0, ML0, SS0 = 0, 768, 1280, 1696
SBROWS = 1312
R_NAK, R_SWK, R_CKV, R_XBC, R_KR = 0, 256, 384, 512, 1280
YCH = 2816
G4 = [[0, 1, 2, 3], [4, 5, 6, 7]]
LSEQ = 8448
NCH = LSEQ // 128


class RotView:
    def __init__(self, rots):
        self.t, self.r, self.d = [], [], []
        for ro in rots:
            self.t += ro.t; self.r += ro.r; self.d += ro.d
        self.i = -1
        self.n = len(self.t)

    def next(self):
        self.i = (self.i + 1) % self.n
        return self.t[self.i], self.r[self.i], self.d[self.i]


def allgather(P, src, dst, reads, writes):
    nc = P.nc
    if "cc" not in P.sem:
        P.sem["cc"] = P.root.enter_context(nc.semaphore("s_cc"))
        P.cnt["cc"] = 0
    P._emit_waits("pool", P._deps("pool", reads, writes))
    ins = nc.gpsimd.collective_compute("AllGather", mybir.AluOpType.bypass, replica_groups=G4, ins=[src.opt()], outs=[dst.opt()])
    P.cnt["cc"] += 1
    ins.then_inc(P.sem["cc"])
    P._record(("cc", P.cnt["cc"], "dma"), reads, writes)
    nc.gpsimd.wait_ge(P.sem["cc"], P.cnt["cc"])
    P.seen["pool"]["cc"] = P.cnt["cc"]


def seg_lat(t0, n):
    out = []
    t = t0
    while t < t0 + n:
        r = t // 2048
        ln = min(t0 + n, (r + 1) * 2048) - t
        out.append((r, t - r * 2048, ln, t - t0))
        t += ln
    return out


def build_M(nlayers=4, final_layer=3, NLW=4):
    P = Prog()
    nc = P.nc
    X = lambda name, shape: P.dram(name, shape, F32, "ExternalInput")
    hT0_d = X("hT0", [D, NTOK]); cv_d = X("cv", [128, KC, 2]); flg_d = X("flg", [128, 16])
    selx_d = X("selx", [128, 2, 64]); selb_d = X("selb", [128, 2, 128])
    w_in_d = X("w_in", [NLW, D, NCOL]); w_mod_d = X("w_mod", [NLW, D, 6 * D]); bmod_d = X("bmod", [NLW, 128, 48])
    g1_d = X("g1", [NLW, 128, KC]); gv_d = X("gv", [NLW, 128, 20])
    wo_d = X("w_out", [NLW, D, D]); w1_d = X("w1", [NLW, D, 4 * D]); w2_d = X("w2", [NLW, 4 * D, D])
    nab_d = X("na_bias", [NLW, 5, 128, 7, 512]); swb_d = X("sw_bias", [128, 4, 128]); swk_d = X("sw_sink", [NLW, 128, 4])
    swcs_d = X("sw_cs", [2, 64, 2304]); mkcs_d = X("m_kcs", [2, 32, LSEQ]); mqcs_d = X("m_qcs", [2, 32, NTOK])
    mgkv_d = X("m_gkv", [NLW, 128, 1]); mgq_d = X("m_gq", [NLW, 128, 2])
    mwkn_d = X("m_wkn", [NLW, 128, 256]); mwkv_d = X("m_wkv", [NLW, 128, 256])
    mwqn_d = X("m_wqn", [NLW, 256, 256]); mwqr_d = X("m_wqr", [NLW, 256, 128]); mwqrs_d = X("m_wqrs", [NLW, 256, 128])
    scw_d = X("s_cw", [NLW, 128, 3, 6]); spar_d = X("s_par", [NLW, 128, 8]); su_d = X("s_u", [2, 128, 128]); id_d = X("ident", [128, 128])
    out_d = P.dram("out", [D, NQ], F32, "ExternalOutput")
    S_ = lambda name, shape: nc.dram_tensor(name, list(shape), F32).ap()
    hT_s = [S_("hTa", [D, NTOK]), S_("hTb", [D, NTOK])]
    pT_d = S_("pT", [NCOL, NTOK]); vtok_d = S_("vtok", [NTOK, 392])
    sb_nr = [64] * 20 + [32]
    SBc = [S_(f"SBc{c}", [sb_nr[c], NTOK]) for c in range(21)]
    RBc = [S_(f"RBc{c}", [4 * sb_nr[c], NTOK]) for c in range(21)]
    vg_nr = [512] * 4 + [256]
    VSc = [S_(f"VSc{i}", [vg_nr[i], 392]) for i in range(5)]
    VGc = [S_(f"VGc{i}", [4 * vg_nr[i], 392]) for i in range(5)]
    YSc = [S_(f"YSc{i}", [64, YCH]) for i in range(3)]
    YGc = [S_(f"YGc{i}", [256, YCH]) for i in range(3)]
    h2_d = S_("h2", [D, NTOK])

    def rbuf(r, row0, nrows, c0, c1):
        c = row0 // 64
        assert row0 + nrows <= 64 * c + sb_nr[c], (row0, nrows)
        o = r * sb_nr[c] + row0 - 64 * c
        return RBc[c][o:o + nrows, c0:c1]

    def vgbuf(r, row0, nrows, c0, c1):
        i = row0 // 512
        assert row0 + nrows <= 512 * i + vg_nr[i], (row0, nrows)
        o = r * vg_nr[i] + row0 - 512 * i
        return VGc[i][o:o + nrows, c0:c1]
    r_pT, r_vtok, r_SB, r_RB1, r_VG, r_YS, r_YG, r_h2, r_RB2 = (P.res() for _ in range(9))
    r_hT = [P.res(), P.res()]

    A = AttnCtx(P)
    extra = Rot(P, "xps", 1, [128, 512], F32, psum=True)
    gen = RotView([A.misc, extra, A.panels, A.accs])
    ones = P.sb("ones", [128, 128], BF16); rones = P.res()
    P.op("dve", lambda: nc.vector.memset(ones[:], 1.0), writes=[rones])
    onesf = P.sb("onesf", [128, 128], F32); ronesf = P.res()
    P.op("pool", lambda: nc.gpsimd.memset(onesf[:], 1.0), writes=[ronesf])
    dc = P.dsem("const")
    ident = P.sb("ident", [128, 128], F32); rid = P.res(); P.dma("sp", ident[:], id_d, dc, writes=[rid])
    flg = P.sb("flg", [128, 16], F32); rflg = P.res(); P.dma("sp", flg[:], flg_d, dc, writes=[rflg])
    U = P.sb("U", [128, 2, 128], F32); rU = P.res()
    P.dma("sp", U[:, 0, :], su_d[0], dc, writes=[rU]); P.dma("sp", U[:, 1, :], su_d[1], dc, writes=[rU])
    selx = P.sb("selx", [128, 2, 64], F32); selb = P.sb("selb", [128, 2, 128], F32); rsel = P.res()
    P.dma("sp", selx[:], selx_d, dc, writes=[rsel]); P.dma("sp", selb[:], selb_d, dc, writes=[rsel])
    cvs = P.sb("cvs", [128, KC, 2], F32); rcv = P.res(); P.dma("sp", cvs[:], cv_d, dc, writes=[rcv])
    ca = P.sb("ca", [128, KC, 2], F32); rca = P.res()
    P.op("act", lambda: nc.scalar.activation(out=ca[:], in_=cvs[:], func=AF.Silu), reads=[rcv], writes=[rca])
    moL = [P.sb("mo", [128, 48, 2], F32) for _ in range(2)]; rmoL = [P.res(), P.res()]
    gs1L = [P.sb("gs1", [128, KC, 2], F32) for _ in range(2)]; gs2L = [P.sb("gs2", [128, KC, 2], F32) for _ in range(2)]; rgsL = [P.res(), P.res()]
    gvsL = [P.sb("gvs", [128, 28], F32) for _ in range(2)]; rgvL = [P.res(), P.res()]
    roT = P.res()
    evi = [0]

    def evac(dst, src, reads, writes):
        evi[0] += 1
        if evi[0] % 2 == 0:
            P.op("act", lambda: nc.scalar.copy(out=dst, in_=src), reads=reads, writes=writes)
        else:
            P.op("dve", lambda: nc.vector.tensor_copy(out=dst, in_=src), reads=reads, writes=writes)

    def rstd_of(x, rx, kc, n, r, rr, s, rs, feat):
        for k in range(kc):
            P.op("act", lambda k=k: nc.scalar.activation(out=s[:, k, :n], in_=x[:, k, :n], func=AF.Square), reads=[rx], writes=[rs])
        pt, rpt, _ = gen.next()
        for k in range(kc):
            P.op("pe", lambda k=k: nc.tensor.matmul(pt[:, :n], lhsT=ones[:], rhs=s[:, k, :n], start=(k == 0), stop=(k == kc - 1)), reads=[rones, rs], writes=[rpt])
        P.op("act", lambda: nc.scalar.activation(out=r[:, :n], in_=pt[:, :n], func=AF.Sqrt, scale=1.0 / feat, bias=EPS), reads=[rpt], writes=[rr])
        P.op("dve", lambda: nc.vector.reciprocal(out=r[:, :n], in_=r[:, :n]), reads=[rr], writes=[rr])

    def emit_mod(l):
        mo, rmo, gs1, gs2, rgs, gvs, rgv = moL[l % 2], rmoL[l % 2], gs1L[l % 2], gs2L[l % 2], rgsL[l % 2], gvsL[l % 2], rgvL[l % 2]
        P.push_scope()
        dm = P.dsem(f"mod{l}")
        bm = P.sb("bm", [128, 48], F32); rbm = P.res()
        P.dma("sp", bm[:], bmod_d[l], dm, writes=[rbm])
        P.dma("sp", gvs[:, 0:20], gv_d[l], dm, writes=[rgv]); P.dma("sp", gvs[:, 20:28], g1_d[l], dm, writes=[rgv])
        wm = Rot(P, "wm", 2, [128, KC, 1024], F32, dma=True)
        for grp in range(6):
            w, rw_, dw_ = wm.next()
            for k in range(KC):
                P.dma("sp", w[:, k, :], w_mod_d[l][k * 128:(k + 1) * 128, grp * 1024:(grp + 1) * 1024], dw_, writes=[rw_])
            pt, rpt, _ = gen.next()
            for cb in range(8):
                for k in range(KC):
                    P.op("pe", lambda cb=cb, k=k: nc.tensor.matmul(pt[:, cb * 2:cb * 2 + 2], lhsT=w[:, k, cb * 128:(cb + 1) * 128], rhs=ca[:, k, :],
                                                                 start=(k == 0), stop=(k == KC - 1)), reads=[rw_, rca], writes=[rpt])
            for j in range(2):
                P.op("dve", lambda j=j: nc.vector.tensor_tensor(out=mo[:, grp * 8:(grp + 1) * 8, j], in0=pt[:, j:16:2], in1=bm[:, grp * 8:(grp + 1) * 8], op=ALU.add),
                     reads=[rpt, rbm], writes=[rmo])
        for j in range(2):
            P.op("dve", lambda j=j: nc.vector.scalar_tensor_tensor(out=gs1[:, :, j], in0=mo[:, 8:16, j], scalar=1.0, in1=gvs[:, 20:28], op0=ALU.add, op1=ALU.mult),
                 reads=[rmo, rgv], writes=[rgs])
            P.op("dve", lambda j=j: nc.vector.scalar_tensor_tensor(out=gs2[:, :, j], in0=mo[:, 32:40, j], scalar=1.0, in1=gvs[:, 0:8], op0=ALU.add, op1=ALU.mult),
                 reads=[rmo, rgv], writes=[rgs])
        P.pop_scope()


    for l in range(nlayers):
        final = (l == final_layer)
        hT_d, r_hin = (hT0_d, None) if l == 0 else (hT_s[(l - 1) % 2], r_hT[(l - 1) % 2])
        hTn_d, r_hn = hT_s[l % 2], r_hT[l % 2]

        mo, rmo, gs1, gs2, rgs, gvs, rgv = moL[l % 2], rmoL[l % 2], gs1L[l % 2], gs2L[l % 2], rgsL[l % 2], gvsL[l % 2], rgvL[l % 2]
        if l == 0:
            emit_mod(0)
        P.push_scope()
        wsb = P.sb("wsb", [128, KC, NCOL], BF16); rw = P.res(); dw = P.dsem(f"w{l}")
        wv = P.sb("wv", [128, KC, 392], BF16)
        for k in range(KC):
            for c0 in range(0, NCOL, 2048):
                c1 = min(NCOL, c0 + 2048)
                P.dma("pool", wsb[:, k, c0:c1], w_in_d[l][k * 128:(k + 1) * 128, c0:c1], dw, writes=[rw])
            P.dma("pool", wv[:, k, 0:256], w_in_d[l][k * 128:(k + 1) * 128, 512:768], dw, writes=[rw])
            P.dma("pool", wv[:, k, 256:384], w_in_d[l][k * 128:(k + 1) * 128, SW0 + 384:SW0 + 512], dw, writes=[rw])
            P.dma("pool", wv[:, k, 384:392], w_in_d[l][k * 128:(k + 1) * 128, SS0 + 1024:SS0 + 1032], dw, writes=[rw])
        hin = Rot(P, "hin", 2, [128, KC, 512], F32, dma=True)
        sq = Rot(P, "sq", 2, [128, KC, 512], BF16)
        xm = Rot(P, "xm", 2, [128, KC, 512], BF16)
        tt = Rot(P, "tt", 2, [128, 512], F32)
        rms = Rot(P, "rms", 2, [128, 512], F32)
        ost = Rot(P, "ost", 4, [128, 512], F32, dma=True)
        blocks = [(t0, 512, 0) for t0 in range(0, NQ, 512)] + [(NQ, 256, 1)]
        ncb = (NCOL + 127) // 128
        def a_prep(t0, n, j):
            h, rh, dh = hin.next()
            P.dma("sp", h[:, :, :n], hT_d.rearrange("(k p) t -> p k t", p=128)[:, :, t0:t0 + n], dh, reads=[r_hin], writes=[rh])
            s, rs, _ = sq.next(); r, rr, _ = rms.next()
            rstd_of(h, rh, KC, n, r, rr, s, rs, D)
            x, rx, _ = xm.next()
            for k in range(KC):
                t, rt, _ = tt.next()
                P.op("dve", lambda k=k, t=t: nc.vector.tensor_tensor(out=t[:, :n], in0=h[:, k, :n], in1=r[:, :n], op=ALU.mult), reads=[rh, rr], writes=[rt])
                P.op("act", lambda k=k, t=t: nc.scalar.activation(out=x[:, k, :n], in_=t[:, :n], func=AF.Identity, scale=gs1[:, k, j:j + 1], bias=mo[:, k, j:j + 1]),
                     reads=[rt, rgs, rmo], writes=[rx])
            return x, rx

        def a_main(t0, n, j, x, rx):
            for cb in range(ncb):
                c0 = cb * 128
                m = min(128, NCOL - c0)
                pt, rpt, _ = gen.next()
                for k in range(KC):
                    P.op("pe", lambda k=k, pt=pt: nc.tensor.matmul(pt[:m, :n], lhsT=wsb[:, k, c0:c0 + m], rhs=x[:, k, :n], start=(k == 0), stop=(k == KC - 1)),
                         reads=[rw, rx], writes=[rpt])
                o, ro, do = ost.next()
                evac(o[:m, :n], pt[:m, :n], [rpt], [ro])
                P.dma("sp", pT_d[c0:c0 + m, t0:t0 + n], o[:m, :n], do, reads=[ro], writes=[r_pT])
            for ti in range(n // 128):
                pt, rpt, _ = gen.next()
                for k in range(KC):
                    P.op("pe", lambda k=k, pt=pt: nc.tensor.matmul(pt[:, 0:392], lhsT=x[:, k, ti * 128:(ti + 1) * 128], rhs=wv[:, k, :], start=(k == 0), stop=(k == KC - 1)),
                         reads=[rw, rx], writes=[rpt])
                o, ro, do = ost.next()
                evac(o[:, 0:392], pt[:, 0:392], [rpt], [ro])
                P.dma("sp", vtok_d[t0 + ti * 128:t0 + (ti + 1) * 128, :], o[:, 0:392], do, reads=[ro], writes=[r_vtok])
        pend_ = None
        for blk in blocks + [None]:
            cur_ = (blk, a_prep(*blk)) if blk is not None else None
            if pend_ is not None:
                a_main(*pend_[0], *pend_[1])
            pend_ = cur_
        P.pop_scope()

        dx = P.dsem(f"x1_{l}")
        for (d0, s0, nr) in ((R_NAK, 256, 256), (R_SWK, SW0 + 256, 128), (R_CKV, ML0 + 256, 128), (R_XBC, SS0 + 256, 768), (R_KR, ML0 + 384, 32)):
            for rr0 in range(0, nr, 64):
                n_ = min(64, nr - rr0)
                P.dma("sp", SBc[(d0 + rr0) // 64][0:n_, :], pT_d[s0 + rr0:s0 + rr0 + n_, :], dx, reads=[r_pT], writes=[r_SB])
        for i in range(5):
            P.dma("sp", VSc[i][:, :], vtok_d[512 * i:512 * i + vg_nr[i], :], dx, reads=[r_vtok], writes=[r_SB])
        P.barrier()
        for c in range(8, 20):
            allgather(P, SBc[c], RBc[c], [r_SB], [r_RB1])
        for i in range(5):
            allgather(P, VSc[i], VGc[i], [r_SB], [r_VG])
        if l + 1 < nlayers:
            emit_mod(l + 1)
        for c in list(range(8)) + [20]:
            allgather(P, SBc[c], RBc[c], [r_SB], [r_RB2])

        P.push_scope()
        ds_ = P.dsem(f"ssm{l}")
        cw = P.sb("cw", [128, 3, 6], F32); rcw = P.res(); P.dma("sp", cw[:], scw_d[l], ds_, writes=[rcw])
        par = P.sb("par", [128, 8], F32); rpar = P.res(); P.dma("sp", par[:], spar_d[l], ds_, writes=[rpar])
        dtall = P.sb("dtall", [128, NCH, 8], F32); rdta_ = P.res()
        P.dma("sp", dtall[:, 0:2, :], vgbuf(0, NQ, 256, 384, 392).rearrange("(c p) e -> p c e", p=128), ds_, reads=[r_VG], writes=[rdta_])
        for r in range(4):
            for i in range(4):
                P.dma("sp", dtall[:, 2 + 16 * r + 4 * i:2 + 16 * r + 4 * (i + 1), :], vgbuf(r, 512 * i, 512, 384, 392).rearrange("(c p) e -> p c e", p=128), ds_,
                      reads=[r_VG], writes=[rdta_])
        dt = P.sb("dt", [128, NCH, 2], F32); rdt = P.res()
        dta = P.sb("dta", [128, NCH, 2], F32); rdta = P.res()
        for d in range(2):
            P.op("dve", lambda d=d: nc.vector.tensor_scalar(out=dt[:, :, d], in0=dtall[:, :, d * 4], scalar1=flg[:, 0:1], scalar2=None, op0=ALU.mult),
                 reads=[rdta_, rflg], writes=[rdt])
            for hh in range(1, 4):
                P.op("dve", lambda d=d, hh=hh: nc.vector.scalar_tensor_tensor(out=dt[:, :, d], in0=dtall[:, :, d * 4 + hh], scalar=flg[:, hh:hh + 1], in1=dt[:, :, d],
                                                                            op0=ALU.mult, op1=ALU.add), reads=[rdta_, rflg, rdt], writes=[rdt])
        av = P.sb("av", [128, 2], F32); rav = P.res()
        P.op("act", lambda: nc.scalar.activation(out=av[:], in_=par[:, 2:4], func=AF.Exp), reads=[rpar], writes=[rav])
        P.op("dve", lambda: nc.vector.tensor_scalar(out=av[:], in0=av[:], scalar1=-1.0, scalar2=None, op0=ALU.mult), reads=[rav], writes=[rav])
        for d in range(2):
            P.op("act", lambda d=d: nc.scalar.activation(out=dt[:, :, d], in_=dt[:, :, d], func=AF.Exp, bias=par[:, d:d + 1]), reads=[rdt, rpar], writes=[rdt])
        for d in range(2):
            P.op("act", lambda d=d: nc.scalar.activation(out=dt[:, :, d], in_=dt[:, :, d], func=AF.Ln, bias=1.0), reads=[rdt], writes=[rdt])
        for d in range(2):
            P.op("dve", lambda d=d: nc.vector.tensor_scalar(out=dta[:, :, d], in0=dt[:, :, d], scalar1=av[:, d:d + 1], scalar2=None, op0=ALU.mult),
                 reads=[rdt, rav], writes=[rdta])
        xT = P.sb("xT", [64, LSEQ], F32); rx_ = P.res()
        bT = P.sb("bT", [128, LSEQ], BF16); rb_ = P.res()
        cT = P.sb("cT", [128, LSEQ], BF16); rc_ = P.res()
        yb = P.sb("yb", [64, LSEQ], F32); ry_ = P.res()
        P.push_scope()
        raw = Rot(P, "raw", 6, [128, 2, 512], F32, dma=True)
        cacc = Rot(P, "cacc", 3, [128, 508], F32)
        CB = 508
        for gi, (row0, npart, sel, dst, rdst) in enumerate(((R_XBC, 64, selx, xT, rx_), (R_XBC + 256, 128, selb, bT, rb_), (R_XBC + 512, 128, selb, cT, rc_))):
            for (s0, sl, is_ctx) in ((0, 256, True), (256, 8192, False)):
                for t0 in range(0, sl, CB):
                    n = min(CB, sl - t0)
                    rt_, rr_, dr_ = raw.next()
                    lo = max(t0 - 2, 0); hi = min(t0 + n + 2, sl)
                    if lo > t0 - 2 or hi < t0 + n + 2:
                        P.op("dve", lambda rt_=rt_: nc.vector.memset(rt_[:], 0.0), writes=[rr_])
                    if is_ctx:
                        segs = [(0, NQ + lo, hi - lo, lo - (t0 - 2))]
                    else:
                        segs = [(r, ls, ln, do_ + lo - (t0 - 2)) for (r, ls, ln, do_) in seg_lat(lo, hi - lo)]
                    for (r, ls, ln, do_) in segs:
                        for kc in range(2):
                            for hf in range(2):
                                P.dma("sp", rt_[hf * 64:(hf + 1) * 64, kc, do_:do_ + ln], rbuf(r, row0 + kc * 128 + hf * 64, 64, ls, ls + ln), dr_,
                                      reads=[r_RB1], writes=[rr_])
                    pt, rpt, _ = gen.next()
                    for kc in range(2):
                        P.op("pe", lambda kc=kc, pt=pt: nc.tensor.matmul(pt[:npart, 0:n + 4], lhsT=sel[:, kc, :], rhs=rt_[:, kc, 0:n + 4], start=(kc == 0), stop=(kc == 1)),
                             reads=[rsel, rr_], writes=[rpt])
                    a, ra, _ = cacc.next()
                    P.op("dve", lambda: nc.vector.tensor_scalar(out=a[:npart, :n], in0=pt[:npart, 0:n], scalar1=cw[:npart, gi, 0:1], scalar2=None, op0=ALU.mult),
                         reads=[rpt, rcw], writes=[ra])
                    for k in range(1, 5):
                        P.op("dve", lambda k=k: nc.vector.scalar_tensor_tensor(out=a[:npart, :n], in0=pt[:npart, k:k + n], scalar=cw[:npart, gi, k:k + 1], in1=a[:npart, :n],
                                                                             op0=ALU.mult, op1=ALU.add), reads=[rpt, rcw, ra], writes=[ra])
                    P.op("act", lambda: nc.scalar.activation(out=dst[:npart, s0 + t0:s0 + t0 + n], in_=a[:npart, :n], func=AF.Silu, bias=cw[:npart, gi, 5:6]),
                         reads=[ra, rcw], writes=[rdst])
        P.pop_scope()
        xtokA = P.sb("xtokA", [128, NCH, 64], F32); rxt = P.res()
        btokA = P.sb("btokA", [128, NCH, 128], BF16); rbtk = P.res()
        gt = Rot(P, "gt", 2, [128, 128], F32)
        for c in range(NCH):
            sl = slice(c * 128, (c + 1) * 128)
            px, rpx, _ = gen.next()
            P.op("pe", lambda: nc.tensor.transpose(px[:, 0:64], xT[:, sl], ident[0:64, 0:64]), reads=[rx_, rid], writes=[rpx])
            evac(xtokA[:, c, :], px[:, 0:64], [rpx], [rxt])
            btf, rbtf, _ = gt.next()
            P.op("act", lambda: nc.scalar.copy(out=btf[:], in_=bT[:, sl]), reads=[rb_], writes=[rbtf])
            pb, rpb, _ = gen.next()
            P.op("pe", lambda: nc.tensor.transpose(pb[:, 0:128], btf[:], ident[:]), reads=[rbtf, rid], writes=[rpb])
            evac(btokA[:, c, :], pb[:, 0:128], [rpb], [rbtk])
        ryc = [P.res() for _ in range(NCH)]
        for c in range(NCH):
            sl = slice(c * 128, (c + 1) * 128)
            P.op("pool", lambda: nc.gpsimd.tensor_scalar(out=yb[:, sl], in0=xT[:, sl], scalar1=par[0:64, 4:5], scalar2=None, op0=ALU.mult), reads=[rx_, rpar], writes=[ryc[c]])
        hs = [P.sb("hs", [128, 64], F32) for _ in range(2)]; rhs_ = [P.res(), P.res()]
        hsb = [P.sb("hsb", [128, 64], BF16) for _ in range(2)]; rhsb = [P.res(), P.res()]
        NS = 4
        dtab = Rot(P, "dtab", NS, [128, 128], F32); ccol = Rot(P, "ccol", 2 * NS, [128, 4], F32)
        dd = Rot(P, "dd", NS, [128, 128], F32); ee = Rot(P, "ee", NS, [128, 128], F32)
        gm = Rot(P, "gm", NS, [128, 128], F32); mt = Rot(P, "mt", NS, [128, 128], BF16)
        ecr = Rot(P, "ecr", NS, [128, 128], F32); cs = Rot(P, "cs", NS, [128, 128], BF16)
        xdt = Rot(P, "xdt", NS, [128, 64], BF16); xdd = Rot(P, "xdd", NS, [128, 64], BF16)
        sst = Rot(P, "sst", NS, [128, 64], F32); hsr = Rot(P, "hsr", NS, [128, 64], BF16)
        hcur = [None, None]
        for d in range(2):
            P.op("dve", lambda d=d: nc.vector.memset(hs[d][:], 0.0), writes=[rhs_[d]])
            P.op("dve", lambda d=d: nc.vector.memset(hsb[d][:], 0.0), writes=[rhsb[d]])
        orders = [list(range(NCH)), [1, 0] + list(range(NCH - 1, 1, -1))]

        def p1(d, c):
            sl = slice(c * 128, (c + 1) * 128)
            tot_col = 127 if d == 0 else 0
            da, rda, _ = dtab.next()
            P.op("pool", lambda: nc.gpsimd.tensor_scalar(out=da[:], in0=onesf[:], scalar1=dta[:, c, d:d + 1], scalar2=None, op0=ALU.mult), reads=[ronesf, rdta], writes=[rda])
            pcr, rpcr, _ = gen.next()
            P.op("pe", lambda: nc.tensor.matmul(pcr[:, 0:128], lhsT=da[:], rhs=U[:, d, :], start=True, stop=True), reads=[rda, rU], writes=[rpcr])
            P.op("pe", lambda: nc.tensor.matmul(pcr[:, 128:130], lhsT=U[:, d, :], rhs=dta[:, c, :], start=True, stop=True), reads=[rU, rdta], writes=[rpcr])
            cc, rcc, _ = ccol.next()
            P.op("dve", lambda: nc.vector.tensor_copy(out=cc[:, 0:1], in_=pcr[:, 128 + d:129 + d]), reads=[rpcr], writes=[rcc])
            P.op("dve", lambda: nc.vector.tensor_copy(out=cc[:, 1:2], in_=pcr[:, tot_col:tot_col + 1]), reads=[rpcr], writes=[rcc])
            P.op("act", lambda: nc.scalar.activation(out=cc[:, 3:4], in_=cc[:, 1:2], func=AF.Exp), reads=[rcc], writes=[rcc])
            dt_, rdd, _ = dd.next()
            P.op("dve", lambda: nc.vector.tensor_scalar(out=dt_[:], in0=pcr[:, 0:128], scalar1=cc[:, 0:1], scalar2=0.0, op0=ALU.subtract, op1=ALU.min), reads=[rpcr, rcc], writes=[rdd])
            e, re_, _ = ee.next()
            P.op("act", lambda: nc.scalar.activation(out=e[:], in_=dt_[:], func=AF.Exp), reads=[rdd], writes=[re_])
            er, rer, _ = ecr.next()
            P.op("act", lambda: nc.scalar.activation(out=er[:], in_=pcr[:, 0:128], func=AF.Exp), reads=[rpcr], writes=[rer])
            pg, rpg, _ = gen.next()
            P.op("pe", lambda: nc.tensor.matmul(pg[:, 0:128], lhsT=bT[:, sl], rhs=cT[:, sl], start=True, stop=True), reads=[rb_, rc_], writes=[rpg])
            g, rg_, _ = gm.next()
            P.op("dve", lambda: nc.vector.tensor_tensor(out=g[:], in0=pg[:, 0:128], in1=U[:, d, :], op=ALU.mult), reads=[rpg, rU], writes=[rg_])
            m, rm, _ = mt.next()
            P.op("pool", lambda: nc.gpsimd.tensor_tensor(out=m[:], in0=g[:], in1=e[:], op=ALU.mult), reads=[rg_, re_], writes=[rm])
            csb, rcs, _ = cs.next()
            P.op("pool", lambda: nc.gpsimd.tensor_tensor(out=csb[:], in0=cT[:, sl], in1=er[:], op=ALU.mult), reads=[rc_, rer], writes=[rcs])
            xd, rxd, _ = xdt.next()
            P.op("dve", lambda: nc.vector.tensor_scalar(out=xd[:], in0=xtokA[:, c, :], scalar1=dt[:, c, d:d + 1], scalar2=None, op0=ALU.mult), reads=[rxt, rdt], writes=[rxd])
            de, rde, _ = ccol.next()
            P.op("act", lambda: nc.scalar.activation(out=de[:, 0:1], in_=cc[:, 0:1], func=AF.Exp, scale=-1.0, bias=cc[:, 1:2]), reads=[rcc], writes=[rde])
            P.op("dve", lambda: nc.vector.tensor_tensor(out=de[:, 1:2], in0=de[:, 0:1], in1=dt[:, c, d:d + 1], op=ALU.mult), reads=[rde, rdt], writes=[rde])
            xe, rxe, _ = xdd.next()
            P.op("dve", lambda: nc.vector.tensor_scalar(out=xe[:], in0=xtokA[:, c, :], scalar1=de[:, 1:2], scalar2=None, op0=ALU.mult), reads=[rxt, rde], writes=[rxe])
            pst, rpst, _ = gen.next()
            P.op("pe", lambda: nc.tensor.matmul(pst[:, 0:64], lhsT=btokA[:, c, :], rhs=xe[:], start=True, stop=True), reads=[rbtk, rxe], writes=[rpst])
            st_, rst_, _ = sst.next()
            P.op("act", lambda: nc.scalar.copy(out=st_[:], in_=pst[:, 0:64]), reads=[rpst], writes=[rst_])
            return (cc, rcc, m, rm, csb, rcs, xd, rxd, st_, rst_)

        def p2(d, c, hnd):
            cc, rcc, m, rm, csb, rcs, xd, rxd, st_, rst_ = hnd
            sl = slice(c * 128, (c + 1) * 128)
            hb, rhb = hcur[d] if hcur[d] is not None else (hsb[d], rhsb[d])
            py, rpy, _ = gen.next()
            P.op("pe", lambda: nc.tensor.matmul(py[0:64, 0:128], lhsT=xd[:], rhs=m[:], start=True, stop=False), reads=[rxd, rm], writes=[rpy])
            P.op("pe", lambda: nc.tensor.matmul(py[0:64, 0:128], lhsT=hb[:], rhs=csb[:], start=False, stop=True), reads=[rhb, rcs], writes=[rpy])
            P.op("dve", lambda: nc.vector.scalar_tensor_tensor(out=hs[d][:], in0=hs[d][:], scalar=cc[:, 3:4], in1=st_[:], op0=ALU.mult, op1=ALU.add),
                 reads=[rhs_[d], rcc, rst_], writes=[rhs_[d]])
            hn, rhn, _ = hsr.next()
            P.op("dve", lambda: nc.vector.tensor_copy(out=hn[:], in_=hs[d][:]), reads=[rhs_[d]], writes=[rhn])
            hcur[d] = (hn, rhn)
            P.op("dve", lambda: nc.vector.tensor_tensor(out=yb[:, sl], in0=yb[:, sl], in1=py[0:64, 0:128], op=ALU.add), reads=[ryc[c], rpy], writes=[ryc[c]])

        pend = None
        for s_ in range(NCH + 1):
            cur = None
            if s_ < NCH:
                cur = [(d, orders[d][s_], p1(d, orders[d][s_])) for d in range(2)]
            if pend is not None:
                for (d, c, hnd) in pend:
                    p2(d, c, hnd)
            pend = cur
        ry_all = ryc
        for i in range(3):
            P.dma("sp", YSc[i][:, :], yb[:, i * YCH:(i + 1) * YCH], ds_, reads=ry_all, writes=[r_YS])
        P.pop_scope()
        for i in range(3):
            allgather(P, YSc[i], YGc[i], [r_YS], [r_YG])

        P.push_scope()
        oTs = P.sb("oTs", [128, 6, NTOK], BF16)
        P.push_scope()
        ost = Rot(P, "tost", 2, [128, 4, 256], F32)

        def emit_oT(o, ro, ncol, tile0, chunk0, mla):
            for c in range(ncol):
                for half in range(2):
                    pt, rpt, _ = A.misc.next()
                    P.op("pe", lambda c=c, half=half, pt=pt: nc.tensor.transpose(pt[:, 0:128], o[:, c, half * 128:(half + 1) * 128], ident[:]), reads=[ro, rid], writes=[rpt])
                    evac(oTs[:, chunk0 + half, (tile0 + c) * 128:(tile0 + c + 1) * 128], pt[:, 0:128], [rpt], [roT])

        P.push_scope()
        EXT = 2816
        dn = P.dsem(f"na{l}")
        q = P.sb("naq", [64, 4, NQ], BF16); rq = P.res()
        k = P.sb("nak", [64, 4, EXT], BF16); rk = P.res()
        v = P.sb("nav", [128, EXT // 128, 4, 65], BF16); rv = P.res()
        qc = P.sb("naqc", [64, 4, NCX], BF16); kc_ = P.sb("nakc", [64, 4, NCX], BF16); vc = P.sb("navc", [128, 2, 4, 65], BF16); rc = P.res()
        P.op("dve", lambda: nc.vector.memset(v[:], 1.0), writes=[rv])
        P.op("dve", lambda: nc.vector.memset(vc[:], 1.0), writes=[rc])
        hk = Rot(P, "hk", 2, [64, 4, 384], F32, dma=True)
        hacc = Rot(P, "hacc", 2, [64, 384], F32)
        hv = Rot(P, "hv", 2, [128, 4, 768], F32, dma=True)
        hvacc = Rot(P, "hvacc", 2, [128, 768], F32)

        def halo_k(dst, row0, npart, width, side, hkr, haccr, rowperm=None):
            t_, rt_, dt__ = hkr.next()
            c0 = NQ - width if side == 0 else 0
            for r in range(4):
                for (d0_, s0_, nr_) in (rowperm or ((0, 0, npart),)):
                    P.dma("sp", t_[d0_:d0_ + nr_, r, :width], rbuf(r, row0 + s0_, nr_, c0, c0 + width), dt__, reads=[r_RB2], writes=[rt_])
            a_, ra_, _ = haccr.next()
            f0 = 4 if side == 0 else 8
            P.op("dve", lambda: nc.vector.tensor_scalar(out=a_[:npart, :width], in0=t_[:npart, 0, :width], scalar1=flg[:npart, f0:f0 + 1], scalar2=None, op0=ALU.mult),
                 reads=[rt_, rflg], writes=[ra_])
            for r in range(1, 4):
                P.op("dve", lambda r=r: nc.vector.scalar_tensor_tensor(out=a_[:npart, :width], in0=t_[:npart, r, :width], scalar=flg[:npart, f0 + r:f0 + r + 1], in1=a_[:npart, :width],
                                                                     op0=ALU.mult, op1=ALU.add), reads=[rt_, rflg, ra_], writes=[ra_])
            return a_, ra_

        def halo_v(col0, ncolv, ntile, side, hvr, hvaccr):
            t_, rt_, dt__ = hvr.next()
            r0 = NQ - ntile * 128 if side == 0 else 0
            for r in range(4):
                P.dma("sp", t_[:, r, 0:ntile * ncolv].rearrange("p (t c) -> p t c", c=ncolv),
                      vgbuf(r, r0, ntile * 128, col0, col0 + ncolv).rearrange("(t p) c -> p t c", p=128), dt__, reads=[r_VG], writes=[rt_])
            a_, ra_, _ = hvaccr.next()
            f0 = 4 if side == 0 else 8
            w_ = ntile * ncolv
            P.op("dve", lambda: nc.vector.tensor_scalar(out=a_[:, :w_], in0=t_[:, 0, :w_], scalar1=flg[:, f0:f0 + 1], scalar2=None, op0=ALU.mult), reads=[rt_, rflg], writes=[ra_])
            for r in range(1, 4):
                P.op("dve", lambda r=r: nc.vector.scalar_tensor_tensor(out=a_[:, :w_], in0=t_[:, r, :w_], scalar=flg[:, f0 + r:f0 + r + 1], in1=a_[:, :w_], op0=ALU.mult, op1=ALU.add),
                     reads=[rt_, rflg, ra_], writes=[ra_])
            return a_, ra_

        for h in range(4):
            P.dma("pool", q[:, h, :], pT_d[h * 64:(h + 1) * 64, 0:NQ], dn, reads=[r_pT], writes=[rq])
            P.dma("pool", k[:, h, 384:384 + NQ], pT_d[256 + h * 64:256 + (h + 1) * 64, 0:NQ], dn, reads=[r_pT], writes=[rk])
            P.dma("pool", qc[:, h, :], pT_d[h * 64:(h + 1) * 64, NQ:NTOK], dn, reads=[r_pT], writes=[rc])
            P.dma("pool", kc_[:, h, :], pT_d[256 + h * 64:256 + (h + 1) * 64, NQ:NTOK], dn, reads=[r_pT], writes=[rc])
            for side in range(2):
                a_, ra_ = halo_k(None, R_NAK + h * 64, 64, 384, side, hk, hacc)
                off = 0 if side == 0 else 384 + NQ
                P.op("act", lambda a_=a_, off=off, h=h: nc.scalar.copy(out=k[:, h, off:off + 384], in_=a_[:64, :384]), reads=[ra_], writes=[rk])
        vsrc = vtok_d[:, 0:256].rearrange("(t p) (h d) -> p t h d", p=128, h=4)
        for t in range(16):
            P.dma("pool", v[:, 3 + t, :, 0:64], vsrc[:, t, :, :], dn, reads=[r_vtok], writes=[rv])
        for t in range(2):
            P.dma("pool", vc[:, t, :, 0:64], vsrc[:, 16 + t, :, :], dn, reads=[r_vtok], writes=[rc])
        for side in range(2):
            a_, ra_ = halo_v(0, 256, 3, side, hv, hvacc)
            t0_ = 0 if side == 0 else 19
            for t in range(3):
                P.op("act", lambda a_=a_, t=t, t0_=t0_: nc.scalar.copy(out=v[:, t0_ + t, :, 0:64], in_=a_[:, t * 256:(t + 1) * 256].rearrange("p (h d) -> p h d", h=4)),
                     reads=[ra_], writes=[rv])
        bias = Rot(P, "nab", 2, [128, 7, 512], F32, dma=True)
        scale = 64 ** -0.5
        for t in range(16 if final else 18):
            items = []
            if t < 16:
                pat = 0 if t == 0 else 1 if t == 1 else 3 if t == 14 else 4 if t == 15 else 2
                bt_, rb, db = bias.next()
                P.dma("sp", bt_[:], nab_d[l][pat], db, writes=[rb])
                qa = [[q[:, h, t * 128:(t + 1) * 128]] for h in range(4)]
                for kt in (range(1, 6) if pat == 2 else range(7)):
                    et = t + kt
                    items.append(dict(k=[[k[:, h, et * 128:(et + 1) * 128]] for h in range(4)], v=[v[:, et, h, :] for h in range(4)], bias=bt_[:, kt, :], bres=[rb], res=[rk, rv]))
                qres = [rq]
            else:
                tc = t - 16
                qa = [[qc[:, h, tc * 128:(tc + 1) * 128]] for h in range(4)]
                qres = [rc]
            for kt in range(2):
                items.append(dict(k=[[kc_[:, h, kt * 128:(kt + 1) * 128]] for h in range(4)], v=[vc[:, kt, h, :] for h in range(4)], bias=None, res=[rc]))
            o, ro, _ = ost.next()
            attend(A, 4, False, qa, qres, items, scale, [o[:, 0, h * 64:(h + 1) * 64] for h in range(4)], ro)
            emit_oT(o, ro, 1, t, 0, False)
        P.pop_scope()

        P.push_scope()
        EXT = 2304
        dn = P.dsem(f"sw{l}")
        q = P.sb("swq", [64, 4, NQ], BF16); rq = P.res()
        k = P.sb("swk", [64, 2, EXT], BF16); rk = P.res()
        v = P.sb("swv", [128, EXT // 128, 2, 65], BF16); rv = P.res()
        qc = P.sb("swqc", [64, 4, NCX], BF16); kc_ = P.sb("swkc", [64, 2, NCX], BF16); vc = P.sb("swvc", [128, 2, 2, 65], BF16); rc = P.res()
        P.op("dve", lambda: nc.vector.memset(v[:], 1.0), writes=[rv])
        P.op("dve", lambda: nc.vector.memset(vc[:], 1.0), writes=[rc])
        cs_ = P.sb("swcs", [64, 2, EXT], F32); rcs_ = P.res()
        bs = P.sb("swb", [128, 4, 128], F32); rbs = P.res()
        sk = P.sb("swsk", [128, 4], F32); rsk = P.res()
        P.dma("sp", cs_[:, 0, :], swcs_d[0], dn, writes=[rcs_]); P.dma("sp", cs_[:, 1, :], swcs_d[1], dn, writes=[rcs_])
        P.dma("sp", bs[:], swb_d, dn, writes=[rbs]); P.dma("sp", sk[:], swk_d[l], dn, writes=[rsk])
        P.op("act", lambda: nc.scalar.activation(out=sk[:], in_=sk[:], func=AF.Exp), reads=[rsk], writes=[rsk])
        stg = Rot(P, "swstg", 4, [64, EXT], F32, dma=True)
        hk2 = Rot(P, "hk2", 2, [64, 4, 128], F32, dma=True)
        hacc2 = Rot(P, "hacc2", 4, [64, 128], F32)
        hv2 = Rot(P, "hv2", 2, [128, 4, 128], F32, dma=True)
        hvacc2 = Rot(P, "hvacc2", 2, [128, 128], F32)
        perm = ((0, 16), (16, 0), (32, 48), (48, 32))

        def rope_rows(dst, res, row0, col0, n, tab0, extra_writer=None):
            a, ra, da = stg.next(); b, rb_, db = stg.next()
            P.dma("sp", a[:, :n], pT_d[row0:row0 + 64, col0:col0 + n], da, reads=[r_pT], writes=[ra])
            for (d0, s0) in perm:
                P.dma("sp", b[d0:d0 + 16, :n], pT_d[row0 + s0:row0 + s0 + 16, col0:col0 + n], db, reads=[r_pT], writes=[rb_])
            P.op("dve", lambda: nc.vector.tensor_tensor(out=a[:, :n], in0=a[:, :n], in1=cs_[:, 0, tab0:tab0 + n], op=ALU.mult), reads=[ra, rcs_], writes=[ra])
            P.op("pool", lambda: nc.gpsimd.tensor_tensor(out=b[:, :n], in0=b[:, :n], in1=cs_[:, 1, tab0:tab0 + n], op=ALU.mult), reads=[rb_, rcs_], writes=[rb_])
            P.op("dve", lambda: nc.vector.tensor_tensor(out=dst, in0=a[:, :n], in1=b[:, :n], op=ALU.add), reads=[ra, rb_], writes=[res])

        for h in range(4):
            rope_rows(q[:, h, :], rq, SW0 + h * 64, 0, NQ, 128)
            P.dma("pool", qc[:, h, :], pT_d[SW0 + h * 64:SW0 + (h + 1) * 64, NQ:NTOK], dn, reads=[r_pT], writes=[rc])
        for g in range(2):
            rope_rows(k[:, g, 128:128 + NQ], rk, SW0 + 256 + g * 64, 0, NQ, 128)
            P.dma("pool", kc_[:, g, :], pT_d[SW0 + 256 + g * 64:SW0 + 256 + (g + 1) * 64, NQ:NTOK], dn, reads=[r_pT], writes=[rc])
            for side in range(2):
                a_, ra_ = halo_k(None, R_SWK + g * 64, 64, 128, side, hk2, hacc2)
                b_, rb2 = halo_k(None, R_SWK + g * 64, 64, 128, side, hk2, hacc2, rowperm=[(d0, s0, 16) for (d0, s0) in perm])
                tab0 = 0 if side == 0 else 128 + NQ
                P.op("dve", lambda a_=a_, tab0=tab0: nc.vector.tensor_tensor(out=a_[:64, :128], in0=a_[:64, :128], in1=cs_[:, 0, tab0:tab0 + 128], op=ALU.mult), reads=[ra_, rcs_], writes=[ra_])
                P.op("dve", lambda b_=b_, tab0=tab0: nc.vector.tensor_tensor(out=b_[:64, :128], in0=b_[:64, :128], in1=cs_[:, 1, tab0:tab0 + 128], op=ALU.mult), reads=[rb2, rcs_], writes=[rb2])
                P.op("dve", lambda a_=a_, b_=b_, g=g, tab0=tab0: nc.vector.tensor_tensor(out=k[:, g, tab0:tab0 + 128], in0=a_[:64, :128], in1=b_[:64, :128], op=ALU.add),
                     reads=[ra_, rb2], writes=[rk])
        vsrc = vtok_d[:, 256:384].rearrange("(t p) (h d) -> p t h d", p=128, h=2)
        for t in range(16):
            P.dma("pool", v[:, 1 + t, :, 0:64], vsrc[:, t, :, :], dn, reads=[r_vtok], writes=[rv])
        for t in range(2):
            P.dma("pool", vc[:, t, :, 0:64], vsrc[:, 16 + t, :, :], dn, reads=[r_vtok], writes=[rc])
        for side in range(2):
            a_, ra_ = halo_v(256, 128, 1, side, hv2, hvacc2)
            t0_ = 0 if side == 0 else 17
            P.op("act", lambda a_=a_, t0_=t0_: nc.scalar.copy(out=v[:, t0_, :, 0:64], in_=a_[:, 0:128].rearrange("p (h d) -> p h d", h=2)), reads=[ra_], writes=[rv])
        bp = P.sb("swbp", [128, 4, 4, 128], F32); rbp = P.res()
        for kind in range(4):
            for h in range(4):
                P.op("dve", lambda kind=kind, h=h: nc.vector.tensor_copy(out=bp[:, kind, h, :], in_=bs[:, kind, :]), reads=[rbs], writes=[rbp])
        for t in range(16 if final else 18):
            items = []
            if t < 16:
                qa = [[q[:, h, t * 128:(t + 1) * 128]] for h in range(4)]
                qres = [rq]
                for kt in range(3):
                    et = t + kt
                    if kt == 0:
                        b = bp[:, 0 if t == 0 else 1, :, :].rearrange('p h q -> p (h q)')
                    elif kt == 2:
                        b = bp[:, 3 if t == 15 else 2, :, :].rearrange('p h q -> p (h q)')
                    else:
                        b = None
                    items.append(dict(k=[[k[:, h // 2, et * 128:(et + 1) * 128]] for h in range(4)], v=[v[:, et, h // 2, :] for h in range(4)], bias=b, bres=[rbp], res=[rk, rv]))
            else:
                tc = t - 16
                qa = [[qc[:, h, tc * 128:(tc + 1) * 128]] for h in range(4)]
                qres = [rc]
            for kt in range(2):
                items.append(dict(k=[[kc_[:, h // 2, kt * 128:(kt + 1) * 128]] for h in range(4)], v=[vc[:, kt, h // 2, :] for h in range(4)], bias=None, res=[rc]))
            o, ro, _ = ost.next()
            attend(A, 4, False, qa, qres, items, scale, [o[:, 0, h * 64:(h + 1) * 64] for h in range(4)], ro, sinkexp=sk[:, 0:4], sink_res=rsk)
            emit_oT(o, ro, 1, t, 2, False)
        P.pop_scope()

        P.push_scope()
        NK = LSEQ
        dsm = P.dsem(f"ml{l}")
        gkv = P.sb("gkv", [128, 1], F32); gq = P.sb("gq", [128, 2], F32); rg = P.res()
        P.dma("sp", gkv[:], mgkv_d[l], dsm, writes=[rg]); P.dma("sp", gq[:], mgq_d[l], dsm, writes=[rg])
        wkn = P.sb("wkn", [128, 256], BF16); wkv = P.sb("wkv", [128, 256], BF16)
        wqn = P.sb("wqn", [128, 2, 256], BF16); wqr = P.sb("wqr", [128, 2, 128], BF16); wqrs = P.sb("wqrs", [128, 2, 128], BF16)
        rw = P.res(); dw = P.dsem(f"mw{l}")
        P.dma("pool", wkn[:], mwkn_d[l], dw, writes=[rw]); P.dma("pool", wkv[:], mwkv_d[l], dw, writes=[rw])
        for c in range(2):
            P.dma("pool", wqn[:, c, :], mwqn_d[l][c * 128:(c + 1) * 128, :], dw, writes=[rw])
            P.dma("pool", wqr[:, c, :], mwqr_d[l][c * 128:(c + 1) * 128, :], dw, writes=[rw])
            P.dma("pool", wqrs[:, c, :], mwqrs_d[l][c * 128:(c + 1) * 128, :], dw, writes=[rw])
        K96 = P.sb("K96", [96, 4, NK], BF16); rkn = P.res(); rkr = P.res()
        vm = P.sb("vm", [128, NK // 128, 4, 65], BF16); rvm = P.res()
        Q96 = P.sb("Q96", [96, 4, NTOK], BF16); rqn = P.res(); rqr = P.res()
        rrow = P.sb("rrow", [65, 512], F32); rrr_ = P.res()
        bcs = P.sb("bcs", [64, 512], F32); rbcs = P.res()
        P.op("dve", lambda: nc.vector.memset(vm[:], 1.0), writes=[rvm])
        xin = Rot(P, "mx", 2, [128, 2, 512], F32, dma=True)
        sq = Rot(P, "msq", 1, [128, 2, 512], BF16)
        rms = Rot(P, "mrms", 1, [128, 512], F32)
        xn = Rot(P, "mxn", 2, [128, 2, 512], BF16)
        tab = Rot(P, "mtab", 2, [32, 2, 512], F32, dma=True)
        rr4 = Rot(P, "mrr", 4, [32, 512], F32, dma=True)
        perm32 = ((0, 8), (8, 0), (16, 24), (24, 16))

        def key_src(row0, nrows, t0, n):
            out = []
            if t0 < 256:
                ln = min(n, 256 - t0)
                out.append((rbuf(0, row0, nrows, NQ + t0, NQ + t0 + ln), 0))
                if ln < n:
                    for (r, ls, l2, do_) in seg_lat(0, n - ln):
                        out.append((rbuf(r, row0, nrows, ls, ls + l2), ln + do_))
            else:
                for (r, ls, l2, do_) in seg_lat(t0 - 256, n):
                    out.append((rbuf(r, row0, nrows, ls, ls + l2), do_))
            return out

        def norm_blk(loads, kc, n, g):
            x, rx, dx = xin.next()
            for (c, p0, ap, off) in loads:
                P.dma("sp", x[p0:p0 + ap.shape[0], c, off:off + ap.shape[1]], ap, dx, reads=[r_RB2, r_pT], writes=[rx])
            s, rs, _ = sq.next(); r, rr, _ = rms.next()
            rstd_of(x, rx, kc, n, r, rr, s, rs, kc * 128)
            y, ry, _ = xn.next()
            for c in range(kc):
                P.op("dve", lambda c=c: nc.vector.scalar_tensor_tensor(out=y[:, c, :n], in0=x[:, c, :n], scalar=g[:, c:c + 1], in1=r[:, :n], op0=ALU.mult, op1=ALU.mult),
                     reads=[rx, rr, rg], writes=[ry])
            return y, ry

        for t0 in range(0, NK, 512):
            n = min(512, NK - t0)
            y, ry = norm_blk([(0, hf * 64, ap, off) for hf in range(2) for (ap, off) in key_src(R_CKV + hf * 64, 64, t0, n)], 1, n, gkv)
            for pr in range(2):
                pt, rpt, _ = A.misc.next()
                P.op("pe", lambda pr=pr, pt=pt: nc.tensor.matmul(pt[:, :n], lhsT=wkn[:, pr * 128:(pr + 1) * 128], rhs=y[:, 0, :n], start=True, stop=True), reads=[rw, ry], writes=[rpt])
                evac(K96[0:64, 2 * pr, t0:t0 + n], pt[0:64, :n], [rpt], [rkn])
                evac(K96[0:64, 2 * pr + 1, t0:t0 + n], pt[64:128, :n], [rpt], [rkn])
            for tt_ in range(n // 128):
                kt = t0 // 128 + tt_
                pt, rpt, _ = A.misc.next()
                P.op("pe", lambda tt_=tt_, pt=pt: nc.tensor.matmul(pt[:, 0:256], lhsT=y[:, 0, tt_ * 128:(tt_ + 1) * 128], rhs=wkv[:], start=True, stop=True), reads=[rw, ry], writes=[rpt])
                evac(vm[:, kt, :, 0:64], pt[:, 0:256].rearrange("p (h d) -> p h d", h=4), [rpt], [rvm])
            tb, rtb, dtb = tab.next()
            P.dma("sp", tb[:, 0, :n], mkcs_d[0][:, t0:t0 + n], dtb, writes=[rtb]); P.dma("sp", tb[:, 1, :n], mkcs_d[1][:, t0:t0 + n], dtb, writes=[rtb])
            a, ra, da = rr4.next(); b, rb_, db = rr4.next()
            for (ap, off) in key_src(R_KR, 32, t0, n):
                P.dma("sp", a[:, off:off + ap.shape[1]], ap, da, reads=[r_RB2], writes=[ra])
            for (d0, s0) in perm32:
                for (ap, off) in key_src(R_KR + s0, 8, t0, n):
                    P.dma("sp", b[d0:d0 + 8, off:off + ap.shape[1]], ap, db, reads=[r_RB2], writes=[rb_])
            P.op("dve", lambda: nc.vector.tensor_tensor(out=a[:, :n], in0=a[:, :n], in1=tb[:, 0, :n], op=ALU.mult), reads=[ra, rtb], writes=[ra])
            P.op("pool", lambda: nc.gpsimd.tensor_tensor(out=b[:, :n], in0=b[:, :n], in1=tb[:, 1, :n], op=ALU.mult), reads=[rb_, rtb], writes=[rb_])
            for h4 in range(4):
                P.op("dve" if h4 % 2 == 0 else "pool", lambda h4=h4: (nc.vector if h4 % 2 == 0 else nc.gpsimd).tensor_tensor(out=K96[64:96, h4, t0:t0 + n], in0=a[:, :n], in1=b[:, :n], op=ALU.add),
                     reads=[ra, rb_], writes=[rkr])
        for t0 in range(0, NTOK, 512):
            n = min(512, NTOK - t0)
            y, ry = norm_blk([(c, 0, pT_d[ML0 + c * 128:ML0 + (c + 1) * 128, t0:t0 + n], 0) for c in range(2)], 2, n, gq)
            for pr in range(2):
                pt, rpt, _ = A.misc.next()
                for c in range(2):
                    P.op("pe", lambda pr=pr, pt=pt, c=c: nc.tensor.matmul(pt[:, :n], lhsT=wqn[:, c, pr * 128:(pr + 1) * 128], rhs=y[:, c, :n], start=(c == 0), stop=(c == 1)),
                         reads=[rw, ry], writes=[rpt])
                evac(Q96[0:64, 2 * pr, t0:t0 + n], pt[0:64, :n], [rpt], [rqn])
                evac(Q96[0:64, 2 * pr + 1, t0:t0 + n], pt[64:128, :n], [rpt], [rqn])
            tb, rtb, dtb = tab.next()
            P.dma("sp", tb[:, 0, :n], mqcs_d[0][:, t0:t0 + n], dtb, writes=[rtb]); P.dma("sp", tb[:, 1, :n], mqcs_d[1][:, t0:t0 + n], dtb, writes=[rtb])
            for h in range(4):
                pa, rpa, _ = A.misc.next()
                for c in range(2):
                    P.op("pe", lambda pa=pa, c=c, h=h: nc.tensor.matmul(pa[0:32, :n], lhsT=wqr[:, c, h * 32:(h + 1) * 32], rhs=y[:, c, :n], start=(c == 0), stop=(c == 1)),
                         reads=[rw, ry], writes=[rpa])
                a, ra, _ = rr4.next()
                P.op("dve", lambda a=a, pa=pa: nc.vector.tensor_tensor(out=a[:, :n], in0=pa[0:32, :n], in1=tb[:, 0, :n], op=ALU.mult), reads=[rpa, rtb], writes=[ra])
                pb, rpb, _ = A.misc.next()
                for c in range(2):
                    P.op("pe", lambda pb=pb, c=c, h=h: nc.tensor.matmul(pb[0:32, :n], lhsT=wqrs[:, c, h * 32:(h + 1) * 32], rhs=y[:, c, :n], start=(c == 0), stop=(c == 1)),
                         reads=[rw, ry], writes=[rpb])
                b, rb_, _ = rr4.next()
                P.op("dve", lambda b=b, pb=pb: nc.vector.tensor_tensor(out=b[:, :n], in0=pb[0:32, :n], in1=tb[:, 1, :n], op=ALU.mult), reads=[rpb, rtb], writes=[rb_])
                P.op("pool", lambda a=a, b=b, h=h: nc.gpsimd.tensor_tensor(out=Q96[64:96, h, t0:t0 + n], in0=a[:, :n], in1=b[:, :n], op=ALU.add), reads=[ra, rb_], writes=[rqr])
        scale = 96 ** -0.5
        groups = [(g * 512, 4, NK // 128) for g in range(4)] + ([] if final else [(2048, 2, 2)])
        for (q0, ncol, nkt) in groups:
            W_ = ncol * 128
            for h in range(4):
                acc, racc, _ = A.accs.next()
                def score(kt):
                    pan, rpan, _ = A.panels.next()
                    P.op("pe", lambda: nc.tensor.matmul(pan[:, 0:W_], lhsT=K96[0:96, h, kt * 128:(kt + 1) * 128], rhs=Q96[0:96, h, q0:q0 + W_], start=True, stop=True),
                         reads=[rkn, rkr, rqn, rqr], writes=[rpan])
                    pt_, rpt_, _ = A.pT.next()
                    P.op("act", lambda: nc.scalar.activation(out=pt_[:, 0:W_], in_=pan[:, 0:W_], func=AF.Exp, scale=scale), reads=[rpan], writes=[rpt_])
                    return pt_, rpt_
                nxt = score(0)
                for kt in range(nkt):
                    pt_, rpt_ = nxt
                    if kt + 1 < nkt:
                        nxt = score(kt + 1)
                    P.op("pe", lambda: nc.tensor.matmul(acc[0:65, 0:W_], lhsT=vm[:, kt, h, :], rhs=pt_[:, 0:W_], start=(kt == 0), stop=(kt == nkt - 1)),
                         reads=[rvm, rpt_], writes=[racc])
                P.op("dve", lambda: nc.vector.reciprocal(out=rrow[64:65, 0:W_], in_=acc[64:65, 0:W_]), reads=[racc], writes=[rrr_])
                pb_, rpb_, _ = A.misc.next()
                P.op("pe", lambda: nc.tensor.matmul(pb_[0:64, 0:W_], lhsT=onesf[64:65, 0:64], rhs=rrow[64:65, 0:W_], start=True, stop=True), reads=[ronesf, rrr_], writes=[rpb_])
                P.op("act", lambda: nc.scalar.copy(out=bcs[:, 0:W_], in_=pb_[0:64, 0:W_]), reads=[rpb_], writes=[rbcs])
                hp_ = (h % 2) * 64
                P.op("dve", lambda: nc.vector.tensor_tensor(out=oTs[hp_:hp_ + 64, 4 + h // 2, q0:q0 + W_], in0=acc[0:64, 0:W_], in1=bcs[:, 0:W_], op=ALU.mult),
                     reads=[racc, rbcs], writes=[roT])
        P.pop_scope()
        P.pop_scope()

        NB = 256
        fblocks = [(t0, NB, 0 if t0 < NQ else 1) for t0 in range(0, NTOK, NB)]
        P.push_scope()
        wo = P.sb("wo", [128, KC, D], BF16); rwo = P.res(); dwo = P.dsem(f"wo{l}")
        for kk in range(KC):
            P.dma("pool", wo[:, kk, :], wo_d[l][kk * 128:(kk + 1) * 128, :], dwo, writes=[rwo])
        hin = Rot(P, "fhin", 3, [128, KC, NB], F32, dma=True)
        ycand = Rot(P, "ycand", 3, [128, 2, 4, NB], F32, dma=True)
        yin = Rot(P, "yin", 3, [128, 2, NB], F32)
        zin = Rot(P, "zin", 3, [128, 2, NB], F32, dma=True)
        sq = Rot(P, "fsq", 2, [128, 2, NB], BF16)
        rms = Rot(P, "frms", 2, [128, NB], F32)
        od = Rot(P, "fod", 3, [128, 2, NB], BF16)
        def f_prep(t0, n, j):
            h, rh, dh = hin.next()
            P.dma("sp", h[:, :, :n], hT_d.rearrange("(k p) t -> p k t", p=128)[:, :, t0:t0 + n], dh, reads=[r_hin], writes=[rh])
            y, ry, _ = yin.next()
            if j == 0:
                yc, ryc, dyc = ycand.next()
                for kk in range(2):
                    for r in range(4):
                        col = 256 + r * NQ + t0
                        P.dma("sp", yc[:, kk, r, :n], YGc[col // YCH][kk * 128:(kk + 1) * 128, col % YCH:col % YCH + n], dyc, reads=[r_YG], writes=[ryc])
                for kk in range(2):
                    P.op("dve", lambda kk=kk: nc.vector.tensor_scalar(out=y[:, kk, :n], in0=yc[:, kk, 0, :n], scalar1=flg[:, 0:1], scalar2=None, op0=ALU.mult),
                         reads=[ryc, rflg], writes=[ry])
                    for r in range(1, 4):
                        P.op("dve", lambda kk=kk, r=r: nc.vector.scalar_tensor_tensor(out=y[:, kk, :n], in0=yc[:, kk, r, :n], scalar=flg[:, r:r + 1], in1=y[:, kk, :n],
                                                                                    op0=ALU.mult, op1=ALU.add), reads=[ryc, rflg, ry], writes=[ry])
            else:
                yc, ryc, dyc = ycand.next()
                for kk in range(2):
                    P.dma("sp", yc[:, kk, 0, :n], YGc[0][kk * 128:(kk + 1) * 128, 0:256], dyc, reads=[r_YG], writes=[ryc])
                P.op("dve", lambda: nc.vector.tensor_copy(out=y[:, :, :n], in_=yc[:, :, 0, :n]), reads=[ryc], writes=[ry])
            z, rz, dz = zin.next()
            P.dma("sp", z[:, :, :n], pT_d[SS0:SS0 + 256, :].rearrange("(k p) t -> p k t", p=128)[:, :, t0:t0 + n], dz, reads=[r_pT], writes=[rz])
            P.op("act", lambda: nc.scalar.activation(out=z[:, :, :n], in_=z[:, :, :n], func=AF.Silu), reads=[rz], writes=[rz])
            P.op("dve", lambda: nc.vector.tensor_tensor(out=y[:, :, :n], in0=y[:, :, :n], in1=z[:, :, :n], op=ALU.mult), reads=[ry, rz], writes=[ry])
            s, rs, _ = sq.next(); r, rr, _ = rms.next()
            rstd_of(y, ry, 2, n, r, rr, s, rs, 256)
            odt, rod, _ = od.next()
            for kk in range(2):
                P.op("dve", lambda kk=kk: nc.vector.scalar_tensor_tensor(out=odt[:, kk, :n], in0=y[:, kk, :n], scalar=gvs[:, 16 + kk:17 + kk], in1=r[:, :n], op0=ALU.mult, op1=ALU.mult),
                     reads=[ry, rgv, rr], writes=[rod])
            return h, rh, dh, odt, rod

        def f_main(t0, n, j, h, rh, dh, odt, rod):
            for cb in range(KC):
                pt, rpt, _ = gen.next()
                for kk in range(KC):
                    rhs = oTs[:, kk, t0:t0 + n] if kk < 6 else odt[:, kk - 6, :n]
                    P.op("pe", lambda kk=kk, rhs=rhs, pt=pt: nc.tensor.matmul(pt[:, :n], lhsT=wo[:, kk, cb * 128:(cb + 1) * 128], rhs=rhs, start=(kk == 0), stop=(kk == KC - 1)),
                         reads=[rwo, roT, rod], writes=[rpt])
                P.op("dve", lambda cb=cb, pt=pt: nc.vector.scalar_tensor_tensor(out=h[:, cb, :n], in0=pt[:, :n], scalar=mo[:, 16 + cb, j:j + 1], in1=h[:, cb, :n], op0=ALU.mult, op1=ALU.add),
                     reads=[rpt, rmo, rh], writes=[rh])
            P.dma("sp", h2_d.rearrange("(k p) t -> p k t", p=128)[:, :, t0:t0 + n], h[:, :, :n], dh, reads=[rh], writes=[r_h2])
        pend_ = None
        for blk in [b_ for b_ in fblocks if not (final and b_[2] == 1)] + [None]:
            cur_ = (blk, f_prep(*blk)) if blk is not None else None
            if pend_ is not None:
                f_main(*pend_[0], *pend_[1])
            pend_ = cur_
        P.pop_scope()
        P.pop_scope()
        P.push_scope()
        w1 = P.sb("w1", [128, KC, 4 * D], BF16); rw1 = P.res(); dw1 = P.dsem(f"w1{l}")
        w2 = P.sb("w2", [128, 32, D], BF16); rw2 = P.res(); dw2 = P.dsem(f"w2{l}")
        for kk in range(KC):
            for c0 in range(0, 4 * D, 2048):
                P.dma("pool", w1[:, kk, c0:c0 + 2048], w1_d[l][kk * 128:(kk + 1) * 128, c0:c0 + 2048], dw1, writes=[rw1])
        for kk in range(32):
            P.dma("pool", w2[:, kk, :], w2_d[l][kk * 128:(kk + 1) * 128, :], dw2, writes=[rw2])
        NB2 = 512
        hin = Rot(P, "h2in", 1, [128, KC, NB2], F32, dma=True)
        rms = Rot(P, "rms2", 1, [128, NB2], F32)
        tt = Rot(P, "tt2", 2, [128, NB2], F32)
        xm = Rot(P, "xm2", 1, [128, KC, NB2], BF16)
        at = Rot(P, "at", 1, [128, 32, NB2], BF16)
        rl = Rot(P, "rl", 2, [128, NB2], F32)
        for (t0, n, j) in [(t0, NB2, 0) for t0 in range(0, NQ, NB2)] + [(NQ, NCX, 1)]:
            if final and j == 1:
                continue
            h, rh, dh = hin.next()
            P.dma("sp", h[:, :, :n], h2_d.rearrange("(k p) t -> p k t", p=128)[:, :, t0:t0 + n], dh, reads=[r_h2], writes=[rh])
            a, ra, _ = at.next()
            s, rs = a, ra
            r, rr, _ = rms.next()
            rstd_of(h, rh, KC, n, r, rr, s, rs, D)
            x, rx, _ = xm.next()
            for kk in range(KC):
                t, rt, _ = tt.next()
                P.op("dve", lambda kk=kk, t=t: nc.vector.tensor_tensor(out=t[:, :n], in0=h[:, kk, :n], in1=r[:, :n], op=ALU.mult), reads=[rh, rr], writes=[rt])
                P.op("act", lambda kk=kk, t=t: nc.scalar.activation(out=x[:, kk, :n], in_=t[:, :n], func=AF.Identity, scale=gs2[:, kk, j:j + 1], bias=mo[:, 24 + kk, j:j + 1]),
                     reads=[rt, rgs, rmo], writes=[rx])
            for cb in range(32):
                pt, rpt, _ = gen.next()
                for kk in range(KC):
                    P.op("pe", lambda kk=kk, pt=pt: nc.tensor.matmul(pt[:, :n], lhsT=w1[:, kk, cb * 128:(cb + 1) * 128], rhs=x[:, kk, :n], start=(kk == 0), stop=(kk == KC - 1)),
                         reads=[rw1, rx], writes=[rpt])
                qq, rq_, _ = rl.next()
                P.op("act", lambda pt=pt, qq=qq: nc.scalar.activation(out=qq[:, :n], in_=pt[:, :n], func=AF.Relu), reads=[rpt], writes=[rq_])
                P.op("pool", lambda cb=cb, qq=qq: nc.gpsimd.tensor_tensor(out=a[:, cb, :n], in0=qq[:, :n], in1=qq[:, :n], op=ALU.mult), reads=[rq_], writes=[ra])
            for cb in range(KC):
                pt, rpt, _ = gen.next()
                for kk in range(32):
                    P.op("pe", lambda kk=kk, pt=pt: nc.tensor.matmul(pt[:, :n], lhsT=w2[:, kk, cb * 128:(cb + 1) * 128], rhs=a[:, kk, :n], start=(kk == 0), stop=(kk == 31)),
                         reads=[rw2, ra], writes=[rpt])
                P.op("dve", lambda cb=cb, pt=pt: nc.vector.scalar_tensor_tensor(out=h[:, cb, :n], in0=pt[:, :n], scalar=mo[:, 40 + cb, j:j + 1], in1=h[:, cb, :n], op0=ALU.mult, op1=ALU.add),
                     reads=[rpt, rmo, rh], writes=[rh])
            if final:
                r, rr, _ = rms.next()
                rstd_of(h, rh, KC, n, r, rr, a, ra, D)
                for kk in range(KC):
                    P.op("dve", lambda kk=kk: nc.vector.scalar_tensor_tensor(out=h[:, kk, :n], in0=h[:, kk, :n], scalar=gvs[:, 8 + kk:9 + kk], in1=r[:, :n], op0=ALU.mult, op1=ALU.mult),
                         reads=[rh, rgv, rr], writes=[rh])
                P.dma("sp", out_d.rearrange("(k p) t -> p k t", p=128)[:, :, t0:t0 + n], h[:, :, :n], dh, reads=[rh], writes=[P.res()], final=True)
            else:
                last = (l == nlayers - 1)
                if last and j == 0:
                    P.dma("sp", out_d.rearrange("(k p) t -> p k t", p=128)[:, :, t0:t0 + n], h[:, :, :n], dh, reads=[rh], writes=[P.res()], final=True)
                P.dma("sp", hTn_d.rearrange("(k p) t -> p k t", p=128)[:, :, t0:t0 + n], h[:, :, :n], dh, reads=[rh], writes=[r_hn])
        P.pop_scope()
    P.finish()
    P.close()
    return P


NEG = -1e30
NA0, SW0, ML0, SS0 = 0, 768, 1280, 1696


def cvec(v, n):
    return np.ascontiguousarray(v.reshape(n, 128).T)


def swap_idx(dim):
    q = dim // 4
    idx = np.arange(dim)
    blk = (idx // q) % 2
    return np.where(blk == 0, idx + q, idx - q)


def rope_tables(pos, dim):
    nf = dim // 4
    inv = (1.0 / (10000.0 ** (np.arange(nf, dtype=np.float32) / nf))).astype(np.float32)
    row = (pos // 64).astype(np.float32)
    col = (pos % 64).astype(np.float32)
    d = np.arange(dim)
    f = d % nf
    p = np.where((d < dim // 2)[:, None], row[None, :], col[None, :]).astype(np.float32)
    ang = (p * inv[f][:, None]).astype(np.float32)
    sign = np.where(((d // nf) % 2) == 0, -1.0, 1.0).astype(np.float32)
    return np.cos(ang).astype(np.float32), (np.sin(ang) * sign[:, None]).astype(np.float32)


def ext_rows(a, lo, hi):
    S = a.shape[0]
    out = np.zeros((hi - lo,) + a.shape[1:], a.dtype)
    l2, h2 = max(lo, 0), min(hi, S)
    out[l2 - lo:h2 - lo] = a[l2:h2]
    return out


def na_bias_tile(rpb, T):
    q = np.arange(128)
    r = 2 * T + q // 64
    qc = q % 64
    rs = np.clip(r - 4, 0, 120)
    cst = np.clip(qc - 8, 0, 48)
    out = np.full((128, 7, 4, 128), NEG, np.float32)
    i = np.arange(128)
    for kt in range(7):
        krow = 2 * T - 6 + 2 * kt + i // 64
        kcol = i % 64
        valid = ((krow[:, None] >= rs[None, :]) & (krow[:, None] < rs[None, :] + 8) & (krow[:, None] >= 0) & (krow[:, None] < 128)
                 & (kcol[:, None] >= cst[None, :]) & (kcol[:, None] < cst[None, :] + 16))
        dr = np.clip(krow[:, None] - r[None, :] + 7, 0, 14)
        dc = np.clip(kcol[:, None] - qc[None, :] + 15, 0, 30)
        for h in range(4):
            out[:, kt, h, :] = np.where(valid, rpb[h][dr, dc], NEG)
    return out


def prep_T(p_b, pc_b, j, W, l):
    o0 = 2048 * j
    T = lambda a: np.ascontiguousarray(a.T)
    m = {}
    own = p_b[o0:o0 + 2048]
    m["na_q"] = T(own[:, 0:256])
    e = ext_rows(p_b[:, 256:768], o0 - 384, o0 + 2048 + 384)
    m["na_k"] = T(e[:, 0:256]); m["na_v"] = np.ascontiguousarray(e[:, 256:512])
    m["na_qc"] = T(pc_b[:, 0:256]); m["na_kc"] = T(pc_b[:, 256:512]); m["na_vc"] = np.ascontiguousarray(pc_b[:, 512:768])
    rpb = W["na_rpb"][l]
    m["na_bias"] = np.stack([na_bias_tile(rpb, 16 * j + t).reshape(128, 7, 512) for t in (0, 1, 8 if j in (0, 3) else 2, 14, 15)])
    sw = swap_idx(64)
    q = own[:, SW0:SW0 + 256].reshape(2048, 4, 64)
    m["sw_q"] = T(q.reshape(2048, 256)); m["sw_qs"] = T(q[:, :, sw].reshape(2048, 256))
    e = ext_rows(p_b[:, SW0 + 256:SW0 + 512], o0 - 128, o0 + 2048 + 128)
    k = e[:, 0:128].reshape(2304, 2, 64)
    m["sw_k"] = T(k.reshape(2304, 128)); m["sw_ks"] = T(k[:, :, sw].reshape(2304, 128))
    m["sw_v"] = np.ascontiguousarray(e[:, 128:256])
    pos = np.arange(o0 - 128, o0 + 2048 + 128)
    c, s = rope_tables(np.clip(pos, 0, 8191), 64)
    m["sw_cs"] = np.stack([c, s])
    m["sw_qc"] = T(pc_b[:, SW0:SW0 + 256]); m["sw_kc"] = T(pc_b[:, SW0 + 256:SW0 + 384]); m["sw_vc"] = np.ascontiguousarray(pc_b[:, SW0 + 384:SW0 + 512])
    i = np.arange(128)
    prev = np.where(i[:, None] >= i[None, :], 0.0, NEG).astype(np.float32)
    nxt = np.where(i[:, None] <= i[None, :], 0.0, NEG).astype(np.float32)
    allneg = np.full((128, 128), NEG, np.float32)
    m["sw_bias"] = np.ascontiguousarray(np.stack([allneg if j == 0 else prev, prev, nxt, allneg if j == 3 else nxt], 1))
    m["sw_sink"] = np.ascontiguousarray(np.broadcast_to(W["swa_sink"][l][None, :], (128, 4)))
    sw32 = swap_idx(32)
    allk = np.concatenate([pc_b[:, ML0 + 256:ML0 + 416], p_b[:, ML0 + 256:ML0 + 416]], 0)
    m["m_ckv"] = T(allk[:, 0:128]); m["m_kr"] = T(allk[:, 128:160]); m["m_krs"] = T(allk[:, 128:160][:, sw32])
    c, s = rope_tables(np.arange(8192), 32)
    c = np.concatenate([np.ones((32, 256), np.float32), c], 1); s = np.concatenate([np.zeros((32, 256), np.float32), s], 1)
    m["m_kcs"] = np.stack([c, s])
    m["m_cq"] = T(np.concatenate([own[:, ML0:ML0 + 256], pc_b[:, ML0:ML0 + 256]], 0))
    c, s = rope_tables(np.arange(o0, o0 + 2048), 32)
    c = np.concatenate([c, np.ones((32, 256), np.float32)], 1); s = np.concatenate([s, np.zeros((32, 256), np.float32)], 1)
    m["m_qcs"] = np.stack([c, s])
    m["m_gkv"] = np.ascontiguousarray(W["mla_g_kv"][l].reshape(128, 1)); m["m_gq"] = cvec(W["mla_g_q"][l], 2)
    wkv = W["mla_w_ukv"][l].reshape(128, 4, 128)
    m["m_wkn"] = np.ascontiguousarray(wkv[:, :, 0:64].reshape(128, 256)); m["m_wkv"] = np.ascontiguousarray(wkv[:, :, 64:128].reshape(128, 256))
    wq = W["mla_w_uq"][l].reshape(256, 4, 96)
    m["m_wqn"] = np.ascontiguousarray(wq[:, :, 0:64].reshape(256, 256))
    m["m_wqr"] = np.ascontiguousarray(wq[:, :, 64:96].reshape(256, 128))
    m["m_wqrs"] = np.ascontiguousarray(wq[:, :, 64:96][:, :, sw32].reshape(256, 128))
    return m


def prep_S(p_b, pc_b, hd, W, l):
    g = hd // 2
    T = lambda a: np.ascontiguousarray(a.T)
    allp = np.concatenate([pc_b[:, SS0:], p_b[:, SS0:]], 0)
    xbc = allp[:, 256:1024]
    m = {}
    m["s_x"] = T(xbc[:, hd * 64:(hd + 1) * 64])
    m["s_b"] = T(xbc[:, 256 + g * 128:256 + (g + 1) * 128])
    m["s_c"] = T(xbc[:, 512 + g * 128:512 + (g + 1) * 128])
    cwv = np.concatenate([W["ssd_conv_w"][l], W["ssd_conv_b"][l][None, :]], 0)
    cw = np.zeros((128, 3, 6), np.float32)
    cw[:64, 0, :] = cwv[:, hd * 64:(hd + 1) * 64].T
    cw[:, 1, :] = cwv[:, 256 + g * 128:256 + (g + 1) * 128].T
    cw[:, 2, :] = cwv[:, 512 + g * 128:512 + (g + 1) * 128].T
    m["s_cw"] = cw
    dt = allp[:, 1024:1032].reshape(-1, 2, 4)[:, :, hd]
    m["s_dt"] = np.ascontiguousarray(dt.reshape(66, 128, 2).transpose(1, 0, 2))
    par = np.zeros((128, 8), np.float32)
    par[:, 0:2] = W["ssd_dt_bias"][l][:, hd]; par[:, 2:4] = W["ssd_a_log"][l][:, hd]; par[:, 4] = W["ssd_d"][l][hd]
    m["s_par"] = par
    i = np.arange(128)
    m["s_u"] = np.stack([(i[:, None] <= i[None, :]), (i[:, None] >= i[None, :])]).astype(np.float32)
    m["s_id"] = np.eye(128, dtype=np.float32)
    return m


def prep_M(I, core):
    b, j = core // 4, core % 4
    T = lambda a: np.ascontiguousarray(a.T)
    L = 4
    m = {}
    m["hT0"] = T(np.concatenate([I['x'][b, 2048 * j:2048 * (j + 1)], I['ctx'][b]], 0))
    m["cv"] = np.ascontiguousarray(np.stack([cvec(I['c'][b], 8), cvec(I['c_ctx'], 8)], -1))
    flg = np.zeros((128, 16), np.float32)
    flg[:, j] = 1.0
    if j > 0: flg[:, 4 + j - 1] = 1.0
    if j < 3: flg[:, 8 + j + 1] = 1.0
    m["flg"] = flg
    selx = np.zeros((128, 2, 64), np.float32); selb = np.zeros((128, 2, 128), np.float32)
    for d in range(64): selx[(j % 2) * 64 + d, j // 2, d] = 1.0
    for n in range(128): selb[n, j // 2, n] = 1.0
    m["selx"] = selx; m["selb"] = selb
    m["w_in"] = I['w_in']; m["w_mod"] = I['w_mod']; m["w_out"] = I['w_out']; m["w1"] = I['w_mlp1']; m["w2"] = I['w_mlp2']
    m["bmod"] = np.stack([cvec(I['b_mod'][l], 48) for l in range(L)])
    m["g1"] = np.stack([cvec(I['g_norm1'][l], 8) for l in range(L)])
    gv = np.zeros((L, 128, 20), np.float32)
    for l in range(L):
        gv[l, :, 0:8] = cvec(I['g_norm2'][l], 8); gv[l, :, 8:16] = cvec(I['g_final'], 8); gv[l, :, 16:18] = cvec(I['ssd_g_norm'][l], 2)
    m["gv"] = gv
    m["na_bias"] = np.stack([np.stack([na_bias_tile(I['na_rpb'][l], 16 * j + t).reshape(128, 7, 512) for t in (0, 1, 8 if j in (0, 3) else 2, 14, 15)]) for l in range(L)])
    i = np.arange(128)
    prev = np.where(i[:, None] >= i[None, :], 0.0, NEG).astype(np.float32)
    nxt = np.where(i[:, None] <= i[None, :], 0.0, NEG).astype(np.float32)
    allneg = np.full((128, 128), NEG, np.float32)
    m["sw_bias"] = np.ascontiguousarray(np.stack([allneg if j == 0 else prev, prev, nxt, allneg if j == 3 else nxt], 1))
    m["sw_sink"] = np.stack([np.ascontiguousarray(np.broadcast_to(I['swa_sink'][l][None, :], (128, 4))) for l in range(L)])
    o0 = 2048 * j
    c, s = rope_tables(np.clip(np.arange(o0 - 128, o0 + 2048 + 128), 0, 8191), 64)
    m["sw_cs"] = np.stack([c, s])
    c, s = rope_tables(np.arange(8192), 32)
    m["m_kcs"] = np.stack([np.concatenate([np.ones((32, 256), np.float32), c], 1), np.concatenate([np.zeros((32, 256), np.float32), s], 1)])
    c, s = rope_tables(np.arange(o0, o0 + 2048), 32)
    m["m_qcs"] = np.stack([np.concatenate([c, np.ones((32, 256), np.float32)], 1), np.concatenate([s, np.zeros((32, 256), np.float32)], 1)])
    sw32 = swap_idx(32)
    m["m_gkv"] = np.stack([I['mla_g_kv'][l].reshape(128, 1) for l in range(L)])
    m["m_gq"] = np.stack([cvec(I['mla_g_q'][l], 2) for l in range(L)])
    wkv = I['mla_w_ukv'].reshape(L, 128, 4, 128)
    m["m_wkn"] = np.ascontiguousarray(wkv[:, :, :, 0:64].reshape(L, 128, 256)); m["m_wkv"] = np.ascontiguousarray(wkv[:, :, :, 64:128].reshape(L, 128, 256))
    wq = I['mla_w_uq'].reshape(L, 256, 4, 96)
    m["m_wqn"] = np.ascontiguousarray(wq[:, :, :, 0:64].reshape(L, 256, 256))
    m["m_wqr"] = np.ascontiguousarray(wq[:, :, :, 64:96].reshape(L, 256, 128))
    m["m_wqrs"] = np.ascontiguousarray(wq[:, :, :, 64:96][:, :, :, sw32].reshape(L, 256, 128))
    hd, g = j, j // 2
    cw = np.zeros((L, 128, 3, 6), np.float32); par = np.zeros((L, 128, 8), np.float32)
    for l in range(L):
        cwv = np.concatenate([I['ssd_conv_w'][l], I['ssd_conv_b'][l][None, :]], 0)
        cw[l, :64, 0, :] = cwv[:, hd * 64:(hd + 1) * 64].T
        cw[l, :, 1, :] = cwv[:, 256 + g * 128:256 + (g + 1) * 128].T
        cw[l, :, 2, :] = cwv[:, 512 + g * 128:512 + (g + 1) * 128].T
        par[l, :, 0:2] = I['ssd_dt_bias'][l][:, hd]; par[l, :, 2:4] = I['ssd_a_log'][l][:, hd]; par[l, :, 4] = I['ssd_d'][l][hd]
    m["s_cw"] = cw; m["s_par"] = par
    m["s_u"] = np.stack([(i[:, None] <= i[None, :]), (i[:, None] >= i[None, :])]).astype(np.float32)
    m["ident"] = np.eye(128, dtype=np.float32)
    return m


from concourse.bass_utils import run_bass_kernel_spmd

_PROG = {}


def kernel(**inputs):
    I = {k: np.asarray(v, dtype=np.float32) for k, v in inputs.items()}
    if "M" not in _PROG:
        _PROG["M"] = build_M(4, 3, 4)
    P = _PROG["M"]
    in_maps = [prep_M(I, c) for c in range(8)]
    res = run_bass_kernel_spmd(P.nc, in_maps, core_ids=list(range(8)))
    out = np.stack([np.concatenate([res.results[b * 4 + j]["out"].T for j in range(4)], 0) for b in range(2)])
    return np.ascontiguousarray(out.astype(np.float32))
```

```python
import numpy as np
from contextlib import ExitStack
import concourse.bass as bass
import concourse.mybir as mybir

F32 = mybir.dt.float32
BF16 = mybir.dt.bfloat16
AF = mybir.ActivationFunctionType
ALU = mybir.AluOpType
AX = mybir.AxisListType


class Res:
    __slots__ = ("name", "w", "r")

    def __init__(self, name):
        self.name = name
        self.w = None
        self.r = []


class Prog:
    ENG = ("pe", "act", "dve", "pool", "sp")

    def __init__(self):
        self.nc = bass.Bass("TRN2", target_bir_lowering=False)
        self.es = ExitStack()
        self.root = self.es
        nc = self.nc
        self.e = {"pe": nc.tensor, "act": nc.scalar, "dve": nc.vector, "pool": nc.gpsimd, "sp": nc.sync}
        self.sem = {}
        self.cnt = {}
        for k in self.ENG:
            self.sem[k] = self.es.enter_context(nc.semaphore("s_" + k))
            self.cnt[k] = 0
        self.seen = {k: {} for k in self.ENG}
        self.ndma = 0
        self.out_toks = []
        self.nwait = 0
        self.nres = 0

    def dram(self, name, shape, dt, kind):
        return self.nc.dram_tensor(name, list(shape), dt, kind=kind).ap()

    def _u(self, name):
        self.nuniq = getattr(self, "nuniq", 0) + 1
        return f"{name}_u{self.nuniq}"

    def sb(self, name, shape, dt):
        return self.es.enter_context(self.nc.sbuf_tensor(self._u(name), list(shape), dt))

    def ps(self, name, shape, dt=F32):
        return self.es.enter_context(self.nc.psum_tensor(self._u(name), list(shape), dt))

    def res(self, name=None):
        self.nres += 1
        return Res(name or f"r{self.nres}")

    def dsem(self, name):
        free = getattr(self, "free_dsems", None)
        if free:
            key = free.pop()
        else:
            name = self._u(name)
            s = self.root.enter_context(self.nc.semaphore("d_" + name))
            key = ("d", name)
            self.sem[key] = s
            self.cnt[key] = 0
        if getattr(self, "_scope_dsems", None):
            self._scope_dsems[-1].append(key)
        return key

    def _deps(self, eng, reads, writes):
        need = {}

        def add(t, same_ok):
            if t is None:
                return
            sk, val, src = t
            if src == eng and not same_ok:
                return
            if need.get(sk, 0) < val:
                need[sk] = val

        for r in reads:
            add(r.w, True)
        for w in writes:
            add(w.w, False)
            for t in w.r:
                add(t, False)
        out = []
        seen = self.seen[eng]
        for sk, val in need.items():
            if seen.get(sk, 0) >= val:
                continue
            seen[sk] = val
            out.append((sk, val))
        return out

    def _emit_waits(self, eng, waits):
        e = self.e[eng]
        for sk, val in waits:
            e.wait_ge(self.sem[sk], val)
            self.nwait += 1

    def _record(self, tok, reads, writes):
        for r in reads:
            r.r.append(tok)
        for w in writes:
            w.w = tok
            w.r = []

    def op(self, eng, fn, reads=(), writes=(), pe_chain=False):
        reads = [r for r in reads if r is not None]
        writes = [w for w in writes if w is not None]
        if eng == "pe":
            waits = self._deps_pe(reads, writes)
        else:
            waits = self._deps(eng, reads, writes)
        self._emit_waits(eng, waits)
        ins = fn()
        self.cnt[eng] += 1
        ins.then_inc(self.sem[eng], 1)
        tok = (eng, self.cnt[eng], eng)
        self._record(tok, reads, writes)
        return ins

    def _deps_pe(self, reads, writes):
        need = {}

        def add(t):
            if t is None:
                return
            sk, val, src = t
            if src == "pe":
                return
            if need.get(sk, 0) < val:
                need[sk] = val

        for r in reads:
            add(r.w)
        for w in writes:
            add(w.w)
            for t in w.r:
                add(t)
        out = []
        seen = self.seen["pe"]
        for sk, val in need.items():
            if seen.get(sk, 0) >= val:
                continue
            seen[sk] = val
            out.append((sk, val))
        return out

    def dma(self, q, dst, src, dsem, reads=(), writes=(), final=False, **kw):
        reads = [r for r in reads if r is not None]
        writes = [w for w in writes if w is not None]
        waits = self._deps(q, reads, writes)
        self._emit_waits(q, waits)
        ins = self.e[q].dma_start(out=dst, in_=src, **kw)
        self.cnt[dsem] += 16
        ins.then_inc(self.sem[dsem], 16)
        tok = (dsem, self.cnt[dsem], "dma")
        self._record(tok, reads, writes)
        self.ndma += 1
        if final:
            self.out_toks.append(tok)
        return ins

    def barrier(self):
        for eng in self.ENG:
            for sk, val in self.cnt.items():
                if val > 0 and self.seen[eng].get(sk, 0) < val:
                    self.e[eng].wait_ge(self.sem[sk], val)
                    self.seen[eng][sk] = val

    def push_scope(self):
        self._scopes = getattr(self, "_scopes", [])
        self._scopes.append(self.es)
        self.es = ExitStack()
        self._scope_dsems = getattr(self, "_scope_dsems", [])
        self._scope_dsems.append([])

    def pop_scope(self):
        self.barrier()
        self.es.close()
        self.es = self._scopes.pop()
        self.free_dsems = getattr(self, "free_dsems", [])
        self.free_dsems.extend(self._scope_dsems.pop())

    def finish(self):
        toks = list(self.out_toks)
        need = {}
        for sk, val, _ in toks:
            need[sk] = max(need.get(sk, 0), val)
        for sk, val in need.items():
            self.e["sp"].wait_ge(self.sem[sk], val)

    def close(self):
        self.es.close()


class Rot:
    def __init__(self, P, name, n, shape, dt, psum=False, dma=False):
        self.t, self.r, self.d = [], [], []
        for i in range(n):
            self.t.append(P.ps(f"{name}{i}", shape, dt) if psum else P.sb(f"{name}{i}", shape, dt))
            self.r.append(P.res(f"{name}{i}"))
            self.d.append(P.dsem(f"{name}{i}") if dma else None)
        self.i = -1
        self.n = n

    def next(self):
        self.i = (self.i + 1) % self.n
        return self.t[self.i], self.r[self.i], self.d[self.i]


EPS = 1e-6


class AttnCtx:
    def __init__(self, P):
        self.P = P
        nc = P.nc
        self.panels = Rot(P, "pan", 3, [128, 512], F32, psum=True)
        self.accs = Rot(P, "acc", 2, [128, 512], F32, psum=True)
        self.misc = Rot(P, "mps", 2, [128, 512], F32, psum=True)
        self.pT = Rot(P, "pT", 3, [128, 512], BF16)
        self.sT = Rot(P, "sT", 2, [128, 512], F32)
        self.rec = Rot(P, "rec", 2, [128, 4], F32)
        self.zeros = P.sb("zeros", [128, 512], BF16)
        self.rz = P.res()
        P.op("dve", lambda: nc.vector.memset(self.zeros[:], 0.0), writes=[self.rz])
        self.ev = 0


def attend(A, ncol, merged, q_aps, q_res, key_items, scale, out_aps, out_res, sinkexp=None, sink_res=None):
    P = A.P
    nc = P.nc
    W = ncol * 128
    acc, racc, _ = A.accs.next()
    P.op("pe", lambda: nc.tensor.matmul(acc[:, 0:ncol * 65], lhsT=A.zeros[:, 0:128], rhs=A.zeros[:, 0:ncol * 65], start=True, stop=False),
         reads=[A.rz], writes=[racc])
    nk = len(key_items)

    def score(it):
        pan, rpan, _ = A.panels.next()
        if merged:
            parts_k = it["k"][0]
            parts_q = q_aps[0]
            for pi, (kp, qp) in enumerate(zip(parts_k, parts_q)):
                P.op("pe", lambda kp=kp, qp=qp, pi=pi: nc.tensor.matmul(pan[:, 0:W], lhsT=kp, rhs=qp, start=(pi == 0), stop=(pi == len(parts_k) - 1)),
                     reads=list(it["res"]) + list(q_res), writes=[rpan])
        else:
            for c in range(ncol):
                parts_k = it["k"][c]
                parts_q = q_aps[c]
                for pi, (kp, qp) in enumerate(zip(parts_k, parts_q)):
                    P.op("pe", lambda kp=kp, qp=qp, pi=pi, c=c, n=len(parts_k): nc.tensor.matmul(
                        pan[:, c * 128:(c + 1) * 128], lhsT=kp, rhs=qp, start=(pi == 0), stop=(pi == n - 1)),
                        reads=list(it["res"]) + list(q_res), writes=[rpan])
        pt, rpt, _ = A.pT.next()
        if it.get("bias") is not None:
            st, rst, _ = A.sT.next()
            P.op("dve", lambda: nc.vector.scalar_tensor_tensor(out=st[:, 0:W], in0=pan[:, 0:W], scalar=scale, in1=it["bias"], op0=ALU.mult, op1=ALU.add),
                 reads=[rpan] + list(it.get("bres", [])), writes=[rst])
            P.op("act", lambda: nc.scalar.activation(out=pt[:, 0:W], in_=st[:, 0:W], func=AF.Exp), reads=[rst], writes=[rpt])
        else:
            P.op("act", lambda: nc.scalar.activation(out=pt[:, 0:W], in_=pan[:, 0:W], func=AF.Exp, scale=scale), reads=[rpan], writes=[rpt])
        return pt, rpt

    nxt = score(key_items[0])
    for ki, it in enumerate(key_items):
        pt, rpt = nxt
        if ki + 1 < nk:
            nxt = score(key_items[ki + 1])
        for c in range(ncol):
            P.op("pe", lambda c=c: nc.tensor.matmul(acc[:, c * 65:(c + 1) * 65], lhsT=pt[:, c * 128:(c + 1) * 128], rhs=it["v"][c], start=False, stop=(ki == nk - 1)),
                 reads=[rpt] + list(it["res"]), writes=[racc])
    rec, rrec, _ = A.rec.next()
    den = acc[:, 64:64 + 65 * (ncol - 1) + 1:65]
    if sinkexp is not None:
        P.op("dve", lambda: nc.vector.tensor_tensor(out=rec[:, 0:ncol], in0=den, in1=sinkexp, op=ALU.add), reads=[racc, sink_res], writes=[rrec])
        P.op("dve", lambda: nc.vector.reciprocal(out=rec[:, 0:ncol], in_=rec[:, 0:ncol]), reads=[rrec], writes=[rrec])
    else:
        P.op("dve", lambda: nc.vector.reciprocal(out=rec[:, 0:ncol], in_=den), reads=[racc], writes=[rrec])
    for c in range(ncol):
        A.ev += 1
        if A.ev % 2 == 0:
            P.op("act", lambda c=c: nc.scalar.activation(out=out_aps[c], in_=acc[:, c * 65:c * 65 + 64], func=AF.Copy, scale=rec[:, c:c + 1]),
                 reads=[racc, rrec], writes=[out_res])
        else:
            P.op("dve", lambda c=c: nc.vector.tensor_scalar(out=out_aps[c], in0=acc[:, c * 65:c * 65 + 64], scalar1=rec[:, c:c + 1], scalar2=None,
                                                            op0=ALU.mult), reads=[racc, rrec], writes=[out_res])


def load_cast(P, dst, src, res, dsem):
    P.dma("pool", dst, src, dsem, writes=[res])


def load_v(P, vt, v_d, ntile, nh, res, dsem):
    nc = P.nc
    P.op("dve", lambda: nc.vector.memset(vt[:], 1.0), writes=[res])
    src = v_d.rearrange("(t p) (h d) -> p t h d", p=128, h=nh)
    for t in range(ntile):
        P.dma("pool", vt[:, t, :, 0:64], src[:, t, :, :], dsem, writes=[res])


def rope_to(P, dst, x_d, xs_d, cos_t, sin_t, rtab, n, npart, stg, res):
    nc = P.nc
    a, ra, da = stg.next()
    b, rb, db = stg.next()
    P.dma("sp", a[:npart, :n], x_d, da, writes=[ra])
    P.dma("sp", b[:npart, :n], xs_d, db, writes=[rb])
    P.op("dve", lambda: nc.vector.tensor_tensor(out=a[:npart, :n], in0=a[:npart, :n], in1=cos_t, op=ALU.mult), reads=[ra, rtab], writes=[ra])
    P.op("pool", lambda: nc.gpsimd.tensor_tensor(out=b[:npart, :n], in0=b[:npart, :n], in1=sin_t, op=ALU.mult), reads=[rb, rtab], writes=[rb])
    P.op("dve", lambda: nc.vector.tensor_tensor(out=dst, in0=a[:npart, :n], in1=b[:npart, :n], op=ALU.add), reads=[ra, rb], writes=[res])


def build_T(do_na=True, do_swa=True, do_mla=True):
    P = Prog()
    nc = P.nc
    A = AttnCtx(P)
    NQ = 2048
    NC = 256
    rout = P.res("out")
    ost = Rot(P, "ost", 2, [128, 4, 256], F32, dma=True)

    if do_na:
        P.push_scope()
        EXT = 2816
        q_d = P.dram("na_q", [256, NQ], F32, "ExternalInput")
        k_d = P.dram("na_k", [256, EXT], F32, "ExternalInput")
        v_d = P.dram("na_v", [EXT, 256], F32, "ExternalInput")
        qc_d = P.dram("na_qc", [256, NC], F32, "ExternalInput")
        kc_d = P.dram("na_kc", [256, NC], F32, "ExternalInput")
        vc_d = P.dram("na_vc", [NC, 256], F32, "ExternalInput")
        b_d = P.dram("na_bias", [5, 128, 7, 512], F32, "ExternalInput")
        o_d = P.dram("o_na", [NQ + NC, 256], F32, "ExternalOutput")
        q = P.sb("naq", [64, 4, NQ], BF16); rq = P.res(); d1 = P.dsem("naq")
        k = P.sb("nak", [64, 4, EXT], BF16); rk = P.res(); d2 = P.dsem("nak")
        v = P.sb("nav", [128, EXT // 128, 4, 65], BF16); rv = P.res(); d3 = P.dsem("nav")
        qc = P.sb("naqc", [64, 4, NC], BF16); kc = P.sb("nakc", [64, 4, NC], BF16); vc = P.sb("navc", [128, 2, 4, 65], BF16)
        rc = P.res(); d4 = P.dsem("nac")
        for h in range(4):
            load_cast(P, q[:, h, :], q_d[h * 64:(h + 1) * 64, :], rq, d1)
            load_cast(P, k[:, h, :], k_d[h * 64:(h + 1) * 64, :], rk, d2)
            load_cast(P, qc[:, h, :], qc_d[h * 64:(h + 1) * 64, :], rc, d4)
            load_cast(P, kc[:, h, :], kc_d[h * 64:(h + 1) * 64, :], rc, d4)
        load_v(P, v, v_d, EXT // 128, 4, rv, d3)
        load_v(P, vc, vc_d, 2, 4, rc, d4)
        bias = Rot(P, "nab", 2, [128, 7, 512], F32, dma=True)
        scale = 64 ** -0.5
        for t in range(16 + 2):
            items = []
            if t < 16:
                pat = 0 if t == 0 else 1 if t == 1 else 3 if t == 14 else 4 if t == 15 else 2
                bt, rb, db = bias.next()
                P.dma("sp", bt[:], b_d[pat], db, writes=[rb])
                qa = [[q[:, h, t * 128:(t + 1) * 128]] for h in range(4)]
                for kt in range(7):
                    et = t + kt
                    items.append(dict(k=[[k[:, h, et * 128:(et + 1) * 128]] for h in range(4)], v=[v[:, et, h, :] for h in range(4)],
                                      bias=bt[:, kt, :], bres=[rb], res=[rk, rv]))
                qres = [rq]
            else:
                tc = t - 16
                qa = [[qc[:, h, tc * 128:(tc + 1) * 128]] for h in range(4)]
                qres = [rc]
            for kt in range(2):
                items.append(dict(k=[[kc[:, h, kt * 128:(kt + 1) * 128]] for h in range(4)], v=[vc[:, kt, h, :] for h in range(4)], bias=None, res=[rc]))
            o, ro, do = ost.next()
            attend(A, 4, False, qa, qres, items, scale, [o[:, 0, h * 64:(h + 1) * 64] for h in range(4)], ro)
            P.dma("sp", o_d[t * 128:(t + 1) * 128, :], o[:, 0, :], do, reads=[ro], writes=[rout], final=True)
        P.pop_scope()

    if do_swa:
        P.push_scope()
        EXT = 2304
        q_d = P.dram("sw_q", [256, NQ], F32, "ExternalInput")
        qs_d = P.dram("sw_qs", [256, NQ], F32, "ExternalInput")
        k_d = P.dram("sw_k", [128, EXT], F32, "ExternalInput")
        ks_d = P.dram("sw_ks", [128, EXT], F32, "ExternalInput")
        cs_d = P.dram("sw_cs", [2, 64, EXT], F32, "ExternalInput")
        v_d = P.dram("sw_v", [EXT, 128], F32, "ExternalInput")
        qc_d = P.dram("sw_qc", [256, NC], F32, "ExternalInput")
        kc_d = P.dram("sw_kc", [128, NC], F32, "ExternalInput")
        vc_d = P.dram("sw_vc", [NC, 128], F32, "ExternalInput")
        b_d = P.dram("sw_bias", [128, 4, 128], F32, "ExternalInput")
        sk_d = P.dram("sw_sink", [128, 4], F32, "ExternalInput")
        o_d = P.dram("o_sw", [NQ + NC, 256], F32, "ExternalOutput")
        q = P.sb("swq", [64, 4, NQ], BF16); rq = P.res()
        k = P.sb("swk", [64, 2, EXT], BF16); rk = P.res()
        v = P.sb("swv", [128, EXT // 128, 2, 65], BF16); rv = P.res(); d3 = P.dsem("swv")
        qc = P.sb("swqc", [64, 4, NC], BF16); kc = P.sb("swkc", [64, 2, NC], BF16); vc = P.sb("swvc", [128, 2, 2, 65], BF16)
        rc = P.res(); d4 = P.dsem("swc")
        cs = P.sb("swcs", [64, 2, EXT], F32); rcs = P.res(); d5 = P.dsem("swcs")
        bs = P.sb("swb", [128, 4, 128], F32); rbs = P.res()
        sk = P.sb("swsk", [128, 4], F32); rsk = P.res()
        P.dma("sp", cs[:, 0, :], cs_d[0], d5, writes=[rcs])
        P.dma("sp", cs[:, 1, :], cs_d[1], d5, writes=[rcs])
        P.dma("sp", bs[:], b_d, d5, writes=[rbs])
        P.dma("sp", sk[:], sk_d, d5, writes=[rsk])
        P.op("act", lambda: nc.scalar.activation(out=sk[:], in_=sk[:], func=AF.Exp), reads=[rsk], writes=[rsk])
        stg = Rot(P, "swstg", 4, [64, EXT], F32, dma=True)
        for h in range(4):
            rope_to(P, q[:, h, :], q_d[h * 64:(h + 1) * 64, :], qs_d[h * 64:(h + 1) * 64, :], cs[:, 0, 128:128 + NQ], cs[:, 1, 128:128 + NQ], rcs, NQ, 64, stg, rq)
            load_cast(P, qc[:, h, :], qc_d[h * 64:(h + 1) * 64, :], rc, d4)
        for g in range(2):
            rope_to(P, k[:, g, :], k_d[g * 64:(g + 1) * 64, :], ks_d[g * 64:(g + 1) * 64, :], cs[:, 0, :], cs[:, 1, :], rcs, EXT, 64, stg, rk)
            load_cast(P, kc[:, g, :], kc_d[g * 64:(g + 1) * 64, :], rc, d4)
        load_v(P, v, v_d, EXT // 128, 2, rv, d3)
        load_v(P, vc, vc_d, 2, 2, rc, d4)
        bp = P.sb("swbp", [128, 4, 4, 128], F32); rbp = P.res()
        for kind in range(4):
            for h in range(4):
                P.op("dve", lambda kind=kind, h=h: nc.vector.tensor_copy(out=bp[:, kind, h, :], in_=bs[:, kind, :]), reads=[rbs], writes=[rbp])
        scale = 64 ** -0.5
        for t in range(16 + 2):
            items = []
            if t < 16:
                qa = [[q[:, h, t * 128:(t + 1) * 128]] for h in range(4)]
                qres = [rq]
                for kt in range(3):
                    et = t + kt
                    if kt == 0:
                        b = bp[:, 0 if t == 0 else 1, :, :].rearrange('p h q -> p (h q)')
                    elif kt == 2:
                        b = bp[:, 3 if t == 15 else 2, :, :].rearrange('p h q -> p (h q)')
                    else:
                        b = None
                    items.append(dict(k=[[k[:, h // 2, et * 128:(et + 1) * 128]] for h in range(4)], v=[v[:, et, h // 2, :] for h in range(4)],
                                      bias=b, bres=[rbp], res=[rk, rv]))
            else:
                tc = t - 16
                qa = [[qc[:, h, tc * 128:(tc + 1) * 128]] for h in range(4)]
                qres = [rc]
            for kt in range(2):
                items.append(dict(k=[[kc[:, h // 2, kt * 128:(kt + 1) * 128]] for h in range(4)], v=[vc[:, kt, h // 2, :] for h in range(4)], bias=None, res=[rc]))
            o, ro, do = ost.next()
            attend(A, 4, False, qa, qres, items, scale, [o[:, 0, h * 64:(h + 1) * 64] for h in range(4)], ro, sinkexp=sk[:, 0:4], sink_res=rsk)
            P.dma("sp", o_d[t * 128:(t + 1) * 128, :], o[:, 0, :], do, reads=[ro], writes=[rout], final=True)
        P.pop_scope()

    if do_mla:
        P.push_scope()
        NK = 8448
        NQA = NQ + NC
        ckv_d = P.dram("m_ckv", [128, NK], F32, "ExternalInput")
        kr_d = P.dram("m_kr", [32, NK], F32, "ExternalInput")
        krs_d = P.dram("m_krs", [32, NK], F32, "ExternalInput")
        kcs_d = P.dram("m_kcs", [2, 32, NK], F32, "ExternalInput")
        cq_d = P.dram("m_cq", [256, NQA], F32, "ExternalInput")
        qcs_d = P.dram("m_qcs", [2, 32, NQA], F32, "ExternalInput")
        gkv_d = P.dram("m_gkv", [128, 1], F32, "ExternalInput")
        gq_d = P.dram("m_gq", [128, 2], F32, "ExternalInput")
        wkn_d = P.dram("m_wkn", [128, 256], F32, "ExternalInput")
        wkv_d = P.dram("m_wkv", [128, 256], F32, "ExternalInput")
        wqn_d = P.dram("m_wqn", [256, 256], F32, "ExternalInput")
        wqr_d = P.dram("m_wqr", [256, 128], F32, "ExternalInput")
        wqrs_d = P.dram("m_wqrs", [256, 128], F32, "ExternalInput")
        o_d = P.dram("o_ml", [NQA, 256], F32, "ExternalOutput")

        ones = P.sb("mones", [128, 128], BF16); rones = P.res()
        P.op("dve", lambda: nc.vector.memset(ones[:], 1.0), writes=[rones])
        dsm = P.dsem("msmall")
        gkv = P.sb("gkv", [128, 1], F32); gq = P.sb("gq", [128, 2], F32); rg = P.res()
        P.dma("sp", gkv[:], gkv_d, dsm, writes=[rg]); P.dma("sp", gq[:], gq_d, dsm, writes=[rg])
        wkn = P.sb("wkn", [128, 256], BF16); wkv = P.sb("wkv", [128, 256], BF16)
        wqn = P.sb("wqn", [128, 2, 256], BF16); wqr = P.sb("wqr", [128, 2, 128], BF16); wqrs = P.sb("wqrs", [128, 2, 128], BF16)
        rw = P.res(); dw = P.dsem("mw")
        load_cast(P, wkn[:], wkn_d, rw, dw); load_cast(P, wkv[:], wkv_d, rw, dw)
        for c in range(2):
            load_cast(P, wqn[:, c, :], wqn_d[c * 128:(c + 1) * 128, :], rw, dw)
            load_cast(P, wqr[:, c, :], wqr_d[c * 128:(c + 1) * 128, :], rw, dw)
            load_cast(P, wqrs[:, c, :], wqrs_d[c * 128:(c + 1) * 128, :], rw, dw)

        kn = P.sb("kn", [128, 2, NK], BF16); rkn = P.res()
        kr = P.sb("kr", [32, NK], BF16); rkr = P.res()
        vm = P.sb("vm", [128, NK // 128, 4, 65], BF16); rvm = P.res()
        qn = P.sb("qn", [128, 2, NQA], BF16); rqn = P.res()
        qr = P.sb("qr", [32, 4, NQA], BF16); rqr = P.res()
        P.op("dve", lambda: nc.vector.memset(vm[:], 1.0), writes=[rvm])

        xin = Rot(P, "mx", 2, [128, 2, 512], F32, dma=True)
        sq = Rot(P, "msq", 2, [128, 2, 512], BF16)
        rms = Rot(P, "mrms", 2, [128, 512], F32)
        tt = Rot(P, "mtt", 2, [128, 512], F32)
        xn = Rot(P, "mxn", 2, [128, 2, 512], BF16)
        tab = Rot(P, "mtab", 2, [32, 2, 512], F32, dma=True)
        rr = Rot(P, "mrr", 4, [32, 512], F32, dma=True)
        evi = [0]

        def evac(dst, src, reads, writes):
            evi[0] += 1
            if evi[0] % 2 == 0:
                P.op("act", lambda: nc.scalar.copy(out=dst, in_=src), reads=reads, writes=writes)
            else:
                P.op("dve", lambda: nc.vector.tensor_copy(out=dst, in_=src), reads=reads, writes=writes)

        def norm_block(src_d, kc, t0, n, g):
            x, rx, dx = xin.next()
            for c in range(kc):
                P.dma("sp", x[:, c, :n], src_d[c * 128:(c + 1) * 128, t0:t0 + n], dx, writes=[rx])
            s, rs, _ = sq.next()
            for c in range(kc):
                P.op("act", lambda c=c: nc.scalar.activation(out=s[:, c, :n], in_=x[:, c, :n], func=AF.Square), reads=[rx], writes=[rs])
            pt, rpt, _ = A.misc.next()
            for c in range(kc):
                P.op("pe", lambda c=c: nc.tensor.matmul(pt[:, :n], lhsT=ones[:], rhs=s[:, c, :n], start=(c == 0), stop=(c == kc - 1)), reads=[rones, rs], writes=[rpt])
            r, rrr, _ = rms.next()
            P.op("act", lambda: nc.scalar.activation(out=r[:, :n], in_=pt[:, :n], func=AF.Sqrt, scale=1.0 / (kc * 128), bias=EPS), reads=[rpt], writes=[rrr])
            P.op("dve", lambda: nc.vector.reciprocal(out=r[:, :n], in_=r[:, :n]), reads=[rrr], writes=[rrr])
            y, ry, _ = xn.next()
            for c in range(kc):
                P.op("dve", lambda c=c: nc.vector.scalar_tensor_tensor(out=y[:, c, :n], in0=x[:, c, :n], scalar=g[:, c:c + 1], in1=r[:, :n],
                                                                     op0=ALU.mult, op1=ALU.mult), reads=[rx, rrr, rg], writes=[ry])
            return y, ry

        def rope_block(dst, x_d, xs_d, cs_d, t0, n, res):
            tb, rtb, dtb = tab.next()
            P.dma("sp", tb[:, 0, :n], cs_d[0][:, t0:t0 + n], dtb, writes=[rtb])
            P.dma("sp", tb[:, 1, :n], cs_d[1][:, t0:t0 + n], dtb, writes=[rtb])
            a, ra, da = rr.next(); b, rb, db = rr.next()
            P.dma("sp", a[:, :n], x_d[:, t0:t0 + n], da, writes=[ra])
            P.dma("sp", b[:, :n], xs_d[:, t0:t0 + n], db, writes=[rb])
            P.op("dve", lambda: nc.vector.tensor_tensor(out=a[:, :n], in0=a[:, :n], in1=tb[:, 0, :n], op=ALU.mult), reads=[ra, rtb], writes=[ra])
            P.op("pool", lambda: nc.gpsimd.tensor_tensor(out=b[:, :n], in0=b[:, :n], in1=tb[:, 1, :n], op=ALU.mult), reads=[rb, rtb], writes=[rb])
            P.op("dve", lambda: nc.vector.tensor_tensor(out=dst, in0=a[:, :n], in1=b[:, :n], op=ALU.add), reads=[ra, rb], writes=[res])

        for t0 in range(0, NK, 512):
            n = min(512, NK - t0)
            y, ry = norm_block(ckv_d, 1, t0, n, gkv)
            for pr in range(2):
                pt, rpt, _ = A.misc.next()
                P.op("pe", lambda pr=pr, pt=pt: nc.tensor.matmul(pt[:, :n], lhsT=wkn[:, pr * 128:(pr + 1) * 128], rhs=y[:, 0, :n], start=True, stop=True),
                     reads=[rw, ry], writes=[rpt])
                evac(kn[:, pr, t0:t0 + n], pt[:, :n], [rpt], [rkn])
            for tt_ in range(n // 128):
                kt = t0 // 128 + tt_
                pt, rpt, _ = A.misc.next()
                P.op("pe", lambda tt_=tt_, pt=pt: nc.tensor.matmul(pt[:, 0:256], lhsT=y[:, 0, tt_ * 128:(tt_ + 1) * 128], rhs=wkv[:], start=True, stop=True),
                     reads=[rw, ry], writes=[rpt])
                evac(vm[:, kt, :, 0:64], pt[:, 0:256].rearrange("p (h d) -> p h d", h=4), [rpt], [rvm])
            rope_block(kr[:, t0:t0 + n], kr_d, krs_d, kcs_d, t0, n, rkr)
        for t0 in range(0, NQA, 512):
            n = min(512, NQA - t0)
            y, ry = norm_block(cq_d, 2, t0, n, gq)
            for pr in range(2):
                pt, rpt, _ = A.misc.next()
                for c in range(2):
                    P.op("pe", lambda pr=pr, pt=pt, c=c: nc.tensor.matmul(pt[:, :n], lhsT=wqn[:, c, pr * 128:(pr + 1) * 128], rhs=y[:, c, :n],
                                                                         start=(c == 0), stop=(c == 1)), reads=[rw, ry], writes=[rpt])
                evac(qn[:, pr, t0:t0 + n], pt[:, :n], [rpt], [rqn])
            tb, rtb, dtb = tab.next()
            P.dma("sp", tb[:, 0, :n], qcs_d[0][:, t0:t0 + n], dtb, writes=[rtb])
            P.dma("sp", tb[:, 1, :n], qcs_d[1][:, t0:t0 + n], dtb, writes=[rtb])
            for h in range(4):
                pa, rpa, _ = A.misc.next()
                for c in range(2):
                    P.op("pe", lambda pa=pa, c=c, h=h: nc.tensor.matmul(pa[0:32, :n], lhsT=wqr[:, c, h * 32:(h + 1) * 32], rhs=y[:, c, :n],
                                                                       start=(c == 0), stop=(c == 1)), reads=[rw, ry], writes=[rpa])
                a, ra, _ = rr.next()
                P.op("dve", lambda a=a, pa=pa: nc.vector.tensor_tensor(out=a[:, :n], in0=pa[0:32, :n], in1=tb[:, 0, :n], op=ALU.mult), reads=[rpa, rtb], writes=[ra])
                pb, rpb, _ = A.misc.next()
                for c in range(2):
                    P.op("pe", lambda pb=pb, c=c, h=h: nc.tensor.matmul(pb[0:32, :n], lhsT=wqrs[:, c, h * 32:(h + 1) * 32], rhs=y[:, c, :n],
                                                                       start=(c == 0), stop=(c == 1)), reads=[rw, ry], writes=[rpb])
                b, rb, _ = rr.next()
                P.op("dve", lambda b=b, pb=pb: nc.vector.tensor_tensor(out=b[:, :n], in0=pb[0:32, :n], in1=tb[:, 1, :n], op=ALU.mult), reads=[rpb, rtb], writes=[rb])
                P.op("pool", lambda a=a, b=b, h=h: nc.gpsimd.tensor_tensor(out=qr[:, h, t0:t0 + n], in0=a[:, :n], in1=b[:, :n], op=ALU.add), reads=[ra, rb], writes=[rqr])
        scale = 96 ** -0.5
        groups = [(g * 512, 4, NK // 128) for g in range(4)] + [(2048, 2, 2)]
        for (q0, ncol, nkt) in groups:
            o, ro, do = ost.next()
            for h in range(4):
                hp, pr = (h % 2) * 64, h // 2
                qa = [[qn[hp:hp + 64, pr, q0:q0 + ncol * 128], qr[:, h, q0:q0 + ncol * 128]]]
                items = []
                for kt in range(nkt):
                    items.append(dict(k=[[kn[hp:hp + 64, pr, kt * 128:(kt + 1) * 128], kr[:, kt * 128:(kt + 1) * 128]]],
                                      v=[vm[:, kt, h, :]] * ncol, bias=None, res=[rkn, rkr, rvm]))
                attend(A, ncol, True, qa, [rqn, rqr], items, scale, [o[:, c, h * 64:(h + 1) * 64] for c in range(ncol)], ro)
            P.dma("sp", o_d[q0:q0 + ncol * 128, :].rearrange("(c p) f -> p c f", p=128), o[:, 0:ncol, :], do, reads=[ro], writes=[rout], final=True)
        P.pop_scope()
    P.finish()
    P.close()
    return P


D = 1024
KC = 8
EPS = 1e-6
NTOK = 2304
NQ = 2048
NCX = 256
NCOL = 2728
NA0, SW0, ML0, SS0 = 0, 768, 1280, 1696
SBROWS = 1312
R_NAK, R_SWK, R_CKV, R_XBC, R_KR = 0, 256, 384, 512, 1280
YCH = 2816
G4 = [[0, 1, 2, 3], [4, 5, 6, 7]]
LSEQ = 8448
NCH = LSEQ // 128


class RotView:
    def __init__(self, rots):
        self.t, self.r, self.d = [], [], []
        for ro in rots:
            self.t += ro.t; self.r += ro.r; self.d += ro.d
        self.i = -1
        self.n = len(self.t)

    def next(self):
        self.i = (self.i + 1) % self.n
        return self.t[self.i], self.r[self.i], self.d[self.i]


def allgather(P, src, dst, reads, writes):
    nc = P.nc
    if "cc" not in P.sem:
        P.sem["cc"] = P.root.enter_context(nc.semaphore("s_cc"))
        P.cnt["cc"] = 0
    P._emit_waits("pool", P._deps("pool", reads, writes))
    ins = nc.gpsimd.collective_compute("AllGather", mybir.AluOpType.bypass, replica_groups=G4, ins=[src.opt()], outs=[dst.opt()])
    P.cnt["cc"] += 1
    ins.then_inc(P.sem["cc"])
    P._record(("cc", P.cnt["cc"], "dma"), reads, writes)
    nc.gpsimd.wait_ge(P.sem["cc"], P.cnt["cc"])
    P.seen["pool"]["cc"] = P.cnt["cc"]


def seg_lat(t0, n):
    out = []
    t = t0
    while t < t0 + n:
        r = t // 2048
        ln = min(t0 + n, (r + 1) * 2048) - t
        out.append((r, t - r * 2048, ln, t - t0))
        t += ln
    return out


def build_M(nlayers=4, final_layer=3, NLW=4):
    P = Prog()
    nc = P.nc
    X = lambda name, shape: P.dram(name, shape, F32, "ExternalInput")
    hT0_d = X("hT0", [D, NTOK]); cv_d = X("cv", [128, KC, 2]); flg_d = X("flg", [128, 16])
    selx_d = X("selx", [128, 2, 64]); selb_d = X("selb", [128, 2, 128])
    w_in_d = X("w_in", [NLW, D, NCOL]); w_mod_d = X("w_mod", [NLW, D, 6 * D]); bmod_d = X("bmod", [NLW, 128, 48])
    g1_d = X("g1", [NLW, 128, KC]); gv_d = X("gv", [NLW, 128, 20])
    wo_d = X("w_out", [NLW, D, D]); w1_d = X("w1", [NLW, D, 4 * D]); w2_d = X("w2", [NLW, 4 * D, D])
    nab_d = X("na_bias", [NLW, 5, 128, 7, 512]); swb_d = X("sw_bias", [128, 4, 128]); swk_d = X("sw_sink", [NLW, 128, 4])
    swcs_d = X("sw_cs", [2, 64, 2304]); mkcs_d = X("m_kcs", [2, 32, LSEQ]); mqcs_d = X("m_qcs", [2, 32, NTOK])
    mgkv_d = X("m_gkv", [NLW, 128, 1]); mgq_d = X("m_gq", [NLW, 128, 2])
    mwkn_d = X("m_wkn", [NLW, 128, 256]); mwkv_d = X("m_wkv", [NLW, 128, 256])
    mwqn_d = X("m_wqn", [NLW, 256, 256]); mwqr_d = X("m_wqr", [NLW, 256, 128]); mwqrs_d = X("m_wqrs", [NLW, 256, 128])
    scw_d = X("s_cw", [NLW, 128, 3, 6]); spar_d = X("s_par", [NLW, 128, 8]); su_d = X("s_u", [2, 128, 128]); id_d = X("ident", [128, 128])
    out_d = P.dram("out", [D, NQ], F32, "ExternalOutput")
    S_ = lambda name, shape: nc.dram_tensor(name, list(shape), F32).ap()
    hT_s = [S_("hTa", [D, NTOK]), S_("hTb", [D, NTOK])]
    pT_d = S_("pT", [NCOL, NTOK]); vtok_d = S_("vtok", [NTOK, 392])
    sb_nr = [64] * 20 + [32]
    SBc = [S_(f"SBc{c}", [sb_nr[c], NTOK]) for c in range(21)]
    RBc = [S_(f"RBc{c}", [4 * sb_nr[c], NTOK]) for c in range(21)]
    vg_nr = [512] * 4 + [256]
    VSc = [S_(f"VSc{i}", [vg_nr[i], 392]) for i in range(5)]
    VGc = [S_(f"VGc{i}", [4 * vg_nr[i], 392]) for i in range(5)]
    YSc = [S_(f"YSc{i}", [64, YCH]) for i in range(3)]
    YGc = [S_(f"YGc{i}", [256, YCH]) for i in range(3)]
    h2_d = S_("h2", [D, NTOK])
    DTS_d = S_("DTS", [NTOK, 8]); DTG_d = S_("DTG", [4 * NTOK, 8]); r_DTG = P.res()

    def rbuf(r, row0, nrows, c0, c1):
        c = row0 // 64
        assert row0 + nrows <= 64 * c + sb_nr[c], (row0, nrows)
        o = r * sb_nr[c] + row0 - 64 * c
        return RBc[c][o:o + nrows, c0:c1]

    def vgbuf(r, row0, nrows, c0, c1):
        i = row0 // 512
        assert row0 + nrows <= 512 * i + vg_nr[i], (row0, nrows)
        o = r * vg_nr[i] + row0 - 512 * i
        return VGc[i][o:o + nrows, c0:c1]
    r_pT, r_vtok, r_SB, r_RB1, r_VG, r_YS, r_YG, r_h2, r_RB2 = (P.res() for _ in range(9))
    r_hT = [P.res(), P.res()]

    A = AttnCtx(P)
    extra = Rot(P, "xps", 1, [128, 512], F32, psum=True)
    gen = RotView([A.misc, extra, A.panels, A.accs])
    ones = P.sb("ones", [128, 128], BF16); rones = P.res()
    P.op("dve", lambda: nc.vector.memset(ones[:], 1.0), writes=[rones])
    onesf = P.sb("onesf", [128, 128], F32); ronesf = P.res()
    P.op("pool", lambda: nc.gpsimd.memset(onesf[:], 1.0), writes=[ronesf])
    dc = P.dsem("const")
    ident = P.sb("ident", [128, 128], F32); rid = P.res(); P.dma("sp", ident[:], id_d, dc, writes=[rid])
    flg = P.sb("flg", [128, 16], F32); rflg = P.res(); P.dma("sp", flg[:], flg_d, dc, writes=[rflg])
    U = P.sb("U", [128, 2, 128], F32); rU = P.res()
    P.dma("sp", U[:, 0, :], su_d[0], dc, writes=[rU]); P.dma("sp", U[:, 1, :], su_d[1], dc, writes=[rU])
    selx = P.sb("selx", [128, 2, 64], F32); selb = P.sb("selb", [128, 2, 128], F32); rsel = P.res()
    P.dma("sp", selx[:], selx_d, dc, writes=[rsel]); P.dma("sp", selb[:], selb_d, dc, writes=[rsel])
    cvs = P.sb("cvs", [128, KC, 2], F32); rcv = P.res(); P.dma("sp", cvs[:], cv_d, dc, writes=[rcv])
    ca = P.sb("ca", [128, KC, 2], F32); rca = P.res()
    P.op("act", lambda: nc.scalar.activation(out=ca[:], in_=cvs[:], func=AF.Silu), reads=[rcv], writes=[rca])
    moL = [P.sb("mo", [128, 48, 2], F32) for _ in range(2)]; rmoL = [P.res(), P.res()]
    gs1L = [P.sb("gs1", [128, KC, 2], F32) for _ in range(2)]; gs2L = [P.sb("gs2", [128, KC, 2], F32) for _ in range(2)]; rgsL = [P.res(), P.res()]
    gvsL = [P.sb("gvs", [128, 28], F32) for _ in range(2)]; rgvL = [P.res(), P.res()]
    roT = P.res()
    evi = [0]

    def evac(dst, src, reads, writes):
        evi[0] += 1
        if evi[0] % 2 == 0:
            P.op("act", lambda: nc.scalar.copy(out=dst, in_=src), reads=reads, writes=writes)
        else:
            P.op("dve", lambda: nc.vector.tensor_copy(out=dst, in_=src), reads=reads, writes=writes)

    def rstd_of(x, rx, kc, n, r, rr, s, rs, feat):
        for k in range(kc):
            P.op("act", lambda k=k: nc.scalar.activation(out=s[:, k, :n], in_=x[:, k, :n], func=AF.Square), reads=[rx], writes=[rs])
        pt, rpt, _ = gen.next()
        for k in range(kc):
            P.op("pe", lambda k=k: nc.tensor.matmul(pt[:, :n], lhsT=ones[:], rhs=s[:, k, :n], start=(k == 0), stop=(k == kc - 1)), reads=[rones, rs], writes=[rpt])
        P.op("act", lambda: nc.scalar.activation(out=r[:, :n], in_=pt[:, :n], func=AF.Sqrt, scale=1.0 / feat, bias=EPS), reads=[rpt], writes=[rr])
        P.op("dve", lambda: nc.vector.reciprocal(out=r[:, :n], in_=r[:, :n]), reads=[rr], writes=[rr])

    def emit_mod(l):
        mo, rmo, gs1, gs2, rgs, gvs, rgv = moL[l % 2], rmoL[l % 2], gs1L[l % 2], gs2L[l % 2], rgsL[l % 2], gvsL[l % 2], rgvL[l % 2]
        P.push_scope()
        dm = P.dsem(f"mod{l}")
        bm = P.sb("bm", [128, 48], F32); rbm = P.res()
        P.dma("sp", bm[:], bmod_d[l], dm, writes=[rbm])
        P.dma("sp", gvs[:, 0:20], gv_d[l], dm, writes=[rgv]); P.dma("sp", gvs[:, 20:28], g1_d[l], dm, writes=[rgv])
        wm = Rot(P, "wm", 2, [128, KC, 1024], F32, dma=True)
        for grp in range(6):
            w, rw_, dw_ = wm.next()
            for k in range(KC):
                P.dma("sp", w[:, k, :], w_mod_d[l][k * 128:(k + 1) * 128, grp * 1024:(grp + 1) * 1024], dw_, writes=[rw_])
            pt, rpt, _ = gen.next()
            for cb in range(8):
                for k in range(KC):
                    P.op("pe", lambda cb=cb, k=k: nc.tensor.matmul(pt[:, cb * 2:cb * 2 + 2], lhsT=w[:, k, cb * 128:(cb + 1) * 128], rhs=ca[:, k, :],
                                                                 start=(k == 0), stop=(k == KC - 1)), reads=[rw_, rca], writes=[rpt])
            for j in range(2):
                P.op("dve", lambda j=j: nc.vector.tensor_tensor(out=mo[:, grp * 8:(grp + 1) * 8, j], in0=pt[:, j:16:2], in1=bm[:, grp * 8:(grp + 1) * 8], op=ALU.add),
                     reads=[rpt, rbm], writes=[rmo])
        for j in range(2):
            P.op("dve", lambda j=j: nc.vector.scalar_tensor_tensor(out=gs1[:, :, j], in0=mo[:, 8:16, j], scalar=1.0, in1=gvs[:, 20:28], op0=ALU.add, op1=ALU.mult),
                 reads=[rmo, rgv], writes=[rgs])
            P.op("dve", lambda j=j: nc.vector.scalar_tensor_tensor(out=gs2[:, :, j], in0=mo[:, 32:40, j], scalar=1.0, in1=gvs[:, 0:8], op0=ALU.add, op1=ALU.mult),
                 reads=[rmo, rgv], writes=[rgs])
        P.pop_scope()


    for l in range(nlayers):
        final = (l == final_layer)
        hT_d, r_hin = (hT0_d, None) if l == 0 else (hT_s[(l - 1) % 2], r_hT[(l - 1) % 2])
        hTn_d, r_hn = hT_s[l % 2], r_hT[l % 2]

        mo, rmo, gs1, gs2, rgs, gvs, rgv = moL[l % 2], rmoL[l % 2], gs1L[l % 2], gs2L[l % 2], rgsL[l % 2], gvsL[l % 2], rgvL[l % 2]
        if l == 0:
            emit_mod(0)
        P.push_scope()
        wsb = P.sb("wsb", [128, KC, NCOL], BF16); rw = P.res(); dw = P.dsem(f"w{l}")
        wv = P.sb("wv", [128, KC, 392], BF16)
        for k in range(KC):
            for c0 in range(0, NCOL, 2048):
                c1 = min(NCOL, c0 + 2048)
                P.dma("pool", wsb[:, k, c0:c1], w_in_d[l][k * 128:(k + 1) * 128, c0:c1], dw, writes=[rw])
            P.dma("pool", wv[:, k, 0:256], w_in_d[l][k * 128:(k + 1) * 128, 512:768], dw, writes=[rw])
            P.dma("pool", wv[:, k, 256:384], w_in_d[l][k * 128:(k + 1) * 128, SW0 + 384:SW0 + 512], dw, writes=[rw])
            P.dma("pool", wv[:, k, 384:392], w_in_d[l][k * 128:(k + 1) * 128, SS0 + 1024:SS0 + 1032], dw, writes=[rw])
        hin = Rot(P, "hin", 2, [128, KC, 512], F32, dma=True)
        sq = Rot(P, "sq", 2, [128, KC, 512], BF16)
        xm = Rot(P, "xm", 2, [128, KC, 512], BF16)
        tt = Rot(P, "tt", 2, [128, 512], F32)
        rms = Rot(P, "rms", 2, [128, 512], F32)
        ost = Rot(P, "ost", 4, [128, 512], F32, dma=True)
        blocks = [(t0, 512, 0) for t0 in range(0, NQ, 512)] + [(NQ, 256, 1)]
        ncb = (NCOL + 127) // 128
        def a_prep(t0, n, j):
            h, rh, dh = hin.next()
            P.dma("sp", h[:, :, :n], hT_d.rearrange("(k p) t -> p k t", p=128)[:, :, t0:t0 + n], dh, reads=[r_hin], writes=[rh])
            s, rs, _ = sq.next(); r, rr, _ = rms.next()
            rstd_of(h, rh, KC, n, r, rr, s, rs, D)
            x, rx, _ = xm.next()
            for k in range(KC):
                t, rt, _ = tt.next()
                P.op("dve", lambda k=k, t=t: nc.vector.tensor_tensor(out=t[:, :n], in0=h[:, k, :n], in1=r[:, :n], op=ALU.mult), reads=[rh, rr], writes=[rt])
                P.op("act", lambda k=k, t=t: nc.scalar.activation(out=x[:, k, :n], in_=t[:, :n], func=AF.Identity, scale=gs1[:, k, j:j + 1], bias=mo[:, k, j:j + 1]),
                     reads=[rt, rgs, rmo], writes=[rx])
            return x, rx

        def a_main(t0, n, j, x, rx):
            for cb in range(ncb):
                c0 = cb * 128
                m = min(128, NCOL - c0)
                pt, rpt, _ = gen.next()
                for k in range(KC):
                    P.op("pe", lambda k=k, pt=pt: nc.tensor.matmul(pt[:m, :n], lhsT=wsb[:, k, c0:c0 + m], rhs=x[:, k, :n], start=(k == 0), stop=(k == KC - 1)),
                         reads=[rw, rx], writes=[rpt])
                o, ro, do = ost.next()
                evac(o[:m, :n], pt[:m, :n], [rpt], [ro])
                P.dma("sp", pT_d[c0:c0 + m, t0:t0 + n], o[:m, :n], do, reads=[ro], writes=[r_pT])
            for ti in range(n // 128):
                pt, rpt, _ = gen.next()
                for k in range(KC):
                    P.op("pe", lambda k=k, pt=pt: nc.tensor.matmul(pt[:, 0:392], lhsT=x[:, k, ti * 128:(ti + 1) * 128], rhs=wv[:, k, :], start=(k == 0), stop=(k == KC - 1)),
                         reads=[rw, rx], writes=[rpt])
                o, ro, do = ost.next()
                evac(o[:, 0:392], pt[:, 0:392], [rpt], [ro])
                P.dma("sp", vtok_d[t0 + ti * 128:t0 + (ti + 1) * 128, :], o[:, 0:392], do, reads=[ro], writes=[r_vtok])
                P.dma("sp", DTS_d[t0 + ti * 128:t0 + (ti + 1) * 128, :], o[:, 384:392], do, reads=[ro], writes=[r_vtok])
        pend_ = None
        for blk in blocks + [None]:
            cur_ = (blk, a_prep(*blk)) if blk is not None else None
            if pend_ is not None:
                a_main(*pend_[0], *pend_[1])
            pend_ = cur_
        P.pop_scope()

        dx = P.dsem(f"x1_{l}")
        for (d0, s0, nr) in ((R_NAK, 256, 256), (R_SWK, SW0 + 256, 128), (R_CKV, ML0 + 256, 128), (R_XBC, SS0 + 256, 768), (R_KR, ML0 + 384, 32)):
            for rr0 in range(0, nr, 64):
                n_ = min(64, nr - rr0)
                P.dma("sp", SBc[(d0 + rr0) // 64][0:n_, :], pT_d[s0 + rr0:s0 + rr0 + n_, :], dx, reads=[r_pT], writes=[r_SB])
        for i in range(5):
            P.dma("sp", VSc[i][:, :], vtok_d[512 * i:512 * i + vg_nr[i], :], dx, reads=[r_vtok], writes=[r_SB])
        P.barrier()
        for c in range(8, 20):
            allgather(P, SBc[c], RBc[c], [r_SB], [r_RB1])
        allgather(P, DTS_d, DTG_d, [r_vtok], [r_DTG])
        if l + 1 < nlayers:
            emit_mod(l + 1)
        for c in list(range(8)) + [20]:
            allgather(P, SBc[c], RBc[c], [r_SB], [r_RB2])
        for i in range(5):
            allgather(P, VSc[i], VGc[i], [r_SB], [r_VG])

        P.push_scope()
        ds_ = P.dsem(f"ssm{l}")
        cw = P.sb("cw", [128, 3, 6], F32); rcw = P.res(); P.dma("sp", cw[:], scw_d[l], ds_, writes=[rcw])
        par = P.sb("par", [128, 8], F32); rpar = P.res(); P.dma("sp", par[:], spar_d[l], ds_, writes=[rpar])
        dtall = P.sb("dtall", [128, NCH, 8], F32); rdta_ = P.res()
        P.dma("sp", dtall[:, 0:2, :], DTG_d[NQ:NQ + 256, :].rearrange("(c p) e -> p c e", p=128), ds_, reads=[r_DTG], writes=[rdta_])
        for r in range(4):
            P.dma("sp", dtall[:, 2 + 16 * r:2 + 16 * (r + 1), :], DTG_d[r * NTOK:r * NTOK + NQ, :].rearrange("(c p) e -> p c e", p=128), ds_,
                  reads=[r_DTG], writes=[rdta_])
        dt = P.sb("dt", [128, NCH, 2], F32); rdt = P.res()
        dta = P.sb("dta", [128, NCH, 2], F32); rdta = P.res()
        for d in range(2):
            P.op("dve", lambda d=d: nc.vector.tensor_scalar(out=dt[:, :, d], in0=dtall[:, :, d * 4], scalar1=flg[:, 0:1], scalar2=None, op0=ALU.mult),
                 reads=[rdta_, rflg], writes=[rdt])
            for hh in range(1, 4):
                P.op("dve", lambda d=d, hh=hh: nc.vector.scalar_tensor_tensor(out=dt[:, :, d], in0=dtall[:, :, d * 4 + hh], scalar=flg[:, hh:hh + 1], in1=dt[:, :, d],
                                                                            op0=ALU.mult, op1=ALU.add), reads=[rdta_, rflg, rdt], writes=[rdt])
        av = P.sb("av", [128, 2], F32); rav = P.res()
        P.op("act", lambda: nc.scalar.activation(out=av[:], in_=par[:, 2:4], func=AF.Exp), reads=[rpar], writes=[rav])
        P.op("dve", lambda: nc.vector.tensor_scalar(out=av[:], in0=av[:], scalar1=-1.0, scalar2=None, op0=ALU.mult), reads=[rav], writes=[rav])
        for d in range(2):
            P.op("act", lambda d=d: nc.scalar.activation(out=dt[:, :, d], in_=dt[:, :, d], func=AF.Exp, bias=par[:, d:d + 1]), reads=[rdt, rpar], writes=[rdt])
        for d in range(2):
            P.op("act", lambda d=d: nc.scalar.activation(out=dt[:, :, d], in_=dt[:, :, d], func=AF.Ln, bias=1.0), reads=[rdt], writes=[rdt])
        for d in range(2):
            P.op("dve", lambda d=d: nc.vector.tensor_scalar(out=dta[:, :, d], in0=dt[:, :, d], scalar1=av[:, d:d + 1], scalar2=None, op0=ALU.mult),
                 reads=[rdt, rav], writes=[rdta])
        xT = P.sb("xT", [64, LSEQ], F32); rx_ = P.res()
        bT = P.sb("bT", [128, LSEQ], BF16); rb_ = P.res()
        cT = P.sb("cT", [128, LSEQ], BF16); rc_ = P.res()
        yb = P.sb("yb", [64, LSEQ], F32); ry_ = P.res()
        P.push_scope()
        raw = Rot(P, "raw", 6, [128, 2, 512], F32, dma=True)
        cacc = Rot(P, "cacc", 3, [128, 508], F32)
        CB = 508
        for gi, (row0, npart, sel, dst, rdst) in enumerate(((R_XBC, 64, selx, xT, rx_), (R_XBC + 256, 128, selb, bT, rb_), (R_XBC + 512, 128, selb, cT, rc_))):
            for (s0, sl, is_ctx) in ((0, 256, True), (256, 8192, False)):
                for t0 in range(0, sl, CB):
                    n = min(CB, sl - t0)
                    rt_, rr_, dr_ = raw.next()
                    lo = max(t0 - 2, 0); hi = min(t0 + n + 2, sl)
                    if lo > t0 - 2 or hi < t0 + n + 2:
                        P.op("dve", lambda rt_=rt_: nc.vector.memset(rt_[:], 0.0), writes=[rr_])
                    if is_ctx:
                        segs = [(0, NQ + lo, hi - lo, lo - (t0 - 2))]
                    else:
                        segs = [(r, ls, ln, do_ + lo - (t0 - 2)) for (r, ls, ln, do_) in seg_lat(lo, hi - lo)]
                    for (r, ls, ln, do_) in segs:
                        for kc in range(2):
                            for hf in range(2):
                                P.dma("sp", rt_[hf * 64:(hf + 1) * 64, kc, do_:do_ + ln], rbuf(r, row0 + kc * 128 + hf * 64, 64, ls, ls + ln), dr_,
                                      reads=[r_RB1], writes=[rr_])
                    pt, rpt, _ = gen.next()
                    for kc in range(2):
                        P.op("pe", lambda kc=kc, pt=pt: nc.tensor.matmul(pt[:npart, 0:n + 4], lhsT=sel[:, kc, :], rhs=rt_[:, kc, 0:n + 4], start=(kc == 0), stop=(kc == 1)),
                             reads=[rsel, rr_], writes=[rpt])
                    a, ra, _ = cacc.next()
                    P.op("dve", lambda: nc.vector.tensor_scalar(out=a[:npart, :n], in0=pt[:npart, 0:n], scalar1=cw[:npart, gi, 0:1], scalar2=None, op0=ALU.mult),
                         reads=[rpt, rcw], writes=[ra])
                    for k in range(1, 5):
                        P.op("dve", lambda k=k: nc.vector.scalar_tensor_tensor(out=a[:npart, :n], in0=pt[:npart, k:k + n], scalar=cw[:npart, gi, k:k + 1], in1=a[:npart, :n],
                                                                             op0=ALU.mult, op1=ALU.add), reads=[rpt, rcw, ra], writes=[ra])
                    P.op("act", lambda: nc.scalar.activation(out=dst[:npart, s0 + t0:s0 + t0 + n], in_=a[:npart, :n], func=AF.Silu, bias=cw[:npart, gi, 5:6]),
                         reads=[ra, rcw], writes=[rdst])
        P.pop_scope()
        xtokA = P.sb("xtokA", [128, NCH, 64], F32); rxt = P.res()
        btokA = P.sb("btokA", [128, NCH, 128], BF16); rbtk = P.res()
        gt = Rot(P, "gt", 2, [128, 128], F32)
        for c in range(NCH):
            sl = slice(c * 128, (c + 1) * 128)
            px, rpx, _ = gen.next()
            P.op("pe", lambda: nc.tensor.transpose(px[:, 0:64], xT[:, sl], ident[0:64, 0:64]), reads=[rx_, rid], writes=[rpx])
            evac(xtokA[:, c, :], px[:, 0:64], [rpx], [rxt])
            btf, rbtf, _ = gt.next()
            P.op("act", lambda: nc.scalar.copy(out=btf[:], in_=bT[:, sl]), reads=[rb_], writes=[rbtf])
            pb, rpb, _ = gen.next()
            P.op("pe", lambda: nc.tensor.transpose(pb[:, 0:128], btf[:], ident[:]), reads=[rbtf, rid], writes=[rpb])
            evac(btokA[:, c, :], pb[:, 0:128], [rpb], [rbtk])
        ryc = [P.res() for _ in range(NCH)]
        for c in range(NCH):
            sl = slice(c * 128, (c + 1) * 128)
            P.op("pool", lambda: nc.gpsimd.tensor_scalar(out=yb[:, sl], in0=xT[:, sl], scalar1=par[0:64, 4:5], scalar2=None, op0=ALU.mult), reads=[rx_, rpar], writes=[ryc[c]])
        hs = [P.sb("hs", [128, 64], F32) for _ in range(2)]; rhs_ = [P.res(), P.res()]
        hsb = [P.sb("hsb", [128, 64], BF16) for _ in range(2)]; rhsb = [P.res(), P.res()]
        NS = 4
        dtab = Rot(P, "dtab", NS, [128, 128], F32); ccol = Rot(P, "ccol", 2 * NS, [128, 4], F32)
        dd = Rot(P, "dd", NS, [128, 128], F32); ee = Rot(P, "ee", NS, [128, 128], F32)
        gm = Rot(P, "gm", NS, [128, 128], F32); mt = Rot(P, "mt", NS, [128, 128], BF16)
        ecr = Rot(P, "ecr", NS, [128, 128], F32); cs = Rot(P, "cs", NS, [128, 128], BF16)
        xdt = Rot(P, "xdt", NS, [128, 64], BF16); xdd = Rot(P, "xdd", NS, [128, 64], BF16)
        sst = Rot(P, "sst", NS, [128, 64], F32); hsr = Rot(P, "hsr", NS, [128, 64], BF16)
        hcur = [None, None]
        for d in range(2):
            P.op("dve", lambda d=d: nc.vector.memset(hs[d][:], 0.0), writes=[rhs_[d]])
            P.op("dve", lambda d=d: nc.vector.memset(hsb[d][:], 0.0), writes=[rhsb[d]])
        orders = [list(range(NCH)), [1, 0] + list(range(NCH - 1, 1, -1))]

        def p1(d, c):
            sl = slice(c * 128, (c + 1) * 128)
            tot_col = 127 if d == 0 else 0
            da, rda, _ = dtab.next()
            P.op("pool", lambda: nc.gpsimd.tensor_scalar(out=da[:], in0=onesf[:], scalar1=dta[:, c, d:d + 1], scalar2=None, op0=ALU.mult), reads=[ronesf, rdta], writes=[rda])
            pcr, rpcr, _ = gen.next()
            P.op("pe", lambda: nc.tensor.matmul(pcr[:, 0:128], lhsT=da[:], rhs=U[:, d, :], start=True, stop=True), reads=[rda, rU], writes=[rpcr])
            P.op("pe", lambda: nc.tensor.matmul(pcr[:, 128:130], lhsT=U[:, d, :], rhs=dta[:, c, :], start=True, stop=True), reads=[rU, rdta], writes=[rpcr])
            cc, rcc, _ = ccol.next()
            P.op("dve", lambda: nc.vector.tensor_copy(out=cc[:, 0:1], in_=pcr[:, 128 + d:129 + d]), reads=[rpcr], writes=[rcc])
            P.op("dve", lambda: nc.vector.tensor_copy(out=cc[:, 1:2], in_=pcr[:, tot_col:tot_col + 1]), reads=[rpcr], writes=[rcc])
            P.op("act", lambda: nc.scalar.activation(out=cc[:, 3:4], in_=cc[:, 1:2], func=AF.Exp), reads=[rcc], writes=[rcc])
            dt_, rdd, _ = dd.next()
            P.op("dve", lambda: nc.vector.tensor_scalar(out=dt_[:], in0=pcr[:, 0:128], scalar1=cc[:, 0:1], scalar2=0.0, op0=ALU.subtract, op1=ALU.min), reads=[rpcr, rcc], writes=[rdd])
            e, re_, _ = ee.next()
            P.op("act", lambda: nc.scalar.activation(out=e[:], in_=dt_[:], func=AF.Exp), reads=[rdd], writes=[re_])
            er, rer, _ = ecr.next()
            P.op("act", lambda: nc.scalar.activation(out=er[:], in_=pcr[:, 0:128], func=AF.Exp), reads=[rpcr], writes=[rer])
            pg, rpg, _ = gen.next()
            P.op("pe", lambda: nc.tensor.matmul(pg[:, 0:128], lhsT=bT[:, sl], rhs=cT[:, sl], start=True, stop=True), reads=[rb_, rc_], writes=[rpg])
            g, rg_, _ = gm.next()
            P.op("dve", lambda: nc.vector.tensor_tensor(out=g[:], in0=pg[:, 0:128], in1=U[:, d, :], op=ALU.mult), reads=[rpg, rU], writes=[rg_])
            m, rm, _ = mt.next()
            P.op("pool", lambda: nc.gpsimd.tensor_tensor(out=m[:], in0=g[:], in1=e[:], op=ALU.mult), reads=[rg_, re_], writes=[rm])
            csb, rcs, _ = cs.next()
            P.op("pool", lambda: nc.gpsimd.tensor_tensor(out=csb[:], in0=cT[:, sl], in1=er[:], op=ALU.mult), reads=[rc_, rer], writes=[rcs])
            xd, rxd, _ = xdt.next()
            P.op("dve", lambda: nc.vector.tensor_scalar(out=xd[:], in0=xtokA[:, c, :], scalar1=dt[:, c, d:d + 1], scalar2=None, op0=ALU.mult), reads=[rxt, rdt], writes=[rxd])
            de, rde, _ = ccol.next()
            P.op("act", lambda: nc.scalar.activation(out=de[:, 0:1], in_=cc[:, 0:1], func=AF.Exp, scale=-1.0, bias=cc[:, 1:2]), reads=[rcc], writes=[rde])
            P.op("dve", lambda: nc.vector.tensor_tensor(out=de[:, 1:2], in0=de[:, 0:1], in1=dt[:, c, d:d + 1], op=ALU.mult), reads=[rde, rdt], writes=[rde])
            xe, rxe, _ = xdd.next()
            P.op("dve", lambda: nc.vector.tensor_scalar(out=xe[:], in0=xtokA[:, c, :], scalar1=de[:, 1:2], scalar2=None, op0=ALU.mult), reads=[rxt, rde], writes=[rxe])
            pst, rpst, _ = gen.next()
            P.op("pe", lambda: nc.tensor.matmul(pst[:, 0:64], lhsT=btokA[:, c, :], rhs=xe[:], start=True, stop=True), reads=[rbtk, rxe], writes=[rpst])
            st_, rst_, _ = sst.next()
            P.op("act", lambda: nc.scalar.copy(out=st_[:], in_=pst[:, 0:64]), reads=[rpst], writes=[rst_])
            return (cc, rcc, m, rm, csb, rcs, xd, rxd, st_, rst_)

        def p2(d, c, hnd):
            cc, rcc, m, rm, csb, rcs, xd, rxd, st_, rst_ = hnd
            sl = slice(c * 128, (c + 1) * 128)
            hb, rhb = hcur[d] if hcur[d] is not None else (hsb[d], rhsb[d])
            py, rpy, _ = gen.next()
            P.op("pe", lambda: nc.tensor.matmul(py[0:64, 0:128], lhsT=xd[:], rhs=m[:], start=True, stop=False), reads=[rxd, rm], writes=[rpy])
            P.op("pe", lambda: nc.tensor.matmul(py[0:64, 0:128], lhsT=hb[:], rhs=csb[:], start=False, stop=True), reads=[rhb, rcs], writes=[rpy])
            P.op("dve", lambda: nc.vector.scalar_tensor_tensor(out=hs[d][:], in0=hs[d][:], scalar=cc[:, 3:4], in1=st_[:], op0=ALU.mult, op1=ALU.add),
                 reads=[rhs_[d], rcc, rst_], writes=[rhs_[d]])
            hn, rhn, _ = hsr.next()
            P.op("dve", lambda: nc.vector.tensor_copy(out=hn[:], in_=hs[d][:]), reads=[rhs_[d]], writes=[rhn])
            hcur[d] = (hn, rhn)
            P.op("dve", lambda: nc.vector.tensor_tensor(out=yb[:, sl], in0=yb[:, sl], in1=py[0:64, 0:128], op=ALU.add), reads=[ryc[c], rpy], writes=[ryc[c]])

        pend = None
        for s_ in range(NCH + 1):
            cur = None
            if s_ < NCH:
                cur = [(d, orders[d][s_], p1(d, orders[d][s_])) for d in range(2)]
            if pend is not None:
                for (d, c, hnd) in pend:
                    p2(d, c, hnd)
            pend = cur
        ry_all = ryc
        for i in range(3):
            P.dma("sp", YSc[i][:, :], yb[:, i * YCH:(i + 1) * YCH], ds_, reads=ry_all, writes=[r_YS])
        P.pop_scope()
        for i in range(3):
            allgather(P, YSc[i], YGc[i], [r_YS], [r_YG])

        P.push_scope()
        oTs = P.sb("oTs", [128, 6, NTOK], BF16)
        P.push_scope()
        ost = Rot(P, "tost", 2, [128, 4, 256], F32)

        def emit_oT(o, ro, ncol, tile0, chunk0, mla):
            for c in range(ncol):
                for half in range(2):
                    pt, rpt, _ = A.misc.next()
                    P.op("pe", lambda c=c, half=half, pt=pt: nc.tensor.transpose(pt[:, 0:128], o[:, c, half * 128:(half + 1) * 128], ident[:]), reads=[ro, rid], writes=[rpt])
                    evac(oTs[:, chunk0 + half, (tile0 + c) * 128:(tile0 + c + 1) * 128], pt[:, 0:128], [rpt], [roT])

        P.push_scope()
        EXT = 2816
        dn = P.dsem(f"na{l}")
        q = P.sb("naq", [64, 4, NQ], BF16); rq = P.res()
        k = P.sb("nak", [64, 4, EXT], BF16); rk = P.res()
        v = P.sb("nav", [128, EXT // 128, 4, 65], BF16); rv = P.res()
        qc = P.sb("naqc", [64, 4, NCX], BF16); kc_ = P.sb("nakc", [64, 4, NCX], BF16); vc = P.sb("navc", [128, 2, 4, 65], BF16); rc = P.res()
        P.op("dve", lambda: nc.vector.memset(v[:], 1.0), writes=[rv])
        P.op("dve", lambda: nc.vector.memset(vc[:], 1.0), writes=[rc])
        hk = Rot(P, "hk", 2, [64, 4, 384], F32, dma=True)
        hacc = Rot(P, "hacc", 2, [64, 384], F32)
        hv = Rot(P, "hv", 2, [128, 4, 768], F32, dma=True)
        hvacc = Rot(P, "hvacc", 2, [128, 768], F32)

        def halo_k(dst, row0, npart, width, side, hkr, haccr, rowperm=None):
            t_, rt_, dt__ = hkr.next()
            c0 = NQ - width if side == 0 else 0
            for r in range(4):
                for (d0_, s0_, nr_) in (rowperm or ((0, 0, npart),)):
                    P.dma("sp", t_[d0_:d0_ + nr_, r, :width], rbuf(r, row0 + s0_, nr_, c0, c0 + width), dt__, reads=[r_RB2], writes=[rt_])
            a_, ra_, _ = haccr.next()
            f0 = 4 if side == 0 else 8
            P.op("dve", lambda: nc.vector.tensor_scalar(out=a_[:npart, :width], in0=t_[:npart, 0, :width], scalar1=flg[:npart, f0:f0 + 1], scalar2=None, op0=ALU.mult),
                 reads=[rt_, rflg], writes=[ra_])
            for r in range(1, 4):
                P.op("dve", lambda r=r: nc.vector.scalar_tensor_tensor(out=a_[:npart, :width], in0=t_[:npart, r, :width], scalar=flg[:npart, f0 + r:f0 + r + 1], in1=a_[:npart, :width],
                                                                     op0=ALU.mult, op1=ALU.add), reads=[rt_, rflg, ra_], writes=[ra_])
            return a_, ra_

        def halo_v(col0, ncolv, ntile, side, hvr, hvaccr):
            t_, rt_, dt__ = hvr.next()
            r0 = NQ - ntile * 128 if side == 0 else 0
            for r in range(4):
                P.dma("sp", t_[:, r, 0:ntile * ncolv].rearrange("p (t c) -> p t c", c=ncolv),
                      vgbuf(r, r0, ntile * 128, col0, col0 + ncolv).rearrange("(t p) c -> p t c", p=128), dt__, reads=[r_VG], writes=[rt_])
            a_, ra_, _ = hvaccr.next()
            f0 = 4 if side == 0 else 8
            w_ = ntile * ncolv
            P.op("dve", lambda: nc.vector.tensor_scalar(out=a_[:, :w_], in0=t_[:, 0, :w_], scalar1=flg[:, f0:f0 + 1], scalar2=None, op0=ALU.mult), reads=[rt_, rflg], writes=[ra_])
            for r in range(1, 4):
                P.op("dve", lambda r=r: nc.vector.scalar_tensor_tensor(out=a_[:, :w_], in0=t_[:, r, :w_], scalar=flg[:, f0 + r:f0 + r + 1], in1=a_[:, :w_], op0=ALU.mult, op1=ALU.add),
                     reads=[rt_, rflg, ra_], writes=[ra_])
            return a_, ra_

        for h in range(4):
            P.dma("pool", q[:, h, :], pT_d[h * 64:(h + 1) * 64, 0:NQ], dn, reads=[r_pT], writes=[rq])
            P.dma("pool", k[:, h, 384:384 + NQ], pT_d[256 + h * 64:256 + (h + 1) * 64, 0:NQ], dn, reads=[r_pT], writes=[rk])
            P.dma("pool", qc[:, h, :], pT_d[h * 64:(h + 1) * 64, NQ:NTOK], dn, reads=[r_pT], writes=[rc])
            P.dma("pool", kc_[:, h, :], pT_d[256 + h * 64:256 + (h + 1) * 64, NQ:NTOK], dn, reads=[r_pT], writes=[rc])
            for side in range(2):
                a_, ra_ = halo_k(None, R_NAK + h * 64, 64, 384, side, hk, hacc)
                off = 0 if side == 0 else 384 + NQ
                P.op("act", lambda a_=a_, off=off, h=h: nc.scalar.copy(out=k[:, h, off:off + 384], in_=a_[:64, :384]), reads=[ra_], writes=[rk])
        vsrc = vtok_d[:, 0:256].rearrange("(t p) (h d) -> p t h d", p=128, h=4)
        for t in range(16):
            P.dma("pool", v[:, 3 + t, :, 0:64], vsrc[:, t, :, :], dn, reads=[r_vtok], writes=[rv])
        for t in range(2):
            P.dma("pool", vc[:, t, :, 0:64], vsrc[:, 16 + t, :, :], dn, reads=[r_vtok], writes=[rc])
        for side in range(2):
            a_, ra_ = halo_v(0, 256, 3, side, hv, hvacc)
            t0_ = 0 if side == 0 else 19
            for t in range(3):
                P.op("act", lambda a_=a_, t=t, t0_=t0_: nc.scalar.copy(out=v[:, t0_ + t, :, 0:64], in_=a_[:, t * 256:(t + 1) * 256].rearrange("p (h d) -> p h d", h=4)),
                     reads=[ra_], writes=[rv])
        bias = Rot(P, "nab", 2, [128, 7, 512], F32, dma=True)
        scale = 64 ** -0.5
        for t in range(16 if final else 18):
            items = []
            if t < 16:
                pat = 0 if t == 0 else 1 if t == 1 else 3 if t == 14 else 4 if t == 15 else 2
                bt_, rb, db = bias.next()
                P.dma("sp", bt_[:], nab_d[l][pat], db, writes=[rb])
                qa = [[q[:, h, t * 128:(t + 1) * 128]] for h in range(4)]
                for kt in (range(1, 6) if pat == 2 else range(7)):
                    et = t + kt
                    items.append(dict(k=[[k[:, h, et * 128:(et + 1) * 128]] for h in range(4)], v=[v[:, et, h, :] for h in range(4)], bias=bt_[:, kt, :], bres=[rb], res=[rk, rv]))
                qres = [rq]
            else:
                tc = t - 16
                qa = [[qc[:, h, tc * 128:(tc + 1) * 128]] for h in range(4)]
                qres = [rc]
            for kt in range(2):
                items.append(dict(k=[[kc_[:, h, kt * 128:(kt + 1) * 128]] for h in range(4)], v=[vc[:, kt, h, :] for h in range(4)], bias=None, res=[rc]))
            o, ro, _ = ost.next()
            attend(A, 4, False, qa, qres, items, scale, [o[:, 0, h * 64:(h + 1) * 64] for h in range(4)], ro)
            emit_oT(o, ro, 1, t, 0, False)
        P.pop_scope()

        P.push_scope()
        EXT = 2304
        dn = P.dsem(f"sw{l}")
        q = P.sb("swq", [64, 4, NQ], BF16); rq = P.res()
        k = P.sb("swk", [64, 2, EXT], BF16); rk = P.res()
        v = P.sb("swv", [128, EXT // 128, 2, 65], BF16); rv = P.res()
        qc = P.sb("swqc", [64, 4, NCX], BF16); kc_ = P.sb("swkc", [64, 2, NCX], BF16); vc = P.sb("swvc", [128, 2, 2, 65], BF16); rc = P.res()
        P.op("dve", lambda: nc.vector.memset(v[:], 1.0), writes=[rv])
        P.op("dve", lambda: nc.vector.memset(vc[:], 1.0), writes=[rc])
        cs_ = P.sb("swcs", [64, 2, EXT], F32); rcs_ = P.res()
        bs = P.sb("swb", [128, 4, 128], F32); rbs = P.res()
        sk = P.sb("swsk", [128, 4], F32); rsk = P.res()
        P.dma("sp", cs_[:, 0, :], swcs_d[0], dn, writes=[rcs_]); P.dma("sp", cs_[:, 1, :], swcs_d[1], dn, writes=[rcs_])
        P.dma("sp", bs[:], swb_d, dn, writes=[rbs]); P.dma("sp", sk[:], swk_d[l], dn, writes=[rsk])
        P.op("act", lambda: nc.scalar.activation(out=sk[:], in_=sk[:], func=AF.Exp), reads=[rsk], writes=[rsk])
        stg = Rot(P, "swstg", 4, [64, EXT], F32, dma=True)
        hk2 = Rot(P, "hk2", 2, [64, 4, 128], F32, dma=True)
        hacc2 = Rot(P, "hacc2", 4, [64, 128], F32)
        hv2 = Rot(P, "hv2", 2, [128, 4, 128], F32, dma=True)
        hvacc2 = Rot(P, "hvacc2", 2, [128, 128], F32)
        perm = ((0, 16), (16, 0), (32, 48), (48, 32))

        def rope_rows(dst, res, row0, col0, n, tab0, extra_writer=None):
            a, ra, da = stg.next(); b, rb_, db = stg.next()
            P.dma("sp", a[:, :n], pT_d[row0:row0 + 64, col0:col0 + n], da, reads=[r_pT], writes=[ra])
            for (d0, s0) in perm:
                P.dma("sp", b[d0:d0 + 16, :n], pT_d[row0 + s0:row0 + s0 + 16, col0:col0 + n], db, reads=[r_pT], writes=[rb_])
            P.op("dve", lambda: nc.vector.tensor_tensor(out=a[:, :n], in0=a[:, :n], in1=cs_[:, 0, tab0:tab0 + n], op=ALU.mult), reads=[ra, rcs_], writes=[ra])
            P.op("pool", lambda: nc.gpsimd.tensor_tensor(out=b[:, :n], in0=b[:, :n], in1=cs_[:, 1, tab0:tab0 + n], op=ALU.mult), reads=[rb_, rcs_], writes=[rb_])
            P.op("dve", lambda: nc.vector.tensor_tensor(out=dst, in0=a[:, :n], in1=b[:, :n], op=ALU.add), reads=[ra, rb_], writes=[res])

        for h in range(4):
            rope_rows(q[:, h, :], rq, SW0 + h * 64, 0, NQ, 128)
            P.dma("pool", qc[:, h, :], pT_d[SW0 + h * 64:SW0 + (h + 1) * 64, NQ:NTOK], dn, reads=[r_pT], writes=[rc])
        for g in range(2):
            rope_rows(k[:, g, 128:128 + NQ], rk, SW0 + 256 + g * 64, 0, NQ, 128)
            P.dma("pool", kc_[:, g, :], pT_d[SW0 + 256 + g * 64:SW0 + 256 + (g + 1) * 64, NQ:NTOK], dn, reads=[r_pT], writes=[rc])
            for side in range(2):
                a_, ra_ = halo_k(None, R_SWK + g * 64, 64, 128, side, hk2, hacc2)
                b_, rb2 = halo_k(None, R_SWK + g * 64, 64, 128, side, hk2, hacc2, rowperm=[(d0, s0, 16) for (d0, s0) in perm])
                tab0 = 0 if side == 0 else 128 + NQ
                P.op("dve", lambda a_=a_, tab0=tab0: nc.vector.tensor_tensor(out=a_[:64, :128], in0=a_[:64, :128], in1=cs_[:, 0, tab0:tab0 + 128], op=ALU.mult), reads=[ra_, rcs_], writes=[ra_])
                P.op("dve", lambda b_=b_, tab0=tab0: nc.vector.tensor_tensor(out=b_[:64, :128], in0=b_[:64, :128], in1=cs_[:, 1, tab0:tab0 + 128], op=ALU.mult), reads=[rb2, rcs_], writes=[rb2])
                P.op("dve", lambda a_=a_, b_=b_, g=g, tab0=tab0: nc.vector.tensor_tensor(out=k[:, g, tab0:tab0 + 128], in0=a_[:64, :128], in1=b_[:64, :128], op=ALU.add),
                     reads=[ra_, rb2], writes=[rk])
        vsrc = vtok_d[:, 256:384].rearrange("(t p) (h d) -> p t h d", p=128, h=2)
        for t in range(16):
            P.dma("pool", v[:, 1 + t, :, 0:64], vsrc[:, t, :, :], dn, reads=[r_vtok], writes=[rv])
        for t in range(2):
            P.dma("pool", vc[:, t, :, 0:64], vsrc[:, 16 + t, :, :], dn, reads=[r_vtok], writes=[rc])
        for side in range(2):
            a_, ra_ = halo_v(256, 128, 1, side, hv2, hvacc2)
            t0_ = 0 if side == 0 else 17
            P.op("act", lambda a_=a_, t0_=t0_: nc.scalar.copy(out=v[:, t0_, :, 0:64], in_=a_[:, 0:128].rearrange("p (h d) -> p h d", h=2)), reads=[ra_], writes=[rv])
        bp = P.sb("swbp", [128, 4, 4, 128], F32); rbp = P.res()
        for kind in range(4):
            for h in range(4):
                P.op("dve", lambda kind=kind, h=h: nc.vector.tensor_copy(out=bp[:, kind, h, :], in_=bs[:, kind, :]), reads=[rbs], writes=[rbp])
        for t in range(16 if final else 18):
            items = []
            if t < 16:
                qa = [[q[:, h, t * 128:(t + 1) * 128]] for h in range(4)]
                qres = [rq]
                for kt in range(3):
                    et = t + kt
                    if kt == 0:
                        b = bp[:, 0 if t == 0 else 1, :, :].rearrange('p h q -> p (h q)')
                    elif kt == 2:
                        b = bp[:, 3 if t == 15 else 2, :, :].rearrange('p h q -> p (h q)')
                    else:
                        b = None
                    items.append(dict(k=[[k[:, h // 2, et * 128:(et + 1) * 128]] for h in range(4)], v=[v[:, et, h // 2, :] for h in range(4)], bias=b, bres=[rbp], res=[rk, rv]))
            else:
                tc = t - 16
                qa = [[qc[:, h, tc * 128:(tc + 1) * 128]] for h in range(4)]
                qres = [rc]
            for kt in range(2):
                items.append(dict(k=[[kc_[:, h // 2, kt * 128:(kt + 1) * 128]] for h in range(4)], v=[vc[:, kt, h // 2, :] for h in range(4)], bias=None, res=[rc]))
            o, ro, _ = ost.next()
            attend(A, 4, False, qa, qres, items, scale, [o[:, 0, h * 64:(h + 1) * 64] for h in range(4)], ro, sinkexp=sk[:, 0:4], sink_res=rsk)
            emit_oT(o, ro, 1, t, 2, False)
        P.pop_scope()

        P.push_scope()
        NK = LSEQ
        dsm = P.dsem(f"ml{l}")
        gkv = P.sb("gkv", [128, 1], F32); gq = P.sb("gq", [128, 2], F32); rg = P.res()
        P.dma("sp", gkv[:], mgkv_d[l], dsm, writes=[rg]); P.dma("sp", gq[:], mgq_d[l], dsm, writes=[rg])
        wkn = P.sb("wkn", [128, 256], BF16); wkv = P.sb("wkv", [128, 256], BF16)
        wqn = P.sb("wqn", [128, 2, 256], BF16); wqr = P.sb("wqr", [128, 2, 128], BF16); wqrs = P.sb("wqrs", [128, 2, 128], BF16)
        rw = P.res(); dw = P.dsem(f"mw{l}")
        P.dma("pool", wkn[:], mwkn_d[l], dw, writes=[rw]); P.dma("pool", wkv[:], mwkv_d[l], dw, writes=[rw])
        for c in range(2):
            P.dma("pool", wqn[:, c, :], mwqn_d[l][c * 128:(c + 1) * 128, :], dw, writes=[rw])
            P.dma("pool", wqr[:, c, :], mwqr_d[l][c * 128:(c + 1) * 128, :], dw, writes=[rw])
            P.dma("pool", wqrs[:, c, :], mwqrs_d[l][c * 128:(c + 1) * 128, :], dw, writes=[rw])
        K96 = P.sb("K96", [96, 4, NK], BF16); rkn = P.res(); rkr = P.res()
        vm = P.sb("vm", [128, NK // 128, 4, 65], BF16); rvm = P.res()
        Q96 = P.sb("Q96", [96, 4, NTOK], BF16); rqn = P.res(); rqr = P.res()
        rrow = P.sb("rrow", [65, 512], F32); rrr_ = P.res()
        bcs = P.sb("bcs", [64, 512], F32); rbcs = P.res()
        P.op("dve", lambda: nc.vector.memset(vm[:], 1.0), writes=[rvm])
        xin = Rot(P, "mx", 2, [128, 2, 512], F32, dma=True)
        sq = Rot(P, "msq", 1, [128, 2, 512], BF16)
        rms = Rot(P, "mrms", 1, [128, 512], F32)
        xn = Rot(P, "mxn", 2, [128, 2, 512], BF16)
        tab = Rot(P, "mtab", 2, [32, 2, 512], F32, dma=True)
        rr4 = Rot(P, "mrr", 4, [32, 512], F32, dma=True)
        perm32 = ((0, 8), (8, 0), (16, 24), (24, 16))

        def key_src(row0, nrows, t0, n):
            out = []
            if t0 < 256:
                ln = min(n, 256 - t0)
                out.append((rbuf(0, row0, nrows, NQ + t0, NQ + t0 + ln), 0))
                if ln < n:
                    for (r, ls, l2, do_) in seg_lat(0, n - ln):
                        out.append((rbuf(r, row0, nrows, ls, ls + l2), ln + do_))
            else:
                for (r, ls, l2, do_) in seg_lat(t0 - 256, n):
                    out.append((rbuf(r, row0, nrows, ls, ls + l2), do_))
            return out

        def norm_blk(loads, kc, n, g):
            x, rx, dx = xin.next()
            for (c, p0, ap, off) in loads:
                P.dma("sp", x[p0:p0 + ap.shape[0], c, off:off + ap.shape[1]], ap, dx, reads=[r_RB2, r_pT], writes=[rx])
            s, rs, _ = sq.next(); r, rr, _ = rms.next()
            rstd_of(x, rx, kc, n, r, rr, s, rs, kc * 128)
            y, ry, _ = xn.next()
            for c in range(kc):
                P.op("dve", lambda c=c: nc.vector.scalar_tensor_tensor(out=y[:, c, :n], in0=x[:, c, :n], scalar=g[:, c:c + 1], in1=r[:, :n], op0=ALU.mult, op1=ALU.mult),
                     reads=[rx, rr, rg], writes=[ry])
            return y, ry

        for t0 in range(0, NK, 512):
            n = min(512, NK - t0)
            y, ry = norm_blk([(0, hf * 64, ap, off) for hf in range(2) for (ap, off) in key_src(R_CKV + hf * 64, 64, t0, n)], 1, n, gkv)
            for pr in range(2):
                pt, rpt, _ = A.misc.next()
                P.op("pe", lambda pr=pr, pt=pt: nc.tensor.matmul(pt[:, :n], lhsT=wkn[:, pr * 128:(pr + 1) * 128], rhs=y[:, 0, :n], start=True, stop=True), reads=[rw, ry], writes=[rpt])
                evac(K96[0:64, 2 * pr, t0:t0 + n], pt[0:64, :n], [rpt], [rkn])
                evac(K96[0:64, 2 * pr + 1, t0:t0 + n], pt[64:128, :n], [rpt], [rkn])
            for tt_ in range(n // 128):
                kt = t0 // 128 + tt_
                pt, rpt, _ = A.misc.next()
                P.op("pe", lambda tt_=tt_, pt=pt: nc.tensor.matmul(pt[:, 0:256], lhsT=y[:, 0, tt_ * 128:(tt_ + 1) * 128], rhs=wkv[:], start=True, stop=True), reads=[rw, ry], writes=[rpt])
                evac(vm[:, kt, :, 0:64], pt[:, 0:256].rearrange("p (h d) -> p h d", h=4), [rpt], [rvm])
            tb, rtb, dtb = tab.next()
            P.dma("sp", tb[:, 0, :n], mkcs_d[0][:, t0:t0 + n], dtb, writes=[rtb]); P.dma("sp", tb[:, 1, :n], mkcs_d[1][:, t0:t0 + n], dtb, writes=[rtb])
            a, ra, da = rr4.next(); b, rb_, db = rr4.next()
            for (ap, off) in key_src(R_KR, 32, t0, n):
                P.dma("sp", a[:, off:off + ap.shape[1]], ap, da, reads=[r_RB2], writes=[ra])
            for (d0, s0) in perm32:
                for (ap, off) in key_src(R_KR + s0, 8, t0, n):
                    P.dma("sp", b[d0:d0 + 8, off:off + ap.shape[1]], ap, db, reads=[r_RB2], writes=[rb_])
            P.op("dve", lambda: nc.vector.tensor_tensor(out=a[:, :n], in0=a[:, :n], in1=tb[:, 0, :n], op=ALU.mult), reads=[ra, rtb], writes=[ra])
            P.op("pool", lambda: nc.gpsimd.tensor_tensor(out=b[:, :n], in0=b[:, :n], in1=tb[:, 1, :n], op=ALU.mult), reads=[rb_, rtb], writes=[rb_])
            for h4 in range(4):
                P.op("dve" if h4 % 2 == 0 else "pool", lambda h4=h4: (nc.vector if h4 % 2 == 0 else nc.gpsimd).tensor_tensor(out=K96[64:96, h4, t0:t0 + n], in0=a[:, :n], in1=b[:, :n], op=ALU.add),
                     reads=[ra, rb_], writes=[rkr])
        for t0 in range(0, NTOK, 512):
            n = min(512, NTOK - t0)
            y, ry = norm_blk([(c, 0, pT_d[ML0 + c * 128:ML0 + (c + 1) * 128, t0:t0 + n], 0) for c in range(2)], 2, n, gq)
            for pr in range(2):
                pt, rpt, _ = A.misc.next()
                for c in range(2):
                    P.op("pe", lambda pr=pr, pt=pt, c=c: nc.tensor.matmul(pt[:, :n], lhsT=wqn[:, c, pr * 128:(pr + 1) * 128], rhs=y[:, c, :n], start=(c == 0), stop=(c == 1)),
                         reads=[rw, ry], writes=[rpt])
                evac(Q96[0:64, 2 * pr, t0:t0 + n], pt[0:64, :n], [rpt], [rqn])
                evac(Q96[0:64, 2 * pr + 1, t0:t0 + n], pt[64:128, :n], [rpt], [rqn])
            tb, rtb, dtb = tab.next()
            P.dma("sp", tb[:, 0, :n], mqcs_d[0][:, t0:t0 + n], dtb, writes=[rtb]); P.dma("sp", tb[:, 1, :n], mqcs_d[1][:, t0:t0 + n], dtb, writes=[rtb])
            for h in range(4):
                pa, rpa, _ = A.misc.next()
                for c in range(2):
                    P.op("pe", lambda pa=pa, c=c, h=h: nc.tensor.matmul(pa[0:32, :n], lhsT=wqr[:, c, h * 32:(h + 1) * 32], rhs=y[:, c, :n], start=(c == 0), stop=(c == 1)),
                         reads=[rw, ry], writes=[rpa])
                a, ra, _ = rr4.next()
                P.op("dve", lambda a=a, pa=pa: nc.vector.tensor_tensor(out=a[:, :n], in0=pa[0:32, :n], in1=tb[:, 0, :n], op=ALU.mult), reads=[rpa, rtb], writes=[ra])
                pb, rpb, _ = A.misc.next()
                for c in range(2):
                    P.op("pe", lambda pb=pb, c=c, h=h: nc.tensor.matmul(pb[0:32, :n], lhsT=wqrs[:, c, h * 32:(h + 1) * 32], rhs=y[:, c, :n], start=(c == 0), stop=(c == 1)),
                         reads=[rw, ry], writes=[rpb])
                b, rb_, _ = rr4.next()
                P.op("dve", lambda b=b, pb=pb: nc.vector.tensor_tensor(out=b[:, :n], in0=pb[0:32, :n], in1=tb[:, 1, :n], op=ALU.mult), reads=[rpb, rtb], writes=[rb_])
                P.op("pool", lambda a=a, b=b, h=h: nc.gpsimd.tensor_tensor(out=Q96[64:96, h, t0:t0 + n], in0=a[:, :n], in1=b[:, :n], op=ALU.add), reads=[ra, rb_], writes=[rqr])
        scale = 96 ** -0.5
        groups = [(g * 512, 4, NK // 128) for g in range(4)] + ([] if final else [(2048, 2, 2)])
        for (q0, ncol, nkt) in groups:
            W_ = ncol * 128
            for h in range(4):
                acc, racc, _ = A.accs.next()
                def score(kt):
                    pan, rpan, _ = A.panels.next()
                    P.op("pe", lambda: nc.tensor.matmul(pan[:, 0:W_], lhsT=K96[0:96, h, kt * 128:(kt + 1) * 128], rhs=Q96[0:96, h, q0:q0 + W_], start=True, stop=True),
                         reads=[rkn, rkr, rqn, rqr], writes=[rpan])
                    pt_, rpt_, _ = A.pT.next()
                    P.op("act", lambda: nc.scalar.activation(out=pt_[:, 0:W_], in_=pan[:, 0:W_], func=AF.Exp, scale=scale), reads=[rpan], writes=[rpt_])
                    return pt_, rpt_
                nxt = score(0)
                for kt in range(nkt):
                    pt_, rpt_ = nxt
                    if kt + 1 < nkt:
                        nxt = score(kt + 1)
                    P.op("pe", lambda: nc.tensor.matmul(acc[0:65, 0:W_], lhsT=vm[:, kt, h, :], rhs=pt_[:, 0:W_], start=(kt == 0), stop=(kt == nkt - 1)),
                         reads=[rvm, rpt_], writes=[racc])
                P.op("dve", lambda: nc.vector.reciprocal(out=rrow[64:65, 0:W_], in_=acc[64:65, 0:W_]), reads=[racc], writes=[rrr_])
                pb_, rpb_, _ = A.misc.next()
                P.op("pe", lambda: nc.tensor.matmul(pb_[0:64, 0:W_], lhsT=onesf[64:65, 0:64], rhs=rrow[64:65, 0:W_], start=True, stop=True), reads=[ronesf, rrr_], writes=[rpb_])
                P.op("act", lambda: nc.scalar.copy(out=bcs[:, 0:W_], in_=pb_[0:64, 0:W_]), reads=[rpb_], writes=[rbcs])
                hp_ = (h % 2) * 64
                P.op("dve", lambda: nc.vector.tensor_tensor(out=oTs[hp_:hp_ + 64, 4 + h // 2, q0:q0 + W_], in0=acc[0:64, 0:W_], in1=bcs[:, 0:W_], op=ALU.mult),
                     reads=[racc, rbcs], writes=[roT])
        P.pop_scope()
        P.pop_scope()

        NB = 256
        fblocks = [(t0, NB, 0 if t0 < NQ else 1) for t0 in range(0, NTOK, NB)]
        P.push_scope()
        wo = P.sb("wo", [128, KC, D], BF16); rwo = P.res(); dwo = P.dsem(f"wo{l}")
        for kk in range(KC):
            P.dma("pool", wo[:, kk, :], wo_d[l][kk * 128:(kk + 1) * 128, :], dwo, writes=[rwo])
        hin = Rot(P, "fhin", 3, [128, KC, NB], F32, dma=True)
        ycand = Rot(P, "ycand", 3, [128, 2, 4, NB], F32, dma=True)
        yin = Rot(P, "yin", 3, [128, 2, NB], F32)
        zin = Rot(P, "zin", 3, [128, 2, NB], F32, dma=True)
        sq = Rot(P, "fsq", 2, [128, 2, NB], BF16)
        rms = Rot(P, "frms", 2, [128, NB], F32)
        od = Rot(P, "fod", 3, [128, 2, NB], BF16)
        def f_prep(t0, n, j):
            h, rh, dh = hin.next()
            P.dma("sp", h[:, :, :n], hT_d.rearrange("(k p) t -> p k t", p=128)[:, :, t0:t0 + n], dh, reads=[r_hin], writes=[rh])
            y, ry, _ = yin.next()
            if j == 0:
                yc, ryc, dyc = ycand.next()
                for kk in range(2):
                    for r in range(4):
                        col = 256 + r * NQ + t0
                        P.dma("sp", yc[:, kk, r, :n], YGc[col // YCH][kk * 128:(kk + 1) * 128, col % YCH:col % YCH + n], dyc, reads=[r_YG], writes=[ryc])
                for kk in range(2):
                    P.op("dve", lambda kk=kk: nc.vector.tensor_scalar(out=y[:, kk, :n], in0=yc[:, kk, 0, :n], scalar1=flg[:, 0:1], scalar2=None, op0=ALU.mult),
                         reads=[ryc, rflg], writes=[ry])
                    for r in range(1, 4):
                        P.op("dve", lambda kk=kk, r=r: nc.vector.scalar_tensor_tensor(out=y[:, kk, :n], in0=yc[:, kk, r, :n], scalar=flg[:, r:r + 1], in1=y[:, kk, :n],
                                                                                    op0=ALU.mult, op1=ALU.add), reads=[ryc, rflg, ry], writes=[ry])
            else:
                yc, ryc, dyc = ycand.next()
                for kk in range(2):
                    P.dma("sp", yc[:, kk, 0, :n], YGc[0][kk * 128:(kk + 1) * 128, 0:256], dyc, reads=[r_YG], writes=[ryc])
                P.op("dve", lambda: nc.vector.tensor_copy(out=y[:, :, :n], in_=yc[:, :, 0, :n]), reads=[ryc], writes=[ry])
            z, rz, dz = zin.next()
            P.dma("sp", z[:, :, :n], pT_d[SS0:SS0 + 256, :].rearrange("(k p) t -> p k t", p=128)[:, :, t0:t0 + n], dz, reads=[r_pT], writes=[rz])
            P.op("act", lambda: nc.scalar.activation(out=z[:, :, :n], in_=z[:, :, :n], func=AF.Silu), reads=[rz], writes=[rz])
            P.op("dve", lambda: nc.vector.tensor_tensor(out=y[:, :, :n], in0=y[:, :, :n], in1=z[:, :, :n], op=ALU.mult), reads=[ry, rz], writes=[ry])
            s, rs, _ = sq.next(); r, rr, _ = rms.next()
            rstd_of(y, ry, 2, n, r, rr, s, rs, 256)
            odt, rod, _ = od.next()
            for kk in range(2):
                P.op("dve", lambda kk=kk: nc.vector.scalar_tensor_tensor(out=odt[:, kk, :n], in0=y[:, kk, :n], scalar=gvs[:, 16 + kk:17 + kk], in1=r[:, :n], op0=ALU.mult, op1=ALU.mult),
                     reads=[ry, rgv, rr], writes=[rod])
            return h, rh, dh, odt, rod

        def f_main(t0, n, j, h, rh, dh, odt, rod):
            for cb in range(KC):
                pt, rpt, _ = gen.next()
                for kk in range(KC):
                    rhs = oTs[:, kk, t0:t0 + n] if kk < 6 else odt[:, kk - 6, :n]
                    P.op("pe", lambda kk=kk, rhs=rhs, pt=pt: nc.tensor.matmul(pt[:, :n], lhsT=wo[:, kk, cb * 128:(cb + 1) * 128], rhs=rhs, start=(kk == 0), stop=(kk == KC - 1)),
                         reads=[rwo, roT, rod], writes=[rpt])
                P.op("dve", lambda cb=cb, pt=pt: nc.vector.scalar_tensor_tensor(out=h[:, cb, :n], in0=pt[:, :n], scalar=mo[:, 16 + cb, j:j + 1], in1=h[:, cb, :n], op0=ALU.mult, op1=ALU.add),
                     reads=[rpt, rmo, rh], writes=[rh])
            P.dma("sp", h2_d.rearrange("(k p) t -> p k t", p=128)[:, :, t0:t0 + n], h[:, :, :n], dh, reads=[rh], writes=[r_h2])
        pend_ = None
        for blk in [b_ for b_ in fblocks if not (final and b_[2] == 1)] + [None]:
            cur_ = (blk, f_prep(*blk)) if blk is not None else None
            if pend_ is not None:
                f_main(*pend_[0], *pend_[1])
            pend_ = cur_
        P.pop_scope()
        P.pop_scope()
        P.push_scope()
        w1 = P.sb("w1", [128, KC, 4 * D], BF16); rw1 = P.res(); dw1 = P.dsem(f"w1{l}")
        w2 = P.sb("w2", [128, 32, D], BF16); rw2 = P.res(); dw2 = P.dsem(f"w2{l}")
        for kk in range(KC):
            for c0 in range(0, 4 * D, 2048):
                P.dma("pool", w1[:, kk, c0:c0 + 2048], w1_d[l][kk * 128:(kk + 1) * 128, c0:c0 + 2048], dw1, writes=[rw1])
        for kk in range(32):
            P.dma("pool", w2[:, kk, :], w2_d[l][kk * 128:(kk + 1) * 128, :], dw2, writes=[rw2])
        NB2 = 512
        hin = Rot(P, "h2in", 1, [128, KC, NB2], F32, dma=True)
        rms = Rot(P, "rms2", 1, [128, NB2], F32)
        tt = Rot(P, "tt2", 2, [128, NB2], F32)
        xm = Rot(P, "xm2", 1, [128, KC, NB2], BF16)
        at = Rot(P, "at", 1, [128, 32, NB2], BF16)
        rl = Rot(P, "rl", 2, [128, NB2], F32)
        for (t0, n, j) in [(t0, NB2, 0) for t0 in range(0, NQ, NB2)] + [(NQ, NCX, 1)]:
            if final and j == 1:
                continue
            h, rh, dh = hin.next()
            P.dma("sp", h[:, :, :n], h2_d.rearrange("(k p) t -> p k t", p=128)[:, :, t0:t0 + n], dh, reads=[r_h2], writes=[rh])
            a, ra, _ = at.next()
            s, rs = a, ra
            r, rr, _ = rms.next()
            rstd_of(h, rh, KC, n, r, rr, s, rs, D)
            x, rx, _ = xm.next()
            for kk in range(KC):
                t, rt, _ = tt.next()
                P.op("dve", lambda kk=kk, t=t: nc.vector.tensor_tensor(out=t[:, :n], in0=h[:, kk, :n], in1=r[:, :n], op=ALU.mult), reads=[rh, rr], writes=[rt])
                P.op("act", lambda kk=kk, t=t: nc.scalar.activation(out=x[:, kk, :n], in_=t[:, :n], func=AF.Identity, scale=gs2[:, kk, j:j + 1], bias=mo[:, 24 + kk, j:j + 1]),
                     reads=[rt, rgs, rmo], writes=[rx])
            for cb in range(32):
                pt, rpt, _ = gen.next()
                for kk in range(KC):
                    P.op("pe", lambda kk=kk, pt=pt: nc.tensor.matmul(pt[:, :n], lhsT=w1[:, kk, cb * 128:(cb + 1) * 128], rhs=x[:, kk, :n], start=(kk == 0), stop=(kk == KC - 1)),
                         reads=[rw1, rx], writes=[rpt])
                qq, rq_, _ = rl.next()
                P.op("act", lambda pt=pt, qq=qq: nc.scalar.activation(out=qq[:, :n], in_=pt[:, :n], func=AF.Relu), reads=[rpt], writes=[rq_])
                P.op("pool", lambda cb=cb, qq=qq: nc.gpsimd.tensor_tensor(out=a[:, cb, :n], in0=qq[:, :n], in1=qq[:, :n], op=ALU.mult), reads=[rq_], writes=[ra])
            for cb in range(KC):
                pt, rpt, _ = gen.next()
                for kk in range(32):
                    P.op("pe", lambda kk=kk, pt=pt: nc.tensor.matmul(pt[:, :n], lhsT=w2[:, kk, cb * 128:(cb + 1) * 128], rhs=a[:, kk, :n], start=(kk == 0), stop=(kk == 31)),
                         reads=[rw2, ra], writes=[rpt])
                P.op("dve", lambda cb=cb, pt=pt: nc.vector.scalar_tensor_tensor(out=h[:, cb, :n], in0=pt[:, :n], scalar=mo[:, 40 + cb, j:j + 1], in1=h[:, cb, :n], op0=ALU.mult, op1=ALU.add),
                     reads=[rpt, rmo, rh], writes=[rh])
            if final:
                r, rr, _ = rms.next()
                rstd_of(h, rh, KC, n, r, rr, a, ra, D)
                for kk in range(KC):
                    P.op("dve", lambda kk=kk: nc.vector.scalar_tensor_tensor(out=h[:, kk, :n], in0=h[:, kk, :n], scalar=gvs[:, 8 + kk:9 + kk], in1=r[:, :n], op0=ALU.mult, op1=ALU.mult),
                         reads=[rh, rgv, rr], writes=[rh])
                P.dma("sp", out_d.rearrange("(k p) t -> p k t", p=128)[:, :, t0:t0 + n], h[:, :, :n], dh, reads=[rh], writes=[P.res()], final=True)
            else:
                last = (l == nlayers - 1)
                if last and j == 0:
                    P.dma("sp", out_d.rearrange("(k p) t -> p k t", p=128)[:, :, t0:t0 + n], h[:, :, :n], dh, reads=[rh], writes=[P.res()], final=True)
                P.dma("sp", hTn_d.rearrange("(k p) t -> p k t", p=128)[:, :, t0:t0 + n], h[:, :, :n], dh, reads=[rh], writes=[r_hn])
        P.pop_scope()
    P.finish()
    P.close()
    return P


NEG = -1e30
NA0, SW0, ML0, SS0 = 0, 768, 1280, 1696


def cvec(v, n):
    return np.ascontiguousarray(v.reshape(n, 128).T)


def swap_idx(dim):
    q = dim // 4
    idx = np.arange(dim)
    blk = (idx // q) % 2
    return np.where(blk == 0, idx + q, idx - q)


def rope_tables(pos, dim):
    nf = dim // 4
    inv = (1.0 / (10000.0 ** (np.arange(nf, dtype=np.float32) / nf))).astype(np.float32)
    row = (pos // 64).astype(np.float32)
    col = (pos % 64).astype(np.float32)
    d = np.arange(dim)
    f = d % nf
    p = np.where((d < dim // 2)[:, None], row[None, :], col[None, :]).astype(np.float32)
    ang = (p * inv[f][:, None]).astype(np.float32)
    sign = np.where(((d // nf) % 2) == 0, -1.0, 1.0).astype(np.float32)
    return np.cos(ang).astype(np.float32), (np.sin(ang) * sign[:, None]).astype(np.float32)


def ext_rows(a, lo, hi):
    S = a.shape[0]
    out = np.zeros((hi - lo,) + a.shape[1:], a.dtype)
    l2, h2 = max(lo, 0), min(hi, S)
    out[l2 - lo:h2 - lo] = a[l2:h2]
    return out


def na_bias_tile(rpb, T):
    q = np.arange(128)
    r = 2 * T + q // 64
    qc = q % 64
    rs = np.clip(r - 4, 0, 120)
    cst = np.clip(qc - 8, 0, 48)
    out = np.full((128, 7, 4, 128), NEG, np.float32)
    i = np.arange(128)
    for kt in range(7):
        krow = 2 * T - 6 + 2 * kt + i // 64
        kcol = i % 64
        valid = ((krow[:, None] >= rs[None, :]) & (krow[:, None] < rs[None, :] + 8) & (krow[:, None] >= 0) & (krow[:, None] < 128)
                 & (kcol[:, None] >= cst[None, :]) & (kcol[:, None] < cst[None, :] + 16))
        dr = np.clip(krow[:, None] - r[None, :] + 7, 0, 14)
        dc = np.clip(kcol[:, None] - qc[None, :] + 15, 0, 30)
        for h in range(4):
            out[:, kt, h, :] = np.where(valid, rpb[h][dr, dc], NEG)
    return out


def prep_T(p_b, pc_b, j, W, l):
    o0 = 2048 * j
    T = lambda a: np.ascontiguousarray(a.T)
    m = {}
    own = p_b[o0:o0 + 2048]
    m["na_q"] = T(own[:, 0:256])
    e = ext_rows(p_b[:, 256:768], o0 - 384, o0 + 2048 + 384)
    m["na_k"] = T(e[:, 0:256]); m["na_v"] = np.ascontiguousarray(e[:, 256:512])
    m["na_qc"] = T(pc_b[:, 0:256]); m["na_kc"] = T(pc_b[:, 256:512]); m["na_vc"] = np.ascontiguousarray(pc_b[:, 512:768])
    rpb = W["na_rpb"][l]
    m["na_bias"] = np.stack([na_bias_tile(rpb, 16 * j + t).reshape(128, 7, 512) for t in (0, 1, 8 if j in (0, 3) else 2, 14, 15)])
    sw = swap_idx(64)
    q = own[:, SW0:SW0 + 256].reshape(2048, 4, 64)
    m["sw_q"] = T(q.reshape(2048, 256)); m["sw_qs"] = T(q[:, :, sw].reshape(2048, 256))
    e = ext_rows(p_b[:, SW0 + 256:SW0 + 512], o0 - 128, o0 + 2048 + 128)
    k = e[:, 0:128].reshape(2304, 2, 64)
    m["sw_k"] = T(k.reshape(2304, 128)); m["sw_ks"] = T(k[:, :, sw].reshape(2304, 128))
    m["sw_v"] = np.ascontiguousarray(e[:, 128:256])
    pos = np.arange(o0 - 128, o0 + 2048 + 128)
    c, s = rope_tables(np.clip(pos, 0, 8191), 64)
    m["sw_cs"] = np.stack([c, s])
    m["sw_qc"] = T(pc_b[:, SW0:SW0 + 256]); m["sw_kc"] = T(pc_b[:, SW0 + 256:SW0 + 384]); m["sw_vc"] = np.ascontiguousarray(pc_b[:, SW0 + 384:SW0 + 512])
    i = np.arange(128)
    prev = np.where(i[:, None] >= i[None, :], 0.0, NEG).astype(np.float32)
    nxt = np.where(i[:, None] <= i[None, :], 0.0, NEG).astype(np.float32)
    allneg = np.full((128, 128), NEG, np.float32)
    m["sw_bias"] = np.ascontiguousarray(np.stack([allneg if j == 0 else prev, prev, nxt, allneg if j == 3 else nxt], 1))
    m["sw_sink"] = np.ascontiguousarray(np.broadcast_to(W["swa_sink"][l][None, :], (128, 4)))
    sw32 = swap_idx(32)
    allk = np.concatenate([pc_b[:, ML0 + 256:ML0 + 416], p_b[:, ML0 + 256:ML0 + 416]], 0)
    m["m_ckv"] = T(allk[:, 0:128]); m["m_kr"] = T(allk[:, 128:160]); m["m_krs"] = T(allk[:, 128:160][:, sw32])
    c, s = rope_tables(np.arange(8192), 32)
    c = np.concatenate([np.ones((32, 256), np.float32), c], 1); s = np.concatenate([np.zeros((32, 256), np.float32), s], 1)
    m["m_kcs"] = np.stack([c, s])
    m["m_cq"] = T(np.concatenate([own[:, ML0:ML0 + 256], pc_b[:, ML0:ML0 + 256]], 0))
    c, s = rope_tables(np.arange(o0, o0 + 2048), 32)
    c = np.concatenate([c, np.ones((32, 256), np.float32)], 1); s = np.concatenate([s, np.zeros((32, 256), np.float32)], 1)
    m["m_qcs"] = np.stack([c, s])
    m["m_gkv"] = np.ascontiguousarray(W["mla_g_kv"][l].reshape(128, 1)); m["m_gq"] = cvec(W["mla_g_q"][l], 2)
    wkv = W["mla_w_ukv"][l].reshape(128, 4, 128)
    m["m_wkn"] = np.ascontiguousarray(wkv[:, :, 0:64].reshape(128, 256)); m["m_wkv"] = np.ascontiguousarray(wkv[:, :, 64:128].reshape(128, 256))
    wq = W["mla_w_uq"][l].reshape(256, 4, 96)
    m["m_wqn"] = np.ascontiguousarray(wq[:, :, 0:64].reshape(256, 256))
    m["m_wqr"] = np.ascontiguousarray(wq[:, :, 64:96].reshape(256, 128))
    m["m_wqrs"] = np.ascontiguousarray(wq[:, :, 64:96][:, :, sw32].reshape(256, 128))
    return m


def prep_S(p_b, pc_b, hd, W, l):
    g = hd // 2
    T = lambda a: np.ascontiguousarray(a.T)
    allp = np.concatenate([pc_b[:, SS0:], p_b[:, SS0:]], 0)
    xbc = allp[:, 256:1024]
    m = {}
    m["s_x"] = T(xbc[:, hd * 64:(hd + 1) * 64])
    m["s_b"] = T(xbc[:, 256 + g * 128:256 + (g + 1) * 128])
    m["s_c"] = T(xbc[:, 512 + g * 128:512 + (g + 1) * 128])
    cwv = np.concatenate([W["ssd_conv_w"][l], W["ssd_conv_b"][l][None, :]], 0)
    cw = np.zeros((128, 3, 6), np.float32)
    cw[:64, 0, :] = cwv[:, hd * 64:(hd + 1) * 64].T
    cw[:, 1, :] = cwv[:, 256 + g * 128:256 + (g + 1) * 128].T
    cw[:, 2, :] = cwv[:, 512 + g * 128:512 + (g + 1) * 128].T
    m["s_cw"] = cw
    dt = allp[:, 1024:1032].reshape(-1, 2, 4)[:, :, hd]
    m["s_dt"] = np.ascontiguousarray(dt.reshape(66, 128, 2).transpose(1, 0, 2))
    par = np.zeros((128, 8), np.float32)
    par[:, 0:2] = W["ssd_dt_bias"][l][:, hd]; par[:, 2:4] = W["ssd_a_log"][l][:, hd]; par[:, 4] = W["ssd_d"][l][hd]
    m["s_par"] = par
    i = np.arange(128)
    m["s_u"] = np.stack([(i[:, None] <= i[None, :]), (i[:, None] >= i[None, :])]).astype(np.float32)
    m["s_id"] = np.eye(128, dtype=np.float32)
    return m


def prep_M(I, core):
    b, j = core // 4, core % 4
    T = lambda a: np.ascontiguousarray(a.T)
    L = 4
    m = {}
    m["hT0"] = T(np.concatenate([I['x'][b, 2048 * j:2048 * (j + 1)], I['ctx'][b]], 0))
    m["cv"] = np.ascontiguousarray(np.stack([cvec(I['c'][b], 8), cvec(I['c_ctx'], 8)], -1))
    flg = np.zeros((128, 16), np.float32)
    flg[:, j] = 1.0
    if j > 0: flg[:, 4 + j - 1] = 1.0
    if j < 3: flg[:, 8 + j + 1] = 1.0
    m["flg"] = flg
    selx = np.zeros((128, 2, 64), np.float32); selb = np.zeros((128, 2, 128), np.float32)
    for d in range(64): selx[(j % 2) * 64 + d, j // 2, d] = 1.0
    for n in range(128): selb[n, j // 2, n] = 1.0
    m["selx"] = selx; m["selb"] = selb
    m["w_in"] = I['w_in']; m["w_mod"] = I['w_mod']; m["w_out"] = I['w_out']; m["w1"] = I['w_mlp1']; m["w2"] = I['w_mlp2']
    m["bmod"] = np.stack([cvec(I['b_mod'][l], 48) for l in range(L)])
    m["g1"] = np.stack([cvec(I['g_norm1'][l], 8) for l in range(L)])
    gv = np.zeros((L, 128, 20), np.float32)
    for l in range(L):
        gv[l, :, 0:8] = cvec(I['g_norm2'][l], 8); gv[l, :, 8:16] = cvec(I['g_final'], 8); gv[l, :, 16:18] = cvec(I['ssd_g_norm'][l], 2)
    m["gv"] = gv
    m["na_bias"] = np.stack([np.stack([na_bias_tile(I['na_rpb'][l], 16 * j + t).reshape(128, 7, 512) for t in (0, 1, 8 if j in (0, 3) else 2, 14, 15)]) for l in range(L)])
    i = np.arange(128)
    prev = np.where(i[:, None] >= i[None, :], 0.0, NEG).astype(np.float32)
    nxt = np.where(i[:, None] <= i[None, :], 0.0, NEG).astype(np.float32)
    allneg = np.full((128, 128), NEG, np.float32)
    m["sw_bias"] = np.ascontiguousarray(np.stack([allneg if j == 0 else prev, prev, nxt, allneg if j == 3 else nxt], 1))
    m["sw_sink"] = np.stack([np.ascontiguousarray(np.broadcast_to(I['swa_sink'][l][None, :], (128, 4))) for l in range(L)])
    o0 = 2048 * j
    c, s = rope_tables(np.clip(np.arange(o0 - 128, o0 + 2048 + 128), 0, 8191), 64)
    m["sw_cs"] = np.stack([c, s])
    c, s = rope_tables(np.arange(8192), 32)
    m["m_kcs"] = np.stack([np.concatenate([np.ones((32, 256), np.float32), c], 1), np.concatenate([np.zeros((32, 256), np.float32), s], 1)])
    c, s = rope_tables(np.arange(o0, o0 + 2048), 32)
    m["m_qcs"] = np.stack([np.concatenate([c, np.ones((32, 256), np.float32)], 1), np.concatenate([s, np.zeros((32, 256), np.float32)], 1)])
    sw32 = swap_idx(32)
    m["m_gkv"] = np.stack([I['mla_g_kv'][l].reshape(128, 1) for l in range(L)])
    m["m_gq"] = np.stack([cvec(I['mla_g_q'][l], 2) for l in range(L)])
    wkv = I['mla_w_ukv'].reshape(L, 128, 4, 128)
    m["m_wkn"] = np.ascontiguousarray(wkv[:, :, :, 0:64].reshape(L, 128, 256)); m["m_wkv"] = np.ascontiguousarray(wkv[:, :, :, 64:128].reshape(L, 128, 256))
    wq = I['mla_w_uq'].reshape(L, 256, 4, 96)
    m["m_wqn"] = np.ascontiguousarray(wq[:, :, :, 0:64].reshape(L, 256, 256))
    m["m_wqr"] = np.ascontiguousarray(wq[:, :, :, 64:96].reshape(L, 256, 128))
    m["m_wqrs"] = np.ascontiguousarray(wq[:, :, :, 64:96][:, :, :, sw32].reshape(L, 256, 128))
    hd, g = j, j // 2
    cw = np.zeros((L, 128, 3, 6), np.float32); par = np.zeros((L, 128, 8), np.float32)
    for l in range(L):
        cwv = np.concatenate([I['ssd_conv_w'][l], I['ssd_conv_b'][l][None, :]], 0)
        cw[l, :64, 0, :] = cwv[:, hd * 64:(hd + 1) * 64].T
        cw[l, :, 1, :] = cwv[:, 256 + g * 128:256 + (g + 1) * 128].T
        cw[l, :, 2, :] = cwv[:, 512 + g * 128:512 + (g + 1) * 128].T
        par[l, :, 0:2] = I['ssd_dt_bias'][l][:, hd]; par[l, :, 2:4] = I['ssd_a_log'][l][:, hd]; par[l, :, 4] = I['ssd_d'][l][hd]
    m["s_cw"] = cw; m["s_par"] = par
    m["s_u"] = np.stack([(i[:, None] <= i[None, :]), (i[:, None] >= i[None, :])]).astype(np.float32)
    m["ident"] = np.eye(128, dtype=np.float32)
    return m


from concourse.bass_utils import run_bass_kernel_spmd

_PROG = {}


def kernel(**inputs):
    I = {k: np.asarray(v, dtype=np.float32) for k, v in inputs.items()}
    if "M" not in _PROG:
        _PROG["M"] = build_M(4, 3, 4)
    P = _PROG["M"]
    in_maps = [prep_M(I, c) for c in range(8)]
    res = run_bass_kernel_spmd(P.nc, in_maps, core_ids=list(range(8)))
    out = np.stack([np.concatenate([res.results[b * 4 + j]["out"].T for j in range(4)], 0) for b in range(2)])
    return np.ascontiguousarray(out.astype(np.float32))
```
